# Optimizing a Trainium2 kernel written in Bass

```python
import jax, jax.numpy as jnp
from jax import lax
import numpy as np

D_MODEL = 1024
BATCH = 16
SEQ = 2048
DEPTH = 2

MIX_WIDTH = D_MODEL
SSD_WIDTH = MIX_WIDTH // 2
SSD_HEADDIM = 64
SSD_HEADS = SSD_WIDTH // SSD_HEADDIM
SSD_GROUPS = 2
SSD_STATE = 128
SSD_CONV = 5
SSD_CHUNK = 128
RWKV_WIDTH = MIX_WIDTH - SSD_WIDTH
RWKV_HEADSIZE = 64
RWKV_HEADS = RWKV_WIDTH // RWKV_HEADSIZE
DECAY_LORA = 64
ICLR_LORA = 64
GATE_LORA = 128
D_FF = 4 * D_MODEL

ALPHA = float((2 * DEPTH) ** 0.25)
BETA = float((8 * DEPTH) ** -0.25)
LN_EPS = 1e-5
RMS_EPS = 1e-5
GN_EPS = 64e-5

SSD_XBC = SSD_WIDTH + 2 * SSD_GROUPS * SSD_STATE
SSD_COLS = SSD_WIDTH + SSD_XBC + 2 * SSD_HEADS
RWKV_COLS = 3 * RWKV_WIDTH + DECAY_LORA + ICLR_LORA + GATE_LORA
IN_COLS = SSD_COLS + RWKV_COLS

kernel_name = "hymba_ssd_rwkv7_deepnorm_encoder"


def layer_norm(x, g, b):
    xf = x.astype(jnp.float32)
    mu = jnp.mean(xf, -1, keepdims=True)
    var = jnp.mean(jnp.square(xf - mu), -1, keepdims=True)
    return ((xf - mu) * lax.rsqrt(var + LN_EPS) * g + b).astype(x.dtype)


def centred_depthwise_conv(u, w, b):
    pad = (w.shape[0] - 1) // 2
    y = lax.conv_general_dilated(u, w[:, None, :].astype(u.dtype), window_strides=(1,),
                                 padding=[(pad, pad)], dimension_numbers=('NWC', 'WIO', 'NWC'),
                                 feature_group_count=u.shape[-1])
    return y + b


def segsum_exp(a):
    n = a.shape[-1]
    cs = jnp.cumsum(a, -1)
    diff = cs[..., :, None] - cs[..., None, :]
    mask = jnp.tril(jnp.ones((n, n), dtype=bool))
    return jnp.exp(jnp.where(mask, diff, -jnp.inf))


def ssd_chunked(xs, a, Bm, Cm):
    b, T, H, P = xs.shape
    G, N = Bm.shape[2], Bm.shape[3]
    E = H // G
    c = T // SSD_CHUNK
    L = SSD_CHUNK
    f32 = jnp.float32
    x = xs.astype(f32).reshape(b, c, L, G, E, P)
    a = a.astype(f32).reshape(b, c, L, G, E).transpose(0, 3, 4, 1, 2)
    Bc = Bm.astype(f32).reshape(b, c, L, G, N)
    Cc = Cm.astype(f32).reshape(b, c, L, G, N)
    a_cs = jnp.cumsum(a, -1)
    lmat = segsum_exp(a)
    scores = jnp.einsum('bclgn,bcsgn->bgcls', Cc, Bc)
    y_diag = jnp.einsum('bgecls,bcsgep->bclgep', scores[:, :, None] * lmat, x)
    decay_states = jnp.exp(a_cs[..., -1:] - a_cs)
    states = jnp.einsum('bclgn,bgecl,bclgep->bcgepn', Bc, decay_states, x)
    a_last = jnp.pad(a_cs[..., -1], ((0, 0), (0, 0), (0, 0), (1, 0)))
    chunk_decay = segsum_exp(a_last)
    states = jnp.concatenate([jnp.zeros_like(states[:, :1]), states], 1)
    states_in = jnp.einsum('bgezc,bcgepn->bzgepn', chunk_decay, states)[:, :-1]
    y_off = jnp.einsum('bclgn,bcgepn,bgecl->bclgep', Cc, states_in, jnp.exp(a_cs))
    return (y_diag + y_off).reshape(b, T, H, P)


def ssd_bidir(z, xbc, dt_raw, conv_w, conv_b, dt_bias, a_log, d_skip, norm_g):
    b, T, _ = z.shape
    H, P, G, N = SSD_HEADS, SSD_HEADDIM, SSD_GROUPS, SSD_STATE
    xbc = jax.nn.silu(centred_depthwise_conv(xbc, conv_w, conv_b))
    xs = xbc[..., :SSD_WIDTH].reshape(b, T, H, P).astype(jnp.float32)
    Bm = xbc[..., SSD_WIDTH:SSD_WIDTH + G * N].reshape(b, T, G, N)
    Cm = xbc[..., SSD_WIDTH + G * N:].reshape(b, T, G, N)
    dt = jax.nn.softplus(dt_raw.reshape(b, T, 2, H).astype(jnp.float32) + dt_bias)
    da = dt * (-jnp.exp(a_log.astype(jnp.float32)))
    flip = lambda t: jnp.flip(t, 1)
    y_f = ssd_chunked(xs * dt[:, :, 0, :, None], da[:, :, 0], Bm, Cm)
    y_b = flip(ssd_chunked(flip(xs * dt[:, :, 1, :, None]), flip(da[:, :, 1]), flip(Bm), flip(Cm)))
    y = (y_f + y_b + d_skip[:, None] * xs).reshape(b, T, SSD_WIDTH)
    y = (y * jax.nn.silu(z.astype(jnp.float32))).reshape(b, T, G, SSD_WIDTH // G)
    y = y * lax.rsqrt(jnp.mean(y * y, -1, keepdims=True) + RMS_EPS)
    return y.reshape(b, T, SSD_WIDTH) * norm_g


def centred_token_shift(u, mu):
    p = jnp.pad(u, ((0, 0), (1, 1), (0, 0)))
    neighbours = 0.5 * (p[:, :-2] + p[:, 2:])
    return u + (neighbours - u) * mu


def rwkv7_step(S, inp):
    r, w, k, v, kk, a = inp
    s_kk = jnp.einsum('dbhvk,dbhk->dbhv', S, kk)
    S = S * w[..., None, :] - s_kk[..., None] * (kk * a)[..., None, :] + v[..., :, None] * k[..., None, :]
    y = jnp.einsum('dbhvk,dbhk->dbhv', S, r)
    return S, y


def _directional(t_f, t_b):
    t = jnp.stack([t_f, jnp.flip(t_b, 1)], 0)
    return jnp.moveaxis(t, 2, 0)


def rwkv7_bidir(u, mu, w0, w_up, a0, a_up, g_up, k_k, k_a, r_k, lnx_g, lnx_b):
    b, T, _ = u.shape
    H, N, W = RWKV_HEADS, RWKV_HEADSIZE, RWKV_WIDTH
    f32 = jnp.float32
    heads = lambda t: t.reshape(t.shape[:-1] + (H, N)).astype(f32)
    u = centred_token_shift(u, mu)
    o1, o2, o3 = W, 2 * W, 3 * W
    o4, o5 = o3 + DECAY_LORA, o3 + DECAY_LORA + ICLR_LORA
    r, k, v = u[..., :o1], u[..., o1:o2], u[..., o2:o3]
    xw, xa, xg = u[..., o3:o4], u[..., o4:o5], u[..., o5:]
    w_log = -jax.nn.softplus(-(w0 + jnp.einsum('btr,drc->btdc', jnp.tanh(xw), w_up))) - 0.5
    decay = jnp.exp(-jnp.exp(w_log.astype(f32)))
    a = jax.nn.sigmoid(a0 + jnp.einsum('btr,drc->btdc', xa, a_up))
    g = jax.nn.sigmoid(xg) @ g_up
    kk = heads(k * k_k)
    kk = kk / jnp.maximum(jnp.sqrt(jnp.sum(kk * kk, -1, keepdims=True)), 1e-12)
    k_dir = k[:, :, None] * (1.0 + (a - 1.0) * k_a)
    rh, vh = heads(r), heads(v)
    kd, ad, wd = heads(k_dir), heads(a), heads(decay)
    inputs = (_directional(rh, rh), _directional(wd[:, :, 0], wd[:, :, 1]),
              _directional(kd[:, :, 0], kd[:, :, 1]), _directional(vh, vh),
              _directional(kk, kk), _directional(ad[:, :, 0], ad[:, :, 1]))
    S0 = jnp.zeros((2, b, H, N, N), f32)
    _, ys = lax.scan(rwkv7_step, S0, inputs)
    y = jnp.moveaxis(ys[:, 0], 0, 1) + jnp.flip(jnp.moveaxis(ys[:, 1], 0, 1), 1)
    mu_y = jnp.mean(y, -1, keepdims=True)
    var_y = jnp.mean(jnp.square(y - mu_y), -1, keepdims=True)
    y = ((y - mu_y) * lax.rsqrt(var_y + GN_EPS)).reshape(b, T, W) * lnx_g + lnx_b
    bonus = jnp.sum(jnp.sum(rh[:, :, None] * kd * r_k, -1, keepdims=True), 2) * vh
    return (y + bonus.reshape(b, T, W)) * g


def hybrid_mixer(h, w_in, conv_w, conv_b, dt_bias, a_log, d_skip, ssd_norm_g,
                 mu_rwkv, w0, w_up, a0, a_up, g_up, k_k, k_a, r_k, lnx_g, lnx_b, w_out):
    proj = h @ w_in
    z = proj[..., :SSD_WIDTH]
    xbc = proj[..., SSD_WIDTH:SSD_WIDTH + SSD_XBC]
    dt_raw = proj[..., SSD_WIDTH + SSD_XBC:SSD_COLS]
    rw = proj[..., SSD_COLS:]
    y_ssd = ssd_bidir(z, xbc, dt_raw, conv_w, conv_b, dt_bias, a_log, d_skip, ssd_norm_g)
    y_rwkv = rwkv7_bidir(rw, mu_rwkv, w0, w_up, a0, a_up, g_up, k_k, k_a, r_k, lnx_g, lnx_b)
    y = jnp.concatenate([y_ssd, y_rwkv], -1).astype(h.dtype)
    return y @ w_out


def sq_relu_mlp(h, w_fc, w_proj):
    return jnp.square(jax.nn.relu(h @ w_fc)) @ w_proj


def setup_inputs(seed: int = 0) -> dict:
    key = jax.random.key(seed)
    ks = jax.random.split(key, 32)
    f32 = jnp.float32
    nrm = lambda k, s: jax.random.normal(k, s, f32)
    uni = lambda k, s, lo, hi: jax.random.uniform(k, s, f32, lo, hi)
    col_scale = np.ones((IN_COLS,), np.float32)
    col_scale[SSD_WIDTH:2 * SSD_WIDTH] = BETA
    col_scale[SSD_COLS + 2 * RWKV_WIDTH:SSD_COLS + 3 * RWKV_WIDTH] = BETA
    dt0 = jnp.exp(uni(ks[5], (DEPTH, 2, SSD_HEADS), float(np.log(1e-3)), float(np.log(1e-1))))
    return {
        "x": nrm(ks[0], (BATCH, SEQ, D_MODEL)),
        "ln0_g": 1.0 + 0.02 * nrm(ks[1], (D_MODEL,)),
        "ln0_b": 0.02 * nrm(ks[2], (D_MODEL,)),
        "w_in": nrm(ks[3], (DEPTH, D_MODEL, IN_COLS)) * (D_MODEL ** -0.5) * jnp.asarray(col_scale),
        "conv_w": nrm(ks[4], (DEPTH, SSD_CONV, SSD_XBC)) * (SSD_CONV ** -0.5),
        "conv_b": 0.02 * nrm(ks[6], (DEPTH, SSD_XBC)),
        "dt_bias": dt0 + jnp.log(-jnp.expm1(-dt0)),
        "a_log": jnp.log(uni(ks[7], (DEPTH, 2, SSD_HEADS), 1.0, 16.0)),
        "d_skip": 1.0 + 0.02 * nrm(ks[8], (DEPTH, SSD_HEADS)),
        "ssd_norm_g": 1.0 + 0.02 * nrm(ks[9], (DEPTH, SSD_WIDTH)),
        "mu_rwkv": uni(ks[10], (DEPTH, RWKV_COLS), 0.0, 1.0),
        "w0": uni(ks[11], (DEPTH, 2, RWKV_WIDTH), -6.5, -1.5),
        "w_up": 0.1 * nrm(ks[12], (DEPTH, 2, DECAY_LORA, RWKV_WIDTH)) * (DECAY_LORA ** -0.5),
        "a0": 0.1 * nrm(ks[13], (DEPTH, 2, RWKV_WIDTH)),
        "a_up": 0.5 * nrm(ks[14], (DEPTH, 2, ICLR_LORA, RWKV_WIDTH)) * (ICLR_LORA ** -0.5),
        "g_up": nrm(ks[15], (DEPTH, GATE_LORA, RWKV_WIDTH)) * (GATE_LORA ** -0.5),
        "k_k": 0.85 + 0.02 * nrm(ks[16], (DEPTH, RWKV_WIDTH)),
        "k_a": 1.0 + 0.02 * nrm(ks[17], (DEPTH, RWKV_WIDTH)),
        "r_k": -0.04 + 0.02 * nrm(ks[18], (DEPTH, RWKV_HEADS, RWKV_HEADSIZE)),
        "lnx_g": 1.0 + 0.02 * nrm(ks[19], (DEPTH, RWKV_WIDTH)),
        "lnx_b": 0.02 * nrm(ks[20], (DEPTH, RWKV_WIDTH)),
        "w_out": nrm(ks[21], (DEPTH, MIX_WIDTH, D_MODEL)) * (MIX_WIDTH ** -0.5) * BETA,
        "ln1_g": 1.0 + 0.02 * nrm(ks[22], (DEPTH, D_MODEL)),
        "ln1_b": 0.02 * nrm(ks[23], (DEPTH, D_MODEL)),
        "w_fc": nrm(ks[24], (DEPTH, D_MODEL, D_FF)) * (D_MODEL ** -0.5),
        "w_proj": nrm(ks[25], (DEPTH, D_FF, D_MODEL)) * (D_FF ** -0.5) * BETA,
        "ln2_g": 1.0 + 0.02 * nrm(ks[26], (DEPTH, D_MODEL)),
        "ln2_b": 0.02 * nrm(ks[27], (DEPTH, D_MODEL)),
    }


def reference(x, ln0_g, ln0_b, w_in, conv_w, conv_b, dt_bias, a_log, d_skip, ssd_norm_g,
              mu_rwkv, w0, w_up, a0, a_up, g_up, k_k, k_a, r_k, lnx_g, lnx_b, w_out,
              ln1_g, ln1_b, w_fc, w_proj, ln2_g, ln2_b):
    h = layer_norm(x, ln0_g, ln0_b)
    for l in range(DEPTH):
        mix = hybrid_mixer(h, w_in[l], conv_w[l], conv_b[l], dt_bias[l], a_log[l], d_skip[l],
                           ssd_norm_g[l], mu_rwkv[l], w0[l], w_up[l], a0[l], a_up[l], g_up[l],
                           k_k[l], k_a[l], r_k[l], lnx_g[l], lnx_b[l], w_out[l])
        h = layer_norm(ALPHA * h + mix, ln1_g[l], ln1_b[l])
        h = layer_norm(ALPHA * h + sq_relu_mlp(h, w_fc[l], w_proj[l]), ln2_g[l], ln2_b[l])
    return h
```

```python
import contextlib
import numpy as np
import concourse.bass as bass
import concourse.mybir as mybir
from concourse.bass_utils import run_bass_kernel_spmd

F32 = mybir.dt.float32
BF16 = mybir.dt.bfloat16
AF = mybir.ActivationFunctionType
ALU = mybir.AluOpType
AX = mybir.AxisListType

ENGS = ('pe', 'act', 'dve', 'pool', 'sp')


class Buf:
    __slots__ = ('name', 'last_w', 'readers')

    def __init__(self, name=''):
        self.name = name
        self.last_w = None
        self.readers = []


class _Op:
    __slots__ = ('eng', 'idx', 'fn', 'deps', 'is_dma', 'dslot', 'dval', 'needs_inc', 'cnt', 'waits', 'prog')

    def __init__(self, eng, idx, fn, is_dma):
        self.eng = eng
        self.idx = idx
        self.fn = fn
        self.is_dma = is_dma
        self.deps = []
        self.dslot = 0
        self.dval = 0
        self.needs_inc = False
        self.cnt = 0
        self.waits = []
        self.prog = None


class SemState:
    def __init__(self, nc, ring=8, stack=None):
        self.stack = stack if stack is not None else contextlib.ExitStack()
        st = self.stack
        self.csem = {e: st.enter_context(nc.semaphore('c_' + e)) for e in ENGS if e != 'sp'}
        self.dsem = {e: [st.enter_context(nc.semaphore('d_%s_%d' % (e, i))) for i in range(ring)] for e in ('sp', 'pool', 'act')}
        self.c_off = {e: 0 for e in ENGS}
        self.d_cnt = {e: 0 for e in ENGS}


class Prog:
    def __init__(self, nc, sems=None, ring=8):
        self.nc = nc
        self.q = {e: [] for e in ENGS}
        self.ring = ring
        self.dma_hist = {e: [] for e in ENGS}
        self.sems = sems if sems is not None else SemState(nc, ring)

    def _add(self, eng, fn, reads, writes, is_dma):
        op = _Op(eng, len(self.q[eng]), fn, is_dma)
        op.prog = self
        deps = []
        for b in reads:
            if b.last_w is not None:
                deps.append(b.last_w)
        for b in writes:
            if b.last_w is not None:
                deps.append(b.last_w)
            deps.extend(b.readers)
        if is_dma:
            h = self.dma_hist[eng]
            k = len(h)
            kg = k + self.sems.d_cnt[eng]
            op.dslot = kg % self.ring
            op.dval = 16 * (kg // self.ring + 1)
            if k >= self.ring:
                deps.append(h[k - self.ring])
            h.append(op)
        seen = set()
        for d in deps:
            if d is not op and id(d) not in seen and getattr(d, 'prog', self) is self:
                seen.add(id(d))
                op.deps.append(d)
        for b in writes:
            b.last_w = op
            b.readers = []
        for b in reads:
            if b.last_w is not op:
                b.readers.append(op)
        self.q[eng].append(op)
        return op

    def op(self, eng, fn, reads=(), writes=()):
        return self._add(eng, fn, reads, writes, False)

    def dma(self, eng, fn, reads=(), writes=()):
        return self._add(eng, fn, reads, writes, True)


    @staticmethod
    def _bufs(lst):
        return [b.buf if hasattr(b, 'buf') else b for b in lst]

    def mm(self, out, lhsT, rhs, start=True, stop=True, r=(), w=()):
        return self.op('pe', lambda e: e.matmul(out, lhsT, rhs, start=start, stop=stop), self._bufs(r), self._bufs(w))

    def tr(self, out, in_, ident, r=(), w=()):
        return self.op('pe', lambda e: e.transpose(out, in_, ident), self._bufs(r), self._bufs(w))

    def act(self, out, in_, func, bias=None, scale=None, accum=None, r=(), w=()):
        kw = {}
        if bias is not None:
            kw['bias'] = bias
        if scale is not None:
            kw['scale'] = scale
        if accum is not None:
            kw['accum_out'] = accum
        return self.op('act', lambda e: e.activation(out, in_, func, **kw), self._bufs(r), self._bufs(w))

    def tt(self, eng, out, in0, in1, op, r=(), w=()):
        return self.op(eng, lambda e: e.tensor_tensor(out, in0, in1, op), self._bufs(r), self._bufs(w))

    def ts(self, eng, out, in0, s1, s2=None, op0=None, op1=None, r=(), w=()):
        if op1 is None:
            return self.op(eng, lambda e: e.tensor_scalar(out, in0, s1, None, op0), self._bufs(r), self._bufs(w))
        return self.op(eng, lambda e: e.tensor_scalar(out, in0, s1, s2, op0, op1), self._bufs(r), self._bufs(w))

    def stt(self, out, in0, scalar, in1, op0, op1, r=(), w=()):
        return self.op('dve', lambda e: e.scalar_tensor_tensor(out, in0, scalar, in1, op0, op1), self._bufs(r), self._bufs(w))

    def cp(self, eng, out, in_, r=(), w=()):
        if eng == 'act':
            return self.op('act', lambda e: e.copy(out, in_), self._bufs(r), self._bufs(w))
        return self.op(eng, lambda e: e.tensor_copy(out, in_), self._bufs(r), self._bufs(w))

    def ld(self, out, in_, r=(), w=(), eng='sp', **kw):
        return self.dma(eng, lambda e: e.dma_start(out=out, in_=in_, **kw), self._bufs(r), self._bufs(w))

    def _last_ops(self, skip=None):
        out = []
        for e in ENGS:
            if e != skip and self.q[e]:
                for o in reversed(self.q[e]):
                    if not o.is_dma and o.fn is not None:
                        out.append(o)
                        break
            h = self.dma_hist[e]
            out.extend(h[-self.ring:])
        return out

    def barrier(self):
        deps_for = {e: self._last_ops(skip=e) for e in ENGS}
        for e in ENGS:
            op = _Op(e, len(self.q[e]), None, False)
            op.prog = self
            op.deps = deps_for[e]
            self.q[e].append(op)

    def finish(self):
        op = _Op('sp', len(self.q['sp']), None, False)
        op.prog = self
        for e in ENGS:
            op.deps.extend(self.dma_hist[e][-self.ring:])
        self.q['sp'].append(op)

    def emit(self):
        nc = self.nc
        self.barrier()
        self.finish()
        for e in ENGS:
            seen = {p: -1 for p in ENGS}
            seen_dma = {}
            for o in self.q[e]:
                best = {}
                for d in o.deps:
                    if d.is_dma:
                        key = (d.eng, d.dslot)
                        if seen_dma.get(key, 0) >= d.dval:
                            continue
                        seen_dma[key] = d.dval
                        o.waits.append(d)
                    else:
                        if d.eng == 'pe' and e == 'pe':
                            continue
                        if d.fn is None:
                            continue
                        if d.idx <= seen[d.eng]:
                            continue
                        if d.eng not in best or best[d.eng].idx < d.idx:
                            best[d.eng] = d
                for p, d in best.items():
                    seen[p] = d.idx
                    d.needs_inc = True
                    o.waits.append(d)
        for e in ENGS:
            c = self.sems.c_off[e]
            for o in self.q[e]:
                if o.needs_inc:
                    c += 1
                o.cnt = c
            self.sems.c_off[e] = c
            self.sems.d_cnt[e] += len(self.dma_hist[e])
        n_wait = sum(len(o.waits) for e in ENGS for o in self.q[e])
        n_ops = sum(len(self.q[e]) for e in ENGS)
        self.stats = dict(n_ops=n_ops, n_wait=n_wait, per_eng={e: len(self.q[e]) for e in ENGS})
        with contextlib.ExitStack() as st:
            csem = self.sems.csem
            dsem = self.sems.dsem
            block = st.enter_context(nc.Block())

            def run(ename, eng):
                for o in self.q[ename]:
                    for d in o.waits:
                        if d.is_dma:
                            eng.wait_ge(dsem[d.eng][d.dslot], d.dval)
                        else:
                            eng.wait_ge(csem[d.eng], d.cnt)
                    if o.fn is None:
                        continue
                    ins = o.fn(eng)
                    if o.is_dma:
                        ins.then_inc(dsem[ename][o.dslot], 16)
                    elif o.needs_inc:
                        ins.then_inc(csem[ename], 1)

            @block.tensor
            def _(eng):
                run('pe', eng)

            @block.scalar
            def _(eng):
                run('act', eng)

            @block.vector
            def _(eng):
                run('dve', eng)

            @block.gpsimd
            def _(eng):
                run('pool', eng)

            @block.sync
            def _(eng):
                run('sp', eng)


D = 1024
IN_COLS = 3344
ALPHA = float(4 ** 0.25)
LN_EPS = 1e-5
RMS_EPS = 1e-5
GN_EPS = 64e-5
DECAY_C = float(np.exp(-0.5))
NEU_DT = F32

W_NAMES = ["ln0_g", "ln0_b", "w_in", "conv_w", "conv_b", "dt_bias", "a_log", "d_skip", "ssd_norm_g",
           "mu_rwkv", "w0", "w_up", "a0", "a_up", "g_up", "k_k", "k_a", "r_k", "lnx_g", "lnx_b", "w_out",
           "ln1_g", "ln1_b", "w_fc", "w_proj", "ln2_g", "ln2_b"]
W_SHAPES = {
    "ln0_g": [1024], "ln0_b": [1024], "w_in": [2, 1024, 3344], "conv_w": [2, 5, 1024], "conv_b": [2, 1024],
    "dt_bias": [2, 2, 8], "a_log": [2, 2, 8], "d_skip": [2, 8], "ssd_norm_g": [2, 512], "mu_rwkv": [2, 1792],
    "w0": [2, 2, 512], "w_up": [2, 2, 64, 512], "a0": [2, 2, 512], "a_up": [2, 2, 64, 512], "g_up": [2, 128, 512],
    "k_k": [2, 512], "k_a": [2, 512], "r_k": [2, 8, 64], "lnx_g": [2, 512], "lnx_b": [2, 512],
    "w_out": [2, 1024, 1024], "ln1_g": [2, 1024], "ln1_b": [2, 1024], "w_fc": [2, 1024, 4096],
    "w_proj": [2, 4096, 1024], "ln2_g": [2, 1024], "ln2_b": [2, 1024],
}


def make_consts():
    i = np.arange(128)
    ident = np.eye(128, dtype=np.float32)
    U = (i[:, None] <= i[None, :]).astype(np.float32)
    Lo = (i[:, None] >= i[None, :]).astype(np.float32)
    Us = (i[:, None] < i[None, :]).astype(np.float32)
    Ls = (i[:, None] > i[None, :]).astype(np.float32)
    ones = np.ones((128, 128), np.float32)
    return np.ascontiguousarray(np.concatenate([ident, U, Lo, Us, Ls, ones], axis=1))


class Tile:
    def __init__(self, t, name=''):
        self.t = t
        self.buf = Buf(name)

    def __getitem__(self, k):
        return self.t[k]


class Ctx:
    pass


def layer_norm_tile(P, C, src, dst, g_t, b_t, stat, eps, r, w):
    st = stat
    P.op('dve', lambda e: e.bn_stats(st[:, 0:6], src[:, 0:512]), P._bufs(r), [st.buf])
    P.op('dve', lambda e: e.bn_stats(st[:, 6:12], src[:, 512:1024]), P._bufs(r), [st.buf])
    P.op('dve', lambda e: e.bn_aggr(st[:, 12:14], st[:, 0:12].rearrange("p (a b) -> p a b", b=6)), [st.buf], [st.buf])
    P.ts('dve', st[:, 14:15], st[:, 13:14], eps, None, ALU.add, r=[st], w=[st])
    P.act(st[:, 15:16], st[:, 14:15], AF.Sqrt, r=[st], w=[st])
    P.op('dve', lambda e: e.reciprocal(st[:, 16:17], st[:, 15:16]), [st.buf], [st.buf])
    P.stt(st[:, 17:18], st[:, 12:13], -1.0, st[:, 16:17], ALU.mult, ALU.mult, r=[st], w=[st])
    P.act(dst, src, AF.Identity, bias=st[:, 17:18], scale=st[:, 16:17], r=list(r) + [st], w=w)
    P.tt('pool', dst, dst, g_t[:, :], ALU.mult, r=list(w) + [g_t], w=w)
    P.tt('dve', dst, dst, b_t[:, :], ALU.add, r=list(w) + [b_t], w=w)


_UID = [0]


_SBUSE = [0]
SB_BUDGET = 176 * 1024


def _sb_release(n):
    _SBUSE[0] -= n


def _sb(st, nc, name, shape, dt):
    _UID[0] += 1
    name = '%s_%d' % (name, _UID[0])
    n = int(np.prod(shape[1:])) * (2 if dt == BF16 else 4)
    n = (n + 31) // 32 * 32
    _SBUSE[0] += n
    assert _SBUSE[0] <= SB_BUDGET, ('SBUF budget exceeded', name, _SBUSE[0])
    t = Tile(st.enter_context(nc.sbuf_tensor(name, shape, dt)), name)
    st.callback(_sb_release, n)
    return t


def _psum(st, nc, n=8):
    _UID[0] += 1
    return [Tile(st.enter_context(nc.psum_tensor('ps%d_%d' % (i, _UID[0]), [128, 512], F32)), 'ps%d' % i) for i in range(n)]


class RR:
    def __init__(self, items):
        self.items = items
        self.i = 0

    def __call__(self):
        x = self.items[self.i % len(self.items)]
        self.i += 1
        return x


def load_bcast(P, tile, src_row, n, eng='sp'):
    P.ld(tile[:, 0:n], src_row.partition_broadcast(128), w=[tile], eng=eng)


def stage_consts(C):
    P = Prog(C.nc, C.sems)
    P.ld(C.cst[:, :], C.cst_d[:, :], w=[C.cst])
    P.emit()


def stage_A(C, l):
    nc, T, NS = C.nc, C.T, C.NS
    TB = min(512, T)
    NJ = TB // 128
    ident = C.cst[:, 0:128]
    with contextlib.ExitStack() as st:
        Win = _sb(st, nc, 'Win', [128, 8, IN_COLS], BF16)
        Wb = [Buf('Win%d' % k) for k in range(8)]
        hin = [_sb(st, nc, 'hin%d' % i, [128, D], F32) for i in range(2)]
        hT = [_sb(st, nc, 'hT%d' % i, [128, 8, TB], BF16) for i in range(2)]
        stat = [_sb(st, nc, 'stat%d' % i, [128, 32], F32) for i in range(2)]
        zo = [_sb(st, nc, 'zo%d' % i, [128, 512], F32) for i in range(2)]
        dto = [_sb(st, nc, 'dto%d' % i, [128, 16], F32) for i in range(2)]
        fo = [_sb(st, nc, 'fo%d' % i, [128, TB], BF16) for i in range(3)]
        if l == 0:
            g0 = _sb(st, nc, 'g0', [128, D], F32)
            b0 = _sb(st, nc, 'b0', [128, D], F32)
        ps = _psum(st, nc)
        P = Prog(nc, C.sems)
        for kc in range(8):
            P.ld(Win[:, kc, :], C.w['w_in'][l, kc * 128:(kc + 1) * 128, :], w=[Wb[kc]], eng='pool', max_dma_last_dim=4096)
        if l == 0:
            load_bcast(P, g0, C.w['ln0_g'], D)
            load_bcast(P, b0, C.w['ln0_b'], D)
        psr = RR(ps)
        evr = RR(['act', 'dve'])
        zor, dtor, forr = RR(zo), RR(dto), RR(fo)
        ti = 0
        for b in range(NS):
            src = C.x[b] if l == 0 else C.hbuf[b]
            for tb in range(T // TB):
                hTt = hT[(b * (T // TB) + tb) % 2]
                for j in range(NJ):
                    rows = slice(tb * TB + j * 128, tb * TB + (j + 1) * 128)
                    hi = hin[ti % 2]
                    P.ld(hi[:, :], src[rows, :], w=[hi])
                    if l == 0:
                        layer_norm_tile(P, C, hi[:, :], hi[:, :], g0, b0, stat[ti % 2], LN_EPS, r=[hi], w=[hi])
                        P.ld(C.hbuf[b, rows, :], hi[:, :], r=[hi])
                    for half in range(2):
                        bank = psr()
                        for q in range(4):
                            kc = half * 4 + q
                            P.tr(bank[:, q * 128:(q + 1) * 128], hi[:, kc * 128:(kc + 1) * 128], ident, r=[hi, C.cst], w=[bank])
                        P.cp(evr(), hTt[:, half * 4:half * 4 + 4, j * 128:(j + 1) * 128],
                             bank[:, :].rearrange("p (a b) -> p a b", b=128), r=[bank], w=[hTt])
                    ti += 1
                for j in range(NJ):
                    rows = slice(tb * TB + j * 128, tb * TB + (j + 1) * 128)
                    bank = psr()
                    for kc in range(8):
                        P.mm(bank[:, :], hTt[:, kc, j * 128:(j + 1) * 128], Win[:, kc, 0:512], start=(kc == 0), stop=(kc == 7),
                             r=[hTt, Wb[kc]], w=[bank])
                    z = zor()
                    P.cp(evr(), z[:, :], bank[:, :], r=[bank], w=[z])
                    P.ld(C.zbuf[b, rows, :], z[:, :], r=[z])
                    bank = psr()
                    for kc in range(8):
                        P.mm(bank[:, 0:16], hTt[:, kc, j * 128:(j + 1) * 128], Win[:, kc, 1536:1552], start=(kc == 0), stop=(kc == 7),
                             r=[hTt, Wb[kc]], w=[bank])
                    dt = dtor()
                    P.cp(evr(), dt[:, :], bank[:, 0:16], r=[bank], w=[dt])
                    P.ld(C.dtbuf[b, rows, :], dt[:, :], r=[dt])
                for cc in range(22):
                    col0 = 512 + cc * 128 if cc < 8 else 1552 + (cc - 8) * 128
                    bank = psr()
                    for kc in range(8):
                        P.mm(bank[:, 0:TB], Win[:, kc, col0:col0 + 128], hTt[:, kc, :], start=(kc == 0), stop=(kc == 7),
                             r=[hTt, Wb[kc]], w=[bank])
                    f = forr()
                    P.cp(evr(), f[:, :], bank[:, 0:TB], r=[bank], w=[f])
                    if cc < 8:
                        dst = C.xbcT[b, cc * 128:(cc + 1) * 128, tb * TB:(tb + 1) * TB]
                    else:
                        dst = C.rwT[b, (cc - 8) * 128:(cc - 7) * 128, tb * TB:(tb + 1) * TB]
                    P.ld(dst, f[:, :], r=[f])
        P.emit()
        C.stats.append(('A', P.stats))


def load_w_bf16(P, tile, bufs, src, nk, eng='pool'):
    for kc in range(nk):
        P.ld(tile[:, kc, :], src[kc * 128:(kc + 1) * 128, :], w=[bufs[kc]], eng=eng, max_dma_last_dim=4096)


def stage_D1(C, l):
    nc, T, NS = C.nc, C.T, C.NS
    ident = C.cst[:, 0:128]
    with contextlib.ExitStack() as st:
        Wo = _sb(st, nc, 'Wo', [128, 8, D], BF16)
        Wob = [Buf('Wo%d' % k) for k in range(8)]
        g1 = _sb(st, nc, 'g1', [128, D], F32)
        b1 = _sb(st, nc, 'b1', [128, D], F32)
        ym = [_sb(st, nc, 'ym%d' % i, [128, D], F32) for i in range(2)]
        yT = [_sb(st, nc, 'yT%d' % i, [128, 8, 128], BF16) for i in range(2)]
        hr = [_sb(st, nc, 'hr%d' % i, [128, D], F32) for i in range(2)]
        t1 = [_sb(st, nc, 't1%d' % i, [128, D], F32) for i in range(2)]
        stat = [_sb(st, nc, 'stat%d' % i, [128, 32], F32) for i in range(2)]
        ps = _psum(st, nc)
        P = Prog(nc, C.sems)
        load_w_bf16(P, Wo, Wob, C.w['w_out'][l], 8)
        load_bcast(P, g1, C.w['ln1_g'][l], D)
        load_bcast(P, b1, C.w['ln1_b'][l], D)
        psr = RR(ps)
        evr = RR(['act', 'dve'])
        ti = 0
        for b in range(NS):
            for c in range(T // 128):
                rows = slice(c * 128, (c + 1) * 128)
                y, yt, h, t, sx = ym[ti % 2], yT[ti % 2], hr[ti % 2], t1[ti % 2], stat[ti % 2]
                P.ld(y[:, :], C.ymix[b, rows, :], w=[y])
                P.ld(h[:, :], C.hbuf[b, rows, :], w=[h])
                for half in range(2):
                    bank = psr()
                    for q in range(4):
                        kc = half * 4 + q
                        P.tr(bank[:, q * 128:(q + 1) * 128], y[:, kc * 128:(kc + 1) * 128], ident, r=[y, C.cst], w=[bank])
                    P.cp(evr(), yt[:, half * 4:half * 4 + 4, :], bank[:, :].rearrange("p (a b) -> p a b", b=128), r=[bank], w=[yt])
                for half in range(2):
                    bank = psr()
                    for kc in range(8):
                        P.mm(bank[:, :], yt[:, kc, :], Wo[:, kc, half * 512:(half + 1) * 512], start=(kc == 0), stop=(kc == 7),
                             r=[yt, Wob[kc]], w=[bank])
                    P.stt(t[:, half * 512:(half + 1) * 512], h[:, half * 512:(half + 1) * 512], ALPHA, bank[:, :], ALU.mult, ALU.add,
                          r=[h, bank], w=[t])
                layer_norm_tile(P, C, t[:, :], t[:, :], g1, b1, sx, LN_EPS, r=[t], w=[t])
                P.ld(C.h1buf[b, rows, :], t[:, :], r=[t])
                ti += 1
        P.emit()
        C.stats.append(('D1', P.stats))


def stage_D2(C, l, last):
    nc, T, NS = C.nc, C.T, C.NS
    ident = C.cst[:, 0:128]
    TB = 256
    with contextlib.ExitStack() as st:
        Wf = _sb(st, nc, 'Wf', [128, 8, 4096], BF16)
        Wfb = [Buf('Wf%d' % k) for k in range(8)]
        Wp = _sb(st, nc, 'Wp', [128, 32, D], BF16)
        Wpb = [Buf('Wp%d' % k) for k in range(32)]
        g2 = _sb(st, nc, 'g2', [128, D], F32)
        b2 = _sb(st, nc, 'b2', [128, D], F32)
        h1 = [_sb(st, nc, 'h1%d' % i, [128, D], F32) for i in range(2)]
        h1T = _sb(st, nc, 'h1T', [128, 8, TB], BF16)
        aT = _sb(st, nc, 'aT', [128, 32, TB], BF16)
        tmp = [_sb(st, nc, 'tmp%d' % i, [128, TB], F32) for i in range(2)]
        t2 = [_sb(st, nc, 't2%d' % i, [128, D], F32) for i in range(1)]
        stat = [_sb(st, nc, 'stat%d' % i, [128, 32], F32) for i in range(2)]
        ps = _psum(st, nc)
        P = Prog(nc, C.sems)
        load_w_bf16(P, Wf, Wfb, C.w['w_fc'][l], 8)
        load_w_bf16(P, Wp, Wpb, C.w['w_proj'][l], 32)
        load_bcast(P, g2, C.w['ln2_g'][l], D)
        load_bcast(P, b2, C.w['ln2_b'][l], D)
        psr = RR(ps)
        evr = RR(['act', 'dve'])
        tmr = RR(tmp)
        ti = 0
        for b in range(NS):
            dstb = C.out[b] if last else C.hbuf[b]
            for tb in range(T // TB):
                for j in range(2):
                    rows = slice(tb * TB + j * 128, tb * TB + (j + 1) * 128)
                    h = h1[j]
                    P.ld(h[:, :], C.h1buf[b, rows, :], w=[h])
                    for half in range(2):
                        bank = psr()
                        for q in range(4):
                            kc = half * 4 + q
                            P.tr(bank[:, q * 128:(q + 1) * 128], h[:, kc * 128:(kc + 1) * 128], ident, r=[h, C.cst], w=[bank])
                        P.cp(evr(), h1T[:, half * 4:half * 4 + 4, j * 128:(j + 1) * 128],
                             bank[:, :].rearrange("p (a b) -> p a b", b=128), r=[bank], w=[h1T])
                for fc in range(32):
                    bank = psr()
                    for kc in range(8):
                        P.mm(bank[:, 0:TB], Wf[:, kc, fc * 128:(fc + 1) * 128], h1T[:, kc, :], start=(kc == 0), stop=(kc == 7),
                             r=[h1T, Wfb[kc]], w=[bank])
                    tm = tmr()
                    if fc % 2 == 0:
                        P.act(tm[:, :], bank[:, 0:TB], AF.Relu, r=[bank], w=[tm])
                    else:
                        P.ts('dve', tm[:, :], bank[:, 0:TB], 0.0, None, ALU.max, r=[bank], w=[tm])
                    P.tt('pool', aT[:, fc, :], tm[:, :], tm[:, :], ALU.mult, r=[tm], w=[aT])
                for j in range(2):
                    rows = slice(tb * TB + j * 128, tb * TB + (j + 1) * 128)
                    h = h1[j]
                    t = t2[0]
                    for half in range(2):
                        bank = psr()
                        for fc in range(32):
                            P.mm(bank[:, :], aT[:, fc, j * 128:(j + 1) * 128], Wp[:, fc, half * 512:(half + 1) * 512],
                                 start=(fc == 0), stop=(fc == 31), r=[aT, Wpb[fc]], w=[bank])
                        P.stt(t[:, half * 512:(half + 1) * 512], h[:, half * 512:(half + 1) * 512], ALPHA, bank[:, :], ALU.mult, ALU.add,
                              r=[h, bank], w=[t])
                    layer_norm_tile(P, C, t[:, :], t[:, :], g2, b2, stat[ti % 2], LN_EPS, r=[t], w=[t])
                    P.ld(dstb[rows, :], t[:, :], r=[t])
                    ti += 1
        P.emit()
        C.stats.append(('D2', P.stats))


def bc3(ap2, n, axis):
    k = ap2.shape[1]
    if axis == 1:
        return ap2.unsqueeze(1).broadcast_to([128, n, k])
    return ap2.unsqueeze(2).broadcast_to([128, k, n])


def stage_B(C, l, b):
    nc, T, NCH = C.nc, C.T, C.NCH
    TB = min(512, T)
    cst = C.cst
    ident, U, Lo, Us, Ls, ones = (cst[:, i * 128:(i + 1) * 128] for i in range(6))
    with contextlib.ExitStack() as st:
        sb = lambda n, sh, dt: _sb(st, nc, n, sh, dt)
        XTb = [Buf('XT%d' % g) for g in range(8)]
        BT = sb('BT', [128, 2, T], BF16)
        CT = sb('CT', [128, 2, T], BF16)
        xbf = sb('xbf', [128, NCH, 512], BF16)
        Btok = sb('Btok', [128, NCH, 256], BF16)
        dtraw = sb('dtraw', [128, NCH, 16], F32)
        dtv = sb('dtv', [128, NCH, 16], F32)
        av = sb('av', [128, NCH, 16], F32)
        dtb = sb('dtb', [128, 16], F32)
        negA = sb('negA', [128, 16], F32)
        dsk8 = sb('dsk8', [128, 8], F32)
        dsk = sb('dsk', [128, 512], F32)
        ng = sb('ng', [128, 512], F32)
        ps = _psum(st, nc)
        st0 = contextlib.ExitStack()
        sb0 = lambda n, sh, dt: _sb(st0, nc, n, sh, dt)
        XT = sb0('XT', [128, 8, T + 4], BF16)
        cw6 = sb0('cw6', [6, 1024], F32)
        cwb = sb0('cwb', [128, 8, 6], F32)
        Dg = sb0('Dg', [128, 5, 8, 128], BF16)
        cbrow = sb0('cbrow', [1, 768], F32)
        cbrow_bf = sb0('cbrow_bf', [1, 768], BF16)
        ones_bf = sb0('ones_bf', [1, 128], BF16)
        P = Prog(nc, C.sems)
        psr = RR(ps)
        P.op('pool', lambda e: e.memset(XT[:, :, 0:2], 0.0), [], XTb)
        P.op('pool', lambda e: e.memset(XT[:, :, T + 2:T + 4], 0.0), [], XTb)
        for g in range(8):
            P.ld(XT[:, g, 2:T + 2], C.xbcT[b, g * 128:(g + 1) * 128, :], w=[XTb[g]])
        P.ld(cw6[0:5, :], C.w['conv_w'][l], w=[cw6])
        P.ld(cw6[5:6, :], C.w['conv_b'][l:l + 1, :], w=[cw6])
        P.ld(cbrow[0:1, :], C.w['conv_b'][l:l + 1, 0:768], w=[cbrow])
        P.cp('dve', cbrow_bf[0:1, :], cbrow[0:1, :], r=[cbrow], w=[cbrow_bf])
        P.cp('dve', ones_bf[0:1, :], ones[0:1, :], r=[cst], w=[ones_bf])
        load_bcast(P, dtb, C.w['dt_bias'][l].rearrange("a b -> (a b)"), 16)
        load_bcast(P, negA, C.w['a_log'][l].rearrange("a b -> (a b)"), 16)
        load_bcast(P, dsk8, C.w['d_skip'][l], 8)
        load_bcast(P, ng, C.w['ssd_norm_g'][l], 512)
        P.act(negA[:, :], negA[:, :], AF.Exp, r=[negA], w=[negA])
        P.ts('dve', negA[:, :], negA[:, :], -1.0, None, ALU.mult, r=[negA], w=[negA])
        P.cp('dve', dsk[:, :].rearrange("p (h q) -> p h q", q=64), bc3(dsk8[:, :], 64, 2), r=[dsk8], w=[dsk])
        bank = psr()
        for g in range(8):
            P.tr(bank[:, g * 6:(g + 1) * 6], cw6[0:6, g * 128:(g + 1) * 128], ident[0:6, 0:6], r=[cw6, cst], w=[bank])
        P.cp('dve', cwb[:, :, :], bank[:, 0:48].rearrange("p (g k) -> p g k", k=6), r=[bank], w=[cwb])
        er = RR(['dve', 'pool'])
        for k in range(5):
            for g in range(8):
                P.ts(er(), Dg[:, k, g, :], ident, cwb[:, g, k:k + 1], None, ALU.mult, r=[cwb, cst], w=[Dg])
        P.ld(dtraw[:, :, :], C.dtbuf[b].rearrange("(c p) k -> p c k", p=128), w=[dtraw])
        P.tt('dve', dtv[:, :, :], dtraw[:, :, :], bc3(dtb[:, :], NCH, 1), ALU.add, r=[dtraw, dtb], w=[dtv])
        P.act(dtv[:, :, :], dtv[:, :, :], AF.Exp, r=[dtv], w=[dtv])
        P.ts('dve', dtv[:, :, :], dtv[:, :, :], 1.0, None, ALU.add, r=[dtv], w=[dtv])
        P.act(dtv[:, :, :], dtv[:, :, :], AF.Ln, r=[dtv], w=[dtv])
        P.tt('dve', av[:, :, :], dtv[:, :, :], bc3(negA[:, :], NCH, 1), ALU.mult, r=[dtv, negA], w=[av])
        for c in range(NCH):
            for (g0, ng_, dst, boff) in ((0, 4, xbf, 0), (4, 2, Btok, 512)):
                bank = psr()
                n = ng_ * 128
                P.mm(bank[:, 0:n], ones_bf[0:1, :], cbrow_bf[0:1, boff:boff + n], start=True, stop=False,
                     r=[ones_bf, cbrow_bf], w=[bank])
                for gi in range(ng_):
                    g = g0 + gi
                    for k in range(5):
                        P.mm(bank[:, gi * 128:(gi + 1) * 128], XT[:, g, c * 128 + k:c * 128 + k + 128], Dg[:, k, g, :],
                             start=False, stop=(gi == ng_ - 1 and k == 4), r=[XTb[g], Dg], w=[bank])
                P.act(dst[:, c, :], bank[:, 0:n], AF.Silu, r=[bank], w=[dst])
        for tb in range(T // TB):
            for gi in range(4):
                g = 4 + gi
                bank = psr()
                for k in range(5):
                    P.mm(bank[:, 0:TB], Dg[:, k, g, :], XT[:, g, tb * TB + k:tb * TB + k + TB], start=(k == 0), stop=(k == 4),
                         r=[XTb[g], Dg], w=[bank])
                dst = BT if gi < 2 else CT
                P.act(dst[:, gi % 2, tb * TB:(tb + 1) * TB], bank[:, 0:TB], AF.Silu, bias=cwb[:, g, 5:6], r=[bank, cwb], w=[dst])
        P.emit()
        C.stats.append(('B0', P.stats))
        st0.close()
        ysc = sb('ysc', [128, NCH, 16], F32)
        cs_sb = [sb('cs_sb%d' % i, [128, 32], F32) for i in range(2)]
        dd = [sb('dd%d' % i, [128, 16], F32) for i in range(2)]
        et = [sb('et%d' % i, [128, 16], F32) for i in range(2)]
        wd = [sb('wd%d' % i, [128, 16], F32) for i in range(2)]
        xw = [sb('xw%d' % i, [128, 2, 512], BF16) for i in range(2)]
        Srun = sb('Srun', [128, 2, 512], F32)
        Sin = sb('Sin', [128, NCH, 2, 512], BF16)
        rhsS = [sb('rhsS%d' % i, [128, 2, 8, 128], F32) for i in range(2)]
        E = [sb('E%d' % i, [128, 2, 8, 128], F32) for i in range(2)]
        SM = [sb('SM%d' % i, [128, 2, 2, 128], F32) for i in range(2)]
        G = [sb('G%d' % i, [128, 2, 8, 128], BF16) for i in range(2)]
        ta = [sb('ta%d' % i, [128, 512], F32) for i in range(2)]
        tb_ = [sb('tb%d' % i, [128, 512], F32) for i in range(2)]
        yt = [sb('yt%d' % i, [128, 512], F32) for i in range(2)]
        zt = [sb('zt%d' % i, [128, 512], F32) for i in range(2)]
        sq = [sb('sq%d' % i, [128, 512], F32) for i in range(2)]
        ss = [sb('ss%d' % i, [128, 8], F32) for i in range(2)]
        P = Prog(nc, C.sems)
        psr = RR(ps)
        P.op('pool', lambda e: e.memset(Srun[:, :, :], 0.0), [], [Srun.buf])
        for i in range(NCH):
            cc = (i, NCH - 1 - i)
            k2 = i % 2
            bank = psr()
            for d in range(2):
                P.mm(bank[:, d * 8:(d + 1) * 8], U if d == 0 else Lo, av[:, cc[d], d * 8:(d + 1) * 8], r=[cst, av], w=[bank])
                P.mm(bank[:, 16 + d * 8:16 + (d + 1) * 8], ones, av[:, cc[d], d * 8:(d + 1) * 8], r=[cst, av], w=[bank])
            P.cp('act', cs_sb[k2][:, :], bank[:, 0:32], r=[bank], w=[cs_sb[k2]])
            P.tt('dve', dd[k2][:, :], cs_sb[k2][:, 16:32], cs_sb[k2][:, 0:16], ALU.subtract, r=[cs_sb[k2]], w=[dd[k2]])
            P.act(dd[k2][:, :], dd[k2][:, :], AF.Exp, r=[dd[k2]], w=[dd[k2]])
            P.act(et[k2][:, :], cs_sb[k2][:, 16:32], AF.Exp, r=[cs_sb[k2]], w=[et[k2]])
            for d in range(2):
                sl = slice(d * 8, (d + 1) * 8)
                P.act(ysc[:, cc[d], sl], cs_sb[k2][:, sl], AF.Exp, r=[cs_sb[k2]], w=[ysc])
                P.tt('dve', wd[k2][:, sl], dd[k2][:, sl], dtv[:, cc[d], sl], ALU.mult, r=[dd[k2], dtv], w=[wd[k2]])
                P.tt('dve', xw[k2][:, d, :].rearrange("p (h q) -> p h q", q=64),
                     xbf[:, cc[d], :].rearrange("p (h q) -> p h q", q=64), bc3(wd[k2][:, sl], 64, 2), ALU.mult,
                     r=[xbf, wd[k2]], w=[xw[k2]])
            for d in range(2):
                bank = psr()
                for g in range(2):
                    P.mm(bank[:, g * 256:(g + 1) * 256], Btok[:, cc[d], g * 128:(g + 1) * 128], xw[k2][:, d, g * 256:(g + 1) * 256],
                         r=[Btok, xw[k2]], w=[bank])
                P.cp('pool', Sin[:, cc[d], d, :], Srun[:, d, :], r=[Srun], w=[Sin])
                P.tt('dve', Srun[:, d, :].rearrange("p (h q) -> p h q", q=64), Srun[:, d, :].rearrange("p (h q) -> p h q", q=64),
                     bc3(et[k2][:, d * 8:(d + 1) * 8], 64, 2), ALU.mult, r=[Srun, et[k2]], w=[Srun])
                P.tt('dve', Srun[:, d, :], Srun[:, d, :], bank[:, :], ALU.add, r=[Srun, bank], w=[Srun])
        for c in range(NCH):
            k2 = c % 2
            rows = slice(c * 128, (c + 1) * 128)
            csl = slice(c * 128, (c + 1) * 128)
            P.ld(zt[k2][:, :], C.zbuf[b, rows, :], w=[zt[k2]])
            for d in range(2):
                P.tt('dve' if d == 0 else 'pool', rhsS[k2][:, d, :, :], bc3(U if d == 0 else Lo, 8, 1),
                     bc3(av[:, c, d * 8:(d + 1) * 8], 128, 2), ALU.mult, r=[cst, av], w=[rhsS[k2]])
            for d in range(2):
                for hh in range(2):
                    bank = psr()
                    P.mm(bank[:, :], Ls if d == 0 else Us, rhsS[k2][:, d, hh * 4:(hh + 1) * 4, :].rearrange("p a b -> p (a b)"),
                         r=[cst, rhsS[k2]], w=[bank])
                    P.act(E[k2][:, d, hh * 4:(hh + 1) * 4, :].rearrange("p a b -> p (a b)"), bank[:, :], AF.Exp, r=[bank], w=[E[k2]])
            bank = psr()
            for g in range(2):
                P.mm(bank[:, g * 128:(g + 1) * 128], BT[:, g, csl], CT[:, g, csl], r=[BT, CT], w=[bank])
            for d in range(2):
                P.tt('dve', SM[k2][:, d, :, :], bank[:, 0:256].rearrange("p (g l) -> p g l", l=128), bc3(U if d == 0 else Lo, 2, 1),
                     ALU.mult, r=[bank, cst], w=[SM[k2]])
            for d in range(2):
                for h in range(8):
                    P.stt(G[k2][:, d, h, :], E[k2][:, d, h, :], dtv[:, c, d * 8 + h:d * 8 + h + 1], SM[k2][:, d, h // 4, :],
                          ALU.mult, ALU.mult, r=[E[k2], dtv, SM[k2]], w=[G[k2]])
            bY1 = psr()
            for h in range(8):
                for d in range(2):
                    P.mm(bY1[:, h * 64:(h + 1) * 64], G[k2][:, d, h, :], xbf[:, c, h * 64:(h + 1) * 64], start=(d == 0), stop=(d == 1),
                         r=[G[k2], xbf], w=[bY1])
            bY2 = [psr(), psr()]
            for d in range(2):
                for g in range(2):
                    P.mm(bY2[d][:, g * 256:(g + 1) * 256], CT[:, g, csl], Sin[:, c, d, g * 256:(g + 1) * 256], r=[CT, Sin], w=[bY2[d]])
            v3 = lambda ap: ap.rearrange("p (h q) -> p h q", q=64)
            P.tt('dve', v3(ta[k2][:, :]), v3(bY2[0][:, :]), bc3(ysc[:, c, 0:8], 64, 2), ALU.mult, r=[bY2[0], ysc], w=[ta[k2]])
            P.tt('dve', v3(tb_[k2][:, :]), v3(bY2[1][:, :]), bc3(ysc[:, c, 8:16], 64, 2), ALU.mult, r=[bY2[1], ysc], w=[tb_[k2]])
            y = yt[k2]
            P.tt('pool', y[:, :], ta[k2][:, :], tb_[k2][:, :], ALU.add, r=[ta[k2], tb_[k2]], w=[y])
            P.tt('dve', y[:, :], y[:, :], bY1[:, :], ALU.add, r=[y, bY1], w=[y])
            P.tt('pool', sq[k2][:, :], xbf[:, c, :], dsk[:, :], ALU.mult, r=[xbf, dsk], w=[sq[k2]])
            P.tt('pool', y[:, :], y[:, :], sq[k2][:, :], ALU.add, r=[y, sq[k2]], w=[y])
            P.act(zt[k2][:, :], zt[k2][:, :], AF.Silu, r=[zt[k2]], w=[zt[k2]])
            P.tt('dve', y[:, :], y[:, :], zt[k2][:, :], ALU.mult, r=[y, zt[k2]], w=[y])
            P.tt('pool', sq[k2][:, :], y[:, :], y[:, :], ALU.mult, r=[y], w=[sq[k2]])
            P.op('dve', lambda e, o=ss[k2][:, 0:2], i_=sq[k2][:, :].rearrange("p (g q) -> p g q", q=256): e.reduce_sum(o, i_, axis=AX.X),
                 [sq[k2].buf], [ss[k2].buf])
            P.ts('dve', ss[k2][:, 2:4], ss[k2][:, 0:2], 1.0 / 256.0, RMS_EPS, ALU.mult, ALU.add, r=[ss[k2]], w=[ss[k2]])
            P.act(ss[k2][:, 4:6], ss[k2][:, 2:4], AF.Sqrt, r=[ss[k2]], w=[ss[k2]])
            P.op('dve', lambda e, o=ss[k2][:, 6:8], i_=ss[k2][:, 4:6]: e.reciprocal(o, i_), [ss[k2].buf], [ss[k2].buf])
            for g in range(2):
                P.ts('dve', y[:, g * 256:(g + 1) * 256], y[:, g * 256:(g + 1) * 256], ss[k2][:, 6 + g:7 + g], None, ALU.mult,
                     r=[y, ss[k2]], w=[y])
            P.tt('pool', y[:, :], y[:, :], ng[:, :], ALU.mult, r=[y, ng], w=[y])
            P.ld(C.ymix[b, rows, 0:512], y[:, :], r=[y])
        P.emit()
        C.stats.append(('B', P.stats))


def stage_C(C, l, b):
    nc, T, NCH = C.nc, C.T, C.NCH
    cst = C.cst
    ident, U, Lo, Us, Ls, ones = (cst[:, i * 128:(i + 1) * 128] for i in range(6))
    v3 = lambda ap: ap.rearrange("p (h q) -> p h q", q=64)
    with contextlib.ExitStack() as st:
        sb = lambda n, sh, dt: _sb(st, nc, n, sh, dt)
        RTc = [sb('RTc%d' % i, [128, 14, 130], BF16) for i in range(2)]
        murow = sb('murow', [14, 128], F32)
        muT = sb('muT', [128, 3, 14], F32)
        Dm = sb('Dm', [128, 14, 128], BF16)
        Dh = sb('Dh', [128, 14, 128], BF16)
        kkb = sb('kkb', [128, 512], F32)
        kab = sb('kab', [128, 512], F32)
        rkb = sb('rkb', [128, 512], F32)
        lgb = sb('lgb', [128, 512], F32)
        lbb = sb('lbb', [128, 512], F32)
        w0b = sb('w0b', [128, 2, 512], F32)
        a0b = sb('a0b', [128, 2, 512], F32)
        LW = sb('LW', [128, 2, 512], BF16)
        GU = sb('GU', [128, 512], BF16)
        MK = sb('MK', [128, 2, 512], F32)
        MKa = sb('MKa', [128, 2, 128], F32)
        bon = sb('bon', [128, NCH, 2, 8], F32)
        r32 = sb('r32', [128, 512], F32)
        k32 = sb('k32', [128, 512], F32)
        v32 = sb('v32', [128, 512], F32)
        vbf = sb('vbf', [128, 512], BF16)
        LT = sb('LT', [128, 128], BF16)
        sg = sb('sg', [128, 128], BF16)
        g32 = sb('g32', [128, 512], F32)
        lw32 = sb('lw32', [128, 512], F32)
        a32 = sb('a32', [128, 512], F32)
        kk = sb('kk', [128, 512], F32)
        tmp = sb('tmp', [128, 512], F32)
        tmp2 = sb('tmp2', [128, 512], F32)
        kd = sb('kd', [128, 512], F32)
        bb = sb('bb', [128, 512], F32)
        sm8 = sb('sm8', [128, 32], F32)
        gC = sb('gC', [128, 4], F32)
        Ep = sb('Ep', [128, 512], F32)
        En = sb('En', [128, 512], F32)
        Ex = sb('Ex', [128, 512], F32)
        rt = sb('rt', [128, 512], F32)
        kt = sb('kt', [128, 512], F32)
        bt = sb('bt', [128, 512], F32)
        kdt = sb('kdt', [128, 512], F32)
        bt_bf = sb('bt_bf', [128, 512], BF16)
        kdt_bf = sb('kdt_bf', [128, 512], BF16)
        KR = sb('KR', [128, 4, 2, 128], BF16)
        btT = sb('btT', [128, 4, 128], BF16)
        kdtT = sb('kdtT', [128, 4, 128], BF16)
        R3 = sb('R3', [128, 8, 3, 128], BF16)
        NM = [sb('NM%d' % i, [128, 8, 2, 128], NEU_DT) for i in range(2)]
        X = [sb('X%d' % i, [128, 8, 128], NEU_DT) for i in range(2)]
        Z1 = sb('Z1', [128, 512], NEU_DT)
        Ut = sb('Ut', [128, 512], F32)
        WT = sb('WT', [128, 4, 128], BF16)
        Mst = sb('Mst', [128, 4, 128], F32)
        Mbf = sb('Mbf', [128, 4, 128], BF16)
        Un = sb('Un', [128, 512], BF16)
        Yo = [sb('Yo%d' % i, [128, 512], F32) for i in range(2)]
        ps = _psum(st, nc)
        P = Prog(nc, C.sems)
        psr = RR(ps)
        evr = RR(['act', 'dve'])
        P.ld(murow[0:14, :], C.w['mu_rwkv'][l].rearrange("(g p) -> g p", p=128), w=[murow])
        bank = psr()
        P.tr(bank[:, 0:14], murow[0:14, :], ident[0:14, 0:14], r=[murow, cst], w=[bank])
        P.cp('dve', muT[:, 0, :], bank[:, 0:14], r=[bank], w=[muT])
        P.ts('dve', muT[:, 1, :], muT[:, 0, :], -1.0, 1.0, ALU.mult, ALU.add, r=[muT], w=[muT])
        P.ts('dve', muT[:, 2, :], muT[:, 0, :], 0.5, None, ALU.mult, r=[muT], w=[muT])
        er = RR(['dve', 'pool'])
        for g in range(14):
            P.ts(er(), Dm[:, g, :], ident, muT[:, 1, g:g + 1], None, ALU.mult, r=[muT, cst], w=[Dm])
            P.ts(er(), Dh[:, g, :], ident, muT[:, 2, g:g + 1], None, ALU.mult, r=[muT, cst], w=[Dh])
        load_bcast(P, kkb, C.w['k_k'][l], 512)
        load_bcast(P, kab, C.w['k_a'][l], 512)
        load_bcast(P, rkb, C.w['r_k'][l].rearrange("a b -> (a b)"), 512)
        load_bcast(P, lgb, C.w['lnx_g'][l], 512)
        load_bcast(P, lbb, C.w['lnx_b'][l], 512)
        for d in range(2):
            P.ld(w0b[:, d, :], C.w['w0'][l, d].partition_broadcast(128), w=[w0b])
            P.ld(a0b[:, d, :], C.w['a0'][l, d].partition_broadcast(128), w=[a0b])
            P.ld(LW[0:64, d, :], C.w['w_up'][l, d], w=[LW], eng='pool')
            P.ld(LW[64:128, d, :], C.w['a_up'][l, d], w=[LW], eng='pool')
        P.ld(GU[:, :], C.w['g_up'][l], w=[GU], eng='pool')
        for d in range(2):
            strict, incl, strict_ts = (Us, U, Ls) if d == 0 else (Ls, Lo, Us)
            P.ts('dve', MK[:, d, 0:128], strict, -1.0, None, ALU.mult, r=[cst], w=[MK])
            P.cp('dve', MK[:, d, 128:256], incl, r=[cst], w=[MK])
            P.cp('dve', MK[:, d, 256:384], strict, r=[cst], w=[MK])
            P.cp('dve', MK[:, d, 384:512], incl, r=[cst], w=[MK])
            P.ts('dve', MKa[:, d, :], strict_ts, -1.0, None, ALU.mult, r=[cst], w=[MKa])
        taps = (Dh, Dm, Dh)
        ydb = [[Buf('yd') for _ in range(NCH)] for _ in range(2)]
        vgb = [[Buf('vg') for _ in range(NCH)] for _ in range(2)]
        for d in range(2):
            P.op('pool', lambda e: e.memset(Mst[:, :, :], 0.0), [], [Mst.buf])
            P.op('pool', lambda e: e.memset(Mbf[:, :, :], 0.0), [], [Mbf.buf])
            for ci in range(NCH):
                c = ci if d == 0 else NCH - 1 - ci
                rows = slice(c * 128, (c + 1) * 128)
                RT = RTc[ci % 2]
                lo, hi = max(c * 128 - 1, 0), min(c * 128 + 129, T)
                if c == 0:
                    P.op('pool', lambda e, t=RT: e.memset(t[:, :, 0:1], 0.0), [], [RT.buf])
                if c == NCH - 1:
                    P.op('pool', lambda e, t=RT: e.memset(t[:, :, 129:130], 0.0), [], [RT.buf])
                P.ld(RT[:, :, lo - (c * 128 - 1):hi - (c * 128 - 1)],
                     C.rwT[b].rearrange("(g p) t -> p g t", p=128)[:, :, lo:hi], w=[RT])
                for (g0, dst) in ((0, r32), (4, k32), (8, v32)):
                    bank = psr()
                    for gi in range(4):
                        g = g0 + gi
                        for k in range(3):
                            P.mm(bank[:, gi * 128:(gi + 1) * 128], RT[:, g, k:k + 128], taps[k][:, g, :],
                                 start=(k == 0), stop=(k == 2), r=[RT, Dm, Dh], w=[bank])
                    P.cp(evr(), dst[:, :], bank[:, :], r=[bank], w=[dst])
                P.cp('pool', vbf[:, :], v32[:, :], r=[v32], w=[vbf])
                bank = psr()
                for gi in range(2):
                    g = 12 + gi
                    for k in range(3):
                        P.mm(bank[:, gi * 128:(gi + 1) * 128], taps[k][:, g, :], RT[:, g, k:k + 128],
                             start=(k == 0), stop=(k == 2), r=[RT, Dm, Dh], w=[bank])
                P.act(LT[0:64, :], bank[0:64, 0:128], AF.Tanh, r=[bank], w=[LT])
                P.cp('dve', LT[64:128, :], bank[64:128, 0:128], r=[bank], w=[LT])
                P.act(sg[:, :], bank[:, 128:256], AF.Sigmoid, r=[bank], w=[sg])
                bW, bA = psr(), psr()
                P.mm(bW[:, :], LT[0:64, :], LW[0:64, d, :], r=[LT, LW], w=[bW])
                P.mm(bA[:, :], LT[64:128, :], LW[64:128, d, :], r=[LT, LW], w=[bA])
                P.tt('dve', lw32[:, :], bW[:, :], w0b[:, d, :], ALU.add, r=[bW, w0b], w=[lw32])
                P.act(lw32[:, :], lw32[:, :], AF.Sigmoid, r=[lw32], w=[lw32])
                P.ts('pool', lw32[:, :], lw32[:, :], -DECAY_C, None, ALU.mult, r=[lw32], w=[lw32])
                P.tt('dve', a32[:, :], bA[:, :], a0b[:, d, :], ALU.add, r=[bA, a0b], w=[a32])
                P.act(a32[:, :], a32[:, :], AF.Sigmoid, r=[a32], w=[a32])
                if d == 0:
                    bG = psr()
                    P.mm(bG[:, :], sg[:, :], GU[:, :], r=[sg, GU], w=[bG])
                    P.cp('act', g32[:, :], bG[:, :], r=[bG], w=[g32])
                    P.ld(C.vg[0, rows, :], v32[:, :], r=[v32], w=[vgb[0][c]])
                    P.ld(C.vg[1, rows, :], g32[:, :], r=[g32], w=[vgb[1][c]])
                P.tt('pool', kk[:, :], k32[:, :], kkb[:, :], ALU.mult, r=[k32, kkb], w=[kk])
                P.tt('pool', tmp[:, :], kk[:, :], kk[:, :], ALU.mult, r=[kk], w=[tmp])
                P.op('dve', lambda e, o=sm8[:, 0:8], i_=v3(tmp[:, :]): e.reduce_sum(o, i_, axis=AX.X), [tmp.buf], [sm8.buf])
                P.act(sm8[:, 8:16], sm8[:, 0:8], AF.Sqrt, r=[sm8], w=[sm8])
                P.ts('dve', sm8[:, 8:16], sm8[:, 8:16], 1e-12, None, ALU.max, r=[sm8], w=[sm8])
                P.op('dve', lambda e, o=sm8[:, 16:24], i_=sm8[:, 8:16]: e.reciprocal(o, i_), [sm8.buf], [sm8.buf])
                P.tt('dve', v3(kk[:, :]), v3(kk[:, :]), bc3(sm8[:, 16:24], 64, 2), ALU.mult, r=[kk, sm8], w=[kk])
                P.stt(tmp2[:, :], a32[:, :], -1.0, kab[:, :], ALU.add, ALU.mult, r=[a32, kab], w=[tmp2])
                P.stt(kd[:, :], tmp2[:, :], 1.0, k32[:, :], ALU.add, ALU.mult, r=[tmp2, k32], w=[kd])
                P.tt('pool', tmp[:, :], r32[:, :], kd[:, :], ALU.mult, r=[r32, kd], w=[tmp])
                P.tt('pool', tmp[:, :], tmp[:, :], rkb[:, :], ALU.mult, r=[tmp, rkb], w=[tmp])
                P.op('dve', lambda e, o=bon[:, c, d, :], i_=v3(tmp[:, :]): e.reduce_sum(o, i_, axis=AX.X), [tmp.buf], [bon.buf])
                P.tt('pool', bb[:, :], kk[:, :], a32[:, :], ALU.mult, r=[kk, a32], w=[bb])
                bC = psr()
                P.mm(bC[:, :], U if d == 0 else Lo, lw32[:, :], r=[cst, lw32], w=[bC])
                bT = psr()
                for jg in range(4):
                    P.mm(bT[:, 2 * jg:2 * jg + 2], lw32[:, jg * 128:(jg + 1) * 128], ones[:, 0:2], r=[lw32, cst], w=[bT])
                P.act(gC[:, :], bT[:, 0:8].rearrange("p (j two) -> p j two", two=2)[:, :, 0], AF.Exp, r=[bT], w=[gC])
                P.act(Ep[:, :], bC[:, :], AF.Exp, r=[bC], w=[Ep])
                P.act(En[:, :], bC[:, :], AF.Exp, scale=-1.0, r=[bC], w=[En])
                P.tt('dve', Ex[:, :], bC[:, :], lw32[:, :], ALU.subtract, r=[bC, lw32], w=[Ex])
                P.act(Ex[:, :], Ex[:, :], AF.Exp, r=[Ex], w=[Ex])
                P.tt('pool', rt[:, :], r32[:, :], Ep[:, :], ALU.mult, r=[r32, Ep], w=[rt])
                P.tt('dve', kt[:, :], kk[:, :], Ex[:, :], ALU.mult, r=[kk, Ex], w=[kt])
                P.tt('pool', bt[:, :], bb[:, :], En[:, :], ALU.mult, r=[bb, En], w=[bt])
                P.tt('dve', kdt[:, :], kd[:, :], En[:, :], ALU.mult, r=[kd, En], w=[kdt])
                P.cp('pool', bt_bf[:, :], bt[:, :], r=[bt], w=[bt_bf])
                P.cp('pool', kdt_bf[:, :], kdt[:, :], r=[kdt], w=[kdt_bf])
                for (src, dstap, dbuf) in ((kt, KR[:, :, 0, :], KR), (rt, KR[:, :, 1, :], KR), (bt, btT[:, :, :], btT), (kdt, kdtT[:, :, :], kdtT)):
                    bank = psr()
                    for jg in range(4):
                        P.tr(bank[:, jg * 128:(jg + 1) * 128], src[:, jg * 128:(jg + 1) * 128], ident, r=[src, cst], w=[bank])
                    P.cp(evr(), dstap, bank[:, :].rearrange("p (a b) -> p a b", b=128), r=[bank], w=[dbuf])
                for h in range(8):
                    jg, rs = h // 2, slice((h % 2) * 64, (h % 2 + 1) * 64)
                    bM = psr()
                    krr = KR[rs, jg, :, :].rearrange("p a b -> p (a b)")
                    P.mm(bM[:, 0:256], btT[rs, jg, :], krr, r=[btT, KR], w=[bM])
                    P.mm(bM[:, 256:512], kdtT[rs, jg, :], krr, r=[kdtT, KR], w=[bM])
                    P.tt('dve', NM[0][:, h, 0, :], bM[:, 0:128], MK[:, d, 0:128], ALU.mult, r=[bM, MK], w=[NM[0]])
                    P.tt('dve', R3[:, h, :, :].rearrange("p a b -> p (a b)"), bM[:, 128:512], MK[:, d, 128:512], ALU.mult,
                         r=[bM, MK], w=[R3])
                for hh in range(2):
                    bank = psr()
                    rs = slice(hh * 64, (hh + 1) * 64)
                    for jg in range(4):
                        P.mm(bank[:, jg * 128:(jg + 1) * 128], KR[rs, jg, 0, :], btT[rs, jg, :], r=[KR, btT], w=[bank])
                    P.tt('dve', NM[0][:, hh:8:2, 1, :], bank[:, :].rearrange("p (a b) -> p a b", b=128),
                         bc3(MKa[:, d, :], 4, 1), ALU.mult, r=[bank, MKa], w=[NM[0]])
                P.tt('pool', X[0][:, :, :], NM[0][:, :, 0, :], bc3(ident, 8, 1), ALU.add, r=[NM[0], cst], w=[X[0]])
                for j in range(6):
                    cur, nxt = NM[j % 2], NM[(j + 1) % 2]
                    Xc, Xn = X[j % 2], X[(j + 1) % 2]
                    for hp in range(4):
                        bank = psr()
                        for q in range(2):
                            h = hp * 2 + q
                            P.mm(bank[:, q * 256:q * 256 + 128], cur[:, h, 1, :], cur[:, h, 0, :], r=[cur], w=[bank])
                            P.mm(bank[:, q * 256 + 128:q * 256 + 256], cur[:, h, 0, :], cur[:, h, 1, :], r=[cur], w=[bank])
                        P.cp(evr(), nxt[:, hp * 2:hp * 2 + 2, :, :].rearrange("p a b c -> p (a b c)"), bank[:, :], r=[bank], w=[nxt])
                    for hp in range(2):
                        bank = psr()
                        for q in range(4):
                            h = hp * 4 + q
                            P.mm(bank[:, q * 128:(q + 1) * 128], nxt[:, h, 1, :], Xc[:, h, :], r=[nxt, Xc], w=[bank])
                        P.tt('dve', Xn[:, hp * 4:(hp + 1) * 4, :].rearrange("p a b -> p (a b)"), bank[:, :],
                             Xc[:, hp * 4:(hp + 1) * 4, :].rearrange("p a b -> p (a b)"), ALU.add, r=[bank, Xc], w=[Xn])
                XF = X[0]
                bZ = psr()
                for h in range(8):
                    P.mm(bZ[:, h * 64:(h + 1) * 64], R3[:, h, 1, :], vbf[:, h * 64:(h + 1) * 64], r=[R3, vbf], w=[bZ])
                P.cp('act', Z1[:, :], bZ[:, :], r=[bZ], w=[Z1])
                bU = psr()
                for h in range(8):
                    P.mm(bU[:, h * 64:(h + 1) * 64], XF[:, h, :], Z1[:, h * 64:(h + 1) * 64], r=[XF, Z1], w=[bU])
                P.cp('act', Ut[:, :], bU[:, :], r=[bU], w=[Ut])
                for hp in range(2):
                    bank = psr()
                    for q in range(4):
                        h = hp * 4 + q
                        jg = h // 2
                        P.mm(bank[:, q * 128:(q + 1) * 128], kt[:, jg * 128:(jg + 1) * 128], XF[:, h, :], r=[kt, XF], w=[bank])
                    for q in range(4):
                        h = hp * 4 + q
                        jg, rs = h // 2, slice((h % 2) * 64, (h % 2 + 1) * 64)
                        P.cp(evr(), WT[rs, jg, :], bank[rs, q * 128:(q + 1) * 128], r=[bank], w=[WT])
                bP = psr()
                for jg in range(4):
                    P.mm(bP[:, jg * 128:(jg + 1) * 128], WT[:, jg, :], Mbf[:, jg, :], r=[WT, Mbf], w=[bP])
                P.stt(Un[:, :], bP[:, :], -1.0, Ut[:, :], ALU.mult, ALU.subtract, r=[bP, Ut], w=[Un])
                bY = psr()
                for jg in range(4):
                    P.mm(bY[:, jg * 128:(jg + 1) * 128], KR[:, jg, 1, :], Mbf[:, jg, :], start=True, stop=False, r=[KR, Mbf], w=[bY])
                    for h in (2 * jg, 2 * jg + 1):
                        hs = slice(h * 64, (h + 1) * 64)
                        P.mm(bY[:, hs], R3[:, h, 2, :], vbf[:, hs], start=False, stop=False, r=[R3, vbf], w=[bY])
                    for h in (2 * jg, 2 * jg + 1):
                        hs = slice(h * 64, (h + 1) * 64)
                        P.mm(bY[:, hs], R3[:, h, 0, :], Un[:, hs], start=False, stop=(h == 2 * jg + 1), r=[R3, Un], w=[bY])
                yo = Yo[ci % 2]
                P.cp('act', yo[:, :], bY[:, :], r=[bY], w=[yo])
                P.ld(C.ydir[d, rows, :], yo[:, :], r=[yo], w=[ydb[d][c]])
                bS = psr()
                for jg in range(4):
                    js = slice(jg * 128, (jg + 1) * 128)
                    P.mm(bS[:, js], bt_bf[:, js], Un[:, js], start=True, stop=False, r=[bt_bf, Un], w=[bS])
                    P.mm(bS[:, js], kdt_bf[:, js], vbf[:, js], start=False, stop=True, r=[kdt_bf, vbf], w=[bS])
                for hh in range(2):
                    rs = slice(hh * 64, (hh + 1) * 64)
                    src = bS[rs, :].rearrange("p (j q) -> p j q", q=128)[:, :, hh * 64:(hh + 1) * 64]
                    mv = Mst[rs, :, hh * 64:(hh + 1) * 64]
                    P.tt('dve', mv, mv, src, ALU.add, r=[Mst, bS], w=[Mst])
                    P.tt('dve', mv, mv, gC[rs, :].unsqueeze(2).broadcast_to([64, 4, 64]), ALU.mult, r=[Mst, gC], w=[Mst])
                P.cp('pool', Mbf[:, :, :], Mst[:, :, :], r=[Mst], w=[Mbf])
        P.emit()
        C.stats.append(('C', P.stats))
        fy = [sb('fy%d' % i, [128, 4, 512], F32) for i in range(2)]
        P = Prog(nc, C.sems)
        for c in range(NCH):
            rows = slice(c * 128, (c + 1) * 128)
            f = fy[c % 2]
            P.ld(f[:, 0, :], C.ydir[0, rows, :], r=[ydb[0][c]], w=[f])
            P.ld(f[:, 1, :], C.ydir[1, rows, :], r=[ydb[1][c]], w=[f])
            P.ld(f[:, 2, :], C.vg[0, rows, :], r=[vgb[0][c]], w=[f])
            P.ld(f[:, 3, :], C.vg[1, rows, :], r=[vgb[1][c]], w=[f])
            y = f[:, 0, :]
            P.tt('dve', y, y, f[:, 1, :], ALU.add, r=[f], w=[f])
            P.op('dve', lambda e, o=sm8[:, 0:8], i_=v3(y): e.reduce_sum(o, i_, axis=AX.X), [f.buf], [sm8.buf])
            P.tt('pool', tmp[:, :], y, y, ALU.mult, r=[f], w=[tmp])
            P.op('dve', lambda e, o=sm8[:, 8:16], i_=v3(tmp[:, :]): e.reduce_sum(o, i_, axis=AX.X), [tmp.buf], [sm8.buf])
            P.ts('dve', sm8[:, 0:16], sm8[:, 0:16], 1.0 / 64.0, None, ALU.mult, r=[sm8], w=[sm8])
            P.tt('dve', sm8[:, 16:24], sm8[:, 0:8], sm8[:, 0:8], ALU.mult, r=[sm8], w=[sm8])
            P.tt('dve', sm8[:, 16:24], sm8[:, 8:16], sm8[:, 16:24], ALU.subtract, r=[sm8], w=[sm8])
            P.ts('dve', sm8[:, 16:24], sm8[:, 16:24], GN_EPS, None, ALU.add, r=[sm8], w=[sm8])
            P.act(sm8[:, 16:24], sm8[:, 16:24], AF.Sqrt, r=[sm8], w=[sm8])
            P.op('dve', lambda e, o=sm8[:, 24:32], i_=sm8[:, 16:24]: e.reciprocal(o, i_), [sm8.buf], [sm8.buf])
            P.tt('dve', v3(y), v3(y), bc3(sm8[:, 0:8], 64, 2), ALU.subtract, r=[f, sm8], w=[f])
            P.tt('dve', v3(y), v3(y), bc3(sm8[:, 24:32], 64, 2), ALU.mult, r=[f, sm8], w=[f])
            P.tt('pool', y, y, lgb[:, :], ALU.mult, r=[f, lgb], w=[f])
            P.tt('pool', y, y, lbb[:, :], ALU.add, r=[f, lbb], w=[f])
            P.tt('dve', sm8[:, 0:8], bon[:, c, 0, :], bon[:, c, 1, :], ALU.add, r=[bon, sm8], w=[sm8])
            P.tt('dve', v3(tmp[:, :]), v3(f[:, 2, :]), bc3(sm8[:, 0:8], 64, 2), ALU.mult, r=[f, sm8], w=[tmp])
            P.tt('pool', y, y, tmp[:, :], ALU.add, r=[f, tmp], w=[f])
            P.tt('pool', y, y, f[:, 3, :], ALU.mult, r=[f], w=[f])
            P.ld(C.ymix[b, rows, 512:1024], y, r=[f])
        P.emit()
        C.stats.append(('C', P.stats))


def build(T=2048, NS=2, depth=2, debug=False, stages='ABCD'):
    nc = bass.Bass("TRN2", target_bir_lowering=False)
    C = Ctx()
    C.nc, C.T, C.NS, C.NCH = nc, T, NS, T // 128
    C.stats = []
    C.x = nc.dram_tensor("x", [NS, T, D], F32, kind="ExternalInput").ap()
    C.w = {n: nc.dram_tensor(n, W_SHAPES[n], F32, kind="ExternalInput").ap() for n in W_NAMES}
    C.cst_d = nc.dram_tensor("cst", [128, 768], F32, kind="ExternalInput").ap()
    C.out = nc.dram_tensor("out", [NS, T, D], F32, kind="ExternalOutput").ap()
    kind = "ExternalOutput" if debug else "Internal"
    C.hbuf = nc.dram_tensor("hbuf", [NS, T, D], F32, kind=kind).ap()
    C.h1buf = nc.dram_tensor("h1buf", [NS, T, D], F32, kind=kind).ap()
    C.zbuf = nc.dram_tensor("zbuf", [NS, T, 512], F32, kind=kind).ap()
    C.dtbuf = nc.dram_tensor("dtbuf", [NS, T, 16], F32, kind=kind).ap()
    C.xbcT = nc.dram_tensor("xbcT", [NS, 1024, T], BF16, kind=kind).ap()
    C.rwT = nc.dram_tensor("rwT", [NS, 1792, T], BF16, kind=kind).ap()
    C.ymix = nc.dram_tensor("ymix", [NS, T, D], F32, kind=kind).ap()
    C.ydir = nc.dram_tensor("ydir", [2, T, 512], F32, kind=kind).ap()
    C.vg = nc.dram_tensor("vg", [3, T, 512], F32, kind=kind).ap()
    with nc.sbuf_tensor('cst_sb', [128, 768], F32) as cst_sb, contextlib.ExitStack() as semstack:
        C.cst = Tile(cst_sb, 'cst')
        C.sems = SemState(nc, 8, semstack)
        stage_consts(C)
        for l in range(depth):
            if 'A' in stages:
                stage_A(C, l)
            for b in range(NS):
                if 'B' in stages:
                    stage_B(C, l, b)
                if 'C' in stages:
                    stage_C(C, l, b)
            if 'D' in stages:
                stage_D1(C, l)
                stage_D2(C, l, last=(l == depth - 1))
    return nc, C


_CACHE = {}


def kernel(**inputs):
    n_cores = 8
    x = np.ascontiguousarray(np.asarray(inputs["x"], dtype=np.float32))
    B, T, _ = x.shape
    NS = B // n_cores
    key = (T, NS)
    if key not in _CACHE:
        _CACHE[key] = build(T=T, NS=NS, depth=2, debug=False)[0]
    nc = _CACHE[key]
    cst = make_consts()
    ws = {n: np.ascontiguousarray(np.asarray(inputs[n], dtype=np.float32)) for n in W_NAMES}
    in_maps = []
    for i in range(n_cores):
        m = dict(ws)
        m["x"] = np.ascontiguousarray(x[i * NS:(i + 1) * NS])
        m["cst"] = cst
        in_maps.append(m)
    res = run_bass_kernel_spmd(nc, in_maps, core_ids=list(range(n_cores)))
    return np.concatenate([np.asarray(r["out"], dtype=np.float32) for r in res.results], axis=0)
```

```python
import contextlib
import numpy as np
import concourse.bass as bass
import concourse.mybir as mybir
from concourse.bass_utils import run_bass_kernel_spmd

F32 = mybir.dt.float32
BF16 = mybir.dt.bfloat16
AF = mybir.ActivationFunctionType
ALU = mybir.AluOpType
AX = mybir.AxisListType

ENGS = ('pe', 'act', 'dve', 'pool', 'sp')


class Buf:
    __slots__ = ('name', 'last_w', 'readers')

    def __init__(self, name=''):
        self.name = name
        self.last_w = None
        self.readers = []


class _Op:
    __slots__ = ('eng', 'idx', 'fn', 'deps', 'is_dma', 'dslot', 'dval', 'needs_inc', 'cnt', 'waits', 'prog')

    def __init__(self, eng, idx, fn, is_dma):
        self.eng = eng
        self.idx = idx
        self.fn = fn
        self.is_dma = is_dma
        self.deps = []
        self.dslot = 0
        self.dval = 0
        self.needs_inc = False
        self.cnt = 0
        self.waits = []
        self.prog = None


class SemState:
    def __init__(self, nc, ring=8, stack=None):
        self.stack = stack if stack is not None else contextlib.ExitStack()
        st = self.stack
        self.csem = {e: st.enter_context(nc.semaphore('c_' + e)) for e in ENGS if e != 'sp'}
        self.dsem = {e: [st.enter_context(nc.semaphore('d_%s_%d' % (e, i))) for i in range(ring)] for e in ('sp', 'pool', 'act')}
        self.c_off = {e: 0 for e in ENGS}
        self.d_cnt = {e: 0 for e in ENGS}


class Prog:
    def __init__(self, nc, sems=None, ring=8):
        self.nc = nc
        self.q = {e: [] for e in ENGS}
        self.ring = ring
        self.dma_hist = {e: [] for e in ENGS}
        self.sems = sems if sems is not None else SemState(nc, ring)

    def _add(self, eng, fn, reads, writes, is_dma):
        op = _Op(eng, len(self.q[eng]), fn, is_dma)
        op.prog = self
        deps = []
        for b in reads:
            if b.last_w is not None:
                deps.append(b.last_w)
        for b in writes:
            if b.last_w is not None:
                deps.append(b.last_w)
            deps.extend(b.readers)
        if is_dma:
            h = self.dma_hist[eng]
            k = len(h)
            kg = k + self.sems.d_cnt[eng]
            op.dslot = kg % self.ring
            op.dval = 16 * (kg // self.ring + 1)
            if k >= self.ring:
                deps.append(h[k - self.ring])
            h.append(op)
        seen = set()
        for d in deps:
            if d is not op and id(d) not in seen and getattr(d, 'prog', self) is self:
                seen.add(id(d))
                op.deps.append(d)
        for b in writes:
            b.last_w = op
            b.readers = []
        for b in reads:
            if b.last_w is not op:
                b.readers.append(op)
        self.q[eng].append(op)
        return op

    def op(self, eng, fn, reads=(), writes=()):
        return self._add(eng, fn, reads, writes, False)

    def dma(self, eng, fn, reads=(), writes=()):
        return self._add(eng, fn, reads, writes, True)


    @staticmethod
    def _bufs(lst):
        return [b.buf if hasattr(b, 'buf') else b for b in lst]

    def mm(self, out, lhsT, rhs, start=True, stop=True, r=(), w=()):
        return self.op('pe', lambda e: e.matmul(out, lhsT, rhs, start=start, stop=stop), self._bufs(r), self._bufs(w))

    def tr(self, out, in_, ident, r=(), w=()):
        return self.op('pe', lambda e: e.transpose(out, in_, ident), self._bufs(r), self._bufs(w))

    def act(self, out, in_, func, bias=None, scale=None, accum=None, r=(), w=()):
        kw = {}
        if bias is not None:
            kw['bias'] = bias
        if scale is not None:
            kw['scale'] = scale
        if accum is not None:
            kw['accum_out'] = accum
        return self.op('act', lambda e: e.activation(out, in_, func, **kw), self._bufs(r), self._bufs(w))

    def tt(self, eng, out, in0, in1, op, r=(), w=()):
        return self.op(eng, lambda e: e.tensor_tensor(out, in0, in1, op), self._bufs(r), self._bufs(w))

    def ts(self, eng, out, in0, s1, s2=None, op0=None, op1=None, r=(), w=()):
        if op1 is None:
            return self.op(eng, lambda e: e.tensor_scalar(out, in0, s1, None, op0), self._bufs(r), self._bufs(w))
        return self.op(eng, lambda e: e.tensor_scalar(out, in0, s1, s2, op0, op1), self._bufs(r), self._bufs(w))

    def stt(self, out, in0, scalar, in1, op0, op1, r=(), w=()):
        return self.op('dve', lambda e: e.scalar_tensor_tensor(out, in0, scalar, in1, op0, op1), self._bufs(r), self._bufs(w))

    def cp(self, eng, out, in_, r=(), w=()):
        if eng == 'act':
            return self.op('act', lambda e: e.copy(out, in_), self._bufs(r), self._bufs(w))
        return self.op(eng, lambda e: e.tensor_copy(out, in_), self._bufs(r), self._bufs(w))

    def ld(self, out, in_, r=(), w=(), eng='sp', **kw):
        return self.dma(eng, lambda e: e.dma_start(out=out, in_=in_, **kw), self._bufs(r), self._bufs(w))

    def _last_ops(self, skip=None):
        out = []
        for e in ENGS:
            if e != skip and self.q[e]:
                for o in reversed(self.q[e]):
                    if not o.is_dma and o.fn is not None:
                        out.append(o)
                        break
            h = self.dma_hist[e]
            out.extend(h[-self.ring:])
        return out

    def barrier(self):
        deps_for = {e: self._last_ops(skip=e) for e in ENGS}
        for e in ENGS:
            op = _Op(e, len(self.q[e]), None, False)
            op.prog = self
            op.deps = deps_for[e]
            self.q[e].append(op)

    def finish(self):
        op = _Op('sp', len(self.q['sp']), None, False)
        op.prog = self
        for e in ENGS:
            op.deps.extend(self.dma_hist[e][-self.ring:])
        self.q['sp'].append(op)

    def emit(self):
        nc = self.nc
        self.barrier()
        self.finish()
        for e in ENGS:
            seen = {p: -1 for p in ENGS}
            seen_dma = {}
            for o in self.q[e]:
                best = {}
                for d in o.deps:
                    if d.is_dma:
                        key = (d.eng, d.dslot)
                        if seen_dma.get(key, 0) >= d.dval:
                            continue
                        seen_dma[key] = d.dval
                        o.waits.append(d)
                    else:
                        if d.eng == 'pe' and e == 'pe':
                            continue
                        if d.fn is None:
                            continue
                        if d.idx <= seen[d.eng]:
                            continue
                        if d.eng not in best or best[d.eng].idx < d.idx:
                            best[d.eng] = d
                for p, d in best.items():
                    seen[p] = d.idx
                    d.needs_inc = True
                    o.waits.append(d)
        for e in ENGS:
            c = self.sems.c_off[e]
            for o in self.q[e]:
                if o.needs_inc:
                    c += 1
                o.cnt = c
            self.sems.c_off[e] = c
            self.sems.d_cnt[e] += len(self.dma_hist[e])
        n_wait = sum(len(o.waits) for e in ENGS for o in self.q[e])
        n_ops = sum(len(self.q[e]) for e in ENGS)
        self.stats = dict(n_ops=n_ops, n_wait=n_wait, per_eng={e: len(self.q[e]) for e in ENGS})
        with contextlib.ExitStack() as st:
            csem = self.sems.csem
            dsem = self.sems.dsem
            block = st.enter_context(nc.Block())

            def run(ename, eng):
                for o in self.q[ename]:
                    for d in o.waits:
                        if d.is_dma:
                            eng.wait_ge(dsem[d.eng][d.dslot], d.dval)
                        else:
                            eng.wait_ge(csem[d.eng], d.cnt)
                    if o.fn is None:
                        continue
                    ins = o.fn(eng)
                    if o.is_dma:
                        ins.then_inc(dsem[ename][o.dslot], 16)
                    elif o.needs_inc:
                        ins.then_inc(csem[ename], 1)

            @block.tensor
            def _(eng):
                run('pe', eng)

            @block.scalar
            def _(eng):
                run('act', eng)

            @block.vector
            def _(eng):
                run('dve', eng)

            @block.gpsimd
            def _(eng):
                run('pool', eng)

            @block.sync
            def _(eng):
                run('sp', eng)


D = 1024
IN_COLS = 3344
ALPHA = float(4 ** 0.25)
LN_EPS = 1e-5
RMS_EPS = 1e-5
GN_EPS = 64e-5
DECAY_C = float(np.exp(-0.5))
NEU_DT = BF16

W_NAMES = ["ln0_g", "ln0_b", "w_in", "conv_w", "conv_b", "dt_bias", "a_log", "d_skip", "ssd_norm_g",
           "mu_rwkv", "w0", "w_up", "a0", "a_up", "g_up", "k_k", "k_a", "r_k", "lnx_g", "lnx_b", "w_out",
           "ln1_g", "ln1_b", "w_fc", "w_proj", "ln2_g", "ln2_b"]
W_SHAPES = {
    "ln0_g": [1024], "ln0_b": [1024], "w_in": [2, 1024, 3344], "conv_w": [2, 5, 1024], "conv_b": [2, 1024],
    "dt_bias": [2, 2, 8], "a_log": [2, 2, 8], "d_skip": [2, 8], "ssd_norm_g": [2, 512], "mu_rwkv": [2, 1792],
    "w0": [2, 2, 512], "w_up": [2, 2, 64, 512], "a0": [2, 2, 512], "a_up": [2, 2, 64, 512], "g_up": [2, 128, 512],
    "k_k": [2, 512], "k_a": [2, 512], "r_k": [2, 8, 64], "lnx_g": [2, 512], "lnx_b": [2, 512],
    "w_out": [2, 1024, 1024], "ln1_g": [2, 1024], "ln1_b": [2, 1024], "w_fc": [2, 1024, 4096],
    "w_proj": [2, 4096, 1024], "ln2_g": [2, 1024], "ln2_b": [2, 1024],
}


def make_consts():
    i = np.arange(128)
    ident = np.eye(128, dtype=np.float32)
    U = (i[:, None] <= i[None, :]).astype(np.float32)
    Lo = (i[:, None] >= i[None, :]).astype(np.float32)
    Us = (i[:, None] < i[None, :]).astype(np.float32)
    Ls = (i[:, None] > i[None, :]).astype(np.float32)
    ones = np.ones((128, 128), np.float32)
    return np.ascontiguousarray(np.concatenate([ident, U, Lo, Us, Ls, ones], axis=1))


class Tile:
    def __init__(self, t, name=''):
        self.t = t
        self.buf = Buf(name)

    def __getitem__(self, k):
        return self.t[k]


class Ctx:
    pass


def layer_norm_tile(P, C, src, dst, g_t, b_t, stat, eps, r, w):
    st = stat
    P.op('dve', lambda e: e.bn_stats(st[:, 0:6], src[:, 0:512]), P._bufs(r), [st.buf])
    P.op('dve', lambda e: e.bn_stats(st[:, 6:12], src[:, 512:1024]), P._bufs(r), [st.buf])
    P.op('dve', lambda e: e.bn_aggr(st[:, 12:14], st[:, 0:12].rearrange("p (a b) -> p a b", b=6)), [st.buf], [st.buf])
    P.ts('dve', st[:, 14:15], st[:, 13:14], eps, None, ALU.add, r=[st], w=[st])
    P.act(st[:, 15:16], st[:, 14:15], AF.Sqrt, r=[st], w=[st])
    P.op('dve', lambda e: e.reciprocal(st[:, 16:17], st[:, 15:16]), [st.buf], [st.buf])
    P.stt(st[:, 17:18], st[:, 12:13], -1.0, st[:, 16:17], ALU.mult, ALU.mult, r=[st], w=[st])
    P.act(dst, src, AF.Identity, bias=st[:, 17:18], scale=st[:, 16:17], r=list(r) + [st], w=w)
    P.tt('pool', dst, dst, g_t[:, :], ALU.mult, r=list(w) + [g_t], w=w)
    P.tt('dve', dst, dst, b_t[:, :], ALU.add, r=list(w) + [b_t], w=w)


_UID = [0]


_SBUSE = [0]
SB_BUDGET = 176 * 1024


def _sb_release(n):
    _SBUSE[0] -= n


def _sb(st, nc, name, shape, dt):
    _UID[0] += 1
    name = '%s_%d' % (name, _UID[0])
    n = int(np.prod(shape[1:])) * (2 if dt == BF16 else 4)
    n = (n + 31) // 32 * 32
    _SBUSE[0] += n
    assert _SBUSE[0] <= SB_BUDGET, ('SBUF budget exceeded', name, _SBUSE[0])
    t = Tile(st.enter_context(nc.sbuf_tensor(name, shape, dt)), name)
    st.callback(_sb_release, n)
    return t


def _psum(st, nc, n=8):
    _UID[0] += 1
    return [Tile(st.enter_context(nc.psum_tensor('ps%d_%d' % (i, _UID[0]), [128, 512], F32)), 'ps%d' % i) for i in range(n)]


class RR:
    def __init__(self, items):
        self.items = items
        self.i = 0

    def __call__(self):
        x = self.items[self.i % len(self.items)]
        self.i += 1
        return x


def load_bcast(P, tile, src_row, n, eng='sp'):
    P.ld(tile[:, 0:n], src_row.partition_broadcast(128), w=[tile], eng=eng)


def stage_consts(C):
    P = Prog(C.nc, C.sems)
    P.ld(C.cst[:, :], C.cst_d[:, :], w=[C.cst])
    P.emit()


def stage_A(C, l):
    nc, T, NS = C.nc, C.T, C.NS
    TB = min(512, T)
    NJ = TB // 128
    ident = C.cst[:, 0:128]
    with contextlib.ExitStack() as st:
        Win = _sb(st, nc, 'Win', [128, 8, IN_COLS], BF16)
        Wb = [Buf('Win%d' % k) for k in range(8)]
        hin = [_sb(st, nc, 'hin%d' % i, [128, D], F32) for i in range(2)]
        hT = [_sb(st, nc, 'hT%d' % i, [128, 8, TB], BF16) for i in range(2)]
        stat = [_sb(st, nc, 'stat%d' % i, [128, 32], F32) for i in range(2)]
        zo = [_sb(st, nc, 'zo%d' % i, [128, 512], F32) for i in range(2)]
        dto = [_sb(st, nc, 'dto%d' % i, [128, 16], F32) for i in range(2)]
        fo = [_sb(st, nc, 'fo%d' % i, [128, TB], BF16) for i in range(3)]
        if l == 0:
            g0 = _sb(st, nc, 'g0', [128, D], F32)
            b0 = _sb(st, nc, 'b0', [128, D], F32)
        ps = _psum(st, nc)
        P = Prog(nc, C.sems)
        for kc in range(8):
            P.ld(Win[:, kc, :], C.w['w_in'][l, kc * 128:(kc + 1) * 128, :], w=[Wb[kc]], eng='pool', max_dma_last_dim=4096)
        if l == 0:
            load_bcast(P, g0, C.w['ln0_g'], D)
            load_bcast(P, b0, C.w['ln0_b'], D)
        psr = RR(ps)
        evr = RR(['act', 'dve'])
        zor, dtor, forr = RR(zo), RR(dto), RR(fo)
        ti = 0
        for b in range(NS):
            src = C.x[b] if l == 0 else C.hbuf[b]
            for tb in range(T // TB):
                hTt = hT[(b * (T // TB) + tb) % 2]
                for j in range(NJ):
                    rows = slice(tb * TB + j * 128, tb * TB + (j + 1) * 128)
                    hi = hin[ti % 2]
                    P.ld(hi[:, :], src[rows, :], w=[hi])
                    if l == 0:
                        layer_norm_tile(P, C, hi[:, :], hi[:, :], g0, b0, stat[ti % 2], LN_EPS, r=[hi], w=[hi])
                        P.ld(C.hbuf[b, rows, :], hi[:, :], r=[hi])
                    for half in range(2):
                        bank = psr()
                        for q in range(4):
                            kc = half * 4 + q
                            P.tr(bank[:, q * 128:(q + 1) * 128], hi[:, kc * 128:(kc + 1) * 128], ident, r=[hi, C.cst], w=[bank])
                        P.cp(evr(), hTt[:, half * 4:half * 4 + 4, j * 128:(j + 1) * 128],
                             bank[:, :].rearrange("p (a b) -> p a b", b=128), r=[bank], w=[hTt])
                    ti += 1
                for j in range(NJ):
                    rows = slice(tb * TB + j * 128, tb * TB + (j + 1) * 128)
                    bank = psr()
                    for kc in range(8):
                        P.mm(bank[:, :], hTt[:, kc, j * 128:(j + 1) * 128], Win[:, kc, 0:512], start=(kc == 0), stop=(kc == 7),
                             r=[hTt, Wb[kc]], w=[bank])
                    z = zor()
                    P.cp(evr(), z[:, :], bank[:, :], r=[bank], w=[z])
                    P.ld(C.zbuf[b, rows, :], z[:, :], r=[z])
                    bank = psr()
                    for kc in range(8):
                        P.mm(bank[:, 0:16], hTt[:, kc, j * 128:(j + 1) * 128], Win[:, kc, 1536:1552], start=(kc == 0), stop=(kc == 7),
                             r=[hTt, Wb[kc]], w=[bank])
                    dt = dtor()
                    P.cp(evr(), dt[:, :], bank[:, 0:16], r=[bank], w=[dt])
                    P.ld(C.dtbuf[b, rows, :], dt[:, :], r=[dt])
                for cc in range(22):
                    col0 = 512 + cc * 128 if cc < 8 else 1552 + (cc - 8) * 128
                    bank = psr()
                    for kc in range(8):
                        P.mm(bank[:, 0:TB], Win[:, kc, col0:col0 + 128], hTt[:, kc, :], start=(kc == 0), stop=(kc == 7),
                             r=[hTt, Wb[kc]], w=[bank])
                    f = forr()
                    P.cp(evr(), f[:, :], bank[:, 0:TB], r=[bank], w=[f])
                    if cc < 8:
                        dst = C.xbcT[b, cc * 128:(cc + 1) * 128, tb * TB:(tb + 1) * TB]
                    else:
                        dst = C.rwT[b, (cc - 8) * 128:(cc - 7) * 128, tb * TB:(tb + 1) * TB]
                    P.ld(dst, f[:, :], r=[f])
        P.emit()
        C.stats.append(('A', P.stats))


def load_w_bf16(P, tile, bufs, src, nk, eng='pool'):
    for kc in range(nk):
        P.ld(tile[:, kc, :], src[kc * 128:(kc + 1) * 128, :], w=[bufs[kc]], eng=eng, max_dma_last_dim=4096)


def stage_D1(C, l):
    nc, T, NS = C.nc, C.T, C.NS
    ident = C.cst[:, 0:128]
    with contextlib.ExitStack() as st:
        Wo = _sb(st, nc, 'Wo', [128, 8, D], BF16)
        Wob = [Buf('Wo%d' % k) for k in range(8)]
        g1 = _sb(st, nc, 'g1', [128, D], F32)
        b1 = _sb(st, nc, 'b1', [128, D], F32)
        ym = [_sb(st, nc, 'ym%d' % i, [128, D], F32) for i in range(2)]
        yT = [_sb(st, nc, 'yT%d' % i, [128, 8, 128], BF16) for i in range(2)]
        hr = [_sb(st, nc, 'hr%d' % i, [128, D], F32) for i in range(2)]
        t1 = [_sb(st, nc, 't1%d' % i, [128, D], F32) for i in range(2)]
        stat = [_sb(st, nc, 'stat%d' % i, [128, 32], F32) for i in range(2)]
        ps = _psum(st, nc)
        P = Prog(nc, C.sems)
        load_w_bf16(P, Wo, Wob, C.w['w_out'][l], 8)
        load_bcast(P, g1, C.w['ln1_g'][l], D)
        load_bcast(P, b1, C.w['ln1_b'][l], D)
        psr = RR(ps)
        evr = RR(['act', 'dve'])
        ti = 0
        for b in range(NS):
            for c in range(T // 128):
                rows = slice(c * 128, (c + 1) * 128)
                y, yt, h, t, sx = ym[ti % 2], yT[ti % 2], hr[ti % 2], t1[ti % 2], stat[ti % 2]
                P.ld(y[:, :], C.ymix[b, rows, :], w=[y])
                P.ld(h[:, :], C.hbuf[b, rows, :], w=[h])
                for half in range(2):
                    bank = psr()
                    for q in range(4):
                        kc = half * 4 + q
                        P.tr(bank[:, q * 128:(q + 1) * 128], y[:, kc * 128:(kc + 1) * 128], ident, r=[y, C.cst], w=[bank])
                    P.cp(evr(), yt[:, half * 4:half * 4 + 4, :], bank[:, :].rearrange("p (a b) -> p a b", b=128), r=[bank], w=[yt])
                for half in range(2):
                    bank = psr()
                    for kc in range(8):
                        P.mm(bank[:, :], yt[:, kc, :], Wo[:, kc, half * 512:(half + 1) * 512], start=(kc == 0), stop=(kc == 7),
                             r=[yt, Wob[kc]], w=[bank])
                    P.stt(t[:, half * 512:(half + 1) * 512], h[:, half * 512:(half + 1) * 512], ALPHA, bank[:, :], ALU.mult, ALU.add,
                          r=[h, bank], w=[t])
                layer_norm_tile(P, C, t[:, :], t[:, :], g1, b1, sx, LN_EPS, r=[t], w=[t])
                P.ld(C.h1buf[b, rows, :], t[:, :], r=[t])
                ti += 1
        P.emit()
        C.stats.append(('D1', P.stats))


def stage_D2(C, l, last):
    nc, T, NS = C.nc, C.T, C.NS
    ident = C.cst[:, 0:128]
    TB = 256
    with contextlib.ExitStack() as st:
        Wf = _sb(st, nc, 'Wf', [128, 8, 4096], BF16)
        Wfb = [Buf('Wf%d' % k) for k in range(8)]
        Wp = _sb(st, nc, 'Wp', [128, 32, D], BF16)
        Wpb = [Buf('Wp%d' % k) for k in range(32)]
        g2 = _sb(st, nc, 'g2', [128, D], F32)
        b2 = _sb(st, nc, 'b2', [128, D], F32)
        h1 = [_sb(st, nc, 'h1%d' % i, [128, D], F32) for i in range(2)]
        h1T = _sb(st, nc, 'h1T', [128, 8, TB], BF16)
        aT = _sb(st, nc, 'aT', [128, 32, TB], BF16)
        tmp = [_sb(st, nc, 'tmp%d' % i, [128, TB], F32) for i in range(2)]
        t2 = [_sb(st, nc, 't2%d' % i, [128, D], F32) for i in range(1)]
        stat = [_sb(st, nc, 'stat%d' % i, [128, 32], F32) for i in range(2)]
        ps = _psum(st, nc)
        P = Prog(nc, C.sems)
        load_w_bf16(P, Wf, Wfb, C.w['w_fc'][l], 8)
        load_w_bf16(P, Wp, Wpb, C.w['w_proj'][l], 32)
        load_bcast(P, g2, C.w['ln2_g'][l], D)
        load_bcast(P, b2, C.w['ln2_b'][l], D)
        psr = RR(ps)
        evr = RR(['act', 'dve'])
        tmr = RR(tmp)
        ti = 0
        for b in range(NS):
            dstb = C.out[b] if last else C.hbuf[b]
            for tb in range(T // TB):
                for j in range(2):
                    rows = slice(tb * TB + j * 128, tb * TB + (j + 1) * 128)
                    h = h1[j]
                    P.ld(h[:, :], C.h1buf[b, rows, :], w=[h])
                    for half in range(2):
                        bank = psr()
                        for q in range(4):
                            kc = half * 4 + q
                            P.tr(bank[:, q * 128:(q + 1) * 128], h[:, kc * 128:(kc + 1) * 128], ident, r=[h, C.cst], w=[bank])
                        P.cp(evr(), h1T[:, half * 4:half * 4 + 4, j * 128:(j + 1) * 128],
                             bank[:, :].rearrange("p (a b) -> p a b", b=128), r=[bank], w=[h1T])
                for fc in range(32):
                    bank = psr()
                    for kc in range(8):
                        P.mm(bank[:, 0:TB], Wf[:, kc, fc * 128:(fc + 1) * 128], h1T[:, kc, :], start=(kc == 0), stop=(kc == 7),
                             r=[h1T, Wfb[kc]], w=[bank])
                    tm = tmr()
                    if fc % 2 == 0:
                        P.act(tm[:, :], bank[:, 0:TB], AF.Relu, r=[bank], w=[tm])
                    else:
                        P.ts('dve', tm[:, :], bank[:, 0:TB], 0.0, None, ALU.max, r=[bank], w=[tm])
                    P.tt('pool', aT[:, fc, :], tm[:, :], tm[:, :], ALU.mult, r=[tm], w=[aT])
                for j in range(2):
                    rows = slice(tb * TB + j * 128, tb * TB + (j + 1) * 128)
                    h = h1[j]
                    t = t2[0]
                    for half in range(2):
                        bank = psr()
                        for fc in range(32):
                            P.mm(bank[:, :], aT[:, fc, j * 128:(j + 1) * 128], Wp[:, fc, half * 512:(half + 1) * 512],
                                 start=(fc == 0), stop=(fc == 31), r=[aT, Wpb[fc]], w=[bank])
                        P.stt(t[:, half * 512:(half + 1) * 512], h[:, half * 512:(half + 1) * 512], ALPHA, bank[:, :], ALU.mult, ALU.add,
                              r=[h, bank], w=[t])
                    layer_norm_tile(P, C, t[:, :], t[:, :], g2, b2, stat[ti % 2], LN_EPS, r=[t], w=[t])
                    P.ld(dstb[rows, :], t[:, :], r=[t])
                    ti += 1
        P.emit()
        C.stats.append(('D2', P.stats))


def bc3(ap2, n, axis):
    k = ap2.shape[1]
    if axis == 1:
        return ap2.unsqueeze(1).broadcast_to([128, n, k])
    return ap2.unsqueeze(2).broadcast_to([128, k, n])


def stage_B(C, l, b):
    nc, T, NCH = C.nc, C.T, C.NCH
    TB = min(512, T)
    cst = C.cst
    ident, U, Lo, Us, Ls, ones = (cst[:, i * 128:(i + 1) * 128] for i in range(6))
    with contextlib.ExitStack() as st:
        sb = lambda n, sh, dt: _sb(st, nc, n, sh, dt)
        XTb = [Buf('XT%d' % g) for g in range(8)]
        BT = sb('BT', [128, 2, T], BF16)
        CT = sb('CT', [128, 2, T], BF16)
        xbf = sb('xbf', [128, NCH, 512], BF16)
        Btok = sb('Btok', [128, NCH, 256], BF16)
        dtraw = sb('dtraw', [128, NCH, 16], F32)
        dtv = sb('dtv', [128, NCH, 16], F32)
        av = sb('av', [128, NCH, 16], F32)
        dtb = sb('dtb', [128, 16], F32)
        negA = sb('negA', [128, 16], F32)
        dsk8 = sb('dsk8', [128, 8], F32)
        dsk = sb('dsk', [128, 512], F32)
        ng = sb('ng', [128, 512], F32)
        ps = _psum(st, nc)
        st0 = contextlib.ExitStack()
        sb0 = lambda n, sh, dt: _sb(st0, nc, n, sh, dt)
        XT = sb0('XT', [128, 8, T + 4], BF16)
        cw6 = sb0('cw6', [6, 1024], F32)
        cwb = sb0('cwb', [128, 8, 6], F32)
        Dg = sb0('Dg', [128, 5, 8, 128], BF16)
        cbrow = sb0('cbrow', [1, 768], F32)
        cbrow_bf = sb0('cbrow_bf', [1, 768], BF16)
        ones_bf = sb0('ones_bf', [1, 128], BF16)
        P = Prog(nc, C.sems)
        psr = RR(ps)
        P.op('pool', lambda e: e.memset(XT[:, :, 0:2], 0.0), [], XTb)
        P.op('pool', lambda e: e.memset(XT[:, :, T + 2:T + 4], 0.0), [], XTb)
        for g in range(8):
            P.ld(XT[:, g, 2:T + 2], C.xbcT[b, g * 128:(g + 1) * 128, :], w=[XTb[g]])
        P.ld(cw6[0:5, :], C.w['conv_w'][l], w=[cw6])
        P.ld(cw6[5:6, :], C.w['conv_b'][l:l + 1, :], w=[cw6])
        P.ld(cbrow[0:1, :], C.w['conv_b'][l:l + 1, 0:768], w=[cbrow])
        P.cp('dve', cbrow_bf[0:1, :], cbrow[0:1, :], r=[cbrow], w=[cbrow_bf])
        P.cp('dve', ones_bf[0:1, :], ones[0:1, :], r=[cst], w=[ones_bf])
        load_bcast(P, dtb, C.w['dt_bias'][l].rearrange("a b -> (a b)"), 16)
        load_bcast(P, negA, C.w['a_log'][l].rearrange("a b -> (a b)"), 16)
        load_bcast(P, dsk8, C.w['d_skip'][l], 8)
        load_bcast(P, ng, C.w['ssd_norm_g'][l], 512)
        P.act(negA[:, :], negA[:, :], AF.Exp, r=[negA], w=[negA])
        P.ts('dve', negA[:, :], negA[:, :], -1.0, None, ALU.mult, r=[negA], w=[negA])
        P.cp('dve', dsk[:, :].rearrange("p (h q) -> p h q", q=64), bc3(dsk8[:, :], 64, 2), r=[dsk8], w=[dsk])
        bank = psr()
        for g in range(8):
            P.tr(bank[:, g * 6:(g + 1) * 6], cw6[0:6, g * 128:(g + 1) * 128], ident[0:6, 0:6], r=[cw6, cst], w=[bank])
        P.cp('dve', cwb[:, :, :], bank[:, 0:48].rearrange("p (g k) -> p g k", k=6), r=[bank], w=[cwb])
        er = RR(['dve', 'pool'])
        for k in range(5):
            for g in range(8):
                P.ts(er(), Dg[:, k, g, :], ident, cwb[:, g, k:k + 1], None, ALU.mult, r=[cwb, cst], w=[Dg])
        P.ld(dtraw[:, :, :], C.dtbuf[b].rearrange("(c p) k -> p c k", p=128), w=[dtraw])
        P.tt('dve', dtv[:, :, :], dtraw[:, :, :], bc3(dtb[:, :], NCH, 1), ALU.add, r=[dtraw, dtb], w=[dtv])
        P.act(dtv[:, :, :], dtv[:, :, :], AF.Exp, r=[dtv], w=[dtv])
        P.ts('dve', dtv[:, :, :], dtv[:, :, :], 1.0, None, ALU.add, r=[dtv], w=[dtv])
        P.act(dtv[:, :, :], dtv[:, :, :], AF.Ln, r=[dtv], w=[dtv])
        P.tt('dve', av[:, :, :], dtv[:, :, :], bc3(negA[:, :], NCH, 1), ALU.mult, r=[dtv, negA], w=[av])
        for c in range(NCH):
            for (g0, ng_, dst, boff) in ((0, 4, xbf, 0), (4, 2, Btok, 512)):
                bank = psr()
                n = ng_ * 128
                P.mm(bank[:, 0:n], ones_bf[0:1, :], cbrow_bf[0:1, boff:boff + n], start=True, stop=False,
                     r=[ones_bf, cbrow_bf], w=[bank])
                for gi in range(ng_):
                    g = g0 + gi
                    for k in range(5):
                        P.mm(bank[:, gi * 128:(gi + 1) * 128], XT[:, g, c * 128 + k:c * 128 + k + 128], Dg[:, k, g, :],
                             start=False, stop=(gi == ng_ - 1 and k == 4), r=[XTb[g], Dg], w=[bank])
                P.act(dst[:, c, :], bank[:, 0:n], AF.Silu, r=[bank], w=[dst])
        for tb in range(T // TB):
            for gi in range(4):
                g = 4 + gi
                bank = psr()
                for k in range(5):
                    P.mm(bank[:, 0:TB], Dg[:, k, g, :], XT[:, g, tb * TB + k:tb * TB + k + TB], start=(k == 0), stop=(k == 4),
                         r=[XTb[g], Dg], w=[bank])
                dst = BT if gi < 2 else CT
                P.act(dst[:, gi % 2, tb * TB:(tb + 1) * TB], bank[:, 0:TB], AF.Silu, bias=cwb[:, g, 5:6], r=[bank, cwb], w=[dst])
        P.emit()
        C.stats.append(('B0', P.stats))
        st0.close()
        ysc = sb('ysc', [128, NCH, 16], F32)
        cs_sb = [sb('cs_sb%d' % i, [128, 32], F32) for i in range(2)]
        dd = [sb('dd%d' % i, [128, 16], F32) for i in range(2)]
        et = [sb('et%d' % i, [128, 16], F32) for i in range(2)]
        wd = [sb('wd%d' % i, [128, 16], F32) for i in range(2)]
        xw = [sb('xw%d' % i, [128, 2, 512], BF16) for i in range(2)]
        Srun = sb('Srun', [128, 2, 512], F32)
        Sin = sb('Sin', [128, NCH, 2, 512], BF16)
        rhsS = [sb('rhsS%d' % i, [128, 2, 8, 128], F32) for i in range(2)]
        E = [sb('E%d' % i, [128, 2, 8, 128], F32) for i in range(2)]
        SM = [sb('SM%d' % i, [128, 2, 2, 128], F32) for i in range(2)]
        G = [sb('G%d' % i, [128, 2, 8, 128], BF16) for i in range(2)]
        ta = [sb('ta%d' % i, [128, 512], F32) for i in range(2)]
        tb_ = [sb('tb%d' % i, [128, 512], F32) for i in range(2)]
        yt = [sb('yt%d' % i, [128, 512], F32) for i in range(2)]
        zt = [sb('zt%d' % i, [128, 512], F32) for i in range(2)]
        sq = [sb('sq%d' % i, [128, 512], F32) for i in range(2)]
        ss = [sb('ss%d' % i, [128, 8], F32) for i in range(2)]
        P = Prog(nc, C.sems)
        psr = RR(ps)
        P.op('pool', lambda e: e.memset(Srun[:, :, :], 0.0), [], [Srun.buf])
        for i in range(NCH):
            cc = (i, NCH - 1 - i)
            k2 = i % 2
            bank = psr()
            for d in range(2):
                P.mm(bank[:, d * 8:(d + 1) * 8], U if d == 0 else Lo, av[:, cc[d], d * 8:(d + 1) * 8], r=[cst, av], w=[bank])
                P.mm(bank[:, 16 + d * 8:16 + (d + 1) * 8], ones, av[:, cc[d], d * 8:(d + 1) * 8], r=[cst, av], w=[bank])
            P.cp('act', cs_sb[k2][:, :], bank[:, 0:32], r=[bank], w=[cs_sb[k2]])
            P.tt('dve', dd[k2][:, :], cs_sb[k2][:, 16:32], cs_sb[k2][:, 0:16], ALU.subtract, r=[cs_sb[k2]], w=[dd[k2]])
            P.act(dd[k2][:, :], dd[k2][:, :], AF.Exp, r=[dd[k2]], w=[dd[k2]])
            P.act(et[k2][:, :], cs_sb[k2][:, 16:32], AF.Exp, r=[cs_sb[k2]], w=[et[k2]])
            for d in range(2):
                sl = slice(d * 8, (d + 1) * 8)
                P.act(ysc[:, cc[d], sl], cs_sb[k2][:, sl], AF.Exp, r=[cs_sb[k2]], w=[ysc])
                P.tt('dve', wd[k2][:, sl], dd[k2][:, sl], dtv[:, cc[d], sl], ALU.mult, r=[dd[k2], dtv], w=[wd[k2]])
                P.tt('dve', xw[k2][:, d, :].rearrange("p (h q) -> p h q", q=64),
                     xbf[:, cc[d], :].rearrange("p (h q) -> p h q", q=64), bc3(wd[k2][:, sl], 64, 2), ALU.mult,
                     r=[xbf, wd[k2]], w=[xw[k2]])
            for d in range(2):
                bank = psr()
                for g in range(2):
                    P.mm(bank[:, g * 256:(g + 1) * 256], Btok[:, cc[d], g * 128:(g + 1) * 128], xw[k2][:, d, g * 256:(g + 1) * 256],
                         r=[Btok, xw[k2]], w=[bank])
                P.cp('pool', Sin[:, cc[d], d, :], Srun[:, d, :], r=[Srun], w=[Sin])
                P.tt('dve', Srun[:, d, :].rearrange("p (h q) -> p h q", q=64), Srun[:, d, :].rearrange("p (h q) -> p h q", q=64),
                     bc3(et[k2][:, d * 8:(d + 1) * 8], 64, 2), ALU.mult, r=[Srun, et[k2]], w=[Srun])
                P.tt('dve', Srun[:, d, :], Srun[:, d, :], bank[:, :], ALU.add, r=[Srun, bank], w=[Srun])
        for c in range(NCH):
            k2 = c % 2
            rows = slice(c * 128, (c + 1) * 128)
            csl = slice(c * 128, (c + 1) * 128)
            P.ld(zt[k2][:, :], C.zbuf[b, rows, :], w=[zt[k2]])
            for d in range(2):
                P.tt('dve' if d == 0 else 'pool', rhsS[k2][:, d, :, :], bc3(U if d == 0 else Lo, 8, 1),
                     bc3(av[:, c, d * 8:(d + 1) * 8], 128, 2), ALU.mult, r=[cst, av], w=[rhsS[k2]])
            for d in range(2):
                for hh in range(2):
                    bank = psr()
                    P.mm(bank[:, :], Ls if d == 0 else Us, rhsS[k2][:, d, hh * 4:(hh + 1) * 4, :].rearrange("p a b -> p (a b)"),
                         r=[cst, rhsS[k2]], w=[bank])
                    P.act(E[k2][:, d, hh * 4:(hh + 1) * 4, :].rearrange("p a b -> p (a b)"), bank[:, :], AF.Exp, r=[bank], w=[E[k2]])
            bank = psr()
            for g in range(2):
                P.mm(bank[:, g * 128:(g + 1) * 128], BT[:, g, csl], CT[:, g, csl], r=[BT, CT], w=[bank])
            for d in range(2):
                P.tt('dve', SM[k2][:, d, :, :], bank[:, 0:256].rearrange("p (g l) -> p g l", l=128), bc3(U if d == 0 else Lo, 2, 1),
                     ALU.mult, r=[bank, cst], w=[SM[k2]])
            for d in range(2):
                for h in range(8):
                    P.stt(G[k2][:, d, h, :], E[k2][:, d, h, :], dtv[:, c, d * 8 + h:d * 8 + h + 1], SM[k2][:, d, h // 4, :],
                          ALU.mult, ALU.mult, r=[E[k2], dtv, SM[k2]], w=[G[k2]])
            bY1 = psr()
            for h in range(8):
                for d in range(2):
                    P.mm(bY1[:, h * 64:(h + 1) * 64], G[k2][:, d, h, :], xbf[:, c, h * 64:(h + 1) * 64], start=(d == 0), stop=(d == 1),
                         r=[G[k2], xbf], w=[bY1])
            bY2 = [psr(), psr()]
            for d in range(2):
                for g in range(2):
                    P.mm(bY2[d][:, g * 256:(g + 1) * 256], CT[:, g, csl], Sin[:, c, d, g * 256:(g + 1) * 256], r=[CT, Sin], w=[bY2[d]])
            v3 = lambda ap: ap.rearrange("p (h q) -> p h q", q=64)
            P.tt('dve', v3(ta[k2][:, :]), v3(bY2[0][:, :]), bc3(ysc[:, c, 0:8], 64, 2), ALU.mult, r=[bY2[0], ysc], w=[ta[k2]])
            P.tt('dve', v3(tb_[k2][:, :]), v3(bY2[1][:, :]), bc3(ysc[:, c, 8:16], 64, 2), ALU.mult, r=[bY2[1], ysc], w=[tb_[k2]])
            y = yt[k2]
            P.tt('pool', y[:, :], ta[k2][:, :], tb_[k2][:, :], ALU.add, r=[ta[k2], tb_[k2]], w=[y])
            P.tt('dve', y[:, :], y[:, :], bY1[:, :], ALU.add, r=[y, bY1], w=[y])
            P.tt('pool', sq[k2][:, :], xbf[:, c, :], dsk[:, :], ALU.mult, r=[xbf, dsk], w=[sq[k2]])
            P.tt('pool', y[:, :], y[:, :], sq[k2][:, :], ALU.add, r=[y, sq[k2]], w=[y])
            P.act(zt[k2][:, :], zt[k2][:, :], AF.Silu, r=[zt[k2]], w=[zt[k2]])
            P.tt('dve', y[:, :], y[:, :], zt[k2][:, :], ALU.mult, r=[y, zt[k2]], w=[y])
            P.tt('pool', sq[k2][:, :], y[:, :], y[:, :], ALU.mult, r=[y], w=[sq[k2]])
            P.op('dve', lambda e, o=ss[k2][:, 0:2], i_=sq[k2][:, :].rearrange("p (g q) -> p g q", q=256): e.reduce_sum(o, i_, axis=AX.X),
                 [sq[k2].buf], [ss[k2].buf])
            P.ts('dve', ss[k2][:, 2:4], ss[k2][:, 0:2], 1.0 / 256.0, RMS_EPS, ALU.mult, ALU.add, r=[ss[k2]], w=[ss[k2]])
            P.act(ss[k2][:, 4:6], ss[k2][:, 2:4], AF.Sqrt, r=[ss[k2]], w=[ss[k2]])
            P.op('dve', lambda e, o=ss[k2][:, 6:8], i_=ss[k2][:, 4:6]: e.reciprocal(o, i_), [ss[k2].buf], [ss[k2].buf])
            for g in range(2):
                P.ts('dve', y[:, g * 256:(g + 1) * 256], y[:, g * 256:(g + 1) * 256], ss[k2][:, 6 + g:7 + g], None, ALU.mult,
                     r=[y, ss[k2]], w=[y])
            P.tt('pool', y[:, :], y[:, :], ng[:, :], ALU.mult, r=[y, ng], w=[y])
            P.ld(C.ymix[b, rows, 0:512], y[:, :], r=[y])
        P.emit()
        C.stats.append(('B', P.stats))


def stage_C(C, l, b):
    nc, T, NCH = C.nc, C.T, C.NCH
    cst = C.cst
    ident, U, Lo, Us, Ls, ones = (cst[:, i * 128:(i + 1) * 128] for i in range(6))
    v3 = lambda ap: ap.rearrange("p (h q) -> p h q", q=64)
    with contextlib.ExitStack() as st:
        sb = lambda n, sh, dt: _sb(st, nc, n, sh, dt)
        RTc = [sb('RTc%d' % i, [128, 14, 130], BF16) for i in range(2)]
        murow = sb('murow', [14, 128], F32)
        muT = sb('muT', [128, 3, 14], F32)
        Dm = sb('Dm', [128, 14, 128], BF16)
        Dh = sb('Dh', [128, 14, 128], BF16)
        kkb = sb('kkb', [128, 512], F32)
        kab = sb('kab', [128, 512], F32)
        rkb = sb('rkb', [128, 512], F32)
        lgb = sb('lgb', [128, 512], F32)
        lbb = sb('lbb', [128, 512], F32)
        w0b = sb('w0b', [128, 2, 512], F32)
        a0b = sb('a0b', [128, 2, 512], F32)
        LW = sb('LW', [128, 2, 512], BF16)
        GU = sb('GU', [128, 512], BF16)
        MK = sb('MK', [128, 2, 512], F32)
        MKa = sb('MKa', [128, 2, 128], F32)
        bon = sb('bon', [128, NCH, 2, 8], F32)
        r32 = sb('r32', [128, 512], F32)
        k32 = sb('k32', [128, 512], F32)
        v32 = sb('v32', [128, 512], F32)
        vbf = sb('vbf', [128, 512], BF16)
        LT = sb('LT', [128, 128], BF16)
        sg = sb('sg', [128, 128], BF16)
        g32 = sb('g32', [128, 512], F32)
        lw32 = sb('lw32', [128, 512], F32)
        a32 = sb('a32', [128, 512], F32)
        kk = sb('kk', [128, 512], F32)
        tmp = sb('tmp', [128, 512], F32)
        tmp2 = sb('tmp2', [128, 512], F32)
        kd = sb('kd', [128, 512], F32)
        bb = sb('bb', [128, 512], F32)
        sm8 = sb('sm8', [128, 32], F32)
        gC = sb('gC', [128, 4], F32)
        Ep = sb('Ep', [128, 512], F32)
        En = sb('En', [128, 512], F32)
        Ex = sb('Ex', [128, 512], F32)
        rt = sb('rt', [128, 512], F32)
        kt = sb('kt', [128, 512], F32)
        bt = sb('bt', [128, 512], F32)
        kdt = sb('kdt', [128, 512], F32)
        bt_bf = sb('bt_bf', [128, 512], BF16)
        kdt_bf = sb('kdt_bf', [128, 512], BF16)
        kt_bf = sb('kt_bf', [128, 512], NEU_DT)
        KR = sb('KR', [128, 4, 2, 128], BF16)
        btT = sb('btT', [128, 4, 128], BF16)
        kdtT = sb('kdtT', [128, 4, 128], BF16)
        R3 = sb('R3', [128, 8, 3, 128], BF16)
        NM = [sb('NM%d' % i, [128, 8, 2, 128], NEU_DT) for i in range(2)]
        X = [sb('X%d' % i, [128, 8, 128], NEU_DT) for i in range(2)]
        Z1 = sb('Z1', [128, 512], NEU_DT)
        Ut = sb('Ut', [128, 512], F32)
        WT = sb('WT', [128, 4, 128], BF16)
        Mst = sb('Mst', [128, 4, 128], F32)
        Mbf = sb('Mbf', [128, 4, 128], BF16)
        Un = sb('Un', [128, 512], BF16)
        Yo = [sb('Yo%d' % i, [128, 512], F32) for i in range(2)]
        ps = _psum(st, nc)
        P = Prog(nc, C.sems)
        psr = RR(ps)
        evr = RR(['act', 'dve'])
        P.ld(murow[0:14, :], C.w['mu_rwkv'][l].rearrange("(g p) -> g p", p=128), w=[murow])
        bank = psr()
        P.tr(bank[:, 0:14], murow[0:14, :], ident[0:14, 0:14], r=[murow, cst], w=[bank])
        P.cp('dve', muT[:, 0, :], bank[:, 0:14], r=[bank], w=[muT])
        P.ts('dve', muT[:, 1, :], muT[:, 0, :], -1.0, 1.0, ALU.mult, ALU.add, r=[muT], w=[muT])
        P.ts('dve', muT[:, 2, :], muT[:, 0, :], 0.5, None, ALU.mult, r=[muT], w=[muT])
        er = RR(['dve', 'pool'])
        for g in range(14):
            P.ts(er(), Dm[:, g, :], ident, muT[:, 1, g:g + 1], None, ALU.mult, r=[muT, cst], w=[Dm])
            P.ts(er(), Dh[:, g, :], ident, muT[:, 2, g:g + 1], None, ALU.mult, r=[muT, cst], w=[Dh])
        load_bcast(P, kkb, C.w['k_k'][l], 512)
        load_bcast(P, kab, C.w['k_a'][l], 512)
        load_bcast(P, rkb, C.w['r_k'][l].rearrange("a b -> (a b)"), 512)
        load_bcast(P, lgb, C.w['lnx_g'][l], 512)
        load_bcast(P, lbb, C.w['lnx_b'][l], 512)
        for d in range(2):
            P.ld(w0b[:, d, :], C.w['w0'][l, d].partition_broadcast(128), w=[w0b])
            P.ld(a0b[:, d, :], C.w['a0'][l, d].partition_broadcast(128), w=[a0b])
            P.ld(LW[0:64, d, :], C.w['w_up'][l, d], w=[LW], eng='pool')
            P.ld(LW[64:128, d, :], C.w['a_up'][l, d], w=[LW], eng='pool')
        P.ld(GU[:, :], C.w['g_up'][l], w=[GU], eng='pool')
        for d in range(2):
            strict, incl, strict_ts = (Us, U, Ls) if d == 0 else (Ls, Lo, Us)
            P.ts('dve', MK[:, d, 0:128], strict, -1.0, None, ALU.mult, r=[cst], w=[MK])
            P.cp('dve', MK[:, d, 128:256], incl, r=[cst], w=[MK])
            P.cp('dve', MK[:, d, 256:384], strict, r=[cst], w=[MK])
            P.cp('dve', MK[:, d, 384:512], incl, r=[cst], w=[MK])
            P.ts('dve', MKa[:, d, :], strict_ts, -1.0, None, ALU.mult, r=[cst], w=[MKa])
        taps = (Dh, Dm, Dh)
        ydb = [[Buf('yd') for _ in range(NCH)] for _ in range(2)]
        vgb = [[Buf('vg') for _ in range(NCH)] for _ in range(2)]
        for d in range(2):
            P.op('pool', lambda e: e.memset(Mst[:, :, :], 0.0), [], [Mst.buf])
            P.op('pool', lambda e: e.memset(Mbf[:, :, :], 0.0), [], [Mbf.buf])
            for ci in range(NCH):
                c = ci if d == 0 else NCH - 1 - ci
                rows = slice(c * 128, (c + 1) * 128)
                RT = RTc[ci % 2]
                lo, hi = max(c * 128 - 1, 0), min(c * 128 + 129, T)
                if c == 0:
                    P.op('pool', lambda e, t=RT: e.memset(t[:, :, 0:1], 0.0), [], [RT.buf])
                if c == NCH - 1:
                    P.op('pool', lambda e, t=RT: e.memset(t[:, :, 129:130], 0.0), [], [RT.buf])
                P.ld(RT[:, :, lo - (c * 128 - 1):hi - (c * 128 - 1)],
                     C.rwT[b].rearrange("(g p) t -> p g t", p=128)[:, :, lo:hi], w=[RT])
                for (g0, dst) in ((0, r32), (4, k32), (8, v32)):
                    bank = psr()
                    for gi in range(4):
                        g = g0 + gi
                        for k in range(3):
                            P.mm(bank[:, gi * 128:(gi + 1) * 128], RT[:, g, k:k + 128], taps[k][:, g, :],
                                 start=(k == 0), stop=(k == 2), r=[RT, Dm, Dh], w=[bank])
                    P.cp(evr(), dst[:, :], bank[:, :], r=[bank], w=[dst])
                P.cp('pool', vbf[:, :], v32[:, :], r=[v32], w=[vbf])
                bank = psr()
                for gi in range(2):
                    g = 12 + gi
                    for k in range(3):
                        P.mm(bank[:, gi * 128:(gi + 1) * 128], taps[k][:, g, :], RT[:, g, k:k + 128],
                             start=(k == 0), stop=(k == 2), r=[RT, Dm, Dh], w=[bank])
                P.act(LT[0:64, :], bank[0:64, 0:128], AF.Tanh, r=[bank], w=[LT])
                P.cp('dve', LT[64:128, :], bank[64:128, 0:128], r=[bank], w=[LT])
                P.act(sg[:, :], bank[:, 128:256], AF.Sigmoid, r=[bank], w=[sg])
                bW, bA = psr(), psr()
                P.mm(bW[:, :], LT[0:64, :], LW[0:64, d, :], r=[LT, LW], w=[bW])
                P.mm(bA[:, :], LT[64:128, :], LW[64:128, d, :], r=[LT, LW], w=[bA])
                P.tt('dve', lw32[:, :], bW[:, :], w0b[:, d, :], ALU.add, r=[bW, w0b], w=[lw32])
                P.act(lw32[:, :], lw32[:, :], AF.Sigmoid, r=[lw32], w=[lw32])
                P.ts('pool', lw32[:, :], lw32[:, :], -DECAY_C, None, ALU.mult, r=[lw32], w=[lw32])
                P.tt('dve', a32[:, :], bA[:, :], a0b[:, d, :], ALU.add, r=[bA, a0b], w=[a32])
                P.act(a32[:, :], a32[:, :], AF.Sigmoid, r=[a32], w=[a32])
                if d == 0:
                    bG = psr()
                    P.mm(bG[:, :], sg[:, :], GU[:, :], r=[sg, GU], w=[bG])
                    P.cp('act', g32[:, :], bG[:, :], r=[bG], w=[g32])
                    P.ld(C.vg[0, rows, :], v32[:, :], r=[v32], w=[vgb[0][c]])
                    P.ld(C.vg[1, rows, :], g32[:, :], r=[g32], w=[vgb[1][c]])
                P.tt('pool', kk[:, :], k32[:, :], kkb[:, :], ALU.mult, r=[k32, kkb], w=[kk])
                P.tt('pool', tmp[:, :], kk[:, :], kk[:, :], ALU.mult, r=[kk], w=[tmp])
                P.op('dve', lambda e, o=sm8[:, 0:8], i_=v3(tmp[:, :]): e.reduce_sum(o, i_, axis=AX.X), [tmp.buf], [sm8.buf])
                P.act(sm8[:, 8:16], sm8[:, 0:8], AF.Sqrt, r=[sm8], w=[sm8])
                P.ts('dve', sm8[:, 8:16], sm8[:, 8:16], 1e-12, None, ALU.max, r=[sm8], w=[sm8])
                P.op('dve', lambda e, o=sm8[:, 16:24], i_=sm8[:, 8:16]: e.reciprocal(o, i_), [sm8.buf], [sm8.buf])
                P.tt('dve', v3(kk[:, :]), v3(kk[:, :]), bc3(sm8[:, 16:24], 64, 2), ALU.mult, r=[kk, sm8], w=[kk])
                P.stt(tmp2[:, :], a32[:, :], -1.0, kab[:, :], ALU.add, ALU.mult, r=[a32, kab], w=[tmp2])
                P.stt(kd[:, :], tmp2[:, :], 1.0, k32[:, :], ALU.add, ALU.mult, r=[tmp2, k32], w=[kd])
                P.tt('pool', tmp[:, :], r32[:, :], kd[:, :], ALU.mult, r=[r32, kd], w=[tmp])
                P.tt('pool', tmp[:, :], tmp[:, :], rkb[:, :], ALU.mult, r=[tmp, rkb], w=[tmp])
                P.op('dve', lambda e, o=bon[:, c, d, :], i_=v3(tmp[:, :]): e.reduce_sum(o, i_, axis=AX.X), [tmp.buf], [bon.buf])
                P.tt('pool', bb[:, :], kk[:, :], a32[:, :], ALU.mult, r=[kk, a32], w=[bb])
                bC = psr()
                P.mm(bC[:, :], U if d == 0 else Lo, lw32[:, :], r=[cst, lw32], w=[bC])
                bT = psr()
                for jg in range(4):
                    P.mm(bT[:, 2 * jg:2 * jg + 2], lw32[:, jg * 128:(jg + 1) * 128], ones[:, 0:2], r=[lw32, cst], w=[bT])
                P.act(gC[:, :], bT[:, 0:8].rearrange("p (j two) -> p j two", two=2)[:, :, 0], AF.Exp, r=[bT], w=[gC])
                P.act(Ep[:, :], bC[:, :], AF.Exp, r=[bC], w=[Ep])
                P.act(En[:, :], bC[:, :], AF.Exp, scale=-1.0, r=[bC], w=[En])
                P.tt('dve', Ex[:, :], bC[:, :], lw32[:, :], ALU.subtract, r=[bC, lw32], w=[Ex])
                P.act(Ex[:, :], Ex[:, :], AF.Exp, r=[Ex], w=[Ex])
                P.tt('pool', rt[:, :], r32[:, :], Ep[:, :], ALU.mult, r=[r32, Ep], w=[rt])
                P.tt('dve', kt[:, :], kk[:, :], Ex[:, :], ALU.mult, r=[kk, Ex], w=[kt])
                P.tt('pool', bt[:, :], bb[:, :], En[:, :], ALU.mult, r=[bb, En], w=[bt])
                P.tt('dve', kdt[:, :], kd[:, :], En[:, :], ALU.mult, r=[kd, En], w=[kdt])
                P.cp('pool', bt_bf[:, :], bt[:, :], r=[bt], w=[bt_bf])
                P.cp('pool', kdt_bf[:, :], kdt[:, :], r=[kdt], w=[kdt_bf])
                P.cp('pool', kt_bf[:, :], kt[:, :], r=[kt], w=[kt_bf])
                for (src, dstap, dbuf) in ((kt, KR[:, :, 0, :], KR), (rt, KR[:, :, 1, :], KR), (bt, btT[:, :, :], btT), (kdt, kdtT[:, :, :], kdtT)):
                    bank = psr()
                    for jg in range(4):
                        P.tr(bank[:, jg * 128:(jg + 1) * 128], src[:, jg * 128:(jg + 1) * 128], ident, r=[src, cst], w=[bank])
                    P.cp(evr(), dstap, bank[:, :].rearrange("p (a b) -> p a b", b=128), r=[bank], w=[dbuf])
                for h in range(8):
                    jg, rs = h // 2, slice((h % 2) * 64, (h % 2 + 1) * 64)
                    bM = psr()
                    krr = KR[rs, jg, :, :].rearrange("p a b -> p (a b)")
                    P.mm(bM[:, 0:256], btT[rs, jg, :], krr, r=[btT, KR], w=[bM])
                    P.mm(bM[:, 256:512], kdtT[rs, jg, :], krr, r=[kdtT, KR], w=[bM])
                    P.tt('dve', NM[0][:, h, 0, :], bM[:, 0:128], MK[:, d, 0:128], ALU.mult, r=[bM, MK], w=[NM[0]])
                    P.tt('dve', R3[:, h, :, :].rearrange("p a b -> p (a b)"), bM[:, 128:512], MK[:, d, 128:512], ALU.mult,
                         r=[bM, MK], w=[R3])
                for hh in range(2):
                    bank = psr()
                    rs = slice(hh * 64, (hh + 1) * 64)
                    for jg in range(4):
                        P.mm(bank[:, jg * 128:(jg + 1) * 128], KR[rs, jg, 0, :], btT[rs, jg, :], r=[KR, btT], w=[bank])
                    P.tt('dve', NM[0][:, hh:8:2, 1, :], bank[:, :].rearrange("p (a b) -> p a b", b=128),
                         bc3(MKa[:, d, :], 4, 1), ALU.mult, r=[bank, MKa], w=[NM[0]])
                P.tt('dve', X[0][:, :, :], NM[0][:, :, 0, :], bc3(ident, 8, 1), ALU.add, r=[NM[0], cst], w=[X[0]])
                for j in range(6):
                    cur, nxt = NM[j % 2], NM[(j + 1) % 2]
                    Xc, Xn = X[j % 2], X[(j + 1) % 2]
                    for hp in range(4):
                        bank = psr()
                        for q in range(2):
                            h = hp * 2 + q
                            P.mm(bank[:, q * 256:q * 256 + 128], cur[:, h, 1, :], cur[:, h, 0, :], r=[cur], w=[bank])
                            P.mm(bank[:, q * 256 + 128:q * 256 + 256], cur[:, h, 0, :], cur[:, h, 1, :], r=[cur], w=[bank])
                        P.cp(evr(), nxt[:, hp * 2:hp * 2 + 2, :, :].rearrange("p a b c -> p (a b c)"), bank[:, :], r=[bank], w=[nxt])
                    for hp in range(2):
                        bank = psr()
                        for q in range(4):
                            h = hp * 4 + q
                            P.mm(bank[:, q * 128:(q + 1) * 128], nxt[:, h, 1, :], Xc[:, h, :], r=[nxt, Xc], w=[bank])
                        P.tt('dve', Xn[:, hp * 4:(hp + 1) * 4, :].rearrange("p a b -> p (a b)"), bank[:, :],
                             Xc[:, hp * 4:(hp + 1) * 4, :].rearrange("p a b -> p (a b)"), ALU.add, r=[bank, Xc], w=[Xn])
                XF = X[0]
                bZ = psr()
                for h in range(8):
                    P.mm(bZ[:, h * 64:(h + 1) * 64], R3[:, h, 1, :], vbf[:, h * 64:(h + 1) * 64], r=[R3, vbf], w=[bZ])
                P.cp('act', Z1[:, :], bZ[:, :], r=[bZ], w=[Z1])
                bU = psr()
                for h in range(8):
                    P.mm(bU[:, h * 64:(h + 1) * 64], XF[:, h, :], Z1[:, h * 64:(h + 1) * 64], r=[XF, Z1], w=[bU])
                P.cp('act', Ut[:, :], bU[:, :], r=[bU], w=[Ut])
                for hp in range(2):
                    bank = psr()
                    for q in range(4):
                        h = hp * 4 + q
                        jg = h // 2
                        P.mm(bank[:, q * 128:(q + 1) * 128], kt_bf[:, jg * 128:(jg + 1) * 128], XF[:, h, :], r=[kt_bf, XF], w=[bank])
                    for q in range(4):
                        h = hp * 4 + q
                        jg, rs = h // 2, slice((h % 2) * 64, (h % 2 + 1) * 64)
                        P.cp(evr(), WT[rs, jg, :], bank[rs, q * 128:(q + 1) * 128], r=[bank], w=[WT])
                bP = psr()
                for jg in range(4):
                    P.mm(bP[:, jg * 128:(jg + 1) * 128], WT[:, jg, :], Mbf[:, jg, :], r=[WT, Mbf], w=[bP])
                P.stt(Un[:, :], bP[:, :], -1.0, Ut[:, :], ALU.mult, ALU.subtract, r=[bP, Ut], w=[Un])
                bY = psr()
                for jg in range(4):
                    P.mm(bY[:, jg * 128:(jg + 1) * 128], KR[:, jg, 1, :], Mbf[:, jg, :], start=True, stop=False, r=[KR, Mbf], w=[bY])
                    for h in (2 * jg, 2 * jg + 1):
                        hs = slice(h * 64, (h + 1) * 64)
                        P.mm(bY[:, hs], R3[:, h, 2, :], vbf[:, hs], start=False, stop=False, r=[R3, vbf], w=[bY])
                    for h in (2 * jg, 2 * jg + 1):
                        hs = slice(h * 64, (h + 1) * 64)
                        P.mm(bY[:, hs], R3[:, h, 0, :], Un[:, hs], start=False, stop=(h == 2 * jg + 1), r=[R3, Un], w=[bY])
                yo = Yo[ci % 2]
                P.cp('act', yo[:, :], bY[:, :], r=[bY], w=[yo])
                P.ld(C.ydir[d, rows, :], yo[:, :], r=[yo], w=[ydb[d][c]])
                bS = psr()
                for jg in range(4):
                    js = slice(jg * 128, (jg + 1) * 128)
                    P.mm(bS[:, js], bt_bf[:, js], Un[:, js], start=True, stop=False, r=[bt_bf, Un], w=[bS])
                    P.mm(bS[:, js], kdt_bf[:, js], vbf[:, js], start=False, stop=True, r=[kdt_bf, vbf], w=[bS])
                for hh in range(2):
                    rs = slice(hh * 64, (hh + 1) * 64)
                    src = bS[rs, :].rearrange("p (j q) -> p j q", q=128)[:, :, hh * 64:(hh + 1) * 64]
                    mv = Mst[rs, :, hh * 64:(hh + 1) * 64]
                    P.tt('dve', mv, mv, src, ALU.add, r=[Mst, bS], w=[Mst])
                    P.tt('dve', mv, mv, gC[rs, :].unsqueeze(2).broadcast_to([64, 4, 64]), ALU.mult, r=[Mst, gC], w=[Mst])
                P.cp('pool', Mbf[:, :, :], Mst[:, :, :], r=[Mst], w=[Mbf])
        P.emit()
        C.stats.append(('C', P.stats))
        fy = [sb('fy%d' % i, [128, 4, 512], F32) for i in range(2)]
        P = Prog(nc, C.sems)
        for c in range(NCH):
            rows = slice(c * 128, (c + 1) * 128)
            f = fy[c % 2]
            P.ld(f[:, 0, :], C.ydir[0, rows, :], r=[ydb[0][c]], w=[f])
            P.ld(f[:, 1, :], C.ydir[1, rows, :], r=[ydb[1][c]], w=[f])
            P.ld(f[:, 2, :], C.vg[0, rows, :], r=[vgb[0][c]], w=[f])
            P.ld(f[:, 3, :], C.vg[1, rows, :], r=[vgb[1][c]], w=[f])
            y = f[:, 0, :]
            P.tt('dve', y, y, f[:, 1, :], ALU.add, r=[f], w=[f])
            P.op('dve', lambda e, o=sm8[:, 0:8], i_=v3(y): e.reduce_sum(o, i_, axis=AX.X), [f.buf], [sm8.buf])
            P.tt('pool', tmp[:, :], y, y, ALU.mult, r=[f], w=[tmp])
            P.op('dve', lambda e, o=sm8[:, 8:16], i_=v3(tmp[:, :]): e.reduce_sum(o, i_, axis=AX.X), [tmp.buf], [sm8.buf])
            P.ts('dve', sm8[:, 0:16], sm8[:, 0:16], 1.0 / 64.0, None, ALU.mult, r=[sm8], w=[sm8])
            P.tt('dve', sm8[:, 16:24], sm8[:, 0:8], sm8[:, 0:8], ALU.mult, r=[sm8], w=[sm8])
            P.tt('dve', sm8[:, 16:24], sm8[:, 8:16], sm8[:, 16:24], ALU.subtract, r=[sm8], w=[sm8])
            P.ts('dve', sm8[:, 16:24], sm8[:, 16:24], GN_EPS, None, ALU.add, r=[sm8], w=[sm8])
            P.act(sm8[:, 16:24], sm8[:, 16:24], AF.Sqrt, r=[sm8], w=[sm8])
            P.op('dve', lambda e, o=sm8[:, 24:32], i_=sm8[:, 16:24]: e.reciprocal(o, i_), [sm8.buf], [sm8.buf])
            P.tt('dve', v3(y), v3(y), bc3(sm8[:, 0:8], 64, 2), ALU.subtract, r=[f, sm8], w=[f])
            P.tt('dve', v3(y), v3(y), bc3(sm8[:, 24:32], 64, 2), ALU.mult, r=[f, sm8], w=[f])
            P.tt('pool', y, y, lgb[:, :], ALU.mult, r=[f, lgb], w=[f])
            P.tt('pool', y, y, lbb[:, :], ALU.add, r=[f, lbb], w=[f])
            P.tt('dve', sm8[:, 0:8], bon[:, c, 0, :], bon[:, c, 1, :], ALU.add, r=[bon, sm8], w=[sm8])
            P.tt('dve', v3(tmp[:, :]), v3(f[:, 2, :]), bc3(sm8[:, 0:8], 64, 2), ALU.mult, r=[f, sm8], w=[tmp])
            P.tt('pool', y, y, tmp[:, :], ALU.add, r=[f, tmp], w=[f])
            P.tt('pool', y, y, f[:, 3, :], ALU.mult, r=[f], w=[f])
            P.ld(C.ymix[b, rows, 512:1024], y, r=[f])
        P.emit()
        C.stats.append(('C', P.stats))


def build(T=2048, NS=2, depth=2, debug=False, stages='ABCD'):
    nc = bass.Bass("TRN2", target_bir_lowering=False)
    C = Ctx()
    C.nc, C.T, C.NS, C.NCH = nc, T, NS, T // 128
    C.stats = []
    C.x = nc.dram_tensor("x", [NS, T, D], F32, kind="ExternalInput").ap()
    C.w = {n: nc.dram_tensor(n, W_SHAPES[n], F32, kind="ExternalInput").ap() for n in W_NAMES}
    C.cst_d = nc.dram_tensor("cst", [128, 768], F32, kind="ExternalInput").ap()
    C.out = nc.dram_tensor("out", [NS, T, D], F32, kind="ExternalOutput").ap()
    kind = "ExternalOutput" if debug else "Internal"
    C.hbuf = nc.dram_tensor("hbuf", [NS, T, D], F32, kind=kind).ap()
    C.h1buf = nc.dram_tensor("h1buf", [NS, T, D], F32, kind=kind).ap()
    C.zbuf = nc.dram_tensor("zbuf", [NS, T, 512], F32, kind=kind).ap()
    C.dtbuf = nc.dram_tensor("dtbuf", [NS, T, 16], F32, kind=kind).ap()
    C.xbcT = nc.dram_tensor("xbcT", [NS, 1024, T], BF16, kind=kind).ap()
    C.rwT = nc.dram_tensor("rwT", [NS, 1792, T], BF16, kind=kind).ap()
    C.ymix = nc.dram_tensor("ymix", [NS, T, D], F32, kind=kind).ap()
    C.ydir = nc.dram_tensor("ydir", [2, T, 512], F32, kind=kind).ap()
    C.vg = nc.dram_tensor("vg", [3, T, 512], F32, kind=kind).ap()
    with nc.sbuf_tensor('cst_sb', [128, 768], F32) as cst_sb, contextlib.ExitStack() as semstack:
        C.cst = Tile(cst_sb, 'cst')
        C.sems = SemState(nc, 8, semstack)
        stage_consts(C)
        for l in range(depth):
            if 'A' in stages:
                stage_A(C, l)
            for b in range(NS):
                if 'B' in stages:
                    stage_B(C, l, b)
                if 'C' in stages:
                    stage_C(C, l, b)
            if 'D' in stages:
                stage_D1(C, l)
                stage_D2(C, l, last=(l == depth - 1))
    return nc, C


_CACHE = {}


def kernel(**inputs):
    n_cores = 8
    x = np.ascontiguousarray(np.asarray(inputs["x"], dtype=np.float32))
    B, T, _ = x.shape
    NS = B // n_cores
    key = (T, NS)
    if key not in _CACHE:
        _CACHE[key] = build(T=T, NS=NS, depth=2, debug=False)[0]
    nc = _CACHE[key]
    cst = make_consts()
    ws = {n: np.ascontiguousarray(np.asarray(inputs[n], dtype=np.float32)) for n in W_NAMES}
    in_maps = []
    for i in range(n_cores):
        m = dict(ws)
        m["x"] = np.ascontiguousarray(x[i * NS:(i + 1) * NS])
        m["cst"] = cst
        in_maps.append(m)
    res = run_bass_kernel_spmd(nc, in_maps, core_ids=list(range(n_cores)))
    return np.concatenate([np.asarray(r["out"], dtype=np.float32) for r in res.results], axis=0)
```

```python
import contextlib
import numpy as np
import concourse.bass as bass
import concourse.mybir as mybir
from concourse.bass_utils import run_bass_kernel_spmd

F32 = mybir.dt.float32
BF16 = mybir.dt.bfloat16
AF = mybir.ActivationFunctionType
ALU = mybir.AluOpType
AX = mybir.AxisListType

ENGS = ('pe', 'act', 'dve', 'pool', 'sp')


class Buf:
    __slots__ = ('name', 'last_w', 'readers')

    def __init__(self, name=''):
        self.name = name
        self.last_w = None
        self.readers = []


class _Op:
    __slots__ = ('eng', 'idx', 'fn', 'deps', 'is_dma', 'dslot', 'dval', 'needs_inc', 'cnt', 'waits', 'prog')

    def __init__(self, eng, idx, fn, is_dma):
        self.eng = eng
        self.idx = idx
        self.fn = fn
        self.is_dma = is_dma
        self.deps = []
        self.dslot = 0
        self.dval = 0
        self.needs_inc = False
        self.cnt = 0
        self.waits = []
        self.prog = None


class SemState:
    def __init__(self, nc, ring=8, stack=None):
        self.stack = stack if stack is not None else contextlib.ExitStack()
        st = self.stack
        self.csem = {e: st.enter_context(nc.semaphore('c_' + e)) for e in ENGS if e != 'sp'}
        self.dsem = {e: [st.enter_context(nc.semaphore('d_%s_%d' % (e, i))) for i in range(ring)] for e in ('sp', 'pool', 'act')}
        self.c_off = {e: 0 for e in ENGS}
        self.d_cnt = {e: 0 for e in ENGS}


class Prog:
    def __init__(self, nc, sems=None, ring=8):
        self.nc = nc
        self.q = {e: [] for e in ENGS}
        self.ring = ring
        self.dma_hist = {e: [] for e in ENGS}
        self.sems = sems if sems is not None else SemState(nc, ring)

    def _add(self, eng, fn, reads, writes, is_dma):
        op = _Op(eng, len(self.q[eng]), fn, is_dma)
        op.prog = self
        deps = []
        for b in reads:
            if b.last_w is not None:
                deps.append(b.last_w)
        for b in writes:
            if b.last_w is not None:
                deps.append(b.last_w)
            deps.extend(b.readers)
        if is_dma:
            h = self.dma_hist[eng]
            k = len(h)
            kg = k + self.sems.d_cnt[eng]
            op.dslot = kg % self.ring
            op.dval = 16 * (kg // self.ring + 1)
            if k >= self.ring:
                deps.append(h[k - self.ring])
            h.append(op)
        seen = set()
        for d in deps:
            if d is not op and id(d) not in seen and getattr(d, 'prog', self) is self:
                seen.add(id(d))
                op.deps.append(d)
        for b in writes:
            b.last_w = op
            b.readers = []
        for b in reads:
            if b.last_w is not op:
                b.readers.append(op)
        self.q[eng].append(op)
        return op

    def op(self, eng, fn, reads=(), writes=()):
        return self._add(eng, fn, reads, writes, False)

    def dma(self, eng, fn, reads=(), writes=()):
        return self._add(eng, fn, reads, writes, True)


    @staticmethod
    def _bufs(lst):
        return [b.buf if hasattr(b, 'buf') else b for b in lst]

    def mm(self, out, lhsT, rhs, start=True, stop=True, r=(), w=()):
        return self.op('pe', lambda e: e.matmul(out, lhsT, rhs, start=start, stop=stop), self._bufs(r), self._bufs(w))

    def tr(self, out, in_, ident, r=(), w=()):
        return self.op('pe', lambda e: e.transpose(out, in_, ident), self._bufs(r), self._bufs(w))

    def act(self, out, in_, func, bias=None, scale=None, accum=None, r=(), w=()):
        kw = {}
        if bias is not None:
            kw['bias'] = bias
        if scale is not None:
            kw['scale'] = scale
        if accum is not None:
            kw['accum_out'] = accum
        return self.op('act', lambda e: e.activation(out, in_, func, **kw), self._bufs(r), self._bufs(w))

    def tt(self, eng, out, in0, in1, op, r=(), w=()):
        return self.op(eng, lambda e: e.tensor_tensor(out, in0, in1, op), self._bufs(r), self._bufs(w))

    def ts(self, eng, out, in0, s1, s2=None, op0=None, op1=None, r=(), w=()):
        if op1 is None:
            return self.op(eng, lambda e: e.tensor_scalar(out, in0, s1, None, op0), self._bufs(r), self._bufs(w))
        return self.op(eng, lambda e: e.tensor_scalar(out, in0, s1, s2, op0, op1), self._bufs(r), self._bufs(w))

    def stt(self, out, in0, scalar, in1, op0, op1, r=(), w=()):
        return self.op('dve', lambda e: e.scalar_tensor_tensor(out, in0, scalar, in1, op0, op1), self._bufs(r), self._bufs(w))

    def cp(self, eng, out, in_, r=(), w=()):
        if eng == 'act':
            return self.op('act', lambda e: e.copy(out, in_), self._bufs(r), self._bufs(w))
        return self.op(eng, lambda e: e.tensor_copy(out, in_), self._bufs(r), self._bufs(w))

    def ld(self, out, in_, r=(), w=(), eng='sp', **kw):
        return self.dma(eng, lambda e: e.dma_start(out=out, in_=in_, **kw), self._bufs(r), self._bufs(w))

    def _last_ops(self, skip=None):
        out = []
        for e in ENGS:
            if e != skip and self.q[e]:
                for o in reversed(self.q[e]):
                    if not o.is_dma and o.fn is not None:
                        out.append(o)
                        break
            h = self.dma_hist[e]
            out.extend(h[-self.ring:])
        return out

    def barrier(self):
        deps_for = {e: self._last_ops(skip=e) for e in ENGS}
        for e in ENGS:
            op = _Op(e, len(self.q[e]), None, False)
            op.prog = self
            op.deps = deps_for[e]
            self.q[e].append(op)

    def finish(self):
        op = _Op('sp', len(self.q['sp']), None, False)
        op.prog = self
        for e in ENGS:
            op.deps.extend(self.dma_hist[e][-self.ring:])
        self.q['sp'].append(op)

    def emit(self):
        nc = self.nc
        self.barrier()
        self.finish()
        for e in ENGS:
            seen = {p: -1 for p in ENGS}
            seen_dma = {}
            for o in self.q[e]:
                best = {}
                for d in o.deps:
                    if d.is_dma:
                        key = (d.eng, d.dslot)
                        if seen_dma.get(key, 0) >= d.dval:
                            continue
                        seen_dma[key] = d.dval
                        o.waits.append(d)
                    else:
                        if d.eng == 'pe' and e == 'pe':
                            continue
                        if d.fn is None:
                            continue
                        if d.idx <= seen[d.eng]:
                            continue
                        if d.eng not in best or best[d.eng].idx < d.idx:
                            best[d.eng] = d
                for p, d in best.items():
                    seen[p] = d.idx
                    d.needs_inc = True
                    o.waits.append(d)
        for e in ENGS:
            c = self.sems.c_off[e]
            for o in self.q[e]:
                if o.needs_inc:
                    c += 1
                o.cnt = c
            self.sems.c_off[e] = c
            self.sems.d_cnt[e] += len(self.dma_hist[e])
        n_wait = sum(len(o.waits) for e in ENGS for o in self.q[e])
        n_ops = sum(len(self.q[e]) for e in ENGS)
        self.stats = dict(n_ops=n_ops, n_wait=n_wait, per_eng={e: len(self.q[e]) for e in ENGS})
        with contextlib.ExitStack() as st:
            csem = self.sems.csem
            dsem = self.sems.dsem
            block = st.enter_context(nc.Block())

            def run(ename, eng):
                for o in self.q[ename]:
                    for d in o.waits:
                        if d.is_dma:
                            eng.wait_ge(dsem[d.eng][d.dslot], d.dval)
                        else:
                            eng.wait_ge(csem[d.eng], d.cnt)
                    if o.fn is None:
                        continue
                    ins = o.fn(eng)
                    if o.is_dma:
                        ins.then_inc(dsem[ename][o.dslot], 16)
                    elif o.needs_inc:
                        ins.then_inc(csem[ename], 1)

            @block.tensor
            def _(eng):
                run('pe', eng)

            @block.scalar
            def _(eng):
                run('act', eng)

            @block.vector
            def _(eng):
                run('dve', eng)

            @block.gpsimd
            def _(eng):
                run('pool', eng)

            @block.sync
            def _(eng):
                run('sp', eng)


D = 1024
IN_COLS = 3344
ALPHA = float(4 ** 0.25)
LN_EPS = 1e-5
RMS_EPS = 1e-5
GN_EPS = 64e-5
DECAY_C = float(np.exp(-0.5))
NEU_DT = BF16

W_NAMES = ["ln0_g", "ln0_b", "w_in", "conv_w", "conv_b", "dt_bias", "a_log", "d_skip", "ssd_norm_g",
           "mu_rwkv", "w0", "w_up", "a0", "a_up", "g_up", "k_k", "k_a", "r_k", "lnx_g", "lnx_b", "w_out",
           "ln1_g", "ln1_b", "w_fc", "w_proj", "ln2_g", "ln2_b"]
W_SHAPES = {
    "ln0_g": [1024], "ln0_b": [1024], "w_in": [2, 1024, 3344], "conv_w": [2, 5, 1024], "conv_b": [2, 1024],
    "dt_bias": [2, 2, 8], "a_log": [2, 2, 8], "d_skip": [2, 8], "ssd_norm_g": [2, 512], "mu_rwkv": [2, 1792],
    "w0": [2, 2, 512], "w_up": [2, 2, 64, 512], "a0": [2, 2, 512], "a_up": [2, 2, 64, 512], "g_up": [2, 128, 512],
    "k_k": [2, 512], "k_a": [2, 512], "r_k": [2, 8, 64], "lnx_g": [2, 512], "lnx_b": [2, 512],
    "w_out": [2, 1024, 1024], "ln1_g": [2, 1024], "ln1_b": [2, 1024], "w_fc": [2, 1024, 4096],
    "w_proj": [2, 4096, 1024], "ln2_g": [2, 1024], "ln2_b": [2, 1024],
}


def make_consts():
    i = np.arange(128)
    ident = np.eye(128, dtype=np.float32)
    U = (i[:, None] <= i[None, :]).astype(np.float32)
    Lo = (i[:, None] >= i[None, :]).astype(np.float32)
    Us = (i[:, None] < i[None, :]).astype(np.float32)
    Ls = (i[:, None] > i[None, :]).astype(np.float32)
    ones = np.ones((128, 128), np.float32)
    return np.ascontiguousarray(np.concatenate([ident, U, Lo, Us, Ls, ones], axis=1))


class Tile:
    def __init__(self, t, name=''):
        self.t = t
        self.buf = Buf(name)

    def __getitem__(self, k):
        return self.t[k]


class Ctx:
    pass


class _Quad:
    def __init__(self, tiles):
        self.tiles = tiles
        self.buf = Buf('quad')

    def __getitem__(self, k):
        p, i, c = k
        return self.tiles[i][p, c]


def layer_norm_tile(P, C, src, dst, g_t, b_t, stat, eps, r, w):
    st = stat
    P.op('dve', lambda e: e.bn_stats(st[:, 0:6], src[:, 0:512]), P._bufs(r), [st.buf])
    P.op('dve', lambda e: e.bn_stats(st[:, 6:12], src[:, 512:1024]), P._bufs(r), [st.buf])
    P.op('dve', lambda e: e.bn_aggr(st[:, 12:14], st[:, 0:12].rearrange("p (a b) -> p a b", b=6)), [st.buf], [st.buf])
    P.ts('dve', st[:, 14:15], st[:, 13:14], eps, None, ALU.add, r=[st], w=[st])
    P.act(st[:, 15:16], st[:, 14:15], AF.Sqrt, r=[st], w=[st])
    P.op('dve', lambda e: e.reciprocal(st[:, 16:17], st[:, 15:16]), [st.buf], [st.buf])
    P.stt(st[:, 17:18], st[:, 12:13], -1.0, st[:, 16:17], ALU.mult, ALU.mult, r=[st], w=[st])
    P.act(dst, src, AF.Identity, bias=st[:, 17:18], scale=st[:, 16:17], r=list(r) + [st], w=w)
    P.tt('pool', dst, dst, g_t[:, :], ALU.mult, r=list(w) + [g_t], w=w)
    P.tt('dve', dst, dst, b_t[:, :], ALU.add, r=list(w) + [b_t], w=w)


_UID = [0]


_SBUSE = [0]
SB_BUDGET = 176 * 1024


def _sb_release(n):
    _SBUSE[0] -= n


def _sb(st, nc, name, shape, dt):
    _UID[0] += 1
    name = '%s_%d' % (name, _UID[0])
    n = int(np.prod(shape[1:])) * (2 if dt == BF16 else 4)
    n = (n + 31) // 32 * 32
    _SBUSE[0] += n
    assert _SBUSE[0] <= SB_BUDGET, ('SBUF budget exceeded', name, _SBUSE[0])
    t = Tile(st.enter_context(nc.sbuf_tensor(name, shape, dt)), name)
    st.callback(_sb_release, n)
    return t


def _psum(st, nc, n=8):
    _UID[0] += 1
    return [Tile(st.enter_context(nc.psum_tensor('ps%d_%d' % (i, _UID[0]), [128, 512], F32)), 'ps%d' % i) for i in range(n)]


class RR:
    def __init__(self, items):
        self.items = items
        self.i = 0

    def __call__(self):
        x = self.items[self.i % len(self.items)]
        self.i += 1
        return x


def load_bcast(P, tile, src_row, n, eng='sp'):
    P.ld(tile[:, 0:n], src_row.partition_broadcast(128), w=[tile], eng=eng)


def stage_consts(C):
    P = Prog(C.nc, C.sems)
    P.ld(C.cst[:, :], C.cst_d[:, :], w=[C.cst])
    P.emit()


def stage_A(C, l):
    nc, T, NS = C.nc, C.T, C.NS
    TB = min(512, T)
    NJ = TB // 128
    ident = C.cst[:, 0:128]
    with contextlib.ExitStack() as st:
        Win = _sb(st, nc, 'Win', [128, 8, IN_COLS], BF16)
        Wb = [Buf('Win%d' % k) for k in range(8)]
        hin = [_sb(st, nc, 'hin%d' % i, [128, D], F32) for i in range(2)]
        hT = [_sb(st, nc, 'hT%d' % i, [128, 8, TB], BF16) for i in range(2)]
        stat = [_sb(st, nc, 'stat%d' % i, [128, 32], F32) for i in range(2)]
        zo = [_sb(st, nc, 'zo%d' % i, [128, 512], F32) for i in range(2)]
        dto = [_sb(st, nc, 'dto%d' % i, [128, 16], F32) for i in range(2)]
        fo = [_sb(st, nc, 'fo%d' % i, [128, TB], BF16) for i in range(3)]
        if l == 0:
            g0 = _sb(st, nc, 'g0', [128, D], F32)
            b0 = _sb(st, nc, 'b0', [128, D], F32)
        ps = _psum(st, nc)
        P = Prog(nc, C.sems)
        for kc in range(8):
            P.ld(Win[:, kc, :], C.w['w_in'][l, kc * 128:(kc + 1) * 128, :], w=[Wb[kc]], eng='pool', max_dma_last_dim=4096)
        if l == 0:
            load_bcast(P, g0, C.w['ln0_g'], D)
            load_bcast(P, b0, C.w['ln0_b'], D)
        psr = RR(ps)
        evr = RR(['act', 'dve'])
        zor, dtor, forr = RR(zo), RR(dto), RR(fo)
        ti = 0
        for b in range(NS):
            src = C.x[b] if l == 0 else C.hbuf[b]
            for tb in range(T // TB):
                hTt = hT[(b * (T // TB) + tb) % 2]
                for j in range(NJ):
                    rows = slice(tb * TB + j * 128, tb * TB + (j + 1) * 128)
                    hi = hin[ti % 2]
                    P.ld(hi[:, :], src[rows, :], w=[hi])
                    if l == 0:
                        layer_norm_tile(P, C, hi[:, :], hi[:, :], g0, b0, stat[ti % 2], LN_EPS, r=[hi], w=[hi])
                        P.ld(C.hbuf[b, rows, :], hi[:, :], r=[hi])
                    for half in range(2):
                        bank = psr()
                        for q in range(4):
                            kc = half * 4 + q
                            P.tr(bank[:, q * 128:(q + 1) * 128], hi[:, kc * 128:(kc + 1) * 128], ident, r=[hi, C.cst], w=[bank])
                        P.cp(evr(), hTt[:, half * 4:half * 4 + 4, j * 128:(j + 1) * 128],
                             bank[:, :].rearrange("p (a b) -> p a b", b=128), r=[bank], w=[hTt])
                    ti += 1
                for j in range(NJ):
                    rows = slice(tb * TB + j * 128, tb * TB + (j + 1) * 128)
                    bank = psr()
                    for kc in range(8):
                        P.mm(bank[:, :], hTt[:, kc, j * 128:(j + 1) * 128], Win[:, kc, 0:512], start=(kc == 0), stop=(kc == 7),
                             r=[hTt, Wb[kc]], w=[bank])
                    z = zor()
                    P.cp(evr(), z[:, :], bank[:, :], r=[bank], w=[z])
                    P.ld(C.zbuf[b, rows, :], z[:, :], r=[z])
                    bank = psr()
                    for kc in range(8):
                        P.mm(bank[:, 0:16], hTt[:, kc, j * 128:(j + 1) * 128], Win[:, kc, 1536:1552], start=(kc == 0), stop=(kc == 7),
                             r=[hTt, Wb[kc]], w=[bank])
                    dt = dtor()
                    P.cp(evr(), dt[:, :], bank[:, 0:16], r=[bank], w=[dt])
                    P.ld(C.dtbuf[b, rows, :], dt[:, :], r=[dt])
                for cc in range(22):
                    col0 = 512 + cc * 128 if cc < 8 else 1552 + (cc - 8) * 128
                    bank = psr()
                    for kc in range(8):
                        P.mm(bank[:, 0:TB], Win[:, kc, col0:col0 + 128], hTt[:, kc, :], start=(kc == 0), stop=(kc == 7),
                             r=[hTt, Wb[kc]], w=[bank])
                    f = forr()
                    P.cp(evr(), f[:, :], bank[:, 0:TB], r=[bank], w=[f])
                    if cc < 8:
                        dst = C.xbcT[b, cc * 128:(cc + 1) * 128, tb * TB:(tb + 1) * TB]
                    else:
                        dst = C.rwT[b, (cc - 8) * 128:(cc - 7) * 128, tb * TB:(tb + 1) * TB]
                    P.ld(dst, f[:, :], r=[f])
        P.emit()
        C.stats.append(('A', P.stats))


def load_w_bf16(P, tile, bufs, src, nk, eng='pool'):
    for kc in range(nk):
        P.ld(tile[:, kc, :], src[kc * 128:(kc + 1) * 128, :], w=[bufs[kc]], eng=eng, max_dma_last_dim=4096)


def stage_D1(C, l):
    nc, T, NS = C.nc, C.T, C.NS
    ident = C.cst[:, 0:128]
    with contextlib.ExitStack() as st:
        Wo = _sb(st, nc, 'Wo', [128, 8, D], BF16)
        Wob = [Buf('Wo%d' % k) for k in range(8)]
        g1 = _sb(st, nc, 'g1', [128, D], F32)
        b1 = _sb(st, nc, 'b1', [128, D], F32)
        ym = [_sb(st, nc, 'ym%d' % i, [128, D], F32) for i in range(2)]
        yT = [_sb(st, nc, 'yT%d' % i, [128, 8, 128], BF16) for i in range(2)]
        hr = [_sb(st, nc, 'hr%d' % i, [128, D], F32) for i in range(2)]
        t1 = [_sb(st, nc, 't1%d' % i, [128, D], F32) for i in range(2)]
        stat = [_sb(st, nc, 'stat%d' % i, [128, 32], F32) for i in range(2)]
        ps = _psum(st, nc)
        P = Prog(nc, C.sems)
        load_w_bf16(P, Wo, Wob, C.w['w_out'][l], 8)
        load_bcast(P, g1, C.w['ln1_g'][l], D)
        load_bcast(P, b1, C.w['ln1_b'][l], D)
        psr = RR(ps)
        evr = RR(['act', 'dve'])
        ti = 0
        for b in range(NS):
            for c in range(T // 128):
                rows = slice(c * 128, (c + 1) * 128)
                y, yt, h, t, sx = ym[ti % 2], yT[ti % 2], hr[ti % 2], t1[ti % 2], stat[ti % 2]
                P.ld(y[:, :], C.ymix[b, rows, :], w=[y])
                P.ld(h[:, :], C.hbuf[b, rows, :], w=[h])
                for half in range(2):
                    bank = psr()
                    for q in range(4):
                        kc = half * 4 + q
                        P.tr(bank[:, q * 128:(q + 1) * 128], y[:, kc * 128:(kc + 1) * 128], ident, r=[y, C.cst], w=[bank])
                    P.cp(evr(), yt[:, half * 4:half * 4 + 4, :], bank[:, :].rearrange("p (a b) -> p a b", b=128), r=[bank], w=[yt])
                for half in range(2):
                    bank = psr()
                    for kc in range(8):
                        P.mm(bank[:, :], yt[:, kc, :], Wo[:, kc, half * 512:(half + 1) * 512], start=(kc == 0), stop=(kc == 7),
                             r=[yt, Wob[kc]], w=[bank])
                    P.stt(t[:, half * 512:(half + 1) * 512], h[:, half * 512:(half + 1) * 512], ALPHA, bank[:, :], ALU.mult, ALU.add,
                          r=[h, bank], w=[t])
                layer_norm_tile(P, C, t[:, :], t[:, :], g1, b1, sx, LN_EPS, r=[t], w=[t])
                P.ld(C.h1buf[b, rows, :], t[:, :], r=[t])
                ti += 1
        P.emit()
        C.stats.append(('D1', P.stats))


def stage_D2(C, l, last):
    nc, T, NS = C.nc, C.T, C.NS
    ident = C.cst[:, 0:128]
    TB = 256
    with contextlib.ExitStack() as st:
        Wf = _sb(st, nc, 'Wf', [128, 8, 4096], BF16)
        Wfb = [Buf('Wf%d' % k) for k in range(8)]
        Wp = _sb(st, nc, 'Wp', [128, 32, D], BF16)
        Wpb = [Buf('Wp%d' % k) for k in range(32)]
        g2 = _sb(st, nc, 'g2', [128, D], F32)
        b2 = _sb(st, nc, 'b2', [128, D], F32)
        h1 = [_sb(st, nc, 'h1%d' % i, [128, D], F32) for i in range(2)]
        h1T = _sb(st, nc, 'h1T', [128, 8, TB], BF16)
        aT = _sb(st, nc, 'aT', [128, 32, TB], BF16)
        tmp = [_sb(st, nc, 'tmp%d' % i, [128, TB], F32) for i in range(2)]
        t2 = [_sb(st, nc, 't2%d' % i, [128, D], F32) for i in range(1)]
        stat = [_sb(st, nc, 'stat%d' % i, [128, 32], F32) for i in range(2)]
        ps = _psum(st, nc)
        P = Prog(nc, C.sems)
        load_w_bf16(P, Wf, Wfb, C.w['w_fc'][l], 8)
        load_w_bf16(P, Wp, Wpb, C.w['w_proj'][l], 32)
        load_bcast(P, g2, C.w['ln2_g'][l], D)
        load_bcast(P, b2, C.w['ln2_b'][l], D)
        psr = RR(ps)
        evr = RR(['act', 'dve'])
        tmr = RR(tmp)
        ti = 0
        for b in range(NS):
            dstb = C.out[b] if last else C.hbuf[b]
            for tb in range(T // TB):
                for j in range(2):
                    rows = slice(tb * TB + j * 128, tb * TB + (j + 1) * 128)
                    h = h1[j]
                    P.ld(h[:, :], C.h1buf[b, rows, :], w=[h])
                    for half in range(2):
                        bank = psr()
                        for q in range(4):
                            kc = half * 4 + q
                            P.tr(bank[:, q * 128:(q + 1) * 128], h[:, kc * 128:(kc + 1) * 128], ident, r=[h, C.cst], w=[bank])
                        P.cp(evr(), h1T[:, half * 4:half * 4 + 4, j * 128:(j + 1) * 128],
                             bank[:, :].rearrange("p (a b) -> p a b", b=128), r=[bank], w=[h1T])
                for fc in range(32):
                    bank = psr()
                    for kc in range(8):
                        P.mm(bank[:, 0:TB], Wf[:, kc, fc * 128:(fc + 1) * 128], h1T[:, kc, :], start=(kc == 0), stop=(kc == 7),
                             r=[h1T, Wfb[kc]], w=[bank])
                    tm = tmr()
                    if fc % 2 == 0:
                        P.act(tm[:, :], bank[:, 0:TB], AF.Relu, r=[bank], w=[tm])
                    else:
                        P.ts('dve', tm[:, :], bank[:, 0:TB], 0.0, None, ALU.max, r=[bank], w=[tm])
                    P.tt('pool', aT[:, fc, :], tm[:, :], tm[:, :], ALU.mult, r=[tm], w=[aT])
                for j in range(2):
                    rows = slice(tb * TB + j * 128, tb * TB + (j + 1) * 128)
                    h = h1[j]
                    t = t2[0]
                    for half in range(2):
                        bank = psr()
                        for fc in range(32):
                            P.mm(bank[:, :], aT[:, fc, j * 128:(j + 1) * 128], Wp[:, fc, half * 512:(half + 1) * 512],
                                 start=(fc == 0), stop=(fc == 31), r=[aT, Wpb[fc]], w=[bank])
                        P.stt(t[:, half * 512:(half + 1) * 512], h[:, half * 512:(half + 1) * 512], ALPHA, bank[:, :], ALU.mult, ALU.add,
                              r=[h, bank], w=[t])
                    layer_norm_tile(P, C, t[:, :], t[:, :], g2, b2, stat[ti % 2], LN_EPS, r=[t], w=[t])
                    P.ld(dstb[rows, :], t[:, :], r=[t])
                    ti += 1
        P.emit()
        C.stats.append(('D2', P.stats))


def bc3(ap2, n, axis):
    k = ap2.shape[1]
    if axis == 1:
        return ap2.unsqueeze(1).broadcast_to([128, n, k])
    return ap2.unsqueeze(2).broadcast_to([128, k, n])


def stage_B(C, l, b):
    nc, T, NCH = C.nc, C.T, C.NCH
    TB = min(512, T)
    cst = C.cst
    ident, U, Lo, Us, Ls, ones = (cst[:, i * 128:(i + 1) * 128] for i in range(6))
    with contextlib.ExitStack() as st:
        sb = lambda n, sh, dt: _sb(st, nc, n, sh, dt)
        XTb = [Buf('XT%d' % g) for g in range(8)]
        BT = sb('BT', [128, 2, T], BF16)
        CT = sb('CT', [128, 2, T], BF16)
        xbf = sb('xbf', [128, NCH, 512], BF16)
        Btok = sb('Btok', [128, NCH, 256], BF16)
        dtraw = sb('dtraw', [128, NCH, 16], F32)
        dtv = sb('dtv', [128, NCH, 16], F32)
        av = sb('av', [128, NCH, 16], F32)
        dtb = sb('dtb', [128, 16], F32)
        negA = sb('negA', [128, 16], F32)
        dsk8 = sb('dsk8', [128, 8], F32)
        dsk = sb('dsk', [128, 512], F32)
        ng = sb('ng', [128, 512], F32)
        ps = _psum(st, nc)
        st0 = contextlib.ExitStack()
        sb0 = lambda n, sh, dt: _sb(st0, nc, n, sh, dt)
        XT = sb0('XT', [128, 8, T + 4], BF16)
        cw6 = sb0('cw6', [6, 1024], F32)
        cwb = sb0('cwb', [128, 8, 6], F32)
        Dg = sb0('Dg', [128, 5, 8, 128], BF16)
        cbrow = sb0('cbrow', [1, 768], F32)
        cbrow_bf = sb0('cbrow_bf', [1, 768], BF16)
        ones_bf = sb0('ones_bf', [1, 128], BF16)
        P = Prog(nc, C.sems)
        psr = RR(ps)
        P.op('pool', lambda e: e.memset(XT[:, :, 0:2], 0.0), [], XTb)
        P.op('pool', lambda e: e.memset(XT[:, :, T + 2:T + 4], 0.0), [], XTb)
        for g in range(8):
            P.ld(XT[:, g, 2:T + 2], C.xbcT[b, g * 128:(g + 1) * 128, :], w=[XTb[g]])
        P.ld(cw6[0:5, :], C.w['conv_w'][l], w=[cw6])
        P.ld(cw6[5:6, :], C.w['conv_b'][l:l + 1, :], w=[cw6])
        P.ld(cbrow[0:1, :], C.w['conv_b'][l:l + 1, 0:768], w=[cbrow])
        P.cp('dve', cbrow_bf[0:1, :], cbrow[0:1, :], r=[cbrow], w=[cbrow_bf])
        P.cp('dve', ones_bf[0:1, :], ones[0:1, :], r=[cst], w=[ones_bf])
        load_bcast(P, dtb, C.w['dt_bias'][l].rearrange("a b -> (a b)"), 16)
        load_bcast(P, negA, C.w['a_log'][l].rearrange("a b -> (a b)"), 16)
        load_bcast(P, dsk8, C.w['d_skip'][l], 8)
        load_bcast(P, ng, C.w['ssd_norm_g'][l], 512)
        P.act(negA[:, :], negA[:, :], AF.Exp, r=[negA], w=[negA])
        P.ts('dve', negA[:, :], negA[:, :], -1.0, None, ALU.mult, r=[negA], w=[negA])
        P.cp('dve', dsk[:, :].rearrange("p (h q) -> p h q", q=64), bc3(dsk8[:, :], 64, 2), r=[dsk8], w=[dsk])
        bank = psr()
        for g in range(8):
            P.tr(bank[:, g * 6:(g + 1) * 6], cw6[0:6, g * 128:(g + 1) * 128], ident[0:6, 0:6], r=[cw6, cst], w=[bank])
        P.cp('dve', cwb[:, :, :], bank[:, 0:48].rearrange("p (g k) -> p g k", k=6), r=[bank], w=[cwb])
        er = RR(['dve', 'pool'])
        for k in range(5):
            for g in range(8):
                P.ts(er(), Dg[:, k, g, :], ident, cwb[:, g, k:k + 1], None, ALU.mult, r=[cwb, cst], w=[Dg])
        P.ld(dtraw[:, :, :], C.dtbuf[b].rearrange("(c p) k -> p c k", p=128), w=[dtraw])
        P.tt('dve', dtv[:, :, :], dtraw[:, :, :], bc3(dtb[:, :], NCH, 1), ALU.add, r=[dtraw, dtb], w=[dtv])
        P.act(dtv[:, :, :], dtv[:, :, :], AF.Exp, r=[dtv], w=[dtv])
        P.ts('dve', dtv[:, :, :], dtv[:, :, :], 1.0, None, ALU.add, r=[dtv], w=[dtv])
        P.act(dtv[:, :, :], dtv[:, :, :], AF.Ln, r=[dtv], w=[dtv])
        P.tt('dve', av[:, :, :], dtv[:, :, :], bc3(negA[:, :], NCH, 1), ALU.mult, r=[dtv, negA], w=[av])
        for c in range(NCH):
            for (g0, ng_, dst, boff) in ((0, 4, xbf, 0), (4, 2, Btok, 512)):
                bank = psr()
                n = ng_ * 128
                P.mm(bank[:, 0:n], ones_bf[0:1, :], cbrow_bf[0:1, boff:boff + n], start=True, stop=False,
                     r=[ones_bf, cbrow_bf], w=[bank])
                for gi in range(ng_):
                    g = g0 + gi
                    for k in range(5):
                        P.mm(bank[:, gi * 128:(gi + 1) * 128], XT[:, g, c * 128 + k:c * 128 + k + 128], Dg[:, k, g, :],
                             start=False, stop=(gi == ng_ - 1 and k == 4), r=[XTb[g], Dg], w=[bank])
                P.act(dst[:, c, :], bank[:, 0:n], AF.Silu, r=[bank], w=[dst])
        for tb in range(T // TB):
            for gi in range(4):
                g = 4 + gi
                bank = psr()
                for k in range(5):
                    P.mm(bank[:, 0:TB], Dg[:, k, g, :], XT[:, g, tb * TB + k:tb * TB + k + TB], start=(k == 0), stop=(k == 4),
                         r=[XTb[g], Dg], w=[bank])
                dst = BT if gi < 2 else CT
                P.act(dst[:, gi % 2, tb * TB:(tb + 1) * TB], bank[:, 0:TB], AF.Silu, bias=cwb[:, g, 5:6], r=[bank, cwb], w=[dst])
        P.emit()
        C.stats.append(('B0', P.stats))
        st0.close()
        ysc = sb('ysc', [128, NCH, 16], F32)
        cs_sb = [sb('cs_sb%d' % i, [128, 32], F32) for i in range(2)]
        dd = [sb('dd%d' % i, [128, 16], F32) for i in range(2)]
        et = [sb('et%d' % i, [128, 16], F32) for i in range(2)]
        wd = [sb('wd%d' % i, [128, 16], F32) for i in range(2)]
        xw = [sb('xw%d' % i, [128, 2, 512], BF16) for i in range(2)]
        Srun = sb('Srun', [128, 2, 512], F32)
        Sin = sb('Sin', [128, NCH, 2, 512], BF16)
        rhsS = [sb('rhsS%d' % i, [128, 2, 8, 128], F32) for i in range(2)]
        E = [sb('E%d' % i, [128, 2, 8, 128], F32) for i in range(2)]
        SM = [sb('SM%d' % i, [128, 2, 2, 128], F32) for i in range(2)]
        G = [sb('G%d' % i, [128, 2, 8, 128], BF16) for i in range(2)]
        ta = [sb('ta%d' % i, [128, 512], F32) for i in range(2)]
        tb_ = [sb('tb%d' % i, [128, 512], F32) for i in range(2)]
        yt = [sb('yt%d' % i, [128, 512], F32) for i in range(2)]
        zt = [sb('zt%d' % i, [128, 512], F32) for i in range(2)]
        sq = [sb('sq%d' % i, [128, 512], F32) for i in range(2)]
        ss = [sb('ss%d' % i, [128, 8], F32) for i in range(2)]
        P = Prog(nc, C.sems)
        psr = RR(ps)
        P.op('pool', lambda e: e.memset(Srun[:, :, :], 0.0), [], [Srun.buf])
        for i in range(NCH):
            cc = (i, NCH - 1 - i)
            k2 = i % 2
            bank = psr()
            for d in range(2):
                P.mm(bank[:, d * 8:(d + 1) * 8], U if d == 0 else Lo, av[:, cc[d], d * 8:(d + 1) * 8], r=[cst, av], w=[bank])
                P.mm(bank[:, 16 + d * 8:16 + (d + 1) * 8], ones, av[:, cc[d], d * 8:(d + 1) * 8], r=[cst, av], w=[bank])
            P.cp('act', cs_sb[k2][:, :], bank[:, 0:32], r=[bank], w=[cs_sb[k2]])
            P.tt('dve', dd[k2][:, :], cs_sb[k2][:, 16:32], cs_sb[k2][:, 0:16], ALU.subtract, r=[cs_sb[k2]], w=[dd[k2]])
            P.act(dd[k2][:, :], dd[k2][:, :], AF.Exp, r=[dd[k2]], w=[dd[k2]])
            P.act(et[k2][:, :], cs_sb[k2][:, 16:32], AF.Exp, r=[cs_sb[k2]], w=[et[k2]])
            for d in range(2):
                sl = slice(d * 8, (d + 1) * 8)
                P.act(ysc[:, cc[d], sl], cs_sb[k2][:, sl], AF.Exp, r=[cs_sb[k2]], w=[ysc])
                P.tt('dve', wd[k2][:, sl], dd[k2][:, sl], dtv[:, cc[d], sl], ALU.mult, r=[dd[k2], dtv], w=[wd[k2]])
                P.tt('dve', xw[k2][:, d, :].rearrange("p (h q) -> p h q", q=64),
                     xbf[:, cc[d], :].rearrange("p (h q) -> p h q", q=64), bc3(wd[k2][:, sl], 64, 2), ALU.mult,
                     r=[xbf, wd[k2]], w=[xw[k2]])
            for d in range(2):
                bank = psr()
                for g in range(2):
                    P.mm(bank[:, g * 256:(g + 1) * 256], Btok[:, cc[d], g * 128:(g + 1) * 128], xw[k2][:, d, g * 256:(g + 1) * 256],
                         r=[Btok, xw[k2]], w=[bank])
                P.cp('pool', Sin[:, cc[d], d, :], Srun[:, d, :], r=[Srun], w=[Sin])
                P.tt('dve', Srun[:, d, :].rearrange("p (h q) -> p h q", q=64), Srun[:, d, :].rearrange("p (h q) -> p h q", q=64),
                     bc3(et[k2][:, d * 8:(d + 1) * 8], 64, 2), ALU.mult, r=[Srun, et[k2]], w=[Srun])
                P.tt('dve', Srun[:, d, :], Srun[:, d, :], bank[:, :], ALU.add, r=[Srun, bank], w=[Srun])
        for c in range(NCH):
            k2 = c % 2
            rows = slice(c * 128, (c + 1) * 128)
            csl = slice(c * 128, (c + 1) * 128)
            P.ld(zt[k2][:, :], C.zbuf[b, rows, :], w=[zt[k2]])
            for d in range(2):
                P.tt('dve' if d == 0 else 'pool', rhsS[k2][:, d, :, :], bc3(U if d == 0 else Lo, 8, 1),
                     bc3(av[:, c, d * 8:(d + 1) * 8], 128, 2), ALU.mult, r=[cst, av], w=[rhsS[k2]])
            for d in range(2):
                for hh in range(2):
                    bank = psr()
                    P.mm(bank[:, :], Ls if d == 0 else Us, rhsS[k2][:, d, hh * 4:(hh + 1) * 4, :].rearrange("p a b -> p (a b)"),
                         r=[cst, rhsS[k2]], w=[bank])
                    P.act(E[k2][:, d, hh * 4:(hh + 1) * 4, :].rearrange("p a b -> p (a b)"), bank[:, :], AF.Exp, r=[bank], w=[E[k2]])
            bank = psr()
            for g in range(2):
                P.mm(bank[:, g * 128:(g + 1) * 128], BT[:, g, csl], CT[:, g, csl], r=[BT, CT], w=[bank])
            for d in range(2):
                P.tt('dve', SM[k2][:, d, :, :], bank[:, 0:256].rearrange("p (g l) -> p g l", l=128), bc3(U if d == 0 else Lo, 2, 1),
                     ALU.mult, r=[bank, cst], w=[SM[k2]])
            for d in range(2):
                for h in range(8):
                    P.stt(G[k2][:, d, h, :], E[k2][:, d, h, :], dtv[:, c, d * 8 + h:d * 8 + h + 1], SM[k2][:, d, h // 4, :],
                          ALU.mult, ALU.mult, r=[E[k2], dtv, SM[k2]], w=[G[k2]])
            bY1 = psr()
            for h in range(8):
                for d in range(2):
                    P.mm(bY1[:, h * 64:(h + 1) * 64], G[k2][:, d, h, :], xbf[:, c, h * 64:(h + 1) * 64], start=(d == 0), stop=(d == 1),
                         r=[G[k2], xbf], w=[bY1])
            bY2 = [psr(), psr()]
            for d in range(2):
                for g in range(2):
                    P.mm(bY2[d][:, g * 256:(g + 1) * 256], CT[:, g, csl], Sin[:, c, d, g * 256:(g + 1) * 256], r=[CT, Sin], w=[bY2[d]])
            v3 = lambda ap: ap.rearrange("p (h q) -> p h q", q=64)
            P.tt('dve', v3(ta[k2][:, :]), v3(bY2[0][:, :]), bc3(ysc[:, c, 0:8], 64, 2), ALU.mult, r=[bY2[0], ysc], w=[ta[k2]])
            P.tt('dve', v3(tb_[k2][:, :]), v3(bY2[1][:, :]), bc3(ysc[:, c, 8:16], 64, 2), ALU.mult, r=[bY2[1], ysc], w=[tb_[k2]])
            y = yt[k2]
            P.tt('pool', y[:, :], ta[k2][:, :], tb_[k2][:, :], ALU.add, r=[ta[k2], tb_[k2]], w=[y])
            P.tt('dve', y[:, :], y[:, :], bY1[:, :], ALU.add, r=[y, bY1], w=[y])
            P.tt('pool', sq[k2][:, :], xbf[:, c, :], dsk[:, :], ALU.mult, r=[xbf, dsk], w=[sq[k2]])
            P.tt('pool', y[:, :], y[:, :], sq[k2][:, :], ALU.add, r=[y, sq[k2]], w=[y])
            P.act(zt[k2][:, :], zt[k2][:, :], AF.Silu, r=[zt[k2]], w=[zt[k2]])
            P.tt('dve', y[:, :], y[:, :], zt[k2][:, :], ALU.mult, r=[y, zt[k2]], w=[y])
            P.tt('pool', sq[k2][:, :], y[:, :], y[:, :], ALU.mult, r=[y], w=[sq[k2]])
            P.op('dve', lambda e, o=ss[k2][:, 0:2], i_=sq[k2][:, :].rearrange("p (g q) -> p g q", q=256): e.reduce_sum(o, i_, axis=AX.X),
                 [sq[k2].buf], [ss[k2].buf])
            P.ts('dve', ss[k2][:, 2:4], ss[k2][:, 0:2], 1.0 / 256.0, RMS_EPS, ALU.mult, ALU.add, r=[ss[k2]], w=[ss[k2]])
            P.act(ss[k2][:, 4:6], ss[k2][:, 2:4], AF.Sqrt, r=[ss[k2]], w=[ss[k2]])
            P.op('dve', lambda e, o=ss[k2][:, 6:8], i_=ss[k2][:, 4:6]: e.reciprocal(o, i_), [ss[k2].buf], [ss[k2].buf])
            for g in range(2):
                P.ts('dve', y[:, g * 256:(g + 1) * 256], y[:, g * 256:(g + 1) * 256], ss[k2][:, 6 + g:7 + g], None, ALU.mult,
                     r=[y, ss[k2]], w=[y])
            P.tt('pool', y[:, :], y[:, :], ng[:, :], ALU.mult, r=[y, ng], w=[y])
            P.ld(C.ymix[b, rows, 0:512], y[:, :], r=[y])
        P.emit()
        C.stats.append(('B', P.stats))


def stage_C(C, l, b):
    nc, T, NCH = C.nc, C.T, C.NCH
    cst = C.cst
    ident, U, Lo, Us, Ls, ones = (cst[:, i * 128:(i + 1) * 128] for i in range(6))
    v3 = lambda ap: ap.rearrange("p (h q) -> p h q", q=64)
    with contextlib.ExitStack() as st:
        sb = lambda n, sh, dt: _sb(st, nc, n, sh, dt)
        murow = sb('murow', [14, 128], F32)
        muT = sb('muT', [128, 3, 14], F32)
        Dm = sb('Dm', [128, 14, 128], BF16)
        Dh = sb('Dh', [128, 14, 128], BF16)
        kkb = sb('kkb', [128, 512], F32)
        kab = sb('kab', [128, 512], F32)
        rkb = sb('rkb', [128, 512], F32)
        w0b = sb('w0b', [128, 2, 512], F32)
        a0b = sb('a0b', [128, 2, 512], F32)
        LW = sb('LW', [128, 2, 512], BF16)
        GU = sb('GU', [128, 512], BF16)
        MK = sb('MK', [128, 2, 512], F32)
        MKa = sb('MKa', [128, 2, 128], F32)
        g32 = sb('g32', [128, 512], F32)
        def alloc_inst():
            RTc = [sb('RTc%d' % i, [128, 14, 130], BF16) for i in range(2)]
            bon = sb('bon', [128, NCH, 8], F32)
            r32 = sb('r32', [128, 512], F32)
            k32 = sb('k32', [128, 512], F32)
            v32 = sb('v32', [128, 512], F32)
            vbf = sb('vbf', [128, 512], BF16)
            LT = sb('LT', [128, 128], BF16)
            sg = sb('sg', [128, 128], BF16)
            lw32 = sb('lw32', [128, 512], F32)
            a32 = sb('a32', [128, 512], F32)
            kk = sb('kk', [128, 512], F32)
            tmp = sb('tmp', [128, 512], F32)
            tmp2 = sb('tmp2', [128, 512], F32)
            kd = sb('kd', [128, 512], F32)
            bb = sb('bb', [128, 512], F32)
            sm8 = sb('sm8', [128, 32], F32)
            gC = sb('gC', [128, 4], F32)
            Ep = sb('Ep', [128, 512], F32)
            En = sb('En', [128, 512], F32)
            bt_bf = sb('bt_bf', [128, 512], BF16)
            kdt_bf = sb('kdt_bf', [128, 512], BF16)
            kt_bf = sb('kt_bf', [128, 512], NEU_DT)
            KR = sb('KR', [128, 4, 2, 128], BF16)
            btT = sb('btT', [128, 4, 128], BF16)
            kdtT = sb('kdtT', [128, 4, 128], BF16)
            R3 = sb('R3', [128, 8, 3, 128], BF16)
            NM = [sb('NM%d' % i, [128, 8, 2, 128], NEU_DT) for i in range(2)]
            X = [sb('X%d' % i, [128, 8, 128], NEU_DT) for i in range(2)]
            Z1 = sb('Z1', [128, 512], NEU_DT)
            Ut = sb('Ut', [128, 512], F32)
            WT = sb('WT', [128, 4, 128], BF16)
            Mst = sb('Mst', [128, 4, 128], F32)
            Mbf = sb('Mbf', [128, 4, 128], BF16)
            Un = sb('Un', [128, 512], BF16)
            Yo = [sb('Yo%d' % i, [128, 512], F32) for i in range(2)]
            return dict(locals())
        IT = [alloc_inst(), alloc_inst()]
        ps = _psum(st, nc)
        P = Prog(nc, C.sems)
        psr = RR(ps)
        evr = RR(['act', 'dve'])
        P.ld(murow[0:14, :], C.w['mu_rwkv'][l].rearrange("(g p) -> g p", p=128), w=[murow])
        bank = psr()
        P.tr(bank[:, 0:14], murow[0:14, :], ident[0:14, 0:14], r=[murow, cst], w=[bank])
        P.cp('dve', muT[:, 0, :], bank[:, 0:14], r=[bank], w=[muT])
        P.ts('dve', muT[:, 1, :], muT[:, 0, :], -1.0, 1.0, ALU.mult, ALU.add, r=[muT], w=[muT])
        P.ts('dve', muT[:, 2, :], muT[:, 0, :], 0.5, None, ALU.mult, r=[muT], w=[muT])
        er = RR(['dve', 'pool'])
        for g in range(14):
            P.ts(er(), Dm[:, g, :], ident, muT[:, 1, g:g + 1], None, ALU.mult, r=[muT, cst], w=[Dm])
            P.ts(er(), Dh[:, g, :], ident, muT[:, 2, g:g + 1], None, ALU.mult, r=[muT, cst], w=[Dh])
        load_bcast(P, kkb, C.w['k_k'][l], 512)
        load_bcast(P, kab, C.w['k_a'][l], 512)
        load_bcast(P, rkb, C.w['r_k'][l].rearrange("a b -> (a b)"), 512)
        for d in range(2):
            P.ld(w0b[:, d, :], C.w['w0'][l, d].partition_broadcast(128), w=[w0b])
            P.ld(a0b[:, d, :], C.w['a0'][l, d].partition_broadcast(128), w=[a0b])
            P.ld(LW[0:64, d, :], C.w['w_up'][l, d], w=[LW], eng='pool')
            P.ld(LW[64:128, d, :], C.w['a_up'][l, d], w=[LW], eng='pool')
        P.ld(GU[:, :], C.w['g_up'][l], w=[GU], eng='pool')
        for d in range(2):
            strict, incl, strict_ts = (Us, U, Ls) if d == 0 else (Ls, Lo, Us)
            P.ts('dve', MK[:, d, 0:128], strict, -1.0, None, ALU.mult, r=[cst], w=[MK])
            P.cp('dve', MK[:, d, 128:256], incl, r=[cst], w=[MK])
            P.cp('dve', MK[:, d, 256:384], strict, r=[cst], w=[MK])
            P.cp('dve', MK[:, d, 384:512], incl, r=[cst], w=[MK])
            P.ts('dve', MKa[:, d, :], strict_ts, -1.0, None, ALU.mult, r=[cst], w=[MKa])
        taps = (Dh, Dm, Dh)
        ydb = [[Buf('yd') for _ in range(NCH)] for _ in range(2)]
        vgb = [[Buf('vg') for _ in range(NCH)] for _ in range(2)]
        def inst(d):
            I_ = IT[d]
            (RTc, r32, k32, v32, vbf, LT, sg, lw32, a32, kk, tmp, tmp2, kd, bb, sm8, gC, Ep, En, bt_bf, kdt_bf, kt_bf, KR, btT, kdtT, R3, NM, X, Z1, Ut, WT, Mst, Mbf, Un, Yo, bon) = (I_[n] for n in ('RTc', 'r32', 'k32', 'v32', 'vbf', 'LT', 'sg', 'lw32', 'a32', 'kk', 'tmp', 'tmp2', 'kd', 'bb', 'sm8', 'gC', 'Ep', 'En', 'bt_bf', 'kdt_bf', 'kt_bf', 'KR', 'btT', 'kdtT', 'R3', 'NM', 'X', 'Z1', 'Ut', 'WT', 'Mst', 'Mbf', 'Un', 'Yo', 'bon'))
            P.op('pool', lambda e: e.memset(Mst[:, :, :], 0.0), [], [Mst.buf])
            P.op('pool', lambda e: e.memset(Mbf[:, :, :], 0.0), [], [Mbf.buf])
            for ci in range(NCH):
                c = ci if d == 0 else NCH - 1 - ci
                rows = slice(c * 128, (c + 1) * 128)
                RT = RTc[ci % 2]
                lo, hi = max(c * 128 - 1, 0), min(c * 128 + 129, T)
                if c == 0:
                    P.op('pool', lambda e, t=RT: e.memset(t[:, :, 0:1], 0.0), [], [RT.buf])
                if c == NCH - 1:
                    P.op('pool', lambda e, t=RT: e.memset(t[:, :, 129:130], 0.0), [], [RT.buf])
                P.ld(RT[:, :, lo - (c * 128 - 1):hi - (c * 128 - 1)],
                     C.rwT[b].rearrange("(g p) t -> p g t", p=128)[:, :, lo:hi], w=[RT])
                for (g0, dst) in ((0, r32), (4, k32), (8, v32)):
                    bank = psr()
                    for gi in range(4):
                        g = g0 + gi
                        for k in range(3):
                            P.mm(bank[:, gi * 128:(gi + 1) * 128], RT[:, g, k:k + 128], taps[k][:, g, :],
                                 start=(k == 0), stop=(k == 2), r=[RT, Dm, Dh], w=[bank])
                    P.cp(evr(), dst[:, :], bank[:, :], r=[bank], w=[dst])
                P.cp('pool', vbf[:, :], v32[:, :], r=[v32], w=[vbf])
                bank = psr()
                for gi in range(2):
                    g = 12 + gi
                    for k in range(3):
                        P.mm(bank[:, gi * 128:(gi + 1) * 128], taps[k][:, g, :], RT[:, g, k:k + 128],
                             start=(k == 0), stop=(k == 2), r=[RT, Dm, Dh], w=[bank])
                P.act(LT[0:64, :], bank[0:64, 0:128], AF.Tanh, r=[bank], w=[LT])
                P.cp('dve', LT[64:128, :], bank[64:128, 0:128], r=[bank], w=[LT])
                P.act(sg[:, :], bank[:, 128:256], AF.Sigmoid, r=[bank], w=[sg])
                yield
                bW, bA = psr(), psr()
                P.mm(bW[:, :], LT[0:64, :], LW[0:64, d, :], r=[LT, LW], w=[bW])
                P.mm(bA[:, :], LT[64:128, :], LW[64:128, d, :], r=[LT, LW], w=[bA])
                P.tt('dve', lw32[:, :], bW[:, :], w0b[:, d, :], ALU.add, r=[bW, w0b], w=[lw32])
                P.act(lw32[:, :], lw32[:, :], AF.Sigmoid, r=[lw32], w=[lw32])
                P.ts('pool', lw32[:, :], lw32[:, :], -DECAY_C, None, ALU.mult, r=[lw32], w=[lw32])
                P.tt('dve', a32[:, :], bA[:, :], a0b[:, d, :], ALU.add, r=[bA, a0b], w=[a32])
                P.act(a32[:, :], a32[:, :], AF.Sigmoid, r=[a32], w=[a32])
                if d == 0:
                    bG = psr()
                    P.mm(bG[:, :], sg[:, :], GU[:, :], r=[sg, GU], w=[bG])
                    P.cp('act', g32[:, :], bG[:, :], r=[bG], w=[g32])
                    P.ld(C.vg[0, rows, :], v32[:, :], r=[v32], w=[vgb[0][c]])
                    P.ld(C.vg[1, rows, :], g32[:, :], r=[g32], w=[vgb[1][c]])
                yield
                P.tt('pool', kk[:, :], k32[:, :], kkb[:, :], ALU.mult, r=[k32, kkb], w=[kk])
                P.tt('pool', tmp[:, :], kk[:, :], kk[:, :], ALU.mult, r=[kk], w=[tmp])
                P.op('dve', lambda e, o=sm8[:, 0:8], i_=v3(tmp[:, :]): e.reduce_sum(o, i_, axis=AX.X), [tmp.buf], [sm8.buf])
                P.act(sm8[:, 8:16], sm8[:, 0:8], AF.Sqrt, r=[sm8], w=[sm8])
                P.ts('dve', sm8[:, 8:16], sm8[:, 8:16], 1e-12, None, ALU.max, r=[sm8], w=[sm8])
                P.op('dve', lambda e, o=sm8[:, 16:24], i_=sm8[:, 8:16]: e.reciprocal(o, i_), [sm8.buf], [sm8.buf])
                P.tt('dve', v3(kk[:, :]), v3(kk[:, :]), bc3(sm8[:, 16:24], 64, 2), ALU.mult, r=[kk, sm8], w=[kk])
                P.stt(tmp2[:, :], a32[:, :], -1.0, kab[:, :], ALU.add, ALU.mult, r=[a32, kab], w=[tmp2])
                P.stt(kd[:, :], tmp2[:, :], 1.0, k32[:, :], ALU.add, ALU.mult, r=[tmp2, k32], w=[kd])
                P.tt('pool', tmp[:, :], r32[:, :], kd[:, :], ALU.mult, r=[r32, kd], w=[tmp])
                P.tt('pool', tmp[:, :], tmp[:, :], rkb[:, :], ALU.mult, r=[tmp, rkb], w=[tmp])
                P.op('dve', lambda e, o=bon[:, c, :], i_=v3(tmp[:, :]): e.reduce_sum(o, i_, axis=AX.X), [tmp.buf], [bon.buf])
                P.tt('pool', bb[:, :], kk[:, :], a32[:, :], ALU.mult, r=[kk, a32], w=[bb])
                yield
                bC = psr()
                P.mm(bC[:, :], U if d == 0 else Lo, lw32[:, :], r=[cst, lw32], w=[bC])
                bT = psr()
                for jg in range(4):
                    P.mm(bT[:, 2 * jg:2 * jg + 2], lw32[:, jg * 128:(jg + 1) * 128], ones[:, 0:2], r=[lw32, cst], w=[bT])
                P.act(gC[:, :], bT[:, 0:8].rearrange("p (j two) -> p j two", two=2)[:, :, 0], AF.Exp, r=[bT], w=[gC])
                P.act(Ep[:, :], bC[:, :], AF.Exp, r=[bC], w=[Ep])
                P.act(En[:, :], bC[:, :], AF.Exp, scale=-1.0, r=[bC], w=[En])
                P.tt('dve', tmp2[:, :], bC[:, :], lw32[:, :], ALU.subtract, r=[bC, lw32], w=[tmp2])
                P.act(tmp2[:, :], tmp2[:, :], AF.Exp, r=[tmp2], w=[tmp2])
                P.tt('pool', r32[:, :], r32[:, :], Ep[:, :], ALU.mult, r=[r32, Ep], w=[r32])
                P.tt('dve', kk[:, :], kk[:, :], tmp2[:, :], ALU.mult, r=[kk, tmp2], w=[kk])
                P.tt('pool', bb[:, :], bb[:, :], En[:, :], ALU.mult, r=[bb, En], w=[bb])
                P.tt('dve', kd[:, :], kd[:, :], En[:, :], ALU.mult, r=[kd, En], w=[kd])
                P.cp('pool', bt_bf[:, :], bb[:, :], r=[bb], w=[bt_bf])
                P.cp('pool', kdt_bf[:, :], kd[:, :], r=[kd], w=[kdt_bf])
                P.cp('pool', kt_bf[:, :], kk[:, :], r=[kk], w=[kt_bf])
                yield
                for (src, dstap, dbuf) in ((kk, KR[:, :, 0, :], KR), (r32, KR[:, :, 1, :], KR), (bb, btT[:, :, :], btT), (kd, kdtT[:, :, :], kdtT)):
                    bank = psr()
                    for jg in range(4):
                        P.tr(bank[:, jg * 128:(jg + 1) * 128], src[:, jg * 128:(jg + 1) * 128], ident, r=[src, cst], w=[bank])
                    P.cp(evr(), dstap, bank[:, :].rearrange("p (a b) -> p a b", b=128), r=[bank], w=[dbuf])
                yield
                for h in range(8):
                    jg, rs = h // 2, slice((h % 2) * 64, (h % 2 + 1) * 64)
                    bM = psr()
                    krr = KR[rs, jg, :, :].rearrange("p a b -> p (a b)")
                    P.mm(bM[:, 0:256], btT[rs, jg, :], krr, r=[btT, KR], w=[bM])
                    P.mm(bM[:, 256:512], kdtT[rs, jg, :], krr, r=[kdtT, KR], w=[bM])
                    P.tt('dve', NM[0][:, h, 0, :], bM[:, 0:128], MK[:, d, 0:128], ALU.mult, r=[bM, MK], w=[NM[0]])
                    P.tt('dve', R3[:, h, :, :].rearrange("p a b -> p (a b)"), bM[:, 128:512], MK[:, d, 128:512], ALU.mult,
                         r=[bM, MK], w=[R3])
                    if h % 2 == 1:
                        yield
                yield
                for hh in range(2):
                    bank = psr()
                    rs = slice(hh * 64, (hh + 1) * 64)
                    for jg in range(4):
                        P.mm(bank[:, jg * 128:(jg + 1) * 128], KR[rs, jg, 0, :], btT[rs, jg, :], r=[KR, btT], w=[bank])
                    P.tt('dve', NM[0][:, hh:8:2, 1, :], bank[:, :].rearrange("p (a b) -> p a b", b=128),
                         bc3(MKa[:, d, :], 4, 1), ALU.mult, r=[bank, MKa], w=[NM[0]])
                P.tt('dve', X[0][:, :, :], NM[0][:, :, 0, :], bc3(ident, 8, 1), ALU.add, r=[NM[0], cst], w=[X[0]])
                yield
                for j in range(6):
                    yield
                    cur, nxt = NM[j % 2], NM[(j + 1) % 2]
                    Xc, Xn = X[j % 2], X[(j + 1) % 2]
                    for hp in range(4):
                        bank = psr()
                        for q in range(2):
                            h = hp * 2 + q
                            P.mm(bank[:, q * 256:q * 256 + 128], cur[:, h, 1, :], cur[:, h, 0, :], r=[cur], w=[bank])
                            P.mm(bank[:, q * 256 + 128:q * 256 + 256], cur[:, h, 0, :], cur[:, h, 1, :], r=[cur], w=[bank])
                        P.cp(evr(), nxt[:, hp * 2:hp * 2 + 2, :, :].rearrange("p a b c -> p (a b c)"), bank[:, :], r=[bank], w=[nxt])
                    yield
                    for hp in range(2):
                        bank = psr()
                        for q in range(4):
                            h = hp * 4 + q
                            P.mm(bank[:, q * 128:(q + 1) * 128], nxt[:, h, 1, :], Xc[:, h, :], r=[nxt, Xc], w=[bank])
                        P.tt('dve', Xn[:, hp * 4:(hp + 1) * 4, :].rearrange("p a b -> p (a b)"), bank[:, :],
                             Xc[:, hp * 4:(hp + 1) * 4, :].rearrange("p a b -> p (a b)"), ALU.add, r=[bank, Xc], w=[Xn])
                yield
                XF = X[0]
                bZ = psr()
                for h in range(8):
                    P.mm(bZ[:, h * 64:(h + 1) * 64], R3[:, h, 1, :], vbf[:, h * 64:(h + 1) * 64], r=[R3, vbf], w=[bZ])
                P.cp('act', Z1[:, :], bZ[:, :], r=[bZ], w=[Z1])
                bU = psr()
                for h in range(8):
                    P.mm(bU[:, h * 64:(h + 1) * 64], XF[:, h, :], Z1[:, h * 64:(h + 1) * 64], r=[XF, Z1], w=[bU])
                P.cp('act', Ut[:, :], bU[:, :], r=[bU], w=[Ut])
                for hp in range(2):
                    bank = psr()
                    for q in range(4):
                        h = hp * 4 + q
                        jg = h // 2
                        P.mm(bank[:, q * 128:(q + 1) * 128], kt_bf[:, jg * 128:(jg + 1) * 128], XF[:, h, :], r=[kt_bf, XF], w=[bank])
                    for q in range(4):
                        h = hp * 4 + q
                        jg, rs = h // 2, slice((h % 2) * 64, (h % 2 + 1) * 64)
                        P.cp(evr(), WT[rs, jg, :], bank[rs, q * 128:(q + 1) * 128], r=[bank], w=[WT])
                yield
                bP = psr()
                for jg in range(4):
                    P.mm(bP[:, jg * 128:(jg + 1) * 128], WT[:, jg, :], Mbf[:, jg, :], r=[WT, Mbf], w=[bP])
                P.stt(Un[:, :], bP[:, :], -1.0, Ut[:, :], ALU.mult, ALU.subtract, r=[bP, Ut], w=[Un])
                yield
                bY = psr()
                for jg in range(4):
                    P.mm(bY[:, jg * 128:(jg + 1) * 128], KR[:, jg, 1, :], Mbf[:, jg, :], start=True, stop=False, r=[KR, Mbf], w=[bY])
                    for h in (2 * jg, 2 * jg + 1):
                        hs = slice(h * 64, (h + 1) * 64)
                        P.mm(bY[:, hs], R3[:, h, 2, :], vbf[:, hs], start=False, stop=False, r=[R3, vbf], w=[bY])
                    for h in (2 * jg, 2 * jg + 1):
                        hs = slice(h * 64, (h + 1) * 64)
                        P.mm(bY[:, hs], R3[:, h, 0, :], Un[:, hs], start=False, stop=(h == 2 * jg + 1), r=[R3, Un], w=[bY])
                yo = Yo[ci % 2]
                P.cp('act', yo[:, :], bY[:, :], r=[bY], w=[yo])
                P.ld(C.ydir[d, rows, :], yo[:, :], r=[yo], w=[ydb[d][c]])
                yield
                bS = psr()
                for jg in range(4):
                    js = slice(jg * 128, (jg + 1) * 128)
                    P.mm(bS[:, js], bt_bf[:, js], Un[:, js], start=True, stop=False, r=[bt_bf, Un], w=[bS])
                    P.mm(bS[:, js], kdt_bf[:, js], vbf[:, js], start=False, stop=True, r=[kdt_bf, vbf], w=[bS])
                for hh in range(2):
                    rs = slice(hh * 64, (hh + 1) * 64)
                    src = bS[rs, :].rearrange("p (j q) -> p j q", q=128)[:, :, hh * 64:(hh + 1) * 64]
                    mv = Mst[rs, :, hh * 64:(hh + 1) * 64]
                    P.tt('dve', mv, mv, src, ALU.add, r=[Mst, bS], w=[Mst])
                    P.tt('dve', mv, mv, gC[rs, :].unsqueeze(2).broadcast_to([64, 4, 64]), ALU.mult, r=[Mst, gC], w=[Mst])
                P.cp('pool', Mbf[:, :, :], Mst[:, :, :], r=[Mst], w=[Mbf])
                yield
        alive = [inst(0), inst(1)]
        while alive:
            for g_ in list(alive):
                try:
                    next(g_)
                except StopIteration:
                    alive.remove(g_)
        P.emit()
        C.stats.append(('C', P.stats))
        fy = [_Quad([IT[i][n] for n in ('r32', 'k32', 'v32', 'lw32')]) for i in range(2)]
        lgb, lbb = IT[1]['a32'], IT[1]['kk']
        sm8, tmp = IT[0]['sm8'], IT[0]['tmp']
        bon0, bon1 = IT[0]['bon'], IT[1]['bon']
        P = Prog(nc, C.sems)
        load_bcast(P, lgb, C.w['lnx_g'][l], 512)
        load_bcast(P, lbb, C.w['lnx_b'][l], 512)
        for c in range(NCH):
            rows = slice(c * 128, (c + 1) * 128)
            f = fy[c % 2]
            P.ld(f[:, 0, :], C.ydir[0, rows, :], r=[ydb[0][c]], w=[f])
            P.ld(f[:, 1, :], C.ydir[1, rows, :], r=[ydb[1][c]], w=[f])
            P.ld(f[:, 2, :], C.vg[0, rows, :], r=[vgb[0][c]], w=[f])
            P.ld(f[:, 3, :], C.vg[1, rows, :], r=[vgb[1][c]], w=[f])
            y = f[:, 0, :]
            P.tt('dve', y, y, f[:, 1, :], ALU.add, r=[f], w=[f])
            P.op('dve', lambda e, o=sm8[:, 0:8], i_=v3(y): e.reduce_sum(o, i_, axis=AX.X), [f.buf], [sm8.buf])
            P.tt('pool', tmp[:, :], y, y, ALU.mult, r=[f], w=[tmp])
            P.op('dve', lambda e, o=sm8[:, 8:16], i_=v3(tmp[:, :]): e.reduce_sum(o, i_, axis=AX.X), [tmp.buf], [sm8.buf])
            P.ts('dve', sm8[:, 0:16], sm8[:, 0:16], 1.0 / 64.0, None, ALU.mult, r=[sm8], w=[sm8])
            P.tt('dve', sm8[:, 16:24], sm8[:, 0:8], sm8[:, 0:8], ALU.mult, r=[sm8], w=[sm8])
            P.tt('dve', sm8[:, 16:24], sm8[:, 8:16], sm8[:, 16:24], ALU.subtract, r=[sm8], w=[sm8])
            P.ts('dve', sm8[:, 16:24], sm8[:, 16:24], GN_EPS, None, ALU.add, r=[sm8], w=[sm8])
            P.act(sm8[:, 16:24], sm8[:, 16:24], AF.Sqrt, r=[sm8], w=[sm8])
            P.op('dve', lambda e, o=sm8[:, 24:32], i_=sm8[:, 16:24]: e.reciprocal(o, i_), [sm8.buf], [sm8.buf])
            P.tt('dve', v3(y), v3(y), bc3(sm8[:, 0:8], 64, 2), ALU.subtract, r=[f, sm8], w=[f])
            P.tt('dve', v3(y), v3(y), bc3(sm8[:, 24:32], 64, 2), ALU.mult, r=[f, sm8], w=[f])
            P.tt('pool', y, y, lgb[:, :], ALU.mult, r=[f, lgb], w=[f])
            P.tt('pool', y, y, lbb[:, :], ALU.add, r=[f, lbb], w=[f])
            P.tt('dve', sm8[:, 0:8], bon0[:, c, :], bon1[:, c, :], ALU.add, r=[bon0, bon1, sm8], w=[sm8])
            P.tt('dve', v3(tmp[:, :]), v3(f[:, 2, :]), bc3(sm8[:, 0:8], 64, 2), ALU.mult, r=[f, sm8], w=[tmp])
            P.tt('pool', y, y, tmp[:, :], ALU.add, r=[f, tmp], w=[f])
            P.tt('pool', y, y, f[:, 3, :], ALU.mult, r=[f], w=[f])
            P.ld(C.ymix[b, rows, 512:1024], y, r=[f])
        P.emit()
        C.stats.append(('C', P.stats))


def build(T=2048, NS=2, depth=2, debug=False, stages='ABCD'):
    nc = bass.Bass("TRN2", target_bir_lowering=False)
    C = Ctx()
    C.nc, C.T, C.NS, C.NCH = nc, T, NS, T // 128
    C.stats = []
    C.x = nc.dram_tensor("x", [NS, T, D], F32, kind="ExternalInput").ap()
    C.w = {n: nc.dram_tensor(n, W_SHAPES[n], F32, kind="ExternalInput").ap() for n in W_NAMES}
    C.cst_d = nc.dram_tensor("cst", [128, 768], F32, kind="ExternalInput").ap()
    C.out = nc.dram_tensor("out", [NS, T, D], F32, kind="ExternalOutput").ap()
    kind = "ExternalOutput" if debug else "Internal"
    C.hbuf = nc.dram_tensor("hbuf", [NS, T, D], F32, kind=kind).ap()
    C.h1buf = nc.dram_tensor("h1buf", [NS, T, D], F32, kind=kind).ap()
    C.zbuf = nc.dram_tensor("zbuf", [NS, T, 512], F32, kind=kind).ap()
    C.dtbuf = nc.dram_tensor("dtbuf", [NS, T, 16], F32, kind=kind).ap()
    C.xbcT = nc.dram_tensor("xbcT", [NS, 1024, T], BF16, kind=kind).ap()
    C.rwT = nc.dram_tensor("rwT", [NS, 1792, T], BF16, kind=kind).ap()
    C.ymix = nc.dram_tensor("ymix", [NS, T, D], F32, kind=kind).ap()
    C.ydir = nc.dram_tensor("ydir", [2, T, 512], F32, kind=kind).ap()
    C.vg = nc.dram_tensor("vg", [3, T, 512], F32, kind=kind).ap()
    with nc.sbuf_tensor('cst_sb', [128, 768], F32) as cst_sb, contextlib.ExitStack() as semstack:
        C.cst = Tile(cst_sb, 'cst')
        C.sems = SemState(nc, 8, semstack)
        stage_consts(C)
        for l in range(depth):
            if 'A' in stages:
                stage_A(C, l)
            for b in range(NS):
                if 'B' in stages:
                    stage_B(C, l, b)
                if 'C' in stages:
                    stage_C(C, l, b)
            if 'D' in stages:
                stage_D1(C, l)
                stage_D2(C, l, last=(l == depth - 1))
    return nc, C


_CACHE = {}


def kernel(**inputs):
    n_cores = 8
    x = np.ascontiguousarray(np.asarray(inputs["x"], dtype=np.float32))
    B, T, _ = x.shape
    NS = B // n_cores
    key = (T, NS)
    if key not in _CACHE:
        _CACHE[key] = build(T=T, NS=NS, depth=2, debug=False)[0]
    nc = _CACHE[key]
    cst = make_consts()
    ws = {n: np.ascontiguousarray(np.asarray(inputs[n], dtype=np.float32)) for n in W_NAMES}
    in_maps = []
    for i in range(n_cores):
        m = dict(ws)
        m["x"] = np.ascontiguousarray(x[i * NS:(i + 1) * NS])
        m["cst"] = cst
        in_maps.append(m)
    res = run_bass_kernel_spmd(nc, in_maps, core_ids=list(range(n_cores)))
    return np.concatenate([np.asarray(r["out"], dtype=np.float32) for r in res.results], axis=0)
```

```python
import contextlib
import numpy as np
import concourse.bass as bass
import concourse.mybir as mybir
from concourse.bass_utils import run_bass_kernel_spmd

F32 = mybir.dt.float32
BF16 = mybir.dt.bfloat16
AF = mybir.ActivationFunctionType
ALU = mybir.AluOpType
AX = mybir.AxisListType

ENGS = ('pe', 'act', 'dve', 'pool', 'sp')


class Buf:
    __slots__ = ('name', 'last_w', 'readers')

    def __init__(self, name=''):
        self.name = name
        self.last_w = None
        self.readers = []


class _Op:
    __slots__ = ('eng', 'idx', 'fn', 'deps', 'is_dma', 'dslot', 'dval', 'needs_inc', 'cnt', 'waits', 'prog')

    def __init__(self, eng, idx, fn, is_dma):
        self.eng = eng
        self.idx = idx
        self.fn = fn
        self.is_dma = is_dma
        self.deps = []
        self.dslot = 0
        self.dval = 0
        self.needs_inc = False
        self.cnt = 0
        self.waits = []
        self.prog = None


class SemState:
    def __init__(self, nc, ring=8, stack=None):
        self.stack = stack if stack is not None else contextlib.ExitStack()
        st = self.stack
        self.csem = {e: st.enter_context(nc.semaphore('c_' + e)) for e in ENGS if e != 'sp'}
        self.dsem = {e: [st.enter_context(nc.semaphore('d_%s_%d' % (e, i))) for i in range(ring)] for e in ('sp', 'pool', 'act')}
        self.c_off = {e: 0 for e in ENGS}
        self.d_cnt = {e: 0 for e in ENGS}


class Prog:
    def __init__(self, nc, sems=None, ring=8):
        self.nc = nc
        self.q = {e: [] for e in ENGS}
        self.ring = ring
        self.dma_hist = {e: [] for e in ENGS}
        self.sems = sems if sems is not None else SemState(nc, ring)

    def _add(self, eng, fn, reads, writes, is_dma):
        op = _Op(eng, len(self.q[eng]), fn, is_dma)
        op.prog = self
        deps = []
        for b in reads:
            if b.last_w is not None:
                deps.append(b.last_w)
        for b in writes:
            if b.last_w is not None:
                deps.append(b.last_w)
            deps.extend(b.readers)
        if is_dma:
            h = self.dma_hist[eng]
            k = len(h)
            kg = k + self.sems.d_cnt[eng]
            op.dslot = kg % self.ring
            op.dval = 16 * (kg // self.ring + 1)
            if k >= self.ring:
                deps.append(h[k - self.ring])
            h.append(op)
        seen = set()
        for d in deps:
            if d is not op and id(d) not in seen and getattr(d, 'prog', self) is self:
                seen.add(id(d))
                op.deps.append(d)
        for b in writes:
            b.last_w = op
            b.readers = []
        for b in reads:
            if b.last_w is not op:
                b.readers.append(op)
        self.q[eng].append(op)
        return op

    def op(self, eng, fn, reads=(), writes=()):
        return self._add(eng, fn, reads, writes, False)

    def dma(self, eng, fn, reads=(), writes=()):
        return self._add(eng, fn, reads, writes, True)


    @staticmethod
    def _bufs(lst):
        return [b.buf if hasattr(b, 'buf') else b for b in lst]

    def mm(self, out, lhsT, rhs, start=True, stop=True, r=(), w=()):
        return self.op('pe', lambda e: e.matmul(out, lhsT, rhs, start=start, stop=stop), self._bufs(r), self._bufs(w))

    def tr(self, out, in_, ident, r=(), w=()):
        return self.op('pe', lambda e: e.transpose(out, in_, ident), self._bufs(r), self._bufs(w))

    def act(self, out, in_, func, bias=None, scale=None, accum=None, r=(), w=()):
        kw = {}
        if bias is not None:
            kw['bias'] = bias
        if scale is not None:
            kw['scale'] = scale
        if accum is not None:
            kw['accum_out'] = accum
        return self.op('act', lambda e: e.activation(out, in_, func, **kw), self._bufs(r), self._bufs(w))

    def tt(self, eng, out, in0, in1, op, r=(), w=()):
        return self.op(eng, lambda e: e.tensor_tensor(out, in0, in1, op), self._bufs(r), self._bufs(w))

    def ts(self, eng, out, in0, s1, s2=None, op0=None, op1=None, r=(), w=()):
        if op1 is None:
            return self.op(eng, lambda e: e.tensor_scalar(out, in0, s1, None, op0), self._bufs(r), self._bufs(w))
        return self.op(eng, lambda e: e.tensor_scalar(out, in0, s1, s2, op0, op1), self._bufs(r), self._bufs(w))

    def stt(self, out, in0, scalar, in1, op0, op1, r=(), w=()):
        return self.op('dve', lambda e: e.scalar_tensor_tensor(out, in0, scalar, in1, op0, op1), self._bufs(r), self._bufs(w))

    def cp(self, eng, out, in_, r=(), w=()):
        if eng == 'act':
            return self.op('act', lambda e: e.copy(out, in_), self._bufs(r), self._bufs(w))
        return self.op(eng, lambda e: e.tensor_copy(out, in_), self._bufs(r), self._bufs(w))

    def ld(self, out, in_, r=(), w=(), eng='sp', **kw):
        return self.dma(eng, lambda e: e.dma_start(out=out, in_=in_, **kw), self._bufs(r), self._bufs(w))

    def _last_ops(self, skip=None):
        out = []
        for e in ENGS:
            if e != skip and self.q[e]:
                for o in reversed(self.q[e]):
                    if not o.is_dma and o.fn is not None:
                        out.append(o)
                        break
            h = self.dma_hist[e]
            out.extend(h[-self.ring:])
        return out

    def barrier(self):
        deps_for = {e: self._last_ops(skip=e) for e in ENGS}
        for e in ENGS:
            op = _Op(e, len(self.q[e]), None, False)
            op.prog = self
            op.deps = deps_for[e]
            self.q[e].append(op)

    def finish(self):
        op = _Op('sp', len(self.q['sp']), None, False)
        op.prog = self
        for e in ENGS:
            op.deps.extend(self.dma_hist[e][-self.ring:])
        self.q['sp'].append(op)

    def emit(self):
        nc = self.nc
        self.barrier()
        self.finish()
        for e in ENGS:
            seen = {p: -1 for p in ENGS}
            seen_dma = {}
            for o in self.q[e]:
                best = {}
                for d in o.deps:
                    if d.is_dma:
                        key = (d.eng, d.dslot)
                        if seen_dma.get(key, 0) >= d.dval:
                            continue
                        seen_dma[key] = d.dval
                        o.waits.append(d)
                    else:
                        if d.eng == 'pe' and e == 'pe':
                            continue
                        if d.fn is None:
                            continue
                        if d.idx <= seen[d.eng]:
                            continue
                        if d.eng not in best or best[d.eng].idx < d.idx:
                            best[d.eng] = d
                for p, d in best.items():
                    seen[p] = d.idx
                    d.needs_inc = True
                    o.waits.append(d)
        for e in ENGS:
            c = self.sems.c_off[e]
            for o in self.q[e]:
                if o.needs_inc:
                    c += 1
                o.cnt = c
            self.sems.c_off[e] = c
            self.sems.d_cnt[e] += len(self.dma_hist[e])
        n_wait = sum(len(o.waits) for e in ENGS for o in self.q[e])
        n_ops = sum(len(self.q[e]) for e in ENGS)
        self.stats = dict(n_ops=n_ops, n_wait=n_wait, per_eng={e: len(self.q[e]) for e in ENGS})
        with contextlib.ExitStack() as st:
            csem = self.sems.csem
            dsem = self.sems.dsem
            block = st.enter_context(nc.Block())

            def run(ename, eng):
                for o in self.q[ename]:
                    for d in o.waits:
                        if d.is_dma:
                            eng.wait_ge(dsem[d.eng][d.dslot], d.dval)
                        else:
                            eng.wait_ge(csem[d.eng], d.cnt)
                    if o.fn is None:
                        continue
                    ins = o.fn(eng)
                    if o.is_dma:
                        ins.then_inc(dsem[ename][o.dslot], 16)
                    elif o.needs_inc:
                        ins.then_inc(csem[ename], 1)

            @block.tensor
            def _(eng):
                run('pe', eng)

            @block.scalar
            def _(eng):
                run('act', eng)

            @block.vector
            def _(eng):
                run('dve', eng)

            @block.gpsimd
            def _(eng):
                run('pool', eng)

            @block.sync
            def _(eng):
                run('sp', eng)


D = 1024
IN_COLS = 3344
ALPHA = float(4 ** 0.25)
LN_EPS = 1e-5
RMS_EPS = 1e-5
GN_EPS = 64e-5
DECAY_C = float(np.exp(-0.5))
NEU_DT = BF16

W_NAMES = ["ln0_g", "ln0_b", "w_in", "conv_w", "conv_b", "dt_bias", "a_log", "d_skip", "ssd_norm_g",
           "mu_rwkv", "w0", "w_up", "a0", "a_up", "g_up", "k_k", "k_a", "r_k", "lnx_g", "lnx_b", "w_out",
           "ln1_g", "ln1_b", "w_fc", "w_proj", "ln2_g", "ln2_b"]
W_SHAPES = {
    "ln0_g": [1024], "ln0_b": [1024], "w_in": [2, 1024, 3344], "conv_w": [2, 5, 1024], "conv_b": [2, 1024],
    "dt_bias": [2, 2, 8], "a_log": [2, 2, 8], "d_skip": [2, 8], "ssd_norm_g": [2, 512], "mu_rwkv": [2, 1792],
    "w0": [2, 2, 512], "w_up": [2, 2, 64, 512], "a0": [2, 2, 512], "a_up": [2, 2, 64, 512], "g_up": [2, 128, 512],
    "k_k": [2, 512], "k_a": [2, 512], "r_k": [2, 8, 64], "lnx_g": [2, 512], "lnx_b": [2, 512],
    "w_out": [2, 1024, 1024], "ln1_g": [2, 1024], "ln1_b": [2, 1024], "w_fc": [2, 1024, 4096],
    "w_proj": [2, 4096, 1024], "ln2_g": [2, 1024], "ln2_b": [2, 1024],
}


def make_consts():
    i = np.arange(128)
    ident = np.eye(128, dtype=np.float32)
    U = (i[:, None] <= i[None, :]).astype(np.float32)
    Lo = (i[:, None] >= i[None, :]).astype(np.float32)
    Us = (i[:, None] < i[None, :]).astype(np.float32)
    Ls = (i[:, None] > i[None, :]).astype(np.float32)
    ones = np.ones((128, 128), np.float32)
    return np.ascontiguousarray(np.concatenate([ident, U, Lo, Us, Ls, ones], axis=1))


class Tile:
    def __init__(self, t, name=''):
        self.t = t
        self.buf = Buf(name)

    def __getitem__(self, k):
        return self.t[k]


class Ctx:
    pass


class _Quad:
    def __init__(self, tiles):
        self.tiles = tiles
        self.buf = Buf('quad')

    def __getitem__(self, k):
        p, i, c = k
        return self.tiles[i][p, c]


def layer_norm_tile(P, C, src, dst, g_t, b_t, stat, eps, r, w, geng='pool'):
    st = stat
    P.op('dve', lambda e: e.bn_stats(st[:, 0:6], src[:, 0:512]), P._bufs(r), [st.buf])
    P.op('dve', lambda e: e.bn_stats(st[:, 6:12], src[:, 512:1024]), P._bufs(r), [st.buf])
    P.op('dve', lambda e: e.bn_aggr(st[:, 12:14], st[:, 0:12].rearrange("p (a b) -> p a b", b=6)), [st.buf], [st.buf])
    P.ts('dve', st[:, 14:15], st[:, 13:14], eps, None, ALU.add, r=[st], w=[st])
    P.act(st[:, 15:16], st[:, 14:15], AF.Sqrt, r=[st], w=[st])
    P.op('dve', lambda e: e.reciprocal(st[:, 16:17], st[:, 15:16]), [st.buf], [st.buf])
    P.stt(st[:, 17:18], st[:, 12:13], -1.0, st[:, 16:17], ALU.mult, ALU.mult, r=[st], w=[st])
    P.act(dst, src, AF.Identity, bias=st[:, 17:18], scale=st[:, 16:17], r=list(r) + [st], w=w)
    P.tt(geng, dst, dst, g_t[:, :], ALU.mult, r=list(w) + [g_t], w=w)
    P.tt('dve', dst, dst, b_t[:, :], ALU.add, r=list(w) + [b_t], w=w)


_UID = [0]


_SBUSE = [0]
SB_BUDGET = 176 * 1024


def _sb_release(n):
    _SBUSE[0] -= n


def _sb(st, nc, name, shape, dt):
    _UID[0] += 1
    name = '%s_%d' % (name, _UID[0])
    n = int(np.prod(shape[1:])) * (2 if dt == BF16 else 4)
    n = (n + 31) // 32 * 32
    _SBUSE[0] += n
    assert _SBUSE[0] <= SB_BUDGET, ('SBUF budget exceeded', name, _SBUSE[0])
    t = Tile(st.enter_context(nc.sbuf_tensor(name, shape, dt)), name)
    st.callback(_sb_release, n)
    return t


def _psum(st, nc, n=8):
    _UID[0] += 1
    return [Tile(st.enter_context(nc.psum_tensor('ps%d_%d' % (i, _UID[0]), [128, 512], F32)), 'ps%d' % i) for i in range(n)]


class RR:
    def __init__(self, items):
        self.items = items
        self.i = 0

    def __call__(self):
        x = self.items[self.i % len(self.items)]
        self.i += 1
        return x


def load_bcast(P, tile, src_row, n, eng='sp'):
    P.ld(tile[:, 0:n], src_row.partition_broadcast(128), w=[tile], eng=eng)


def stage_consts(C):
    P = Prog(C.nc, C.sems)
    P.ld(C.cst[:, :], C.cst_d[:, :], w=[C.cst])
    P.emit()


def stage_A(C, l):
    nc, T, NS = C.nc, C.T, C.NS
    TB = min(512, T)
    NJ = TB // 128
    ident = C.cst[:, 0:128]
    with contextlib.ExitStack() as st:
        Win = _sb(st, nc, 'Win', [128, 8, IN_COLS], BF16)
        Wb = [Buf('Win%d' % k) for k in range(8)]
        hin = [_sb(st, nc, 'hin%d' % i, [128, D], F32) for i in range(2)]
        hT = [_sb(st, nc, 'hT%d' % i, [128, 8, TB], BF16) for i in range(2)]
        stat = [_sb(st, nc, 'stat%d' % i, [128, 32], F32) for i in range(2)]
        zo = [_sb(st, nc, 'zo%d' % i, [128, 512], F32) for i in range(2)]
        dto = [_sb(st, nc, 'dto%d' % i, [128, 16], F32) for i in range(2)]
        fo = [_sb(st, nc, 'fo%d' % i, [128, TB], BF16) for i in range(3)]
        if l == 0:
            g0 = _sb(st, nc, 'g0', [128, D], F32)
            b0 = _sb(st, nc, 'b0', [128, D], F32)
        ps = _psum(st, nc)
        P = Prog(nc, C.sems)
        for kc in range(8):
            P.ld(Win[:, kc, :], C.w['w_in'][l, kc * 128:(kc + 1) * 128, :], w=[Wb[kc]], eng='pool', max_dma_last_dim=4096)
        if l == 0:
            load_bcast(P, g0, C.w['ln0_g'], D)
            load_bcast(P, b0, C.w['ln0_b'], D)
        psr = RR(ps)
        evr = RR(['act', 'dve'])
        zor, dtor, forr = RR(zo), RR(dto), RR(fo)
        ti = 0
        for b in range(NS):
            src = C.x[b] if l == 0 else C.hbuf[b]
            for tb in range(T // TB):
                hTt = hT[(b * (T // TB) + tb) % 2]
                for j in range(NJ):
                    rows = slice(tb * TB + j * 128, tb * TB + (j + 1) * 128)
                    hi = hin[ti % 2]
                    P.ld(hi[:, :], src[rows, :], w=[hi])
                    if l == 0:
                        layer_norm_tile(P, C, hi[:, :], hi[:, :], g0, b0, stat[ti % 2], LN_EPS, r=[hi], w=[hi])
                        P.ld(C.hbuf[b, rows, :], hi[:, :], r=[hi])
                    for half in range(2):
                        bank = psr()
                        for q in range(4):
                            kc = half * 4 + q
                            P.tr(bank[:, q * 128:(q + 1) * 128], hi[:, kc * 128:(kc + 1) * 128], ident, r=[hi, C.cst], w=[bank])
                        P.cp(evr(), hTt[:, half * 4:half * 4 + 4, j * 128:(j + 1) * 128],
                             bank[:, :].rearrange("p (a b) -> p a b", b=128), r=[bank], w=[hTt])
                    ti += 1
                for j in range(NJ):
                    rows = slice(tb * TB + j * 128, tb * TB + (j + 1) * 128)
                    bank = psr()
                    for kc in range(8):
                        P.mm(bank[:, :], hTt[:, kc, j * 128:(j + 1) * 128], Win[:, kc, 0:512], start=(kc == 0), stop=(kc == 7),
                             r=[hTt, Wb[kc]], w=[bank])
                    z = zor()
                    P.cp(evr(), z[:, :], bank[:, :], r=[bank], w=[z])
                    P.ld(C.zbuf[b, rows, :], z[:, :], r=[z])
                    bank = psr()
                    for kc in range(8):
                        P.mm(bank[:, 0:16], hTt[:, kc, j * 128:(j + 1) * 128], Win[:, kc, 1536:1552], start=(kc == 0), stop=(kc == 7),
                             r=[hTt, Wb[kc]], w=[bank])
                    dt = dtor()
                    P.cp(evr(), dt[:, :], bank[:, 0:16], r=[bank], w=[dt])
                    P.ld(C.dtbuf[b, rows, :], dt[:, :], r=[dt])
                for cc in range(22):
                    col0 = 512 + cc * 128 if cc < 8 else 1552 + (cc - 8) * 128
                    bank = psr()
                    for kc in range(8):
                        P.mm(bank[:, 0:TB], Win[:, kc, col0:col0 + 128], hTt[:, kc, :], start=(kc == 0), stop=(kc == 7),
                             r=[hTt, Wb[kc]], w=[bank])
                    f = forr()
                    P.cp(evr(), f[:, :], bank[:, 0:TB], r=[bank], w=[f])
                    if cc < 8:
                        dst = C.xbcT[b, cc * 128:(cc + 1) * 128, tb * TB:(tb + 1) * TB]
                    else:
                        dst = C.rwT[b, (cc - 8) * 128:(cc - 7) * 128, tb * TB:(tb + 1) * TB]
                    P.ld(dst, f[:, :], r=[f])
        P.emit()
        C.stats.append(('A', P.stats))


def load_w_bf16(P, tile, bufs, src, nk, eng='pool'):
    for kc in range(nk):
        P.ld(tile[:, kc, :], src[kc * 128:(kc + 1) * 128, :], w=[bufs[kc]], eng=eng, max_dma_last_dim=4096)


def stage_D1(C, l, pre):
    nc, T, NS = C.nc, C.T, C.NS
    ident = C.cst[:, 0:128]
    with contextlib.ExitStack() as st:
        Wo = _sb(st, nc, 'Wo', [128, 8, D], BF16)
        Wob = [Buf('Wo%d' % k) for k in range(8)]
        g1 = _sb(st, nc, 'g1', [128, D], F32)
        b1 = _sb(st, nc, 'b1', [128, D], F32)
        ym = [_sb(st, nc, 'ym%d' % i, [128, D], F32) for i in range(2)]
        yT = [_sb(st, nc, 'yT%d' % i, [128, 8, 128], BF16) for i in range(2)]
        hr = [_sb(st, nc, 'hr%d' % i, [128, D], F32) for i in range(1)]
        t1 = [_sb(st, nc, 't1%d' % i, [128, D], F32) for i in range(1)]
        stat = [_sb(st, nc, 'stat%d' % i, [128, 32], F32) for i in range(2)]
        ps = _psum(st, nc)
        P = Prog(nc, C.sems)
        load_w_bf16(P, Wo, Wob, C.w['w_out'][l], 8)
        load_bcast(P, g1, C.w['ln1_g'][l], D)
        load_bcast(P, b1, C.w['ln1_b'][l], D)
        Wf, Wfb, Wp, Wpb = pre
        load_w_bf16(P, Wf, Wfb, C.w['w_fc'][l], 8)
        load_w_bf16(P, Wp, Wpb, C.w['w_proj'][l], 32)
        psr = RR(ps)
        evr = RR(['act', 'dve'])
        ti = 0
        for b in range(NS):
            for c in range(T // 128):
                rows = slice(c * 128, (c + 1) * 128)
                y, yt, h, t, sx = ym[ti % 2], yT[ti % 2], hr[0], t1[0], stat[ti % 2]
                P.ld(y[:, :], C.ymix[b, rows, :], w=[y])
                P.ld(h[:, :], C.hbuf[b, rows, :], w=[h])
                for half in range(2):
                    bank = psr()
                    for q in range(4):
                        kc = half * 4 + q
                        P.tr(bank[:, q * 128:(q + 1) * 128], y[:, kc * 128:(kc + 1) * 128], ident, r=[y, C.cst], w=[bank])
                    P.cp(evr(), yt[:, half * 4:half * 4 + 4, :], bank[:, :].rearrange("p (a b) -> p a b", b=128), r=[bank], w=[yt])
                for half in range(2):
                    bank = psr()
                    for kc in range(8):
                        P.mm(bank[:, :], yt[:, kc, :], Wo[:, kc, half * 512:(half + 1) * 512], start=(kc == 0), stop=(kc == 7),
                             r=[yt, Wob[kc]], w=[bank])
                    P.stt(t[:, half * 512:(half + 1) * 512], h[:, half * 512:(half + 1) * 512], ALPHA, bank[:, :], ALU.mult, ALU.add,
                          r=[h, bank], w=[t])
                layer_norm_tile(P, C, t[:, :], t[:, :], g1, b1, sx, LN_EPS, r=[t], w=[t], geng='dve')
                P.ld(C.h1buf[b, rows, :], t[:, :], r=[t])
                ti += 1
        P.emit()
        C.stats.append(('D1', P.stats))


def stage_D2(C, l, last, pre):
    nc, T, NS = C.nc, C.T, C.NS
    ident = C.cst[:, 0:128]
    TB = 256
    with contextlib.ExitStack() as st:
        Wf, Wfb, Wp, Wpb = pre
        g2 = _sb(st, nc, 'g2', [128, D], F32)
        b2 = _sb(st, nc, 'b2', [128, D], F32)
        h1 = [_sb(st, nc, 'h1%d' % i, [128, D], F32) for i in range(2)]
        h1T = _sb(st, nc, 'h1T', [128, 8, TB], BF16)
        aT = _sb(st, nc, 'aT', [128, 32, TB], BF16)
        tmp = [_sb(st, nc, 'tmp%d' % i, [128, TB], F32) for i in range(2)]
        t2 = [_sb(st, nc, 't2%d' % i, [128, D], F32) for i in range(1)]
        stat = [_sb(st, nc, 'stat%d' % i, [128, 32], F32) for i in range(2)]
        ps = _psum(st, nc)
        P = Prog(nc, C.sems)
        load_bcast(P, g2, C.w['ln2_g'][l], D)
        load_bcast(P, b2, C.w['ln2_b'][l], D)
        psr = RR(ps)
        evr = RR(['act', 'dve'])
        tmr = RR(tmp)
        ti = 0
        for b in range(NS):
            dstb = C.out[b] if last else C.hbuf[b]
            for tb in range(T // TB):
                for j in range(2):
                    rows = slice(tb * TB + j * 128, tb * TB + (j + 1) * 128)
                    h = h1[j]
                    P.ld(h[:, :], C.h1buf[b, rows, :], w=[h])
                    for half in range(2):
                        bank = psr()
                        for q in range(4):
                            kc = half * 4 + q
                            P.tr(bank[:, q * 128:(q + 1) * 128], h[:, kc * 128:(kc + 1) * 128], ident, r=[h, C.cst], w=[bank])
                        P.cp(evr(), h1T[:, half * 4:half * 4 + 4, j * 128:(j + 1) * 128],
                             bank[:, :].rearrange("p (a b) -> p a b", b=128), r=[bank], w=[h1T])
                for fc in range(32):
                    bank = psr()
                    for kc in range(8):
                        P.mm(bank[:, 0:TB], Wf[:, kc, fc * 128:(fc + 1) * 128], h1T[:, kc, :], start=(kc == 0), stop=(kc == 7),
                             r=[h1T, Wfb[kc]], w=[bank])
                    tm = tmr()
                    if fc % 2 == 0:
                        P.act(tm[:, :], bank[:, 0:TB], AF.Relu, r=[bank], w=[tm])
                    else:
                        P.ts('dve', tm[:, :], bank[:, 0:TB], 0.0, None, ALU.max, r=[bank], w=[tm])
                    P.tt('pool', aT[:, fc, :], tm[:, :], tm[:, :], ALU.mult, r=[tm], w=[aT])
                for j in range(2):
                    rows = slice(tb * TB + j * 128, tb * TB + (j + 1) * 128)
                    h = h1[j]
                    t = t2[0]
                    for half in range(2):
                        bank = psr()
                        for fc in range(32):
                            P.mm(bank[:, :], aT[:, fc, j * 128:(j + 1) * 128], Wp[:, fc, half * 512:(half + 1) * 512],
                                 start=(fc == 0), stop=(fc == 31), r=[aT, Wpb[fc]], w=[bank])
                        P.stt(t[:, half * 512:(half + 1) * 512], h[:, half * 512:(half + 1) * 512], ALPHA, bank[:, :], ALU.mult, ALU.add,
                              r=[h, bank], w=[t])
                    layer_norm_tile(P, C, t[:, :], t[:, :], g2, b2, stat[ti % 2], LN_EPS, r=[t], w=[t])
                    P.ld(dstb[rows, :], t[:, :], r=[t])
                    ti += 1
        P.emit()
        C.stats.append(('D2', P.stats))


def bc3(ap2, n, axis):
    k = ap2.shape[1]
    if axis == 1:
        return ap2.unsqueeze(1).broadcast_to([128, n, k])
    return ap2.unsqueeze(2).broadcast_to([128, k, n])


def stage_B(C, l, b):
    nc, T, NCH = C.nc, C.T, C.NCH
    TB = min(512, T)
    cst = C.cst
    ident, U, Lo, Us, Ls, ones = (cst[:, i * 128:(i + 1) * 128] for i in range(6))
    with contextlib.ExitStack() as st:
        sb = lambda n, sh, dt: _sb(st, nc, n, sh, dt)
        XTb = [Buf('XT%d' % g) for g in range(8)]
        BT = sb('BT', [128, 2, T], BF16)
        CT = sb('CT', [128, 2, T], BF16)
        xbf = sb('xbf', [128, NCH, 512], BF16)
        Btok = sb('Btok', [128, NCH, 256], BF16)
        dtraw = sb('dtraw', [128, NCH, 16], F32)
        dtv = sb('dtv', [128, NCH, 16], F32)
        av = sb('av', [128, NCH, 16], F32)
        dtb = sb('dtb', [128, 16], F32)
        negA = sb('negA', [128, 16], F32)
        dsk8 = sb('dsk8', [128, 8], F32)
        dsk = sb('dsk', [128, 512], F32)
        ng = sb('ng', [128, 512], F32)
        ps = _psum(st, nc)
        st0 = contextlib.ExitStack()
        sb0 = lambda n, sh, dt: _sb(st0, nc, n, sh, dt)
        XT = sb0('XT', [128, 8, T + 4], BF16)
        cw6 = sb0('cw6', [6, 1024], F32)
        cwb = sb0('cwb', [128, 8, 6], F32)
        Dg = sb0('Dg', [128, 5, 8, 128], BF16)
        cbrow = sb0('cbrow', [1, 768], F32)
        cbrow_bf = sb0('cbrow_bf', [1, 768], BF16)
        ones_bf = sb0('ones_bf', [1, 128], BF16)
        P = Prog(nc, C.sems)
        psr = RR(ps)
        P.op('pool', lambda e: e.memset(XT[:, :, 0:2], 0.0), [], XTb)
        P.op('pool', lambda e: e.memset(XT[:, :, T + 2:T + 4], 0.0), [], XTb)
        for g in range(8):
            P.ld(XT[:, g, 2:T + 2], C.xbcT[b, g * 128:(g + 1) * 128, :], w=[XTb[g]])
        P.ld(cw6[0:5, :], C.w['conv_w'][l], w=[cw6])
        P.ld(cw6[5:6, :], C.w['conv_b'][l:l + 1, :], w=[cw6])
        P.ld(cbrow[0:1, :], C.w['conv_b'][l:l + 1, 0:768], w=[cbrow])
        P.cp('dve', cbrow_bf[0:1, :], cbrow[0:1, :], r=[cbrow], w=[cbrow_bf])
        P.cp('dve', ones_bf[0:1, :], ones[0:1, :], r=[cst], w=[ones_bf])
        load_bcast(P, dtb, C.w['dt_bias'][l].rearrange("a b -> (a b)"), 16)
        load_bcast(P, negA, C.w['a_log'][l].rearrange("a b -> (a b)"), 16)
        load_bcast(P, dsk8, C.w['d_skip'][l], 8)
        load_bcast(P, ng, C.w['ssd_norm_g'][l], 512)
        P.act(negA[:, :], negA[:, :], AF.Exp, r=[negA], w=[negA])
        P.ts('dve', negA[:, :], negA[:, :], -1.0, None, ALU.mult, r=[negA], w=[negA])
        P.cp('dve', dsk[:, :].rearrange("p (h q) -> p h q", q=64), bc3(dsk8[:, :], 64, 2), r=[dsk8], w=[dsk])
        bank = psr()
        for g in range(8):
            P.tr(bank[:, g * 6:(g + 1) * 6], cw6[0:6, g * 128:(g + 1) * 128], ident[0:6, 0:6], r=[cw6, cst], w=[bank])
        P.cp('dve', cwb[:, :, :], bank[:, 0:48].rearrange("p (g k) -> p g k", k=6), r=[bank], w=[cwb])
        er = RR(['dve', 'pool'])
        for k in range(5):
            for g in range(8):
                P.ts(er(), Dg[:, k, g, :], ident, cwb[:, g, k:k + 1], None, ALU.mult, r=[cwb, cst], w=[Dg])
        P.ld(dtraw[:, :, :], C.dtbuf[b].rearrange("(c p) k -> p c k", p=128), w=[dtraw])
        P.tt('dve', dtv[:, :, :], dtraw[:, :, :], bc3(dtb[:, :], NCH, 1), ALU.add, r=[dtraw, dtb], w=[dtv])
        P.act(dtv[:, :, :], dtv[:, :, :], AF.Exp, r=[dtv], w=[dtv])
        P.ts('dve', dtv[:, :, :], dtv[:, :, :], 1.0, None, ALU.add, r=[dtv], w=[dtv])
        P.act(dtv[:, :, :], dtv[:, :, :], AF.Ln, r=[dtv], w=[dtv])
        P.tt('dve', av[:, :, :], dtv[:, :, :], bc3(negA[:, :], NCH, 1), ALU.mult, r=[dtv, negA], w=[av])
        for c in range(NCH):
            for (g0, ng_, dst, boff) in ((0, 4, xbf, 0), (4, 2, Btok, 512)):
                bank = psr()
                n = ng_ * 128
                P.mm(bank[:, 0:n], ones_bf[0:1, :], cbrow_bf[0:1, boff:boff + n], start=True, stop=False,
                     r=[ones_bf, cbrow_bf], w=[bank])
                for gi in range(ng_):
                    g = g0 + gi
                    for k in range(5):
                        P.mm(bank[:, gi * 128:(gi + 1) * 128], XT[:, g, c * 128 + k:c * 128 + k + 128], Dg[:, k, g, :],
                             start=False, stop=(gi == ng_ - 1 and k == 4), r=[XTb[g], Dg], w=[bank])
                P.act(dst[:, c, :], bank[:, 0:n], AF.Silu, r=[bank], w=[dst])
        for tb in range(T // TB):
            for gi in range(4):
                g = 4 + gi
                bank = psr()
                for k in range(5):
                    P.mm(bank[:, 0:TB], Dg[:, k, g, :], XT[:, g, tb * TB + k:tb * TB + k + TB], start=(k == 0), stop=(k == 4),
                         r=[XTb[g], Dg], w=[bank])
                dst = BT if gi < 2 else CT
                P.act(dst[:, gi % 2, tb * TB:(tb + 1) * TB], bank[:, 0:TB], AF.Silu, bias=cwb[:, g, 5:6], r=[bank, cwb], w=[dst])
        P.emit()
        C.stats.append(('B0', P.stats))
        st0.close()
        ysc = sb('ysc', [128, NCH, 16], F32)
        cs_sb = [sb('cs_sb%d' % i, [128, 32], F32) for i in range(2)]
        dd = [sb('dd%d' % i, [128, 16], F32) for i in range(2)]
        et = [sb('et%d' % i, [128, 16], F32) for i in range(2)]
        wd = [sb('wd%d' % i, [128, 16], F32) for i in range(2)]
        xw = [sb('xw%d' % i, [128, 2, 512], BF16) for i in range(2)]
        Srun = sb('Srun', [128, 2, 512], F32)
        Sin = sb('Sin', [128, NCH, 2, 512], BF16)
        rhsS = [sb('rhsS%d' % i, [128, 2, 8, 128], F32) for i in range(2)]
        E = [sb('E%d' % i, [128, 2, 8, 128], F32) for i in range(2)]
        SM = [sb('SM%d' % i, [128, 2, 2, 128], F32) for i in range(2)]
        G = [sb('G%d' % i, [128, 2, 8, 128], BF16) for i in range(2)]
        ta = [sb('ta%d' % i, [128, 512], F32) for i in range(2)]
        tb_ = [sb('tb%d' % i, [128, 512], F32) for i in range(2)]
        yt = [sb('yt%d' % i, [128, 512], F32) for i in range(2)]
        zt = [sb('zt%d' % i, [128, 512], F32) for i in range(2)]
        sq = [sb('sq%d' % i, [128, 512], F32) for i in range(2)]
        ss = [sb('ss%d' % i, [128, 8], F32) for i in range(2)]
        P = Prog(nc, C.sems)
        psr = RR(ps)
        P.op('pool', lambda e: e.memset(Srun[:, :, :], 0.0), [], [Srun.buf])
        for i in range(NCH):
            cc = (i, NCH - 1 - i)
            k2 = i % 2
            bank = psr()
            for d in range(2):
                P.mm(bank[:, d * 8:(d + 1) * 8], U if d == 0 else Lo, av[:, cc[d], d * 8:(d + 1) * 8], r=[cst, av], w=[bank])
                P.mm(bank[:, 16 + d * 8:16 + (d + 1) * 8], ones, av[:, cc[d], d * 8:(d + 1) * 8], r=[cst, av], w=[bank])
            P.cp('act', cs_sb[k2][:, :], bank[:, 0:32], r=[bank], w=[cs_sb[k2]])
            P.tt('dve', dd[k2][:, :], cs_sb[k2][:, 16:32], cs_sb[k2][:, 0:16], ALU.subtract, r=[cs_sb[k2]], w=[dd[k2]])
            P.act(dd[k2][:, :], dd[k2][:, :], AF.Exp, r=[dd[k2]], w=[dd[k2]])
            P.act(et[k2][:, :], cs_sb[k2][:, 16:32], AF.Exp, r=[cs_sb[k2]], w=[et[k2]])
            for d in range(2):
                sl = slice(d * 8, (d + 1) * 8)
                P.act(ysc[:, cc[d], sl], cs_sb[k2][:, sl], AF.Exp, r=[cs_sb[k2]], w=[ysc])
                P.tt('dve', wd[k2][:, sl], dd[k2][:, sl], dtv[:, cc[d], sl], ALU.mult, r=[dd[k2], dtv], w=[wd[k2]])
                P.tt('dve', xw[k2][:, d, :].rearrange("p (h q) -> p h q", q=64),
                     xbf[:, cc[d], :].rearrange("p (h q) -> p h q", q=64), bc3(wd[k2][:, sl], 64, 2), ALU.mult,
                     r=[xbf, wd[k2]], w=[xw[k2]])
            for d in range(2):
                bank = psr()
                for g in range(2):
                    P.mm(bank[:, g * 256:(g + 1) * 256], Btok[:, cc[d], g * 128:(g + 1) * 128], xw[k2][:, d, g * 256:(g + 1) * 256],
                         r=[Btok, xw[k2]], w=[bank])
                P.cp('pool', Sin[:, cc[d], d, :], Srun[:, d, :], r=[Srun], w=[Sin])
                P.tt('dve', Srun[:, d, :].rearrange("p (h q) -> p h q", q=64), Srun[:, d, :].rearrange("p (h q) -> p h q", q=64),
                     bc3(et[k2][:, d * 8:(d + 1) * 8], 64, 2), ALU.mult, r=[Srun, et[k2]], w=[Srun])
                P.tt('dve', Srun[:, d, :], Srun[:, d, :], bank[:, :], ALU.add, r=[Srun, bank], w=[Srun])
        def pass2(par):
            for c in range(par, NCH, 2):
                k2 = c % 2
                rows = slice(c * 128, (c + 1) * 128)
                csl = slice(c * 128, (c + 1) * 128)
                P.ld(zt[k2][:, :], C.zbuf[b, rows, :], w=[zt[k2]])
                for d in range(2):
                    P.tt('dve' if d == 0 else 'pool', rhsS[k2][:, d, :, :], bc3(U if d == 0 else Lo, 8, 1),
                         bc3(av[:, c, d * 8:(d + 1) * 8], 128, 2), ALU.mult, r=[cst, av], w=[rhsS[k2]])
                yield
                for d in range(2):
                    for hh in range(2):
                        bank = psr()
                        P.mm(bank[:, :], Ls if d == 0 else Us, rhsS[k2][:, d, hh * 4:(hh + 1) * 4, :].rearrange("p a b -> p (a b)"),
                             r=[cst, rhsS[k2]], w=[bank])
                        P.act(E[k2][:, d, hh * 4:(hh + 1) * 4, :].rearrange("p a b -> p (a b)"), bank[:, :], AF.Exp, r=[bank], w=[E[k2]])
                yield
                bank = psr()
                for g in range(2):
                    P.mm(bank[:, g * 128:(g + 1) * 128], BT[:, g, csl], CT[:, g, csl], r=[BT, CT], w=[bank])
                for d in range(2):
                    P.tt('dve', SM[k2][:, d, :, :], bank[:, 0:256].rearrange("p (g l) -> p g l", l=128), bc3(U if d == 0 else Lo, 2, 1),
                         ALU.mult, r=[bank, cst], w=[SM[k2]])
                yield
                for d in range(2):
                    for h in range(8):
                        P.stt(G[k2][:, d, h, :], E[k2][:, d, h, :], dtv[:, c, d * 8 + h:d * 8 + h + 1], SM[k2][:, d, h // 4, :],
                              ALU.mult, ALU.mult, r=[E[k2], dtv, SM[k2]], w=[G[k2]])
                yield
                bY1 = psr()
                for h in range(8):
                    for d in range(2):
                        P.mm(bY1[:, h * 64:(h + 1) * 64], G[k2][:, d, h, :], xbf[:, c, h * 64:(h + 1) * 64], start=(d == 0), stop=(d == 1),
                             r=[G[k2], xbf], w=[bY1])
                yield
                bY2 = [psr(), psr()]
                for d in range(2):
                    for g in range(2):
                        P.mm(bY2[d][:, g * 256:(g + 1) * 256], CT[:, g, csl], Sin[:, c, d, g * 256:(g + 1) * 256], r=[CT, Sin], w=[bY2[d]])
                yield
                v3 = lambda ap: ap.rearrange("p (h q) -> p h q", q=64)
                P.tt('dve', v3(ta[k2][:, :]), v3(bY2[0][:, :]), bc3(ysc[:, c, 0:8], 64, 2), ALU.mult, r=[bY2[0], ysc], w=[ta[k2]])
                P.tt('dve', v3(tb_[k2][:, :]), v3(bY2[1][:, :]), bc3(ysc[:, c, 8:16], 64, 2), ALU.mult, r=[bY2[1], ysc], w=[tb_[k2]])
                y = yt[k2]
                P.tt('pool', y[:, :], ta[k2][:, :], tb_[k2][:, :], ALU.add, r=[ta[k2], tb_[k2]], w=[y])
                P.tt('dve', y[:, :], y[:, :], bY1[:, :], ALU.add, r=[y, bY1], w=[y])
                P.tt('pool', sq[k2][:, :], xbf[:, c, :], dsk[:, :], ALU.mult, r=[xbf, dsk], w=[sq[k2]])
                P.tt('pool', y[:, :], y[:, :], sq[k2][:, :], ALU.add, r=[y, sq[k2]], w=[y])
                yield
                P.act(zt[k2][:, :], zt[k2][:, :], AF.Silu, r=[zt[k2]], w=[zt[k2]])
                P.tt('dve', y[:, :], y[:, :], zt[k2][:, :], ALU.mult, r=[y, zt[k2]], w=[y])
                P.tt('pool', sq[k2][:, :], y[:, :], y[:, :], ALU.mult, r=[y], w=[sq[k2]])
                P.op('dve', lambda e, o=ss[k2][:, 0:2], i_=sq[k2][:, :].rearrange("p (g q) -> p g q", q=256): e.reduce_sum(o, i_, axis=AX.X),
                     [sq[k2].buf], [ss[k2].buf])
                yield
                P.ts('dve', ss[k2][:, 2:4], ss[k2][:, 0:2], 1.0 / 256.0, RMS_EPS, ALU.mult, ALU.add, r=[ss[k2]], w=[ss[k2]])
                P.act(ss[k2][:, 4:6], ss[k2][:, 2:4], AF.Sqrt, r=[ss[k2]], w=[ss[k2]])
                P.op('dve', lambda e, o=ss[k2][:, 6:8], i_=ss[k2][:, 4:6]: e.reciprocal(o, i_), [ss[k2].buf], [ss[k2].buf])
                for g in range(2):
                    P.ts('dve', y[:, g * 256:(g + 1) * 256], y[:, g * 256:(g + 1) * 256], ss[k2][:, 6 + g:7 + g], None, ALU.mult,
                         r=[y, ss[k2]], w=[y])
                P.tt('pool', y[:, :], y[:, :], ng[:, :], ALU.mult, r=[y, ng], w=[y])
                P.ld(C.ymix[b, rows, 0:512], y[:, :], r=[y])
                yield
        alive = [pass2(0), pass2(1)]
        while alive:
            for g_ in list(alive):
                try:
                    next(g_)
                except StopIteration:
                    alive.remove(g_)
        P.emit()
        C.stats.append(('B', P.stats))


def stage_C(C, l, b):
    nc, T, NCH = C.nc, C.T, C.NCH
    cst = C.cst
    ident, U, Lo, Us, Ls, ones = (cst[:, i * 128:(i + 1) * 128] for i in range(6))
    v3 = lambda ap: ap.rearrange("p (h q) -> p h q", q=64)
    with contextlib.ExitStack() as st:
        sb = lambda n, sh, dt: _sb(st, nc, n, sh, dt)
        murow = sb('murow', [14, 128], F32)
        muT = sb('muT', [128, 3, 14], F32)
        Dm = sb('Dm', [128, 14, 128], BF16)
        Dh = sb('Dh', [128, 14, 128], BF16)
        kkb = sb('kkb', [128, 512], F32)
        kab = sb('kab', [128, 512], F32)
        rkb = sb('rkb', [128, 512], F32)
        w0b = sb('w0b', [128, 2, 512], F32)
        a0b = sb('a0b', [128, 2, 512], F32)
        LW = sb('LW', [128, 2, 512], BF16)
        GU = sb('GU', [128, 512], BF16)
        MK = sb('MK', [128, 2, 512], F32)
        MKa = sb('MKa', [128, 2, 128], F32)
        g32 = sb('g32', [128, 512], F32)
        def alloc_inst():
            RTc = [sb('RTc%d' % i, [128, 14, 130], BF16) for i in range(2)]
            bon = sb('bon', [128, NCH, 8], F32)
            r32 = sb('r32', [128, 512], F32)
            k32 = sb('k32', [128, 512], F32)
            v32 = sb('v32', [128, 512], F32)
            vbf = sb('vbf', [128, 512], BF16)
            LT = sb('LT', [128, 128], BF16)
            sg = sb('sg', [128, 128], BF16)
            lw32 = sb('lw32', [128, 512], F32)
            a32 = sb('a32', [128, 512], F32)
            kk = sb('kk', [128, 512], F32)
            tmp = sb('tmp', [128, 512], F32)
            tmp2 = sb('tmp2', [128, 512], F32)
            kd = sb('kd', [128, 512], F32)
            bb = sb('bb', [128, 512], F32)
            sm8 = sb('sm8', [128, 32], F32)
            gC = sb('gC', [128, 4], F32)
            Ep = sb('Ep', [128, 512], F32)
            En = sb('En', [128, 512], F32)
            bt_bf = sb('bt_bf', [128, 512], BF16)
            kdt_bf = sb('kdt_bf', [128, 512], BF16)
            kt_bf = sb('kt_bf', [128, 512], NEU_DT)
            KR = sb('KR', [128, 4, 2, 128], BF16)
            btT = sb('btT', [128, 4, 128], BF16)
            kdtT = sb('kdtT', [128, 4, 128], BF16)
            R3 = sb('R3', [128, 8, 3, 128], BF16)
            NM = [sb('NM%d' % i, [128, 8, 2, 128], NEU_DT) for i in range(2)]
            X = [sb('X%d' % i, [128, 8, 128], NEU_DT) for i in range(2)]
            Z1 = sb('Z1', [128, 512], NEU_DT)
            Ut = sb('Ut', [128, 512], F32)
            WT = sb('WT', [128, 4, 128], BF16)
            Mst = sb('Mst', [128, 4, 128], F32)
            Mbf = sb('Mbf', [128, 4, 128], BF16)
            Un = sb('Un', [128, 512], BF16)
            Yo = [sb('Yo%d' % i, [128, 512], F32) for i in range(2)]
            return dict(locals())
        IT = [alloc_inst(), alloc_inst()]
        ps = _psum(st, nc)
        P = Prog(nc, C.sems)
        psr = RR(ps)
        evr = RR(['act'])
        P.ld(murow[0:14, :], C.w['mu_rwkv'][l].rearrange("(g p) -> g p", p=128), w=[murow])
        bank = psr()
        P.tr(bank[:, 0:14], murow[0:14, :], ident[0:14, 0:14], r=[murow, cst], w=[bank])
        P.cp('dve', muT[:, 0, :], bank[:, 0:14], r=[bank], w=[muT])
        P.ts('dve', muT[:, 1, :], muT[:, 0, :], -1.0, 1.0, ALU.mult, ALU.add, r=[muT], w=[muT])
        P.ts('dve', muT[:, 2, :], muT[:, 0, :], 0.5, None, ALU.mult, r=[muT], w=[muT])
        er = RR(['dve', 'pool'])
        for g in range(14):
            P.ts(er(), Dm[:, g, :], ident, muT[:, 1, g:g + 1], None, ALU.mult, r=[muT, cst], w=[Dm])
            P.ts(er(), Dh[:, g, :], ident, muT[:, 2, g:g + 1], None, ALU.mult, r=[muT, cst], w=[Dh])
        load_bcast(P, kkb, C.w['k_k'][l], 512)
        load_bcast(P, kab, C.w['k_a'][l], 512)
        load_bcast(P, rkb, C.w['r_k'][l].rearrange("a b -> (a b)"), 512)
        for d in range(2):
            P.ld(w0b[:, d, :], C.w['w0'][l, d].partition_broadcast(128), w=[w0b])
            P.ld(a0b[:, d, :], C.w['a0'][l, d].partition_broadcast(128), w=[a0b])
            P.ld(LW[0:64, d, :], C.w['w_up'][l, d], w=[LW], eng='pool')
            P.ld(LW[64:128, d, :], C.w['a_up'][l, d], w=[LW], eng='pool')
        P.ld(GU[:, :], C.w['g_up'][l], w=[GU], eng='pool')
        for d in range(2):
            strict, incl, strict_ts = (Us, U, Ls) if d == 0 else (Ls, Lo, Us)
            P.ts('dve', MK[:, d, 0:128], strict, -1.0, None, ALU.mult, r=[cst], w=[MK])
            P.cp('dve', MK[:, d, 128:256], incl, r=[cst], w=[MK])
            P.cp('dve', MK[:, d, 256:384], strict, r=[cst], w=[MK])
            P.cp('dve', MK[:, d, 384:512], incl, r=[cst], w=[MK])
            P.ts('dve', MKa[:, d, :], strict_ts, -1.0, None, ALU.mult, r=[cst], w=[MKa])
        taps = (Dh, Dm, Dh)
        ydb = [[Buf('yd') for _ in range(NCH)] for _ in range(2)]
        vgb = [[Buf('vg') for _ in range(NCH)] for _ in range(2)]
        def inst(d):
            I_ = IT[d]
            (RTc, r32, k32, v32, vbf, LT, sg, lw32, a32, kk, tmp, tmp2, kd, bb, sm8, gC, Ep, En, bt_bf, kdt_bf, kt_bf, KR, btT, kdtT, R3, NM, X, Z1, Ut, WT, Mst, Mbf, Un, Yo, bon) = (I_[n] for n in ('RTc', 'r32', 'k32', 'v32', 'vbf', 'LT', 'sg', 'lw32', 'a32', 'kk', 'tmp', 'tmp2', 'kd', 'bb', 'sm8', 'gC', 'Ep', 'En', 'bt_bf', 'kdt_bf', 'kt_bf', 'KR', 'btT', 'kdtT', 'R3', 'NM', 'X', 'Z1', 'Ut', 'WT', 'Mst', 'Mbf', 'Un', 'Yo', 'bon'))
            P.op('pool', lambda e: e.memset(Mst[:, :, :], 0.0), [], [Mst.buf])
            P.op('pool', lambda e: e.memset(Mbf[:, :, :], 0.0), [], [Mbf.buf])
            for ci in range(NCH):
                c = ci if d == 0 else NCH - 1 - ci
                rows = slice(c * 128, (c + 1) * 128)
                RT = RTc[ci % 2]
                lo, hi = max(c * 128 - 1, 0), min(c * 128 + 129, T)
                if c == 0:
                    P.op('pool', lambda e, t=RT: e.memset(t[:, :, 0:1], 0.0), [], [RT.buf])
                if c == NCH - 1:
                    P.op('pool', lambda e, t=RT: e.memset(t[:, :, 129:130], 0.0), [], [RT.buf])
                P.ld(RT[:, :, lo - (c * 128 - 1):hi - (c * 128 - 1)],
                     C.rwT[b].rearrange("(g p) t -> p g t", p=128)[:, :, lo:hi], w=[RT])
                for (g0, dst) in ((0, r32), (4, k32), (8, v32)):
                    bank = psr()
                    for gi in range(4):
                        g = g0 + gi
                        for k in range(3):
                            P.mm(bank[:, gi * 128:(gi + 1) * 128], RT[:, g, k:k + 128], taps[k][:, g, :],
                                 start=(k == 0), stop=(k == 2), r=[RT, Dm, Dh], w=[bank])
                    P.cp(evr(), dst[:, :], bank[:, :], r=[bank], w=[dst])
                P.cp('pool', vbf[:, :], v32[:, :], r=[v32], w=[vbf])
                bank = psr()
                for gi in range(2):
                    g = 12 + gi
                    for k in range(3):
                        P.mm(bank[:, gi * 128:(gi + 1) * 128], taps[k][:, g, :], RT[:, g, k:k + 128],
                             start=(k == 0), stop=(k == 2), r=[RT, Dm, Dh], w=[bank])
                P.act(LT[0:64, :], bank[0:64, 0:128], AF.Tanh, r=[bank], w=[LT])
                P.cp('dve', LT[64:128, :], bank[64:128, 0:128], r=[bank], w=[LT])
                P.act(sg[:, :], bank[:, 128:256], AF.Sigmoid, r=[bank], w=[sg])
                yield
                bW, bA = psr(), psr()
                P.mm(bW[:, :], LT[0:64, :], LW[0:64, d, :], r=[LT, LW], w=[bW])
                P.mm(bA[:, :], LT[64:128, :], LW[64:128, d, :], r=[LT, LW], w=[bA])
                P.tt('dve', lw32[:, :], bW[:, :], w0b[:, d, :], ALU.add, r=[bW, w0b], w=[lw32])
                P.act(lw32[:, :], lw32[:, :], AF.Sigmoid, r=[lw32], w=[lw32])
                P.ts('pool', lw32[:, :], lw32[:, :], -DECAY_C, None, ALU.mult, r=[lw32], w=[lw32])
                P.tt('dve', a32[:, :], bA[:, :], a0b[:, d, :], ALU.add, r=[bA, a0b], w=[a32])
                P.act(a32[:, :], a32[:, :], AF.Sigmoid, r=[a32], w=[a32])
                if d == 0:
                    bG = psr()
                    P.mm(bG[:, :], sg[:, :], GU[:, :], r=[sg, GU], w=[bG])
                    P.cp('act', g32[:, :], bG[:, :], r=[bG], w=[g32])
                    P.ld(C.vg[0, rows, :], v32[:, :], r=[v32], w=[vgb[0][c]])
                    P.ld(C.vg[1, rows, :], g32[:, :], r=[g32], w=[vgb[1][c]])
                yield
                P.tt('pool', kk[:, :], k32[:, :], kkb[:, :], ALU.mult, r=[k32, kkb], w=[kk])
                P.tt('pool', tmp[:, :], kk[:, :], kk[:, :], ALU.mult, r=[kk], w=[tmp])
                P.op('dve', lambda e, o=sm8[:, 0:8], i_=v3(tmp[:, :]): e.reduce_sum(o, i_, axis=AX.X), [tmp.buf], [sm8.buf])
                P.act(sm8[:, 8:16], sm8[:, 0:8], AF.Sqrt, r=[sm8], w=[sm8])
                P.ts('dve', sm8[:, 8:16], sm8[:, 8:16], 1e-12, None, ALU.max, r=[sm8], w=[sm8])
                P.op('dve', lambda e, o=sm8[:, 16:24], i_=sm8[:, 8:16]: e.reciprocal(o, i_), [sm8.buf], [sm8.buf])
                P.tt('dve', v3(kk[:, :]), v3(kk[:, :]), bc3(sm8[:, 16:24], 64, 2), ALU.mult, r=[kk, sm8], w=[kk])
                P.stt(tmp2[:, :], a32[:, :], -1.0, kab[:, :], ALU.add, ALU.mult, r=[a32, kab], w=[tmp2])
                P.stt(kd[:, :], tmp2[:, :], 1.0, k32[:, :], ALU.add, ALU.mult, r=[tmp2, k32], w=[kd])
                P.tt('pool', tmp[:, :], r32[:, :], kd[:, :], ALU.mult, r=[r32, kd], w=[tmp])
                P.tt('pool', tmp[:, :], tmp[:, :], rkb[:, :], ALU.mult, r=[tmp, rkb], w=[tmp])
                P.op('dve', lambda e, o=bon[:, c, :], i_=v3(tmp[:, :]): e.reduce_sum(o, i_, axis=AX.X), [tmp.buf], [bon.buf])
                P.tt('pool', bb[:, :], kk[:, :], a32[:, :], ALU.mult, r=[kk, a32], w=[bb])
                yield
                bC = psr()
                P.mm(bC[:, :], U if d == 0 else Lo, lw32[:, :], r=[cst, lw32], w=[bC])
                bT = psr()
                for jg in range(4):
                    P.mm(bT[:, 2 * jg:2 * jg + 2], lw32[:, jg * 128:(jg + 1) * 128], ones[:, 0:2], r=[lw32, cst], w=[bT])
                P.act(gC[:, :], bT[:, 0:8].rearrange("p (j two) -> p j two", two=2)[:, :, 0], AF.Exp, r=[bT], w=[gC])
                P.act(Ep[:, :], bC[:, :], AF.Exp, r=[bC], w=[Ep])
                P.act(En[:, :], bC[:, :], AF.Exp, scale=-1.0, r=[bC], w=[En])
                P.tt('dve', tmp2[:, :], bC[:, :], lw32[:, :], ALU.subtract, r=[bC, lw32], w=[tmp2])
                P.act(tmp2[:, :], tmp2[:, :], AF.Exp, r=[tmp2], w=[tmp2])
                P.tt('pool', r32[:, :], r32[:, :], Ep[:, :], ALU.mult, r=[r32, Ep], w=[r32])
                P.tt('dve', kk[:, :], kk[:, :], tmp2[:, :], ALU.mult, r=[kk, tmp2], w=[kk])
                P.tt('pool', bb[:, :], bb[:, :], En[:, :], ALU.mult, r=[bb, En], w=[bb])
                P.tt('dve', kd[:, :], kd[:, :], En[:, :], ALU.mult, r=[kd, En], w=[kd])
                P.cp('pool', bt_bf[:, :], bb[:, :], r=[bb], w=[bt_bf])
                P.cp('pool', kdt_bf[:, :], kd[:, :], r=[kd], w=[kdt_bf])
                P.cp('pool', kt_bf[:, :], kk[:, :], r=[kk], w=[kt_bf])
                yield
                for (src, dstap, dbuf) in ((kk, KR[:, :, 0, :], KR), (r32, KR[:, :, 1, :], KR), (bb, btT[:, :, :], btT), (kd, kdtT[:, :, :], kdtT)):
                    bank = psr()
                    for jg in range(4):
                        P.tr(bank[:, jg * 128:(jg + 1) * 128], src[:, jg * 128:(jg + 1) * 128], ident, r=[src, cst], w=[bank])
                    P.cp(evr(), dstap, bank[:, :].rearrange("p (a b) -> p a b", b=128), r=[bank], w=[dbuf])
                yield
                for h in range(8):
                    jg, rs = h // 2, slice((h % 2) * 64, (h % 2 + 1) * 64)
                    bM = psr()
                    krr = KR[rs, jg, :, :].rearrange("p a b -> p (a b)")
                    P.mm(bM[:, 0:256], btT[rs, jg, :], krr, r=[btT, KR], w=[bM])
                    P.mm(bM[:, 256:512], kdtT[rs, jg, :], krr, r=[kdtT, KR], w=[bM])
                    P.tt('dve', NM[0][:, h, 0, :], bM[:, 0:128], MK[:, d, 0:128], ALU.mult, r=[bM, MK], w=[NM[0]])
                    P.tt('dve', R3[:, h, :, :].rearrange("p a b -> p (a b)"), bM[:, 128:512], MK[:, d, 128:512], ALU.mult,
                         r=[bM, MK], w=[R3])
                    if h % 2 == 1:
                        yield
                yield
                for hh in range(2):
                    bank = psr()
                    rs = slice(hh * 64, (hh + 1) * 64)
                    for jg in range(4):
                        P.mm(bank[:, jg * 128:(jg + 1) * 128], KR[rs, jg, 0, :], btT[rs, jg, :], r=[KR, btT], w=[bank])
                    P.tt('dve', NM[0][:, hh:8:2, 1, :], bank[:, :].rearrange("p (a b) -> p a b", b=128),
                         bc3(MKa[:, d, :], 4, 1), ALU.mult, r=[bank, MKa], w=[NM[0]])
                P.tt('dve', X[0][:, :, :], NM[0][:, :, 0, :], bc3(ident, 8, 1), ALU.add, r=[NM[0], cst], w=[X[0]])
                yield
                for j in range(6):
                    yield
                    cur, nxt = NM[j % 2], NM[(j + 1) % 2]
                    Xc, Xn = X[j % 2], X[(j + 1) % 2]
                    for hp in range(4):
                        bank = psr()
                        for q in range(2):
                            h = hp * 2 + q
                            P.mm(bank[:, q * 256:q * 256 + 128], cur[:, h, 1, :], cur[:, h, 0, :], r=[cur], w=[bank])
                            P.mm(bank[:, q * 256 + 128:q * 256 + 256], cur[:, h, 0, :], cur[:, h, 1, :], r=[cur], w=[bank])
                        P.cp(evr(), nxt[:, hp * 2:hp * 2 + 2, :, :].rearrange("p a b c -> p (a b c)"), bank[:, :], r=[bank], w=[nxt])
                    yield
                    for hp in range(2):
                        bank = psr()
                        for q in range(4):
                            h = hp * 4 + q
                            P.mm(bank[:, q * 128:(q + 1) * 128], nxt[:, h, 1, :], Xc[:, h, :], r=[nxt, Xc], w=[bank])
                        P.tt('dve', Xn[:, hp * 4:(hp + 1) * 4, :].rearrange("p a b -> p (a b)"), bank[:, :],
                             Xc[:, hp * 4:(hp + 1) * 4, :].rearrange("p a b -> p (a b)"), ALU.add, r=[bank, Xc], w=[Xn])
                yield
                XF = X[0]
                bZ = psr()
                for h in range(8):
                    P.mm(bZ[:, h * 64:(h + 1) * 64], R3[:, h, 1, :], vbf[:, h * 64:(h + 1) * 64], r=[R3, vbf], w=[bZ])
                P.cp('act', Z1[:, :], bZ[:, :], r=[bZ], w=[Z1])
                bU = psr()
                for h in range(8):
                    P.mm(bU[:, h * 64:(h + 1) * 64], XF[:, h, :], Z1[:, h * 64:(h + 1) * 64], r=[XF, Z1], w=[bU])
                P.cp('act', Ut[:, :], bU[:, :], r=[bU], w=[Ut])
                for hp in range(2):
                    bank = psr()
                    for q in range(4):
                        h = hp * 4 + q
                        jg = h // 2
                        P.mm(bank[:, q * 128:(q + 1) * 128], kt_bf[:, jg * 128:(jg + 1) * 128], XF[:, h, :], r=[kt_bf, XF], w=[bank])
                    for q in range(4):
                        h = hp * 4 + q
                        jg, rs = h // 2, slice((h % 2) * 64, (h % 2 + 1) * 64)
                        P.cp(evr(), WT[rs, jg, :], bank[rs, q * 128:(q + 1) * 128], r=[bank], w=[WT])
                yield
                bP = psr()
                for jg in range(4):
                    P.mm(bP[:, jg * 128:(jg + 1) * 128], WT[:, jg, :], Mbf[:, jg, :], r=[WT, Mbf], w=[bP])
                P.stt(Un[:, :], bP[:, :], -1.0, Ut[:, :], ALU.mult, ALU.subtract, r=[bP, Ut], w=[Un])
                yield
                bY = psr()
                for jg in range(4):
                    P.mm(bY[:, jg * 128:(jg + 1) * 128], KR[:, jg, 1, :], Mbf[:, jg, :], start=True, stop=False, r=[KR, Mbf], w=[bY])
                    for h in (2 * jg, 2 * jg + 1):
                        hs = slice(h * 64, (h + 1) * 64)
                        P.mm(bY[:, hs], R3[:, h, 2, :], vbf[:, hs], start=False, stop=False, r=[R3, vbf], w=[bY])
                    for h in (2 * jg, 2 * jg + 1):
                        hs = slice(h * 64, (h + 1) * 64)
                        P.mm(bY[:, hs], R3[:, h, 0, :], Un[:, hs], start=False, stop=(h == 2 * jg + 1), r=[R3, Un], w=[bY])
                yo = Yo[ci % 2]
                P.cp('act', yo[:, :], bY[:, :], r=[bY], w=[yo])
                P.ld(C.ydir[d, rows, :], yo[:, :], r=[yo], w=[ydb[d][c]])
                yield
                bS = psr()
                for jg in range(4):
                    js = slice(jg * 128, (jg + 1) * 128)
                    P.mm(bS[:, js], bt_bf[:, js], Un[:, js], start=True, stop=False, r=[bt_bf, Un], w=[bS])
                    P.mm(bS[:, js], kdt_bf[:, js], vbf[:, js], start=False, stop=True, r=[kdt_bf, vbf], w=[bS])
                for hh in range(2):
                    rs = slice(hh * 64, (hh + 1) * 64)
                    src = bS[rs, :].rearrange("p (j q) -> p j q", q=128)[:, :, hh * 64:(hh + 1) * 64]
                    mv = Mst[rs, :, hh * 64:(hh + 1) * 64]
                    P.tt('dve', mv, mv, src, ALU.add, r=[Mst, bS], w=[Mst])
                    P.tt('dve', mv, mv, gC[rs, :].unsqueeze(2).broadcast_to([64, 4, 64]), ALU.mult, r=[Mst, gC], w=[Mst])
                P.cp('pool', Mbf[:, :, :], Mst[:, :, :], r=[Mst], w=[Mbf])
                yield
        alive = [inst(0), inst(1)]
        while alive:
            for g_ in list(alive):
                try:
                    next(g_)
                except StopIteration:
                    alive.remove(g_)
        P.emit()
        C.stats.append(('C', P.stats))
        fy = [_Quad([IT[i][n] for n in ('r32', 'k32', 'v32', 'lw32')]) for i in range(2)]
        lgb, lbb = IT[1]['a32'], IT[1]['kk']
        sm8, tmp = IT[0]['sm8'], IT[0]['tmp']
        bon0, bon1 = IT[0]['bon'], IT[1]['bon']
        P = Prog(nc, C.sems)
        load_bcast(P, lgb, C.w['lnx_g'][l], 512)
        load_bcast(P, lbb, C.w['lnx_b'][l], 512)
        for c in range(NCH):
            rows = slice(c * 128, (c + 1) * 128)
            f = fy[c % 2]
            P.ld(f[:, 0, :], C.ydir[0, rows, :], r=[ydb[0][c]], w=[f])
            P.ld(f[:, 1, :], C.ydir[1, rows, :], r=[ydb[1][c]], w=[f])
            P.ld(f[:, 2, :], C.vg[0, rows, :], r=[vgb[0][c]], w=[f])
            P.ld(f[:, 3, :], C.vg[1, rows, :], r=[vgb[1][c]], w=[f])
            y = f[:, 0, :]
            P.tt('dve', y, y, f[:, 1, :], ALU.add, r=[f], w=[f])
            P.op('dve', lambda e, o=sm8[:, 0:8], i_=v3(y): e.reduce_sum(o, i_, axis=AX.X), [f.buf], [sm8.buf])
            P.tt('pool', tmp[:, :], y, y, ALU.mult, r=[f], w=[tmp])
            P.op('dve', lambda e, o=sm8[:, 8:16], i_=v3(tmp[:, :]): e.reduce_sum(o, i_, axis=AX.X), [tmp.buf], [sm8.buf])
            P.ts('dve', sm8[:, 0:16], sm8[:, 0:16], 1.0 / 64.0, None, ALU.mult, r=[sm8], w=[sm8])
            P.tt('dve', sm8[:, 16:24], sm8[:, 0:8], sm8[:, 0:8], ALU.mult, r=[sm8], w=[sm8])
            P.tt('dve', sm8[:, 16:24], sm8[:, 8:16], sm8[:, 16:24], ALU.subtract, r=[sm8], w=[sm8])
            P.ts('dve', sm8[:, 16:24], sm8[:, 16:24], GN_EPS, None, ALU.add, r=[sm8], w=[sm8])
            P.act(sm8[:, 16:24], sm8[:, 16:24], AF.Sqrt, r=[sm8], w=[sm8])
            P.op('dve', lambda e, o=sm8[:, 24:32], i_=sm8[:, 16:24]: e.reciprocal(o, i_), [sm8.buf], [sm8.buf])
            P.tt('dve', v3(y), v3(y), bc3(sm8[:, 0:8], 64, 2), ALU.subtract, r=[f, sm8], w=[f])
            P.tt('dve', v3(y), v3(y), bc3(sm8[:, 24:32], 64, 2), ALU.mult, r=[f, sm8], w=[f])
            P.tt('pool', y, y, lgb[:, :], ALU.mult, r=[f, lgb], w=[f])
            P.tt('pool', y, y, lbb[:, :], ALU.add, r=[f, lbb], w=[f])
            P.tt('dve', sm8[:, 0:8], bon0[:, c, :], bon1[:, c, :], ALU.add, r=[bon0, bon1, sm8], w=[sm8])
            P.tt('dve', v3(tmp[:, :]), v3(f[:, 2, :]), bc3(sm8[:, 0:8], 64, 2), ALU.mult, r=[f, sm8], w=[tmp])
            P.tt('pool', y, y, tmp[:, :], ALU.add, r=[f, tmp], w=[f])
            P.tt('pool', y, y, f[:, 3, :], ALU.mult, r=[f], w=[f])
            P.ld(C.ymix[b, rows, 512:1024], y, r=[f])
        P.emit()
        C.stats.append(('C', P.stats))


def build(T=2048, NS=2, depth=2, debug=False, stages='ABCD'):
    nc = bass.Bass("TRN2", target_bir_lowering=False)
    C = Ctx()
    C.nc, C.T, C.NS, C.NCH = nc, T, NS, T // 128
    C.stats = []
    C.x = nc.dram_tensor("x", [NS, T, D], F32, kind="ExternalInput").ap()
    C.w = {n: nc.dram_tensor(n, W_SHAPES[n], F32, kind="ExternalInput").ap() for n in W_NAMES}
    C.cst_d = nc.dram_tensor("cst", [128, 768], F32, kind="ExternalInput").ap()
    C.out = nc.dram_tensor("out", [NS, T, D], F32, kind="ExternalOutput").ap()
    kind = "ExternalOutput" if debug else "Internal"
    C.hbuf = nc.dram_tensor("hbuf", [NS, T, D], F32, kind=kind).ap()
    C.h1buf = nc.dram_tensor("h1buf", [NS, T, D], F32, kind=kind).ap()
    C.zbuf = nc.dram_tensor("zbuf", [NS, T, 512], F32, kind=kind).ap()
    C.dtbuf = nc.dram_tensor("dtbuf", [NS, T, 16], F32, kind=kind).ap()
    C.xbcT = nc.dram_tensor("xbcT", [NS, 1024, T], BF16, kind=kind).ap()
    C.rwT = nc.dram_tensor("rwT", [NS, 1792, T], BF16, kind=kind).ap()
    C.ymix = nc.dram_tensor("ymix", [NS, T, D], F32, kind=kind).ap()
    C.ydir = nc.dram_tensor("ydir", [2, T, 512], F32, kind=kind).ap()
    C.vg = nc.dram_tensor("vg", [3, T, 512], F32, kind=kind).ap()
    with nc.sbuf_tensor('cst_sb', [128, 768], F32) as cst_sb, contextlib.ExitStack() as semstack:
        C.cst = Tile(cst_sb, 'cst')
        C.sems = SemState(nc, 8, semstack)
        stage_consts(C)
        for l in range(depth):
            if 'A' in stages:
                stage_A(C, l)
            for b in range(NS):
                if 'B' in stages:
                    stage_B(C, l, b)
                if 'C' in stages:
                    stage_C(C, l, b)
            if 'D' in stages:
                with contextlib.ExitStack() as wst:
                    Wf = _sb(wst, nc, 'Wf', [128, 8, 4096], BF16)
                    Wp = _sb(wst, nc, 'Wp', [128, 32, D], BF16)
                    pre = (Wf, [Buf('Wf%d' % k) for k in range(8)], Wp, [Buf('Wp%d' % k) for k in range(32)])
                    stage_D1(C, l, pre)
                    stage_D2(C, l, (l == depth - 1), pre)
    return nc, C


_CACHE = {}


def kernel(**inputs):
    n_cores = 8
    x = np.ascontiguousarray(np.asarray(inputs["x"], dtype=np.float32))
    B, T, _ = x.shape
    NS = B // n_cores
    key = (T, NS)
    if key not in _CACHE:
        _CACHE[key] = build(T=T, NS=NS, depth=2, debug=False)[0]
    nc = _CACHE[key]
    cst = make_consts()
    ws = {n: np.ascontiguousarray(np.asarray(inputs[n], dtype=np.float32)) for n in W_NAMES}
    in_maps = []
    for i in range(n_cores):
        m = dict(ws)
        m["x"] = np.ascontiguousarray(x[i * NS:(i + 1) * NS])
        m["cst"] = cst
        in_maps.append(m)
    res = run_bass_kernel_spmd(nc, in_maps, core_ids=list(range(n_cores)))
    return np.concatenate([np.asarray(r["out"], dtype=np.float32) for r in res.results], axis=0)
```

```python
import contextlib
import numpy as np
import concourse.bass as bass
import concourse.mybir as mybir
from concourse.bass_utils import run_bass_kernel_spmd

F32 = mybir.dt.float32
BF16 = mybir.dt.bfloat16
AF = mybir.ActivationFunctionType
ALU = mybir.AluOpType
AX = mybir.AxisListType

ENGS = ('pe', 'act', 'dve', 'pool', 'sp')
LANE_GRAN = 8


class Buf:
    __slots__ = ('name', 'last_w', 'readers')

    def __init__(self, name=''):
        self.name = name
        self.last_w = None
        self.readers = []


class _Op:
    __slots__ = ('eng', 'idx', 'fn', 'deps', 'is_dma', 'dslot', 'dval', 'needs_inc', 'cnt', 'waits', 'prog', 'key')

    def __init__(self, eng, idx, fn, is_dma):
        self.eng = eng
        self.idx = idx
        self.fn = fn
        self.is_dma = is_dma
        self.deps = []
        self.dslot = 0
        self.dval = 0
        self.needs_inc = False
        self.cnt = 0
        self.waits = []
        self.prog = None
        self.key = (1 << 60, 0, 0)


class SemState:
    def __init__(self, nc, ring=8, stack=None):
        self.stack = stack if stack is not None else contextlib.ExitStack()
        st = self.stack
        self.csem = {e: st.enter_context(nc.semaphore('c_' + e)) for e in ENGS if e != 'sp'}
        self.dsem = {e: [st.enter_context(nc.semaphore('d_%s_%d' % (e, i))) for i in range(ring)] for e in ('sp', 'pool', 'act')}
        self.c_off = {e: 0 for e in ENGS}
        self.d_cnt = {e: 0 for e in ENGS}


class Prog:
    def __init__(self, nc, sems=None, ring=8):
        self.nc = nc
        self.q = {e: [] for e in ENGS}
        self.ring = ring
        self.dma_hist = {e: [] for e in ENGS}
        self.sems = sems if sems is not None else SemState(nc, ring)
        self.gseq = 0
        self.lane = 0
        self.lane_base = 0
        self.lane_cnt = {}

    def _add(self, eng, fn, reads, writes, is_dma):
        op = _Op(eng, len(self.q[eng]), fn, is_dma)
        op.prog = self
        deps = []
        for b in reads:
            if b.last_w is not None:
                deps.append(b.last_w)
        for b in writes:
            if b.last_w is not None:
                deps.append(b.last_w)
            deps.extend(b.readers)
        if self.lane == 0:
            op.key = (self.gseq, 0, 0)
            self.gseq += 1
        else:
            c = self.lane_cnt.get(self.lane, 0)
            op.key = (self.lane_base + c // LANE_GRAN, self.lane, c)
            self.lane_cnt[self.lane] = c + 1
        seen = set()
        for d in deps:
            if d is not op and id(d) not in seen and getattr(d, 'prog', self) is self:
                seen.add(id(d))
                op.deps.append(d)
        for b in writes:
            b.last_w = op
            b.readers = []
        for b in reads:
            if b.last_w is not op:
                b.readers.append(op)
        self.q[eng].append(op)
        return op

    def set_lane(self, k):
        if self.lane == 0 and k != 0:
            self.lane_base = self.gseq
            self.lane_cnt = {}
        if k == 0 and self.lane != 0:
            self.gseq = self.lane_base + max(self.lane_cnt.values(), default=0) // LANE_GRAN + 1
        self.lane = k

    def op(self, eng, fn, reads=(), writes=()):
        return self._add(eng, fn, reads, writes, False)

    def dma(self, eng, fn, reads=(), writes=()):
        return self._add(eng, fn, reads, writes, True)


    @staticmethod
    def _bufs(lst):
        return [b.buf if hasattr(b, 'buf') else b for b in lst]

    def mm(self, out, lhsT, rhs, start=True, stop=True, r=(), w=()):
        return self.op('pe', lambda e: e.matmul(out, lhsT, rhs, start=start, stop=stop), self._bufs(r), self._bufs(w))

    def tr(self, out, in_, ident, r=(), w=()):
        return self.op('pe', lambda e: e.transpose(out, in_, ident), self._bufs(r), self._bufs(w))

    def act(self, out, in_, func, bias=None, scale=None, accum=None, r=(), w=()):
        kw = {}
        if bias is not None:
            kw['bias'] = bias
        if scale is not None:
            kw['scale'] = scale
        if accum is not None:
            kw['accum_out'] = accum
        return self.op('act', lambda e: e.activation(out, in_, func, **kw), self._bufs(r), self._bufs(w))

    def tt(self, eng, out, in0, in1, op, r=(), w=()):
        return self.op(eng, lambda e: e.tensor_tensor(out, in0, in1, op), self._bufs(r), self._bufs(w))

    def ts(self, eng, out, in0, s1, s2=None, op0=None, op1=None, r=(), w=()):
        if op1 is None:
            return self.op(eng, lambda e: e.tensor_scalar(out, in0, s1, None, op0), self._bufs(r), self._bufs(w))
        return self.op(eng, lambda e: e.tensor_scalar(out, in0, s1, s2, op0, op1), self._bufs(r), self._bufs(w))

    def stt(self, out, in0, scalar, in1, op0, op1, r=(), w=()):
        return self.op('dve', lambda e: e.scalar_tensor_tensor(out, in0, scalar, in1, op0, op1), self._bufs(r), self._bufs(w))

    def cp(self, eng, out, in_, r=(), w=()):
        if eng == 'act':
            return self.op('act', lambda e: e.copy(out, in_), self._bufs(r), self._bufs(w))
        return self.op(eng, lambda e: e.tensor_copy(out, in_), self._bufs(r), self._bufs(w))

    def ld(self, out, in_, r=(), w=(), eng='sp', **kw):
        return self.dma(eng, lambda e: e.dma_start(out=out, in_=in_, **kw), self._bufs(r), self._bufs(w))

    def _last_ops(self, skip=None):
        out = []
        for e in ENGS:
            if e != skip and self.q[e]:
                for o in reversed(self.q[e]):
                    if not o.is_dma and o.fn is not None:
                        out.append(o)
                        break
            h = self.dma_hist[e]
            out.extend(h[-self.ring:])
        return out

    def barrier(self):
        deps_for = {e: self._last_ops(skip=e) for e in ENGS}
        for e in ENGS:
            op = _Op(e, len(self.q[e]), None, False)
            op.prog = self
            op.deps = deps_for[e]
            self.q[e].append(op)

    def finish(self):
        op = _Op('sp', len(self.q['sp']), None, False)
        op.prog = self
        for e in ENGS:
            op.deps.extend(self.dma_hist[e][-self.ring:])
        self.q['sp'].append(op)

    def emit(self):
        nc = self.nc
        self.set_lane(0)
        for e in ENGS:
            self.q[e].sort(key=lambda o: o.key)
            for i, o in enumerate(self.q[e]):
                o.idx = i
            h = [o for o in self.q[e] if o.is_dma]
            self.dma_hist[e] = h
            for k, o in enumerate(h):
                kg = k + self.sems.d_cnt[e]
                o.dslot = kg % self.ring
                o.dval = 16 * (kg // self.ring + 1)
                if k >= self.ring and h[k - self.ring] not in o.deps:
                    o.deps.append(h[k - self.ring])
        self.barrier()
        self.finish()
        for e in ENGS:
            seen = {p: -1 for p in ENGS}
            seen_dma = {}
            for o in self.q[e]:
                best = {}
                for d in o.deps:
                    if d.is_dma:
                        key = (d.eng, d.dslot)
                        if seen_dma.get(key, 0) >= d.dval:
                            continue
                        seen_dma[key] = d.dval
                        o.waits.append(d)
                    else:
                        if d.eng == 'pe' and e == 'pe':
                            continue
                        if d.fn is None:
                            continue
                        if d.idx <= seen[d.eng]:
                            continue
                        if d.eng not in best or best[d.eng].idx < d.idx:
                            best[d.eng] = d
                for p, d in best.items():
                    seen[p] = d.idx
                    d.needs_inc = True
                    o.waits.append(d)
        for e in ENGS:
            c = self.sems.c_off[e]
            for o in self.q[e]:
                if o.needs_inc:
                    c += 1
                o.cnt = c
            self.sems.c_off[e] = c
            self.sems.d_cnt[e] += len(self.dma_hist[e])
        n_wait = sum(len(o.waits) for e in ENGS for o in self.q[e])
        n_ops = sum(len(self.q[e]) for e in ENGS)
        self.stats = dict(n_ops=n_ops, n_wait=n_wait, per_eng={e: len(self.q[e]) for e in ENGS})
        with contextlib.ExitStack() as st:
            csem = self.sems.csem
            dsem = self.sems.dsem
            block = st.enter_context(nc.Block())

            def run(ename, eng):
                for o in self.q[ename]:
                    for d in o.waits:
                        if d.is_dma:
                            eng.wait_ge(dsem[d.eng][d.dslot], d.dval)
                        else:
                            eng.wait_ge(csem[d.eng], d.cnt)
                    if o.fn is None:
                        continue
                    ins = o.fn(eng)
                    if o.is_dma:
                        ins.then_inc(dsem[ename][o.dslot], 16)
                    elif o.needs_inc:
                        ins.then_inc(csem[ename], 1)

            @block.tensor
            def _(eng):
                run('pe', eng)

            @block.scalar
            def _(eng):
                run('act', eng)

            @block.vector
            def _(eng):
                run('dve', eng)

            @block.gpsimd
            def _(eng):
                run('pool', eng)

            @block.sync
            def _(eng):
                run('sp', eng)


D = 1024
IN_COLS = 3344
ALPHA = float(4 ** 0.25)
LN_EPS = 1e-5
RMS_EPS = 1e-5
GN_EPS = 64e-5
DECAY_C = float(np.exp(-0.5))
NEU_DT = BF16

W_NAMES = ["ln0_g", "ln0_b", "w_in", "conv_w", "conv_b", "dt_bias", "a_log", "d_skip", "ssd_norm_g",
           "mu_rwkv", "w0", "w_up", "a0", "a_up", "g_up", "k_k", "k_a", "r_k", "lnx_g", "lnx_b", "w_out",
           "ln1_g", "ln1_b", "w_fc", "w_proj", "ln2_g", "ln2_b"]
W_SHAPES = {
    "ln0_g": [1024], "ln0_b": [1024], "w_in": [2, 1024, 3344], "conv_w": [2, 5, 1024], "conv_b": [2, 1024],
    "dt_bias": [2, 2, 8], "a_log": [2, 2, 8], "d_skip": [2, 8], "ssd_norm_g": [2, 512], "mu_rwkv": [2, 1792],
    "w0": [2, 2, 512], "w_up": [2, 2, 64, 512], "a0": [2, 2, 512], "a_up": [2, 2, 64, 512], "g_up": [2, 128, 512],
    "k_k": [2, 512], "k_a": [2, 512], "r_k": [2, 8, 64], "lnx_g": [2, 512], "lnx_b": [2, 512],
    "w_out": [2, 1024, 1024], "ln1_g": [2, 1024], "ln1_b": [2, 1024], "w_fc": [2, 1024, 4096],
    "w_proj": [2, 4096, 1024], "ln2_g": [2, 1024], "ln2_b": [2, 1024],
}


def make_consts():
    i = np.arange(128)
    ident = np.eye(128, dtype=np.float32)
    U = (i[:, None] <= i[None, :]).astype(np.float32)
    Lo = (i[:, None] >= i[None, :]).astype(np.float32)
    Us = (i[:, None] < i[None, :]).astype(np.float32)
    Ls = (i[:, None] > i[None, :]).astype(np.float32)
    ones = np.ones((128, 128), np.float32)
    return np.ascontiguousarray(np.concatenate([ident, U, Lo, Us, Ls, ones], axis=1))


class Tile:
    def __init__(self, t, name=''):
        self.t = t
        self.buf = Buf(name)

    def __getitem__(self, k):
        return self.t[k]


class Ctx:
    pass


class _Quad:
    def __init__(self, tiles):
        self.tiles = tiles
        self.buf = Buf('quad')

    def __getitem__(self, k):
        p, i, c = k
        return self.tiles[i][p, c]


def layer_norm_tile(P, C, src, dst, g_t, b_t, stat, eps, r, w, geng='pool'):
    st = stat
    P.op('dve', lambda e: e.bn_stats(st[:, 0:6], src[:, 0:512]), P._bufs(r), [st.buf])
    P.op('dve', lambda e: e.bn_stats(st[:, 6:12], src[:, 512:1024]), P._bufs(r), [st.buf])
    P.op('dve', lambda e: e.bn_aggr(st[:, 12:14], st[:, 0:12].rearrange("p (a b) -> p a b", b=6)), [st.buf], [st.buf])
    P.ts('dve', st[:, 14:15], st[:, 13:14], eps, None, ALU.add, r=[st], w=[st])
    P.act(st[:, 15:16], st[:, 14:15], AF.Sqrt, r=[st], w=[st])
    P.op('dve', lambda e: e.reciprocal(st[:, 16:17], st[:, 15:16]), [st.buf], [st.buf])
    P.stt(st[:, 17:18], st[:, 12:13], -1.0, st[:, 16:17], ALU.mult, ALU.mult, r=[st], w=[st])
    P.act(dst, src, AF.Identity, bias=st[:, 17:18], scale=st[:, 16:17], r=list(r) + [st], w=w)
    P.tt(geng, dst, dst, g_t[:, :], ALU.mult, r=list(w) + [g_t], w=w)
    P.tt('dve', dst, dst, b_t[:, :], ALU.add, r=list(w) + [b_t], w=w)


_UID = [0]


_SBUSE = [0]
SB_BUDGET = 176 * 1024


def _sb_release(n):
    _SBUSE[0] -= n


def _sb(st, nc, name, shape, dt):
    _UID[0] += 1
    name = '%s_%d' % (name, _UID[0])
    n = int(np.prod(shape[1:])) * (2 if dt == BF16 else 4)
    n = (n + 31) // 32 * 32
    _SBUSE[0] += n
    assert _SBUSE[0] <= SB_BUDGET, ('SBUF budget exceeded', name, _SBUSE[0])
    t = Tile(st.enter_context(nc.sbuf_tensor(name, shape, dt)), name)
    st.callback(_sb_release, n)
    return t


def _psum(st, nc, n=8):
    _UID[0] += 1
    return [Tile(st.enter_context(nc.psum_tensor('ps%d_%d' % (i, _UID[0]), [128, 512], F32)), 'ps%d' % i) for i in range(n)]


class RR:
    def __init__(self, items):
        self.items = items
        self.i = 0

    def __call__(self):
        x = self.items[self.i % len(self.items)]
        self.i += 1
        return x


def load_bcast(P, tile, src_row, n, eng='sp'):
    P.ld(tile[:, 0:n], src_row.partition_broadcast(128), w=[tile], eng=eng)


def stage_consts(C):
    P = Prog(C.nc, C.sems)
    P.ld(C.cst[:, :], C.cst_d[:, :], w=[C.cst])
    P.emit()


def stage_A(C, l):
    nc, T, NS = C.nc, C.T, C.NS
    TB = min(512, T)
    NJ = TB // 128
    ident = C.cst[:, 0:128]
    with contextlib.ExitStack() as st:
        Win = _sb(st, nc, 'Win', [128, 8, IN_COLS], BF16)
        Wb = [Buf('Win%d' % k) for k in range(8)]
        hin = [_sb(st, nc, 'hin%d' % i, [128, D], F32) for i in range(2)]
        hT = [_sb(st, nc, 'hT%d' % i, [128, 8, TB], BF16) for i in range(2)]
        stat = [_sb(st, nc, 'stat%d' % i, [128, 32], F32) for i in range(2)]
        zo = [_sb(st, nc, 'zo%d' % i, [128, 512], F32) for i in range(2)]
        dto = [_sb(st, nc, 'dto%d' % i, [128, 16], F32) for i in range(2)]
        fo = [_sb(st, nc, 'fo%d' % i, [128, TB], BF16) for i in range(3)]
        if l == 0:
            g0 = _sb(st, nc, 'g0', [128, D], F32)
            b0 = _sb(st, nc, 'b0', [128, D], F32)
        ps = _psum(st, nc)
        P = Prog(nc, C.sems)
        for kc in range(8):
            P.ld(Win[:, kc, :], C.w['w_in'][l, kc * 128:(kc + 1) * 128, :], w=[Wb[kc]], eng='pool', max_dma_last_dim=4096)
        if l == 0:
            load_bcast(P, g0, C.w['ln0_g'], D)
            load_bcast(P, b0, C.w['ln0_b'], D)
        psr = RR(ps)
        evr = RR(['act', 'dve'])
        zor, dtor, forr = RR(zo), RR(dto), RR(fo)
        ti = 0
        for b in range(NS):
            src = C.x[b] if l == 0 else C.hbuf[b]
            for tb in range(T // TB):
                hTt = hT[(b * (T // TB) + tb) % 2]
                for j in range(NJ):
                    rows = slice(tb * TB + j * 128, tb * TB + (j + 1) * 128)
                    hi = hin[ti % 2]
                    P.ld(hi[:, :], src[rows, :], w=[hi])
                    if l == 0:
                        layer_norm_tile(P, C, hi[:, :], hi[:, :], g0, b0, stat[ti % 2], LN_EPS, r=[hi], w=[hi])
                        P.ld(C.hbuf[b, rows, :], hi[:, :], r=[hi])
                    for half in range(2):
                        bank = psr()
                        for q in range(4):
                            kc = half * 4 + q
                            P.tr(bank[:, q * 128:(q + 1) * 128], hi[:, kc * 128:(kc + 1) * 128], ident, r=[hi, C.cst], w=[bank])
                        P.cp(evr(), hTt[:, half * 4:half * 4 + 4, j * 128:(j + 1) * 128],
                             bank[:, :].rearrange("p (a b) -> p a b", b=128), r=[bank], w=[hTt])
                    ti += 1
                for j in range(NJ):
                    rows = slice(tb * TB + j * 128, tb * TB + (j + 1) * 128)
                    bank = psr()
                    for kc in range(8):
                        P.mm(bank[:, :], hTt[:, kc, j * 128:(j + 1) * 128], Win[:, kc, 0:512], start=(kc == 0), stop=(kc == 7),
                             r=[hTt, Wb[kc]], w=[bank])
                    z = zor()
                    P.cp(evr(), z[:, :], bank[:, :], r=[bank], w=[z])
                    P.ld(C.zbuf[b, rows, :], z[:, :], r=[z])
                    bank = psr()
                    for kc in range(8):
                        P.mm(bank[:, 0:16], hTt[:, kc, j * 128:(j + 1) * 128], Win[:, kc, 1536:1552], start=(kc == 0), stop=(kc == 7),
                             r=[hTt, Wb[kc]], w=[bank])
                    dt = dtor()
                    P.cp(evr(), dt[:, :], bank[:, 0:16], r=[bank], w=[dt])
                    P.ld(C.dtbuf[b, rows, :], dt[:, :], r=[dt])
                for cc in range(22):
                    col0 = 512 + cc * 128 if cc < 8 else 1552 + (cc - 8) * 128
                    bank = psr()
                    for kc in range(8):
                        P.mm(bank[:, 0:TB], Win[:, kc, col0:col0 + 128], hTt[:, kc, :], start=(kc == 0), stop=(kc == 7),
                             r=[hTt, Wb[kc]], w=[bank])
                    f = forr()
                    P.cp(evr(), f[:, :], bank[:, 0:TB], r=[bank], w=[f])
                    if cc < 8:
                        dst = C.xbcT[b, cc * 128:(cc + 1) * 128, tb * TB:(tb + 1) * TB]
                    else:
                        dst = C.rwT[b, (cc - 8) * 128:(cc - 7) * 128, tb * TB:(tb + 1) * TB]
                    P.ld(dst, f[:, :], r=[f])
        P.emit()
        C.stats.append(('A', P.stats))


def load_w_bf16(P, tile, bufs, src, nk, eng='pool'):
    for kc in range(nk):
        P.ld(tile[:, kc, :], src[kc * 128:(kc + 1) * 128, :], w=[bufs[kc]], eng=eng, max_dma_last_dim=4096)


def stage_D1(C, l, pre):
    nc, T, NS = C.nc, C.T, C.NS
    ident = C.cst[:, 0:128]
    with contextlib.ExitStack() as st:
        Wo = _sb(st, nc, 'Wo', [128, 8, D], BF16)
        Wob = [Buf('Wo%d' % k) for k in range(8)]
        g1 = _sb(st, nc, 'g1', [128, D], F32)
        b1 = _sb(st, nc, 'b1', [128, D], F32)
        ym = [_sb(st, nc, 'ym%d' % i, [128, D], F32) for i in range(2)]
        yT = [_sb(st, nc, 'yT%d' % i, [128, 8, 128], BF16) for i in range(2)]
        hr = [_sb(st, nc, 'hr%d' % i, [128, D], F32) for i in range(1)]
        t1 = [_sb(st, nc, 't1%d' % i, [128, D], F32) for i in range(1)]
        stat = [_sb(st, nc, 'stat%d' % i, [128, 32], F32) for i in range(2)]
        ps = _psum(st, nc)
        P = Prog(nc, C.sems)
        load_w_bf16(P, Wo, Wob, C.w['w_out'][l], 8)
        load_bcast(P, g1, C.w['ln1_g'][l], D)
        load_bcast(P, b1, C.w['ln1_b'][l], D)
        Wf, Wfb, Wp, Wpb = pre
        load_w_bf16(P, Wf, Wfb, C.w['w_fc'][l], 8)
        load_w_bf16(P, Wp, Wpb, C.w['w_proj'][l], 32)
        psr = RR(ps)
        evr = RR(['act', 'dve'])
        ti = 0
        for b in range(NS):
            for c in range(T // 128):
                rows = slice(c * 128, (c + 1) * 128)
                y, yt, h, t, sx = ym[ti % 2], yT[ti % 2], hr[0], t1[0], stat[ti % 2]
                P.ld(y[:, :], C.ymix[b, rows, :], w=[y])
                P.ld(h[:, :], C.hbuf[b, rows, :], w=[h])
                for half in range(2):
                    bank = psr()
                    for q in range(4):
                        kc = half * 4 + q
                        P.tr(bank[:, q * 128:(q + 1) * 128], y[:, kc * 128:(kc + 1) * 128], ident, r=[y, C.cst], w=[bank])
                    P.cp(evr(), yt[:, half * 4:half * 4 + 4, :], bank[:, :].rearrange("p (a b) -> p a b", b=128), r=[bank], w=[yt])
                for half in range(2):
                    bank = psr()
                    for kc in range(8):
                        P.mm(bank[:, :], yt[:, kc, :], Wo[:, kc, half * 512:(half + 1) * 512], start=(kc == 0), stop=(kc == 7),
                             r=[yt, Wob[kc]], w=[bank])
                    P.stt(t[:, half * 512:(half + 1) * 512], h[:, half * 512:(half + 1) * 512], ALPHA, bank[:, :], ALU.mult, ALU.add,
                          r=[h, bank], w=[t])
                layer_norm_tile(P, C, t[:, :], t[:, :], g1, b1, sx, LN_EPS, r=[t], w=[t], geng='dve')
                P.ld(C.h1buf[b, rows, :], t[:, :], r=[t])
                ti += 1
        P.emit()
        C.stats.append(('D1', P.stats))


def stage_D2(C, l, last, pre):
    nc, T, NS = C.nc, C.T, C.NS
    ident = C.cst[:, 0:128]
    TB = 256
    with contextlib.ExitStack() as st:
        Wf, Wfb, Wp, Wpb = pre
        g2 = _sb(st, nc, 'g2', [128, D], F32)
        b2 = _sb(st, nc, 'b2', [128, D], F32)
        h1 = [_sb(st, nc, 'h1%d' % i, [128, D], F32) for i in range(2)]
        h1T = _sb(st, nc, 'h1T', [128, 8, TB], BF16)
        aT = _sb(st, nc, 'aT', [128, 32, TB], BF16)
        tmp = [_sb(st, nc, 'tmp%d' % i, [128, TB], F32) for i in range(2)]
        t2 = [_sb(st, nc, 't2%d' % i, [128, D], F32) for i in range(1)]
        stat = [_sb(st, nc, 'stat%d' % i, [128, 32], F32) for i in range(2)]
        ps = _psum(st, nc)
        P = Prog(nc, C.sems)
        load_bcast(P, g2, C.w['ln2_g'][l], D)
        load_bcast(P, b2, C.w['ln2_b'][l], D)
        psr = RR(ps)
        evr = RR(['act', 'dve'])
        tmr = RR(tmp)
        ti = 0
        for b in range(NS):
            dstb = C.out[b] if last else C.hbuf[b]
            for tb in range(T // TB):
                for j in range(2):
                    rows = slice(tb * TB + j * 128, tb * TB + (j + 1) * 128)
                    h = h1[j]
                    P.ld(h[:, :], C.h1buf[b, rows, :], w=[h])
                    for half in range(2):
                        bank = psr()
                        for q in range(4):
                            kc = half * 4 + q
                            P.tr(bank[:, q * 128:(q + 1) * 128], h[:, kc * 128:(kc + 1) * 128], ident, r=[h, C.cst], w=[bank])
                        P.cp(evr(), h1T[:, half * 4:half * 4 + 4, j * 128:(j + 1) * 128],
                             bank[:, :].rearrange("p (a b) -> p a b", b=128), r=[bank], w=[h1T])
                for fc in range(32):
                    bank = psr()
                    for kc in range(8):
                        P.mm(bank[:, 0:TB], Wf[:, kc, fc * 128:(fc + 1) * 128], h1T[:, kc, :], start=(kc == 0), stop=(kc == 7),
                             r=[h1T, Wfb[kc]], w=[bank])
                    tm = tmr()
                    if fc % 2 == 0:
                        P.act(tm[:, :], bank[:, 0:TB], AF.Relu, r=[bank], w=[tm])
                    else:
                        P.ts('dve', tm[:, :], bank[:, 0:TB], 0.0, None, ALU.max, r=[bank], w=[tm])
                    P.tt('pool', aT[:, fc, :], tm[:, :], tm[:, :], ALU.mult, r=[tm], w=[aT])
                for j in range(2):
                    rows = slice(tb * TB + j * 128, tb * TB + (j + 1) * 128)
                    h = h1[j]
                    t = t2[0]
                    for half in range(2):
                        bank = psr()
                        for fc in range(32):
                            P.mm(bank[:, :], aT[:, fc, j * 128:(j + 1) * 128], Wp[:, fc, half * 512:(half + 1) * 512],
                                 start=(fc == 0), stop=(fc == 31), r=[aT, Wpb[fc]], w=[bank])
                        P.stt(t[:, half * 512:(half + 1) * 512], h[:, half * 512:(half + 1) * 512], ALPHA, bank[:, :], ALU.mult, ALU.add,
                              r=[h, bank], w=[t])
                    layer_norm_tile(P, C, t[:, :], t[:, :], g2, b2, stat[ti % 2], LN_EPS, r=[t], w=[t])
                    P.ld(dstb[rows, :], t[:, :], r=[t])
                    ti += 1
        P.emit()
        C.stats.append(('D2', P.stats))


def bc3(ap2, n, axis):
    k = ap2.shape[1]
    if axis == 1:
        return ap2.unsqueeze(1).broadcast_to([128, n, k])
    return ap2.unsqueeze(2).broadcast_to([128, k, n])


def stage_B(C, l, b):
    nc, T, NCH = C.nc, C.T, C.NCH
    TB = min(512, T)
    cst = C.cst
    ident, U, Lo, Us, Ls, ones = (cst[:, i * 128:(i + 1) * 128] for i in range(6))
    with contextlib.ExitStack() as st:
        sb = lambda n, sh, dt: _sb(st, nc, n, sh, dt)
        XTb = [Buf('XT%d' % g) for g in range(8)]
        BT = sb('BT', [128, 2, T], BF16)
        CT = sb('CT', [128, 2, T], BF16)
        xbf = sb('xbf', [128, NCH, 512], BF16)
        Btok = sb('Btok', [128, NCH, 256], BF16)
        dtraw = sb('dtraw', [128, NCH, 16], F32)
        dtv = sb('dtv', [128, NCH, 16], F32)
        av = sb('av', [128, NCH, 16], F32)
        dtb = sb('dtb', [128, 16], F32)
        negA = sb('negA', [128, 16], F32)
        dsk8 = sb('dsk8', [128, 8], F32)
        dsk = sb('dsk', [128, 512], F32)
        ng = sb('ng', [128, 512], F32)
        ps = _psum(st, nc)
        st0 = contextlib.ExitStack()
        sb0 = lambda n, sh, dt: _sb(st0, nc, n, sh, dt)
        XT = sb0('XT', [128, 8, T + 4], BF16)
        cw6 = sb0('cw6', [6, 1024], F32)
        cwb = sb0('cwb', [128, 8, 6], F32)
        Dg = sb0('Dg', [128, 5, 8, 128], BF16)
        cbrow = sb0('cbrow', [1, 768], F32)
        cbrow_bf = sb0('cbrow_bf', [1, 768], BF16)
        ones_bf = sb0('ones_bf', [1, 128], BF16)
        P = Prog(nc, C.sems)
        psr = RR(ps)
        P.op('pool', lambda e: e.memset(XT[:, :, 0:2], 0.0), [], XTb)
        P.op('pool', lambda e: e.memset(XT[:, :, T + 2:T + 4], 0.0), [], XTb)
        for g in range(8):
            P.ld(XT[:, g, 2:T + 2], C.xbcT[b, g * 128:(g + 1) * 128, :], w=[XTb[g]])
        P.ld(cw6[0:5, :], C.w['conv_w'][l], w=[cw6])
        P.ld(cw6[5:6, :], C.w['conv_b'][l:l + 1, :], w=[cw6])
        P.ld(cbrow[0:1, :], C.w['conv_b'][l:l + 1, 0:768], w=[cbrow])
        P.cp('dve', cbrow_bf[0:1, :], cbrow[0:1, :], r=[cbrow], w=[cbrow_bf])
        P.cp('dve', ones_bf[0:1, :], ones[0:1, :], r=[cst], w=[ones_bf])
        load_bcast(P, dtb, C.w['dt_bias'][l].rearrange("a b -> (a b)"), 16)
        load_bcast(P, negA, C.w['a_log'][l].rearrange("a b -> (a b)"), 16)
        load_bcast(P, dsk8, C.w['d_skip'][l], 8)
        load_bcast(P, ng, C.w['ssd_norm_g'][l], 512)
        P.act(negA[:, :], negA[:, :], AF.Exp, r=[negA], w=[negA])
        P.ts('dve', negA[:, :], negA[:, :], -1.0, None, ALU.mult, r=[negA], w=[negA])
        P.cp('dve', dsk[:, :].rearrange("p (h q) -> p h q", q=64), bc3(dsk8[:, :], 64, 2), r=[dsk8], w=[dsk])
        bank = psr()
        for g in range(8):
            P.tr(bank[:, g * 6:(g + 1) * 6], cw6[0:6, g * 128:(g + 1) * 128], ident[0:6, 0:6], r=[cw6, cst], w=[bank])
        P.cp('dve', cwb[:, :, :], bank[:, 0:48].rearrange("p (g k) -> p g k", k=6), r=[bank], w=[cwb])
        er = RR(['dve', 'pool'])
        for k in range(5):
            for g in range(8):
                P.ts(er(), Dg[:, k, g, :], ident, cwb[:, g, k:k + 1], None, ALU.mult, r=[cwb, cst], w=[Dg])
        P.ld(dtraw[:, :, :], C.dtbuf[b].rearrange("(c p) k -> p c k", p=128), w=[dtraw])
        P.tt('dve', dtv[:, :, :], dtraw[:, :, :], bc3(dtb[:, :], NCH, 1), ALU.add, r=[dtraw, dtb], w=[dtv])
        P.act(dtv[:, :, :], dtv[:, :, :], AF.Exp, r=[dtv], w=[dtv])
        P.ts('dve', dtv[:, :, :], dtv[:, :, :], 1.0, None, ALU.add, r=[dtv], w=[dtv])
        P.act(dtv[:, :, :], dtv[:, :, :], AF.Ln, r=[dtv], w=[dtv])
        P.tt('dve', av[:, :, :], dtv[:, :, :], bc3(negA[:, :], NCH, 1), ALU.mult, r=[dtv, negA], w=[av])
        for c in range(NCH):
            for (g0, ng_, dst, boff) in ((0, 4, xbf, 0), (4, 2, Btok, 512)):
                bank = psr()
                n = ng_ * 128
                P.mm(bank[:, 0:n], ones_bf[0:1, :], cbrow_bf[0:1, boff:boff + n], start=True, stop=False,
                     r=[ones_bf, cbrow_bf], w=[bank])
                for gi in range(ng_):
                    g = g0 + gi
                    for k in range(5):
                        P.mm(bank[:, gi * 128:(gi + 1) * 128], XT[:, g, c * 128 + k:c * 128 + k + 128], Dg[:, k, g, :],
                             start=False, stop=(gi == ng_ - 1 and k == 4), r=[XTb[g], Dg], w=[bank])
                P.act(dst[:, c, :], bank[:, 0:n], AF.Silu, r=[bank], w=[dst])
        for tb in range(T // TB):
            for gi in range(4):
                g = 4 + gi
                bank = psr()
                for k in range(5):
                    P.mm(bank[:, 0:TB], Dg[:, k, g, :], XT[:, g, tb * TB + k:tb * TB + k + TB], start=(k == 0), stop=(k == 4),
                         r=[XTb[g], Dg], w=[bank])
                dst = BT if gi < 2 else CT
                P.act(dst[:, gi % 2, tb * TB:(tb + 1) * TB], bank[:, 0:TB], AF.Silu, bias=cwb[:, g, 5:6], r=[bank, cwb], w=[dst])
        P.emit()
        C.stats.append(('B0', P.stats))
        st0.close()
        ysc = sb('ysc', [128, NCH, 16], F32)
        cs_sb = [sb('cs_sb%d' % i, [128, 32], F32) for i in range(2)]
        dd = [sb('dd%d' % i, [128, 16], F32) for i in range(2)]
        et = [sb('et%d' % i, [128, 16], F32) for i in range(2)]
        wd = [sb('wd%d' % i, [128, 16], F32) for i in range(2)]
        xw = [sb('xw%d' % i, [128, 2, 512], BF16) for i in range(2)]
        Srun = sb('Srun', [128, 2, 512], F32)
        Sin = sb('Sin', [128, NCH, 2, 512], BF16)
        rhsS = [sb('rhsS%d' % i, [128, 2, 8, 128], F32) for i in range(2)]
        E = [sb('E%d' % i, [128, 2, 8, 128], F32) for i in range(2)]
        SM = [sb('SM%d' % i, [128, 2, 2, 128], F32) for i in range(2)]
        G = [sb('G%d' % i, [128, 2, 8, 128], BF16) for i in range(2)]
        ta = [sb('ta%d' % i, [128, 512], F32) for i in range(2)]
        tb_ = [sb('tb%d' % i, [128, 512], F32) for i in range(2)]
        yt = [sb('yt%d' % i, [128, 512], F32) for i in range(2)]
        zt = [sb('zt%d' % i, [128, 512], F32) for i in range(2)]
        sq = [sb('sq%d' % i, [128, 512], F32) for i in range(2)]
        ss = [sb('ss%d' % i, [128, 8], F32) for i in range(2)]
        P = Prog(nc, C.sems)
        psr = RR(ps)
        P.op('pool', lambda e: e.memset(Srun[:, :, :], 0.0), [], [Srun.buf])
        for i in range(NCH):
            cc = (i, NCH - 1 - i)
            k2 = i % 2
            bank = psr()
            for d in range(2):
                P.mm(bank[:, d * 8:(d + 1) * 8], U if d == 0 else Lo, av[:, cc[d], d * 8:(d + 1) * 8], r=[cst, av], w=[bank])
                P.mm(bank[:, 16 + d * 8:16 + (d + 1) * 8], ones, av[:, cc[d], d * 8:(d + 1) * 8], r=[cst, av], w=[bank])
            P.cp('act', cs_sb[k2][:, :], bank[:, 0:32], r=[bank], w=[cs_sb[k2]])
            P.tt('dve', dd[k2][:, :], cs_sb[k2][:, 16:32], cs_sb[k2][:, 0:16], ALU.subtract, r=[cs_sb[k2]], w=[dd[k2]])
            P.act(dd[k2][:, :], dd[k2][:, :], AF.Exp, r=[dd[k2]], w=[dd[k2]])
            P.act(et[k2][:, :], cs_sb[k2][:, 16:32], AF.Exp, r=[cs_sb[k2]], w=[et[k2]])
            for d in range(2):
                sl = slice(d * 8, (d + 1) * 8)
                P.act(ysc[:, cc[d], sl], cs_sb[k2][:, sl], AF.Exp, r=[cs_sb[k2]], w=[ysc])
                P.tt('dve', wd[k2][:, sl], dd[k2][:, sl], dtv[:, cc[d], sl], ALU.mult, r=[dd[k2], dtv], w=[wd[k2]])
                P.tt('dve', xw[k2][:, d, :].rearrange("p (h q) -> p h q", q=64),
                     xbf[:, cc[d], :].rearrange("p (h q) -> p h q", q=64), bc3(wd[k2][:, sl], 64, 2), ALU.mult,
                     r=[xbf, wd[k2]], w=[xw[k2]])
            for d in range(2):
                bank = psr()
                for g in range(2):
                    P.mm(bank[:, g * 256:(g + 1) * 256], Btok[:, cc[d], g * 128:(g + 1) * 128], xw[k2][:, d, g * 256:(g + 1) * 256],
                         r=[Btok, xw[k2]], w=[bank])
                P.cp('pool', Sin[:, cc[d], d, :], Srun[:, d, :], r=[Srun], w=[Sin])
                P.tt('dve', Srun[:, d, :].rearrange("p (h q) -> p h q", q=64), Srun[:, d, :].rearrange("p (h q) -> p h q", q=64),
                     bc3(et[k2][:, d * 8:(d + 1) * 8], 64, 2), ALU.mult, r=[Srun, et[k2]], w=[Srun])
                P.tt('dve', Srun[:, d, :], Srun[:, d, :], bank[:, :], ALU.add, r=[Srun, bank], w=[Srun])
        def pass2(par):
            psr = RR(ps[4 * par:4 * par + 4])
            for c in range(par, NCH, 2):
                k2 = c % 2
                rows = slice(c * 128, (c + 1) * 128)
                csl = slice(c * 128, (c + 1) * 128)
                P.ld(zt[k2][:, :], C.zbuf[b, rows, :], w=[zt[k2]])
                for d in range(2):
                    P.tt('dve' if d == 0 else 'pool', rhsS[k2][:, d, :, :], bc3(U if d == 0 else Lo, 8, 1),
                         bc3(av[:, c, d * 8:(d + 1) * 8], 128, 2), ALU.mult, r=[cst, av], w=[rhsS[k2]])
                yield
                for d in range(2):
                    for hh in range(2):
                        bank = psr()
                        P.mm(bank[:, :], Ls if d == 0 else Us, rhsS[k2][:, d, hh * 4:(hh + 1) * 4, :].rearrange("p a b -> p (a b)"),
                             r=[cst, rhsS[k2]], w=[bank])
                        P.act(E[k2][:, d, hh * 4:(hh + 1) * 4, :].rearrange("p a b -> p (a b)"), bank[:, :], AF.Exp, r=[bank], w=[E[k2]])
                yield
                bank = psr()
                for g in range(2):
                    P.mm(bank[:, g * 128:(g + 1) * 128], BT[:, g, csl], CT[:, g, csl], r=[BT, CT], w=[bank])
                for d in range(2):
                    P.tt('dve', SM[k2][:, d, :, :], bank[:, 0:256].rearrange("p (g l) -> p g l", l=128), bc3(U if d == 0 else Lo, 2, 1),
                         ALU.mult, r=[bank, cst], w=[SM[k2]])
                yield
                for d in range(2):
                    for h in range(8):
                        P.stt(G[k2][:, d, h, :], E[k2][:, d, h, :], dtv[:, c, d * 8 + h:d * 8 + h + 1], SM[k2][:, d, h // 4, :],
                              ALU.mult, ALU.mult, r=[E[k2], dtv, SM[k2]], w=[G[k2]])
                yield
                bY1 = psr()
                for h in range(8):
                    for d in range(2):
                        P.mm(bY1[:, h * 64:(h + 1) * 64], G[k2][:, d, h, :], xbf[:, c, h * 64:(h + 1) * 64], start=(d == 0), stop=(d == 1),
                             r=[G[k2], xbf], w=[bY1])
                yield
                bY2 = [psr(), psr()]
                for d in range(2):
                    for g in range(2):
                        P.mm(bY2[d][:, g * 256:(g + 1) * 256], CT[:, g, csl], Sin[:, c, d, g * 256:(g + 1) * 256], r=[CT, Sin], w=[bY2[d]])
                yield
                v3 = lambda ap: ap.rearrange("p (h q) -> p h q", q=64)
                P.tt('dve', v3(ta[k2][:, :]), v3(bY2[0][:, :]), bc3(ysc[:, c, 0:8], 64, 2), ALU.mult, r=[bY2[0], ysc], w=[ta[k2]])
                P.tt('dve', v3(tb_[k2][:, :]), v3(bY2[1][:, :]), bc3(ysc[:, c, 8:16], 64, 2), ALU.mult, r=[bY2[1], ysc], w=[tb_[k2]])
                y = yt[k2]
                P.tt('pool', y[:, :], ta[k2][:, :], tb_[k2][:, :], ALU.add, r=[ta[k2], tb_[k2]], w=[y])
                P.tt('dve', y[:, :], y[:, :], bY1[:, :], ALU.add, r=[y, bY1], w=[y])
                P.tt('pool', sq[k2][:, :], xbf[:, c, :], dsk[:, :], ALU.mult, r=[xbf, dsk], w=[sq[k2]])
                P.tt('pool', y[:, :], y[:, :], sq[k2][:, :], ALU.add, r=[y, sq[k2]], w=[y])
                yield
                P.act(zt[k2][:, :], zt[k2][:, :], AF.Silu, r=[zt[k2]], w=[zt[k2]])
                P.tt('dve', y[:, :], y[:, :], zt[k2][:, :], ALU.mult, r=[y, zt[k2]], w=[y])
                P.tt('pool', sq[k2][:, :], y[:, :], y[:, :], ALU.mult, r=[y], w=[sq[k2]])
                P.op('dve', lambda e, o=ss[k2][:, 0:2], i_=sq[k2][:, :].rearrange("p (g q) -> p g q", q=256): e.reduce_sum(o, i_, axis=AX.X),
                     [sq[k2].buf], [ss[k2].buf])
                yield
                P.ts('dve', ss[k2][:, 2:4], ss[k2][:, 0:2], 1.0 / 256.0, RMS_EPS, ALU.mult, ALU.add, r=[ss[k2]], w=[ss[k2]])
                P.act(ss[k2][:, 4:6], ss[k2][:, 2:4], AF.Sqrt, r=[ss[k2]], w=[ss[k2]])
                P.op('dve', lambda e, o=ss[k2][:, 6:8], i_=ss[k2][:, 4:6]: e.reciprocal(o, i_), [ss[k2].buf], [ss[k2].buf])
                for g in range(2):
                    P.ts('dve', y[:, g * 256:(g + 1) * 256], y[:, g * 256:(g + 1) * 256], ss[k2][:, 6 + g:7 + g], None, ALU.mult,
                         r=[y, ss[k2]], w=[y])
                P.tt('pool', y[:, :], y[:, :], ng[:, :], ALU.mult, r=[y, ng], w=[y])
                P.ld(C.ymix[b, rows, 0:512], y[:, :], r=[y])
                yield
        for par_ in range(2):
            P.set_lane(par_ + 1)
            for _ in pass2(par_):
                pass
        P.set_lane(0)
        P.emit()
        C.stats.append(('B', P.stats))


def stage_C(C, l, b):
    nc, T, NCH = C.nc, C.T, C.NCH
    cst = C.cst
    ident, U, Lo, Us, Ls, ones = (cst[:, i * 128:(i + 1) * 128] for i in range(6))
    v3 = lambda ap: ap.rearrange("p (h q) -> p h q", q=64)
    with contextlib.ExitStack() as st:
        sb = lambda n, sh, dt: _sb(st, nc, n, sh, dt)
        murow = sb('murow', [14, 128], F32)
        muT = sb('muT', [128, 3, 14], F32)
        Dm = sb('Dm', [128, 14, 128], BF16)
        Dh = sb('Dh', [128, 14, 128], BF16)
        kkb = sb('kkb', [128, 512], F32)
        kab = sb('kab', [128, 512], F32)
        rkb = sb('rkb', [128, 512], F32)
        w0b = sb('w0b', [128, 2, 512], F32)
        a0b = sb('a0b', [128, 2, 512], F32)
        LW = sb('LW', [128, 2, 512], BF16)
        GU = sb('GU', [128, 512], BF16)
        MK = sb('MK', [128, 2, 512], F32)
        MKa = sb('MKa', [128, 2, 128], F32)
        g32 = sb('g32', [128, 512], F32)
        def alloc_inst():
            RTc = [sb('RTc%d' % i, [128, 14, 130], BF16) for i in range(2)]
            bon = sb('bon', [128, NCH, 8], F32)
            r32 = sb('r32', [128, 512], F32)
            k32 = sb('k32', [128, 512], F32)
            v32 = sb('v32', [128, 512], F32)
            vbf = sb('vbf', [128, 512], BF16)
            LT = sb('LT', [128, 128], BF16)
            sg = sb('sg', [128, 128], BF16)
            lw32 = sb('lw32', [128, 512], F32)
            a32 = sb('a32', [128, 512], F32)
            kk = sb('kk', [128, 512], F32)
            tmp = sb('tmp', [128, 512], F32)
            tmp2 = sb('tmp2', [128, 512], F32)
            kd = sb('kd', [128, 512], F32)
            bb = sb('bb', [128, 512], F32)
            sm8 = sb('sm8', [128, 32], F32)
            gC = sb('gC', [128, 4], F32)
            Ep = sb('Ep', [128, 512], F32)
            En = sb('En', [128, 512], F32)
            bt_bf = sb('bt_bf', [128, 512], BF16)
            kdt_bf = sb('kdt_bf', [128, 512], BF16)
            kt_bf = sb('kt_bf', [128, 512], NEU_DT)
            KR = sb('KR', [128, 4, 2, 128], BF16)
            btT = sb('btT', [128, 4, 128], BF16)
            kdtT = sb('kdtT', [128, 4, 128], BF16)
            R3 = sb('R3', [128, 8, 3, 128], BF16)
            NM = [sb('NM%d' % i, [128, 8, 2, 128], NEU_DT) for i in range(2)]
            X = [sb('X%d' % i, [128, 8, 128], NEU_DT) for i in range(2)]
            Z1 = sb('Z1', [128, 512], NEU_DT)
            Ut = sb('Ut', [128, 512], F32)
            WT = sb('WT', [128, 4, 128], BF16)
            Mst = sb('Mst', [128, 4, 128], F32)
            Mbf = sb('Mbf', [128, 4, 128], BF16)
            Un = sb('Un', [128, 512], BF16)
            Yo = [sb('Yo%d' % i, [128, 512], F32) for i in range(2)]
            return dict(locals())
        IT = [alloc_inst(), alloc_inst()]
        ps = _psum(st, nc)
        P = Prog(nc, C.sems)
        psr = RR(ps)
        evr = RR(['act'])
        P.ld(murow[0:14, :], C.w['mu_rwkv'][l].rearrange("(g p) -> g p", p=128), w=[murow])
        bank = psr()
        P.tr(bank[:, 0:14], murow[0:14, :], ident[0:14, 0:14], r=[murow, cst], w=[bank])
        P.cp('dve', muT[:, 0, :], bank[:, 0:14], r=[bank], w=[muT])
        P.ts('dve', muT[:, 1, :], muT[:, 0, :], -1.0, 1.0, ALU.mult, ALU.add, r=[muT], w=[muT])
        P.ts('dve', muT[:, 2, :], muT[:, 0, :], 0.5, None, ALU.mult, r=[muT], w=[muT])
        er = RR(['dve', 'pool'])
        for g in range(14):
            P.ts(er(), Dm[:, g, :], ident, muT[:, 1, g:g + 1], None, ALU.mult, r=[muT, cst], w=[Dm])
            P.ts(er(), Dh[:, g, :], ident, muT[:, 2, g:g + 1], None, ALU.mult, r=[muT, cst], w=[Dh])
        load_bcast(P, kkb, C.w['k_k'][l], 512)
        load_bcast(P, kab, C.w['k_a'][l], 512)
        load_bcast(P, rkb, C.w['r_k'][l].rearrange("a b -> (a b)"), 512)
        for d in range(2):
            P.ld(w0b[:, d, :], C.w['w0'][l, d].partition_broadcast(128), w=[w0b])
            P.ld(a0b[:, d, :], C.w['a0'][l, d].partition_broadcast(128), w=[a0b])
            P.ld(LW[0:64, d, :], C.w['w_up'][l, d], w=[LW], eng='pool')
            P.ld(LW[64:128, d, :], C.w['a_up'][l, d], w=[LW], eng='pool')
        P.ld(GU[:, :], C.w['g_up'][l], w=[GU], eng='pool')
        for d in range(2):
            strict, incl, strict_ts = (Us, U, Ls) if d == 0 else (Ls, Lo, Us)
            P.ts('dve', MK[:, d, 0:128], strict, -1.0, None, ALU.mult, r=[cst], w=[MK])
            P.cp('dve', MK[:, d, 128:256], incl, r=[cst], w=[MK])
            P.cp('dve', MK[:, d, 256:384], strict, r=[cst], w=[MK])
            P.cp('dve', MK[:, d, 384:512], incl, r=[cst], w=[MK])
            P.ts('dve', MKa[:, d, :], strict_ts, -1.0, None, ALU.mult, r=[cst], w=[MKa])
        taps = (Dh, Dm, Dh)
        ydb = [[Buf('yd') for _ in range(NCH)] for _ in range(2)]
        vgb = [[Buf('vg') for _ in range(NCH)] for _ in range(2)]
        def inst(d):
            I_ = IT[d]
            psr = RR(ps[4 * d:4 * d + 4])
            (RTc, r32, k32, v32, vbf, LT, sg, lw32, a32, kk, tmp, tmp2, kd, bb, sm8, gC, Ep, En, bt_bf, kdt_bf, kt_bf, KR, btT, kdtT, R3, NM, X, Z1, Ut, WT, Mst, Mbf, Un, Yo, bon) = (I_[n] for n in ('RTc', 'r32', 'k32', 'v32', 'vbf', 'LT', 'sg', 'lw32', 'a32', 'kk', 'tmp', 'tmp2', 'kd', 'bb', 'sm8', 'gC', 'Ep', 'En', 'bt_bf', 'kdt_bf', 'kt_bf', 'KR', 'btT', 'kdtT', 'R3', 'NM', 'X', 'Z1', 'Ut', 'WT', 'Mst', 'Mbf', 'Un', 'Yo', 'bon'))
            P.op('pool', lambda e: e.memset(Mst[:, :, :], 0.0), [], [Mst.buf])
            P.op('pool', lambda e: e.memset(Mbf[:, :, :], 0.0), [], [Mbf.buf])
            for ci in range(NCH):
                c = ci if d == 0 else NCH - 1 - ci
                rows = slice(c * 128, (c + 1) * 128)
                RT = RTc[ci % 2]
                lo, hi = max(c * 128 - 1, 0), min(c * 128 + 129, T)
                if c == 0:
                    P.op('pool', lambda e, t=RT: e.memset(t[:, :, 0:1], 0.0), [], [RT.buf])
                if c == NCH - 1:
                    P.op('pool', lambda e, t=RT: e.memset(t[:, :, 129:130], 0.0), [], [RT.buf])
                P.ld(RT[:, :, lo - (c * 128 - 1):hi - (c * 128 - 1)],
                     C.rwT[b].rearrange("(g p) t -> p g t", p=128)[:, :, lo:hi], w=[RT])
                for (g0, dst) in ((0, r32), (4, k32), (8, v32)):
                    bank = psr()
                    for gi in range(4):
                        g = g0 + gi
                        for k in range(3):
                            P.mm(bank[:, gi * 128:(gi + 1) * 128], RT[:, g, k:k + 128], taps[k][:, g, :],
                                 start=(k == 0), stop=(k == 2), r=[RT, Dm, Dh], w=[bank])
                    P.cp(evr(), dst[:, :], bank[:, :], r=[bank], w=[dst])
                    yield
                P.cp('act', vbf[:, :], v32[:, :], r=[v32], w=[vbf])
                bank = psr()
                for gi in range(2):
                    g = 12 + gi
                    for k in range(3):
                        P.mm(bank[:, gi * 128:(gi + 1) * 128], taps[k][:, g, :], RT[:, g, k:k + 128],
                             start=(k == 0), stop=(k == 2), r=[RT, Dm, Dh], w=[bank])
                P.act(LT[0:64, :], bank[0:64, 0:128], AF.Tanh, r=[bank], w=[LT])
                P.cp('dve', LT[64:128, :], bank[64:128, 0:128], r=[bank], w=[LT])
                P.act(sg[:, :], bank[:, 128:256], AF.Sigmoid, r=[bank], w=[sg])
                yield
                bW, bA = psr(), psr()
                P.mm(bW[:, :], LT[0:64, :], LW[0:64, d, :], r=[LT, LW], w=[bW])
                yield
                P.mm(bA[:, :], LT[64:128, :], LW[64:128, d, :], r=[LT, LW], w=[bA])
                yield
                P.tt('dve', lw32[:, :], bW[:, :], w0b[:, d, :], ALU.add, r=[bW, w0b], w=[lw32])
                yield
                P.act(lw32[:, :], lw32[:, :], AF.Sigmoid, r=[lw32], w=[lw32])
                yield
                P.act(lw32[:, :], lw32[:, :], AF.Identity, scale=-DECAY_C, r=[lw32], w=[lw32])
                yield
                P.tt('dve', a32[:, :], bA[:, :], a0b[:, d, :], ALU.add, r=[bA, a0b], w=[a32])
                yield
                P.act(a32[:, :], a32[:, :], AF.Sigmoid, r=[a32], w=[a32])
                yield
                if d == 0:
                    bG = psr()
                    P.mm(bG[:, :], sg[:, :], GU[:, :], r=[sg, GU], w=[bG])
                    P.cp('act', g32[:, :], bG[:, :], r=[bG], w=[g32])
                    P.ld(C.vg[0, rows, :], v32[:, :], r=[v32], w=[vgb[0][c]])
                    P.ld(C.vg[1, rows, :], g32[:, :], r=[g32], w=[vgb[1][c]])
                yield
                P.tt('pool', kk[:, :], k32[:, :], kkb[:, :], ALU.mult, r=[k32, kkb], w=[kk])
                yield
                P.tt('pool', tmp[:, :], kk[:, :], kk[:, :], ALU.mult, r=[kk], w=[tmp])
                yield
                P.op('dve', lambda e, o=sm8[:, 0:8], i_=v3(tmp[:, :]): e.reduce_sum(o, i_, axis=AX.X), [tmp.buf], [sm8.buf])
                yield
                P.act(sm8[:, 8:16], sm8[:, 0:8], AF.Sqrt, r=[sm8], w=[sm8])
                yield
                P.ts('dve', sm8[:, 8:16], sm8[:, 8:16], 1e-12, None, ALU.max, r=[sm8], w=[sm8])
                yield
                P.op('dve', lambda e, o=sm8[:, 16:24], i_=sm8[:, 8:16]: e.reciprocal(o, i_), [sm8.buf], [sm8.buf])
                yield
                P.tt('dve', v3(kk[:, :]), v3(kk[:, :]), bc3(sm8[:, 16:24], 64, 2), ALU.mult, r=[kk, sm8], w=[kk])
                yield
                P.stt(tmp2[:, :], a32[:, :], -1.0, kab[:, :], ALU.add, ALU.mult, r=[a32, kab], w=[tmp2])
                yield
                P.stt(kd[:, :], tmp2[:, :], 1.0, k32[:, :], ALU.add, ALU.mult, r=[tmp2, k32], w=[kd])
                yield
                P.tt('pool', tmp[:, :], r32[:, :], kd[:, :], ALU.mult, r=[r32, kd], w=[tmp])
                yield
                P.tt('pool', tmp[:, :], tmp[:, :], rkb[:, :], ALU.mult, r=[tmp, rkb], w=[tmp])
                yield
                P.op('dve', lambda e, o=bon[:, c, :], i_=v3(tmp[:, :]): e.reduce_sum(o, i_, axis=AX.X), [tmp.buf], [bon.buf])
                yield
                P.tt('pool', bb[:, :], kk[:, :], a32[:, :], ALU.mult, r=[kk, a32], w=[bb])
                yield
                yield
                bC = psr()
                P.mm(bC[:, :], U if d == 0 else Lo, lw32[:, :], r=[cst, lw32], w=[bC])
                yield
                bT = psr()
                for jg in range(4):
                    P.mm(bT[:, 2 * jg:2 * jg + 2], lw32[:, jg * 128:(jg + 1) * 128], ones[:, 0:2], r=[lw32, cst], w=[bT])
                P.act(gC[:, :], bT[:, 0:8].rearrange("p (j two) -> p j two", two=2)[:, :, 0], AF.Exp, r=[bT], w=[gC])
                yield
                P.act(Ep[:, :], bC[:, :], AF.Exp, r=[bC], w=[Ep])
                yield
                P.act(En[:, :], bC[:, :], AF.Exp, scale=-1.0, r=[bC], w=[En])
                yield
                P.tt('dve', tmp2[:, :], bC[:, :], lw32[:, :], ALU.subtract, r=[bC, lw32], w=[tmp2])
                yield
                P.act(tmp2[:, :], tmp2[:, :], AF.Exp, r=[tmp2], w=[tmp2])
                yield
                P.tt('pool', r32[:, :], r32[:, :], Ep[:, :], ALU.mult, r=[r32, Ep], w=[r32])
                yield
                P.tt('dve', kk[:, :], kk[:, :], tmp2[:, :], ALU.mult, r=[kk, tmp2], w=[kk])
                yield
                P.tt('pool', bb[:, :], bb[:, :], En[:, :], ALU.mult, r=[bb, En], w=[bb])
                yield
                P.tt('dve', kd[:, :], kd[:, :], En[:, :], ALU.mult, r=[kd, En], w=[kd])
                yield
                P.cp('act', bt_bf[:, :], bb[:, :], r=[bb], w=[bt_bf])
                yield
                P.cp('act', kdt_bf[:, :], kd[:, :], r=[kd], w=[kdt_bf])
                yield
                P.cp('act', kt_bf[:, :], kk[:, :], r=[kk], w=[kt_bf])
                yield
                yield
                for (src, dstap, dbuf) in ((kk, KR[:, :, 0, :], KR), (r32, KR[:, :, 1, :], KR), (bb, btT[:, :, :], btT), (kd, kdtT[:, :, :], kdtT)):
                    bank = psr()
                    for jg in range(4):
                        P.tr(bank[:, jg * 128:(jg + 1) * 128], src[:, jg * 128:(jg + 1) * 128], ident, r=[src, cst], w=[bank])
                    P.cp(evr(), dstap, bank[:, :].rearrange("p (a b) -> p a b", b=128), r=[bank], w=[dbuf])
                    yield
                yield
                for h in range(8):
                    jg, rs = h // 2, slice((h % 2) * 64, (h % 2 + 1) * 64)
                    bM = psr()
                    krr = KR[rs, jg, :, :].rearrange("p a b -> p (a b)")
                    P.mm(bM[:, 0:256], btT[rs, jg, :], krr, r=[btT, KR], w=[bM])
                    P.mm(bM[:, 256:512], kdtT[rs, jg, :], krr, r=[kdtT, KR], w=[bM])
                    P.tt('dve', NM[0][:, h, 0, :], bM[:, 0:128], MK[:, d, 0:128], ALU.mult, r=[bM, MK], w=[NM[0]])
                    P.tt('dve', R3[:, h, :, :].rearrange("p a b -> p (a b)"), bM[:, 128:512], MK[:, d, 128:512], ALU.mult,
                         r=[bM, MK], w=[R3])
                    if h % 2 == 1:
                        yield
                yield
                for hh in range(2):
                    bank = psr()
                    rs = slice(hh * 64, (hh + 1) * 64)
                    for jg in range(4):
                        P.mm(bank[:, jg * 128:(jg + 1) * 128], KR[rs, jg, 0, :], btT[rs, jg, :], r=[KR, btT], w=[bank])
                    P.tt('dve', NM[0][:, hh:8:2, 1, :], bank[:, :].rearrange("p (a b) -> p a b", b=128),
                         bc3(MKa[:, d, :], 4, 1), ALU.mult, r=[bank, MKa], w=[NM[0]])
                P.tt('dve', X[0][:, :, :], NM[0][:, :, 0, :], bc3(ident, 8, 1), ALU.add, r=[NM[0], cst], w=[X[0]])
                yield
                for j in range(6):
                    yield
                    cur, nxt = NM[j % 2], NM[(j + 1) % 2]
                    Xc, Xn = X[j % 2], X[(j + 1) % 2]
                    for hp in range(4):
                        bank = psr()
                        for q in range(2):
                            h = hp * 2 + q
                            P.mm(bank[:, q * 256:q * 256 + 128], cur[:, h, 1, :], cur[:, h, 0, :], r=[cur], w=[bank])
                            P.mm(bank[:, q * 256 + 128:q * 256 + 256], cur[:, h, 0, :], cur[:, h, 1, :], r=[cur], w=[bank])
                        P.cp(evr(), nxt[:, hp * 2:hp * 2 + 2, :, :].rearrange("p a b c -> p (a b c)"), bank[:, :], r=[bank], w=[nxt])
                    yield
                    for hp in range(2):
                        bank = psr()
                        for q in range(4):
                            h = hp * 4 + q
                            P.mm(bank[:, q * 128:(q + 1) * 128], nxt[:, h, 1, :], Xc[:, h, :], r=[nxt, Xc], w=[bank])
                        P.tt('dve', Xn[:, hp * 4:(hp + 1) * 4, :].rearrange("p a b -> p (a b)"), bank[:, :],
                             Xc[:, hp * 4:(hp + 1) * 4, :].rearrange("p a b -> p (a b)"), ALU.add, r=[bank, Xc], w=[Xn])
                yield
                XF = X[0]
                bZ = psr()
                for h in range(8):
                    P.mm(bZ[:, h * 64:(h + 1) * 64], R3[:, h, 1, :], vbf[:, h * 64:(h + 1) * 64], r=[R3, vbf], w=[bZ])
                P.cp('act', Z1[:, :], bZ[:, :], r=[bZ], w=[Z1])
                bU = psr()
                for h in range(8):
                    P.mm(bU[:, h * 64:(h + 1) * 64], XF[:, h, :], Z1[:, h * 64:(h + 1) * 64], r=[XF, Z1], w=[bU])
                P.cp('act', Ut[:, :], bU[:, :], r=[bU], w=[Ut])
                for hp in range(2):
                    bank = psr()
                    for q in range(4):
                        h = hp * 4 + q
                        jg = h // 2
                        P.mm(bank[:, q * 128:(q + 1) * 128], kt_bf[:, jg * 128:(jg + 1) * 128], XF[:, h, :], r=[kt_bf, XF], w=[bank])
                    for q in range(4):
                        h = hp * 4 + q
                        jg, rs = h // 2, slice((h % 2) * 64, (h % 2 + 1) * 64)
                        P.cp(evr(), WT[rs, jg, :], bank[rs, q * 128:(q + 1) * 128], r=[bank], w=[WT])
                yield
                bP = psr()
                for jg in range(4):
                    P.mm(bP[:, jg * 128:(jg + 1) * 128], WT[:, jg, :], Mbf[:, jg, :], r=[WT, Mbf], w=[bP])
                P.stt(Un[:, :], bP[:, :], -1.0, Ut[:, :], ALU.mult, ALU.subtract, r=[bP, Ut], w=[Un])
                yield
                bY = psr()
                for jg in range(4):
                    P.mm(bY[:, jg * 128:(jg + 1) * 128], KR[:, jg, 1, :], Mbf[:, jg, :], start=True, stop=False, r=[KR, Mbf], w=[bY])
                    for h in (2 * jg, 2 * jg + 1):
                        hs = slice(h * 64, (h + 1) * 64)
                        P.mm(bY[:, hs], R3[:, h, 2, :], vbf[:, hs], start=False, stop=False, r=[R3, vbf], w=[bY])
                    for h in (2 * jg, 2 * jg + 1):
                        hs = slice(h * 64, (h + 1) * 64)
                        P.mm(bY[:, hs], R3[:, h, 0, :], Un[:, hs], start=False, stop=(h == 2 * jg + 1), r=[R3, Un], w=[bY])
                yo = Yo[ci % 2]
                P.cp('act', yo[:, :], bY[:, :], r=[bY], w=[yo])
                P.ld(C.ydir[d, rows, :], yo[:, :], r=[yo], w=[ydb[d][c]])
                yield
                bS = psr()
                for jg in range(4):
                    js = slice(jg * 128, (jg + 1) * 128)
                    P.mm(bS[:, js], bt_bf[:, js], Un[:, js], start=True, stop=False, r=[bt_bf, Un], w=[bS])
                    P.mm(bS[:, js], kdt_bf[:, js], vbf[:, js], start=False, stop=True, r=[kdt_bf, vbf], w=[bS])
                for hh in range(2):
                    rs = slice(hh * 64, (hh + 1) * 64)
                    src = bS[rs, :].rearrange("p (j q) -> p j q", q=128)[:, :, hh * 64:(hh + 1) * 64]
                    mv = Mst[rs, :, hh * 64:(hh + 1) * 64]
                    P.tt('dve', mv, mv, src, ALU.add, r=[Mst, bS], w=[Mst])
                    P.tt('dve', mv, mv, gC[rs, :].unsqueeze(2).broadcast_to([64, 4, 64]), ALU.mult, r=[Mst, gC], w=[Mst])
                P.cp('act', Mbf[:, :, :], Mst[:, :, :], r=[Mst], w=[Mbf])
                yield
        alive = [inst(0), inst(1)]
        while alive:
            for g_ in list(alive):
                try:
                    next(g_)
                except StopIteration:
                    alive.remove(g_)
        P.emit()
        C.stats.append(('C', P.stats))
        fy = [_Quad([IT[i][n] for n in ('r32', 'k32', 'v32', 'lw32')]) for i in range(2)]
        lgb, lbb = IT[1]['a32'], IT[1]['kk']
        sm8, tmp = IT[0]['sm8'], IT[0]['tmp']
        bon0, bon1 = IT[0]['bon'], IT[1]['bon']
        P = Prog(nc, C.sems)
        load_bcast(P, lgb, C.w['lnx_g'][l], 512)
        load_bcast(P, lbb, C.w['lnx_b'][l], 512)
        def fin(par):
          sm8, tmp = IT[par]['sm8'], IT[par]['tmp']
          for c in range(par, NCH, 2):
            rows = slice(c * 128, (c + 1) * 128)
            f = fy[par]
            P.ld(f[:, 0, :], C.ydir[0, rows, :], r=[ydb[0][c]], w=[f])
            P.ld(f[:, 1, :], C.ydir[1, rows, :], r=[ydb[1][c]], w=[f])
            P.ld(f[:, 2, :], C.vg[0, rows, :], r=[vgb[0][c]], w=[f])
            P.ld(f[:, 3, :], C.vg[1, rows, :], r=[vgb[1][c]], w=[f])
            y = f[:, 0, :]
            P.tt('dve', y, y, f[:, 1, :], ALU.add, r=[f], w=[f])
            P.op('dve', lambda e, o=sm8[:, 0:8], i_=v3(y): e.reduce_sum(o, i_, axis=AX.X), [f.buf], [sm8.buf])
            P.tt('pool', tmp[:, :], y, y, ALU.mult, r=[f], w=[tmp])
            P.op('dve', lambda e, o=sm8[:, 8:16], i_=v3(tmp[:, :]): e.reduce_sum(o, i_, axis=AX.X), [tmp.buf], [sm8.buf])
            P.ts('dve', sm8[:, 0:16], sm8[:, 0:16], 1.0 / 64.0, None, ALU.mult, r=[sm8], w=[sm8])
            P.tt('dve', sm8[:, 16:24], sm8[:, 0:8], sm8[:, 0:8], ALU.mult, r=[sm8], w=[sm8])
            P.tt('dve', sm8[:, 16:24], sm8[:, 8:16], sm8[:, 16:24], ALU.subtract, r=[sm8], w=[sm8])
            P.ts('dve', sm8[:, 16:24], sm8[:, 16:24], GN_EPS, None, ALU.add, r=[sm8], w=[sm8])
            P.act(sm8[:, 16:24], sm8[:, 16:24], AF.Sqrt, r=[sm8], w=[sm8])
            P.op('dve', lambda e, o=sm8[:, 24:32], i_=sm8[:, 16:24]: e.reciprocal(o, i_), [sm8.buf], [sm8.buf])
            P.tt('dve', v3(y), v3(y), bc3(sm8[:, 0:8], 64, 2), ALU.subtract, r=[f, sm8], w=[f])
            P.tt('dve', v3(y), v3(y), bc3(sm8[:, 24:32], 64, 2), ALU.mult, r=[f, sm8], w=[f])
            P.tt('pool', y, y, lgb[:, :], ALU.mult, r=[f, lgb], w=[f])
            P.tt('pool', y, y, lbb[:, :], ALU.add, r=[f, lbb], w=[f])
            P.tt('dve', sm8[:, 0:8], bon0[:, c, :], bon1[:, c, :], ALU.add, r=[bon0, bon1, sm8], w=[sm8])
            P.tt('dve', v3(tmp[:, :]), v3(f[:, 2, :]), bc3(sm8[:, 0:8], 64, 2), ALU.mult, r=[f, sm8], w=[tmp])
            P.tt('pool', y, y, tmp[:, :], ALU.add, r=[f, tmp], w=[f])
            P.tt('pool', y, y, f[:, 3, :], ALU.mult, r=[f], w=[f])
            P.ld(C.ymix[b, rows, 512:1024], y, r=[f])
        for par_ in range(2):
            P.set_lane(par_ + 1)
            fin(par_)
        P.set_lane(0)
        P.emit()
        C.stats.append(('C', P.stats))


def build(T=2048, NS=2, depth=2, debug=False, stages='ABCD'):
    nc = bass.Bass("TRN2", target_bir_lowering=False)
    C = Ctx()
    C.nc, C.T, C.NS, C.NCH = nc, T, NS, T // 128
    C.stats = []
    C.x = nc.dram_tensor("x", [NS, T, D], F32, kind="ExternalInput").ap()
    C.w = {n: nc.dram_tensor(n, W_SHAPES[n], F32, kind="ExternalInput").ap() for n in W_NAMES}
    C.cst_d = nc.dram_tensor("cst", [128, 768], F32, kind="ExternalInput").ap()
    C.out = nc.dram_tensor("out", [NS, T, D], F32, kind="ExternalOutput").ap()
    kind = "ExternalOutput" if debug else "Internal"
    C.hbuf = nc.dram_tensor("hbuf", [NS, T, D], F32, kind=kind).ap()
    C.h1buf = nc.dram_tensor("h1buf", [NS, T, D], F32, kind=kind).ap()
    C.zbuf = nc.dram_tensor("zbuf", [NS, T, 512], F32, kind=kind).ap()
    C.dtbuf = nc.dram_tensor("dtbuf", [NS, T, 16], F32, kind=kind).ap()
    C.xbcT = nc.dram_tensor("xbcT", [NS, 1024, T], BF16, kind=kind).ap()
    C.rwT = nc.dram_tensor("rwT", [NS, 1792, T], BF16, kind=kind).ap()
    C.ymix = nc.dram_tensor("ymix", [NS, T, D], F32, kind=kind).ap()
    C.ydir = nc.dram_tensor("ydir", [2, T, 512], F32, kind=kind).ap()
    C.vg = nc.dram_tensor("vg", [3, T, 512], F32, kind=kind).ap()
    with nc.sbuf_tensor('cst_sb', [128, 768], F32) as cst_sb, contextlib.ExitStack() as semstack:
        C.cst = Tile(cst_sb, 'cst')
        C.sems = SemState(nc, 8, semstack)
        stage_consts(C)
        for l in range(depth):
            if 'A' in stages:
                stage_A(C, l)
            for b in range(NS):
                if 'B' in stages:
                    stage_B(C, l, b)
                if 'C' in stages:
                    stage_C(C, l, b)
            if 'D' in stages:
                with contextlib.ExitStack() as wst:
                    Wf = _sb(wst, nc, 'Wf', [128, 8, 4096], BF16)
                    Wp = _sb(wst, nc, 'Wp', [128, 32, D], BF16)
                    pre = (Wf, [Buf('Wf%d' % k) for k in range(8)], Wp, [Buf('Wp%d' % k) for k in range(32)])
                    stage_D1(C, l, pre)
                    stage_D2(C, l, (l == depth - 1), pre)
    return nc, C


_CACHE = {}


def kernel(**inputs):
    n_cores = 8
    x = np.ascontiguousarray(np.asarray(inputs["x"], dtype=np.float32))
    B, T, _ = x.shape
    NS = B // n_cores
    key = (T, NS)
    if key not in _CACHE:
        _CACHE[key] = build(T=T, NS=NS, depth=2, debug=False)[0]
    nc = _CACHE[key]
    cst = make_consts()
    ws = {n: np.ascontiguousarray(np.asarray(inputs[n], dtype=np.float32)) for n in W_NAMES}
    in_maps = []
    for i in range(n_cores):
        m = dict(ws)
        m["x"] = np.ascontiguousarray(x[i * NS:(i + 1) * NS])
        m["cst"] = cst
        in_maps.append(m)
    res = run_bass_kernel_spmd(nc, in_maps, core_ids=list(range(n_cores)))
    return np.concatenate([np.asarray(r["out"], dtype=np.float32) for r in res.results], axis=0)
```

```python
import contextlib
import numpy as np
import concourse.bass as bass
import concourse.mybir as mybir
from concourse.bass_utils import run_bass_kernel_spmd

F32 = mybir.dt.float32
BF16 = mybir.dt.bfloat16
AF = mybir.ActivationFunctionType
ALU = mybir.AluOpType
AX = mybir.AxisListType

ENGS = ('pe', 'act', 'dve', 'pool', 'sp')
LANE_GRAN = 8


class Buf:
    __slots__ = ('name', 'last_w', 'readers')

    def __init__(self, name=''):
        self.name = name
        self.last_w = None
        self.readers = []


class _Op:
    __slots__ = ('eng', 'idx', 'fn', 'deps', 'is_dma', 'dslot', 'dval', 'needs_inc', 'cnt', 'waits', 'prog', 'key')

    def __init__(self, eng, idx, fn, is_dma):
        self.eng = eng
        self.idx = idx
        self.fn = fn
        self.is_dma = is_dma
        self.deps = []
        self.dslot = 0
        self.dval = 0
        self.needs_inc = False
        self.cnt = 0
        self.waits = []
        self.prog = None
        self.key = (1 << 60, 0, 0)


class SemState:
    def __init__(self, nc, ring=8, stack=None):
        self.stack = stack if stack is not None else contextlib.ExitStack()
        st = self.stack
        self.csem = {e: st.enter_context(nc.semaphore('c_' + e)) for e in ENGS if e != 'sp'}
        self.dsem = {e: [st.enter_context(nc.semaphore('d_%s_%d' % (e, i))) for i in range(ring)] for e in ('sp', 'pool', 'act')}
        self.c_off = {e: 0 for e in ENGS}
        self.d_cnt = {e: 0 for e in ENGS}


class Prog:
    def __init__(self, nc, sems=None, ring=8):
        self.nc = nc
        self.q = {e: [] for e in ENGS}
        self.ring = ring
        self.dma_hist = {e: [] for e in ENGS}
        self.sems = sems if sems is not None else SemState(nc, ring)
        self.gseq = 0
        self.lane = 0
        self.lane_base = 0
        self.lane_cnt = {}

    def _add(self, eng, fn, reads, writes, is_dma):
        op = _Op(eng, len(self.q[eng]), fn, is_dma)
        op.prog = self
        deps = []
        for b in reads:
            if b.last_w is not None:
                deps.append(b.last_w)
        for b in writes:
            if b.last_w is not None:
                deps.append(b.last_w)
            deps.extend(b.readers)
        if self.lane == 0:
            op.key = (self.gseq, 0, 0)
            self.gseq += 1
        else:
            c = self.lane_cnt.get(self.lane, 0)
            op.key = (self.lane_base + c // LANE_GRAN, self.lane, c)
            self.lane_cnt[self.lane] = c + 1
        seen = set()
        for d in deps:
            if d is not op and id(d) not in seen and getattr(d, 'prog', self) is self:
                seen.add(id(d))
                op.deps.append(d)
        for b in writes:
            b.last_w = op
            b.readers = []
        for b in reads:
            if b.last_w is not op:
                b.readers.append(op)
        self.q[eng].append(op)
        return op

    def set_lane(self, k):
        if self.lane == 0 and k != 0:
            self.lane_base = self.gseq
            self.lane_cnt = {}
        if k == 0 and self.lane != 0:
            self.gseq = self.lane_base + max(self.lane_cnt.values(), default=0) // LANE_GRAN + 1
        self.lane = k

    def op(self, eng, fn, reads=(), writes=()):
        return self._add(eng, fn, reads, writes, False)

    def dma(self, eng, fn, reads=(), writes=()):
        return self._add(eng, fn, reads, writes, True)


    @staticmethod
    def _bufs(lst):
        return [b.buf if hasattr(b, 'buf') else b for b in lst]

    def mm(self, out, lhsT, rhs, start=True, stop=True, r=(), w=()):
        return self.op('pe', lambda e: e.matmul(out, lhsT, rhs, start=start, stop=stop), self._bufs(r), self._bufs(w))

    def tr(self, out, in_, ident, r=(), w=()):
        return self.op('pe', lambda e: e.transpose(out, in_, ident), self._bufs(r), self._bufs(w))

    def act(self, out, in_, func, bias=None, scale=None, accum=None, r=(), w=()):
        kw = {}
        if bias is not None:
            kw['bias'] = bias
        if scale is not None:
            kw['scale'] = scale
        if accum is not None:
            kw['accum_out'] = accum
        return self.op('act', lambda e: e.activation(out, in_, func, **kw), self._bufs(r), self._bufs(w))

    def tt(self, eng, out, in0, in1, op, r=(), w=()):
        return self.op(eng, lambda e: e.tensor_tensor(out, in0, in1, op), self._bufs(r), self._bufs(w))

    def ts(self, eng, out, in0, s1, s2=None, op0=None, op1=None, r=(), w=()):
        if op1 is None:
            return self.op(eng, lambda e: e.tensor_scalar(out, in0, s1, None, op0), self._bufs(r), self._bufs(w))
        return self.op(eng, lambda e: e.tensor_scalar(out, in0, s1, s2, op0, op1), self._bufs(r), self._bufs(w))

    def stt(self, out, in0, scalar, in1, op0, op1, r=(), w=()):
        return self.op('dve', lambda e: e.scalar_tensor_tensor(out, in0, scalar, in1, op0, op1), self._bufs(r), self._bufs(w))

    def cp(self, eng, out, in_, r=(), w=()):
        if eng == 'act':
            return self.op('act', lambda e: e.copy(out, in_), self._bufs(r), self._bufs(w))
        return self.op(eng, lambda e: e.tensor_copy(out, in_), self._bufs(r), self._bufs(w))

    def ld(self, out, in_, r=(), w=(), eng='sp', **kw):
        return self.dma(eng, lambda e: e.dma_start(out=out, in_=in_, **kw), self._bufs(r), self._bufs(w))

    def _last_ops(self, skip=None):
        out = []
        for e in ENGS:
            if e != skip and self.q[e]:
                for o in reversed(self.q[e]):
                    if not o.is_dma and o.fn is not None:
                        out.append(o)
                        break
            h = self.dma_hist[e]
            out.extend(h[-self.ring:])
        return out

    def barrier(self):
        deps_for = {e: self._last_ops(skip=e) for e in ENGS}
        for e in ENGS:
            op = _Op(e, len(self.q[e]), None, False)
            op.prog = self
            op.deps = deps_for[e]
            self.q[e].append(op)

    def finish(self):
        op = _Op('sp', len(self.q['sp']), None, False)
        op.prog = self
        for e in ENGS:
            op.deps.extend(self.dma_hist[e][-self.ring:])
        self.q['sp'].append(op)

    def emit(self):
        nc = self.nc
        self.set_lane(0)
        for e in ENGS:
            self.q[e].sort(key=lambda o: o.key)
            for i, o in enumerate(self.q[e]):
                o.idx = i
            h = [o for o in self.q[e] if o.is_dma]
            self.dma_hist[e] = h
            for k, o in enumerate(h):
                kg = k + self.sems.d_cnt[e]
                o.dslot = kg % self.ring
                o.dval = 16 * (kg // self.ring + 1)
                if k >= self.ring and h[k - self.ring] not in o.deps:
                    o.deps.append(h[k - self.ring])
        self.barrier()
        self.finish()
        for e in ENGS:
            seen = {p: -1 for p in ENGS}
            seen_dma = {}
            for o in self.q[e]:
                best = {}
                for d in o.deps:
                    if d.is_dma:
                        key = (d.eng, d.dslot)
                        if seen_dma.get(key, 0) >= d.dval:
                            continue
                        seen_dma[key] = d.dval
                        o.waits.append(d)
                    else:
                        if d.eng == 'pe' and e == 'pe':
                            continue
                        if d.fn is None:
                            continue
                        if d.idx <= seen[d.eng]:
                            continue
                        if d.eng not in best or best[d.eng].idx < d.idx:
                            best[d.eng] = d
                for p, d in best.items():
                    seen[p] = d.idx
                    d.needs_inc = True
                    o.waits.append(d)
        for e in ENGS:
            c = self.sems.c_off[e]
            for o in self.q[e]:
                if o.needs_inc:
                    c += 1
                o.cnt = c
            self.sems.c_off[e] = c
            self.sems.d_cnt[e] += len(self.dma_hist[e])
        n_wait = sum(len(o.waits) for e in ENGS for o in self.q[e])
        n_ops = sum(len(self.q[e]) for e in ENGS)
        self.stats = dict(n_ops=n_ops, n_wait=n_wait, per_eng={e: len(self.q[e]) for e in ENGS})
        with contextlib.ExitStack() as st:
            csem = self.sems.csem
            dsem = self.sems.dsem
            block = st.enter_context(nc.Block())

            def run(ename, eng):
                for o in self.q[ename]:
                    for d in o.waits:
                        if d.is_dma:
                            eng.wait_ge(dsem[d.eng][d.dslot], d.dval)
                        else:
                            eng.wait_ge(csem[d.eng], d.cnt)
                    if o.fn is None:
                        continue
                    ins = o.fn(eng)
                    if o.is_dma:
                        ins.then_inc(dsem[ename][o.dslot], 16)
                    elif o.needs_inc:
                        ins.then_inc(csem[ename], 1)

            @block.tensor
            def _(eng):
                run('pe', eng)

            @block.scalar
            def _(eng):
                run('act', eng)

            @block.vector
            def _(eng):
                run('dve', eng)

            @block.gpsimd
            def _(eng):
                run('pool', eng)

            @block.sync
            def _(eng):
                run('sp', eng)


D = 1024
IN_COLS = 3344
ALPHA = float(4 ** 0.25)
LN_EPS = 1e-5
RMS_EPS = 1e-5
GN_EPS = 64e-5
DECAY_C = float(np.exp(-0.5))
NEU_DT = BF16

W_NAMES = ["ln0_g", "ln0_b", "w_in", "conv_w", "conv_b", "dt_bias", "a_log", "d_skip", "ssd_norm_g",
           "mu_rwkv", "w0", "w_up", "a0", "a_up", "g_up", "k_k", "k_a", "r_k", "lnx_g", "lnx_b", "w_out",
           "ln1_g", "ln1_b", "w_fc", "w_proj", "ln2_g", "ln2_b"]
W_SHAPES = {
    "ln0_g": [1024], "ln0_b": [1024], "w_in": [2, 1024, 3344], "conv_w": [2, 5, 1024], "conv_b": [2, 1024],
    "dt_bias": [2, 2, 8], "a_log": [2, 2, 8], "d_skip": [2, 8], "ssd_norm_g": [2, 512], "mu_rwkv": [2, 1792],
    "w0": [2, 2, 512], "w_up": [2, 2, 64, 512], "a0": [2, 2, 512], "a_up": [2, 2, 64, 512], "g_up": [2, 128, 512],
    "k_k": [2, 512], "k_a": [2, 512], "r_k": [2, 8, 64], "lnx_g": [2, 512], "lnx_b": [2, 512],
    "w_out": [2, 1024, 1024], "ln1_g": [2, 1024], "ln1_b": [2, 1024], "w_fc": [2, 1024, 4096],
    "w_proj": [2, 4096, 1024], "ln2_g": [2, 1024], "ln2_b": [2, 1024],
}


def make_consts():
    i = np.arange(128)
    ident = np.eye(128, dtype=np.float32)
    U = (i[:, None] <= i[None, :]).astype(np.float32)
    Lo = (i[:, None] >= i[None, :]).astype(np.float32)
    Us = (i[:, None] < i[None, :]).astype(np.float32)
    Ls = (i[:, None] > i[None, :]).astype(np.float32)
    ones = np.ones((128, 128), np.float32)
    return np.ascontiguousarray(np.concatenate([ident, U, Lo, Us, Ls, ones], axis=1))


class Tile:
    def __init__(self, t, name=''):
        self.t = t
        self.buf = Buf(name)

    def __getitem__(self, k):
        return self.t[k]


class Ctx:
    pass


class _Quad:
    def __init__(self, tiles):
        self.tiles = tiles
        self.buf = Buf('quad')

    def __getitem__(self, k):
        p, i, c = k
        return self.tiles[i][p, c]


def layer_norm_tile(P, C, src, dst, g_t, b_t, stat, eps, r, w, geng='pool'):
    st = stat
    P.op('dve', lambda e: e.bn_stats(st[:, 0:6], src[:, 0:512]), P._bufs(r), [st.buf])
    P.op('dve', lambda e: e.bn_stats(st[:, 6:12], src[:, 512:1024]), P._bufs(r), [st.buf])
    P.op('dve', lambda e: e.bn_aggr(st[:, 12:14], st[:, 0:12].rearrange("p (a b) -> p a b", b=6)), [st.buf], [st.buf])
    P.ts('dve', st[:, 14:15], st[:, 13:14], eps, None, ALU.add, r=[st], w=[st])
    P.act(st[:, 15:16], st[:, 14:15], AF.Sqrt, r=[st], w=[st])
    P.op('dve', lambda e: e.reciprocal(st[:, 16:17], st[:, 15:16]), [st.buf], [st.buf])
    P.stt(st[:, 17:18], st[:, 12:13], -1.0, st[:, 16:17], ALU.mult, ALU.mult, r=[st], w=[st])
    P.act(dst, src, AF.Identity, bias=st[:, 17:18], scale=st[:, 16:17], r=list(r) + [st], w=w)
    P.tt(geng, dst, dst, g_t[:, :], ALU.mult, r=list(w) + [g_t], w=w)
    P.tt('dve', dst, dst, b_t[:, :], ALU.add, r=list(w) + [b_t], w=w)


_UID = [0]


_SBUSE = [0]
SB_BUDGET = 176 * 1024


def _sb_release(n):
    _SBUSE[0] -= n


def _sb(st, nc, name, shape, dt):
    _UID[0] += 1
    name = '%s_%d' % (name, _UID[0])
    n = int(np.prod(shape[1:])) * (2 if dt == BF16 else 4)
    n = (n + 31) // 32 * 32
    _SBUSE[0] += n
    assert _SBUSE[0] <= SB_BUDGET, ('SBUF budget exceeded', name, _SBUSE[0])
    t = Tile(st.enter_context(nc.sbuf_tensor(name, shape, dt)), name)
    st.callback(_sb_release, n)
    return t


def _psum(st, nc, n=8):
    _UID[0] += 1
    return [Tile(st.enter_context(nc.psum_tensor('ps%d_%d' % (i, _UID[0]), [128, 512], F32)), 'ps%d' % i) for i in range(n)]


class RR:
    def __init__(self, items):
        self.items = items
        self.i = 0

    def __call__(self):
        x = self.items[self.i % len(self.items)]
        self.i += 1
        return x


def load_bcast(P, tile, src_row, n, eng='sp'):
    P.ld(tile[:, 0:n], src_row.partition_broadcast(128), w=[tile], eng=eng)


def stage_consts(C):
    P = Prog(C.nc, C.sems)
    P.ld(C.cst[:, :], C.cst_d[:, :], w=[C.cst])
    P.emit()


def stage_A(C, l):
    nc, T, NS = C.nc, C.T, C.NS
    TB = min(512, T)
    NJ = TB // 128
    ident = C.cst[:, 0:128]
    with contextlib.ExitStack() as st:
        Win = _sb(st, nc, 'Win', [128, 8, IN_COLS], BF16)
        Wb = [Buf('Win%d' % k) for k in range(8)]
        hin = [_sb(st, nc, 'hin%d' % i, [128, D], F32) for i in range(2)]
        hT = [_sb(st, nc, 'hT%d' % i, [128, 8, TB], BF16) for i in range(2)]
        stat = [_sb(st, nc, 'stat%d' % i, [128, 32], F32) for i in range(2)]
        zo = [_sb(st, nc, 'zo%d' % i, [128, 512], F32) for i in range(2)]
        dto = [_sb(st, nc, 'dto%d' % i, [128, 16], F32) for i in range(2)]
        fo = [_sb(st, nc, 'fo%d' % i, [128, TB], BF16) for i in range(3)]
        if l == 0:
            g0 = _sb(st, nc, 'g0', [128, D], F32)
            b0 = _sb(st, nc, 'b0', [128, D], F32)
        ps = _psum(st, nc)
        P = Prog(nc, C.sems)
        for kc in range(8):
            P.ld(Win[:, kc, :], C.w['w_in'][l, kc * 128:(kc + 1) * 128, :], w=[Wb[kc]], eng='pool', max_dma_last_dim=4096)
        if l == 0:
            load_bcast(P, g0, C.w['ln0_g'], D)
            load_bcast(P, b0, C.w['ln0_b'], D)
        psr = RR(ps)
        evr = RR(['act', 'dve'])
        zor, dtor, forr = RR(zo), RR(dto), RR(fo)
        ti = 0
        for b in range(NS):
            src = C.x[b] if l == 0 else C.hbuf[b]
            for tb in range(T // TB):
                hTt = hT[(b * (T // TB) + tb) % 2]
                for j in range(NJ):
                    rows = slice(tb * TB + j * 128, tb * TB + (j + 1) * 128)
                    hi = hin[ti % 2]
                    P.ld(hi[:, :], src[rows, :], w=[hi])
                    if l == 0:
                        layer_norm_tile(P, C, hi[:, :], hi[:, :], g0, b0, stat[ti % 2], LN_EPS, r=[hi], w=[hi])
                        P.ld(C.hbuf[b, rows, :], hi[:, :], r=[hi])
                    for half in range(2):
                        bank = psr()
                        for q in range(4):
                            kc = half * 4 + q
                            P.tr(bank[:, q * 128:(q + 1) * 128], hi[:, kc * 128:(kc + 1) * 128], ident, r=[hi, C.cst], w=[bank])
                        P.cp(evr(), hTt[:, half * 4:half * 4 + 4, j * 128:(j + 1) * 128],
                             bank[:, :].rearrange("p (a b) -> p a b", b=128), r=[bank], w=[hTt])
                    ti += 1
                for j in range(NJ):
                    rows = slice(tb * TB + j * 128, tb * TB + (j + 1) * 128)
                    bank = psr()
                    for kc in range(8):
                        P.mm(bank[:, :], hTt[:, kc, j * 128:(j + 1) * 128], Win[:, kc, 0:512], start=(kc == 0), stop=(kc == 7),
                             r=[hTt, Wb[kc]], w=[bank])
                    z = zor()
                    P.cp(evr(), z[:, :], bank[:, :], r=[bank], w=[z])
                    P.ld(C.zbuf[b, rows, :], z[:, :], r=[z])
                    bank = psr()
                    for kc in range(8):
                        P.mm(bank[:, 0:16], hTt[:, kc, j * 128:(j + 1) * 128], Win[:, kc, 1536:1552], start=(kc == 0), stop=(kc == 7),
                             r=[hTt, Wb[kc]], w=[bank])
                    dt = dtor()
                    P.cp(evr(), dt[:, :], bank[:, 0:16], r=[bank], w=[dt])
                    P.ld(C.dtbuf[b, rows, :], dt[:, :], r=[dt])
                for cc in range(22):
                    col0 = 512 + cc * 128 if cc < 8 else 1552 + (cc - 8) * 128
                    bank = psr()
                    for kc in range(8):
                        P.mm(bank[:, 0:TB], Win[:, kc, col0:col0 + 128], hTt[:, kc, :], start=(kc == 0), stop=(kc == 7),
                             r=[hTt, Wb[kc]], w=[bank])
                    f = forr()
                    P.cp(evr(), f[:, :], bank[:, 0:TB], r=[bank], w=[f])
                    if cc < 8:
                        dst = C.xbcT[b, cc * 128:(cc + 1) * 128, tb * TB:(tb + 1) * TB]
                    else:
                        dst = C.rwT[b, (cc - 8) * 128:(cc - 7) * 128, tb * TB:(tb + 1) * TB]
                    P.ld(dst, f[:, :], r=[f])
        P.emit()
        C.stats.append(('A', P.stats))


def load_w_bf16(P, tile, bufs, src, nk, eng='pool'):
    for kc in range(nk):
        P.ld(tile[:, kc, :], src[kc * 128:(kc + 1) * 128, :], w=[bufs[kc]], eng=eng, max_dma_last_dim=4096)


def stage_D1(C, l, pre):
    nc, T, NS = C.nc, C.T, C.NS
    ident = C.cst[:, 0:128]
    with contextlib.ExitStack() as st:
        Wo = _sb(st, nc, 'Wo', [128, 8, D], BF16)
        Wob = [Buf('Wo%d' % k) for k in range(8)]
        g1 = _sb(st, nc, 'g1', [128, D], F32)
        b1 = _sb(st, nc, 'b1', [128, D], F32)
        ym = [_sb(st, nc, 'ym%d' % i, [128, D], F32) for i in range(2)]
        yT = [_sb(st, nc, 'yT%d' % i, [128, 8, 128], BF16) for i in range(2)]
        hr = [_sb(st, nc, 'hr%d' % i, [128, D], F32) for i in range(1)]
        t1 = [_sb(st, nc, 't1%d' % i, [128, D], F32) for i in range(1)]
        stat = [_sb(st, nc, 'stat%d' % i, [128, 32], F32) for i in range(2)]
        ps = _psum(st, nc)
        P = Prog(nc, C.sems)
        load_w_bf16(P, Wo, Wob, C.w['w_out'][l], 8)
        load_bcast(P, g1, C.w['ln1_g'][l], D)
        load_bcast(P, b1, C.w['ln1_b'][l], D)
        Wf, Wfb, Wp, Wpb = pre
        load_w_bf16(P, Wf, Wfb, C.w['w_fc'][l], 8)
        load_w_bf16(P, Wp, Wpb, C.w['w_proj'][l], 32)
        psr = RR(ps)
        evr = RR(['act', 'dve'])
        ti = 0
        for b in range(NS):
            for c in range(T // 128):
                rows = slice(c * 128, (c + 1) * 128)
                y, yt, h, t, sx = ym[ti % 2], yT[ti % 2], hr[0], t1[0], stat[ti % 2]
                P.ld(y[:, :], C.ymix[b, rows, :], w=[y])
                P.ld(h[:, :], C.hbuf[b, rows, :], w=[h])
                for half in range(2):
                    bank = psr()
                    for q in range(4):
                        kc = half * 4 + q
                        P.tr(bank[:, q * 128:(q + 1) * 128], y[:, kc * 128:(kc + 1) * 128], ident, r=[y, C.cst], w=[bank])
                    P.cp(evr(), yt[:, half * 4:half * 4 + 4, :], bank[:, :].rearrange("p (a b) -> p a b", b=128), r=[bank], w=[yt])
                for half in range(2):
                    bank = psr()
                    for kc in range(8):
                        P.mm(bank[:, :], yt[:, kc, :], Wo[:, kc, half * 512:(half + 1) * 512], start=(kc == 0), stop=(kc == 7),
                             r=[yt, Wob[kc]], w=[bank])
                    P.stt(t[:, half * 512:(half + 1) * 512], h[:, half * 512:(half + 1) * 512], ALPHA, bank[:, :], ALU.mult, ALU.add,
                          r=[h, bank], w=[t])
                layer_norm_tile(P, C, t[:, :], t[:, :], g1, b1, sx, LN_EPS, r=[t], w=[t], geng='dve')
                P.ld(C.h1buf[b, rows, :], t[:, :], r=[t])
                ti += 1
        P.emit()
        C.stats.append(('D1', P.stats))


def stage_D2(C, l, last, pre):
    nc, T, NS = C.nc, C.T, C.NS
    ident = C.cst[:, 0:128]
    TB = 256
    with contextlib.ExitStack() as st:
        Wf, Wfb, Wp, Wpb = pre
        g2 = _sb(st, nc, 'g2', [128, D], F32)
        b2 = _sb(st, nc, 'b2', [128, D], F32)
        h1 = [_sb(st, nc, 'h1%d' % i, [128, D], F32) for i in range(2)]
        h1T = _sb(st, nc, 'h1T', [128, 8, TB], BF16)
        aT = _sb(st, nc, 'aT', [128, 32, TB], BF16)
        tmp = [_sb(st, nc, 'tmp%d' % i, [128, TB], F32) for i in range(2)]
        t2 = [_sb(st, nc, 't2%d' % i, [128, D], F32) for i in range(1)]
        stat = [_sb(st, nc, 'stat%d' % i, [128, 32], F32) for i in range(2)]
        ps = _psum(st, nc)
        P = Prog(nc, C.sems)
        load_bcast(P, g2, C.w['ln2_g'][l], D)
        load_bcast(P, b2, C.w['ln2_b'][l], D)
        psr = RR(ps)
        evr = RR(['act', 'dve'])
        tmr = RR(tmp)
        ti = 0
        for b in range(NS):
            dstb = C.out[b] if last else C.hbuf[b]
            for tb in range(T // TB):
                for j in range(2):
                    rows = slice(tb * TB + j * 128, tb * TB + (j + 1) * 128)
                    h = h1[j]
                    P.ld(h[:, :], C.h1buf[b, rows, :], w=[h])
                    for half in range(2):
                        bank = psr()
                        for q in range(4):
                            kc = half * 4 + q
                            P.tr(bank[:, q * 128:(q + 1) * 128], h[:, kc * 128:(kc + 1) * 128], ident, r=[h, C.cst], w=[bank])
                        P.cp(evr(), h1T[:, half * 4:half * 4 + 4, j * 128:(j + 1) * 128],
                             bank[:, :].rearrange("p (a b) -> p a b", b=128), r=[bank], w=[h1T])
                for fc in range(32):
                    bank = psr()
                    for kc in range(8):
                        P.mm(bank[:, 0:TB], Wf[:, kc, fc * 128:(fc + 1) * 128], h1T[:, kc, :], start=(kc == 0), stop=(kc == 7),
                             r=[h1T, Wfb[kc]], w=[bank])
                    tm = tmr()
                    if fc % 2 == 0:
                        P.act(tm[:, :], bank[:, 0:TB], AF.Relu, r=[bank], w=[tm])
                    else:
                        P.ts('dve', tm[:, :], bank[:, 0:TB], 0.0, None, ALU.max, r=[bank], w=[tm])
                    P.tt('pool', aT[:, fc, :], tm[:, :], tm[:, :], ALU.mult, r=[tm], w=[aT])
                for j in range(2):
                    rows = slice(tb * TB + j * 128, tb * TB + (j + 1) * 128)
                    h = h1[j]
                    t = t2[0]
                    for half in range(2):
                        bank = psr()
                        for fc in range(32):
                            P.mm(bank[:, :], aT[:, fc, j * 128:(j + 1) * 128], Wp[:, fc, half * 512:(half + 1) * 512],
                                 start=(fc == 0), stop=(fc == 31), r=[aT, Wpb[fc]], w=[bank])
                        P.stt(t[:, half * 512:(half + 1) * 512], h[:, half * 512:(half + 1) * 512], ALPHA, bank[:, :], ALU.mult, ALU.add,
                              r=[h, bank], w=[t])
                    layer_norm_tile(P, C, t[:, :], t[:, :], g2, b2, stat[ti % 2], LN_EPS, r=[t], w=[t])
                    P.ld(dstb[rows, :], t[:, :], r=[t])
                    ti += 1
        P.emit()
        C.stats.append(('D2', P.stats))


def bc3(ap2, n, axis):
    k = ap2.shape[1]
    if axis == 1:
        return ap2.unsqueeze(1).broadcast_to([128, n, k])
    return ap2.unsqueeze(2).broadcast_to([128, k, n])


def stage_B(C, l, b):
    nc, T, NCH = C.nc, C.T, C.NCH
    TB = min(512, T)
    cst = C.cst
    ident, U, Lo, Us, Ls, ones = (cst[:, i * 128:(i + 1) * 128] for i in range(6))
    with contextlib.ExitStack() as st:
        sb = lambda n, sh, dt: _sb(st, nc, n, sh, dt)
        XTb = [Buf('XT%d' % g) for g in range(8)]
        BT = sb('BT', [128, 2, T], BF16)
        CT = sb('CT', [128, 2, T], BF16)
        xbf = sb('xbf', [128, NCH, 512], BF16)
        Btok = sb('Btok', [128, NCH, 256], BF16)
        dtraw = sb('dtraw', [128, NCH, 16], F32)
        dtv = sb('dtv', [128, NCH, 16], F32)
        av = sb('av', [128, NCH, 16], F32)
        dtb = sb('dtb', [128, 16], F32)
        negA = sb('negA', [128, 16], F32)
        dsk8 = sb('dsk8', [128, 8], F32)
        dsk = sb('dsk', [128, 512], F32)
        ng = sb('ng', [128, 512], F32)
        ps = _psum(st, nc)
        st0 = contextlib.ExitStack()
        sb0 = lambda n, sh, dt: _sb(st0, nc, n, sh, dt)
        XT = sb0('XT', [128, 8, T + 4], BF16)
        cw6 = sb0('cw6', [6, 1024], F32)
        cwb = sb0('cwb', [128, 8, 6], F32)
        Dg = sb0('Dg', [128, 5, 8, 128], BF16)
        cbrow = sb0('cbrow', [1, 768], F32)
        cbrow_bf = sb0('cbrow_bf', [1, 768], BF16)
        ones_bf = sb0('ones_bf', [1, 128], BF16)
        P = Prog(nc, C.sems)
        psr = RR(ps)
        P.op('pool', lambda e: e.memset(XT[:, :, 0:2], 0.0), [], XTb)
        P.op('pool', lambda e: e.memset(XT[:, :, T + 2:T + 4], 0.0), [], XTb)
        for g in range(8):
            P.ld(XT[:, g, 2:T + 2], C.xbcT[b, g * 128:(g + 1) * 128, :], w=[XTb[g]])
        P.ld(cw6[0:5, :], C.w['conv_w'][l], w=[cw6])
        P.ld(cw6[5:6, :], C.w['conv_b'][l:l + 1, :], w=[cw6])
        P.ld(cbrow[0:1, :], C.w['conv_b'][l:l + 1, 0:768], w=[cbrow])
        P.cp('dve', cbrow_bf[0:1, :], cbrow[0:1, :], r=[cbrow], w=[cbrow_bf])
        P.cp('dve', ones_bf[0:1, :], ones[0:1, :], r=[cst], w=[ones_bf])
        load_bcast(P, dtb, C.w['dt_bias'][l].rearrange("a b -> (a b)"), 16)
        load_bcast(P, negA, C.w['a_log'][l].rearrange("a b -> (a b)"), 16)
        load_bcast(P, dsk8, C.w['d_skip'][l], 8)
        load_bcast(P, ng, C.w['ssd_norm_g'][l], 512)
        P.act(negA[:, :], negA[:, :], AF.Exp, r=[negA], w=[negA])
        P.ts('dve', negA[:, :], negA[:, :], -1.0, None, ALU.mult, r=[negA], w=[negA])
        P.cp('dve', dsk[:, :].rearrange("p (h q) -> p h q", q=64), bc3(dsk8[:, :], 64, 2), r=[dsk8], w=[dsk])
        bank = psr()
        for g in range(8):
            P.tr(bank[:, g * 6:(g + 1) * 6], cw6[0:6, g * 128:(g + 1) * 128], ident[0:6, 0:6], r=[cw6, cst], w=[bank])
        P.cp('dve', cwb[:, :, :], bank[:, 0:48].rearrange("p (g k) -> p g k", k=6), r=[bank], w=[cwb])
        er = RR(['dve', 'pool'])
        for k in range(5):
            for g in range(8):
                P.ts(er(), Dg[:, k, g, :], ident, cwb[:, g, k:k + 1], None, ALU.mult, r=[cwb, cst], w=[Dg])
        P.ld(dtraw[:, :, :], C.dtbuf[b].rearrange("(c p) k -> p c k", p=128), w=[dtraw])
        P.tt('dve', dtv[:, :, :], dtraw[:, :, :], bc3(dtb[:, :], NCH, 1), ALU.add, r=[dtraw, dtb], w=[dtv])
        P.act(dtv[:, :, :], dtv[:, :, :], AF.Exp, r=[dtv], w=[dtv])
        P.ts('dve', dtv[:, :, :], dtv[:, :, :], 1.0, None, ALU.add, r=[dtv], w=[dtv])
        P.act(dtv[:, :, :], dtv[:, :, :], AF.Ln, r=[dtv], w=[dtv])
        P.tt('dve', av[:, :, :], dtv[:, :, :], bc3(negA[:, :], NCH, 1), ALU.mult, r=[dtv, negA], w=[av])
        for c in range(NCH):
            for (g0, ng_, dst, boff) in ((0, 4, xbf, 0), (4, 2, Btok, 512)):
                bank = psr()
                n = ng_ * 128
                P.mm(bank[:, 0:n], ones_bf[0:1, :], cbrow_bf[0:1, boff:boff + n], start=True, stop=False,
                     r=[ones_bf, cbrow_bf], w=[bank])
                for gi in range(ng_):
                    g = g0 + gi
                    for k in range(5):
                        P.mm(bank[:, gi * 128:(gi + 1) * 128], XT[:, g, c * 128 + k:c * 128 + k + 128], Dg[:, k, g, :],
                             start=False, stop=(gi == ng_ - 1 and k == 4), r=[XTb[g], Dg], w=[bank])
                P.act(dst[:, c, :], bank[:, 0:n], AF.Silu, r=[bank], w=[dst])
        for tb in range(T // TB):
            for gi in range(4):
                g = 4 + gi
                bank = psr()
                for k in range(5):
                    P.mm(bank[:, 0:TB], Dg[:, k, g, :], XT[:, g, tb * TB + k:tb * TB + k + TB], start=(k == 0), stop=(k == 4),
                         r=[XTb[g], Dg], w=[bank])
                dst = BT if gi < 2 else CT
                P.act(dst[:, gi % 2, tb * TB:(tb + 1) * TB], bank[:, 0:TB], AF.Silu, bias=cwb[:, g, 5:6], r=[bank, cwb], w=[dst])
        P.emit()
        C.stats.append(('B0', P.stats))
        st0.close()
        ysc = sb('ysc', [128, NCH, 16], F32)
        cs_sb = [sb('cs_sb%d' % i, [128, 32], F32) for i in range(2)]
        dd = [sb('dd%d' % i, [128, 16], F32) for i in range(2)]
        et = [sb('et%d' % i, [128, 16], F32) for i in range(2)]
        wd = [sb('wd%d' % i, [128, 16], F32) for i in range(2)]
        xw = [sb('xw%d' % i, [128, 2, 512], BF16) for i in range(2)]
        Srun = sb('Srun', [128, 2, 512], F32)
        Sin = sb('Sin', [128, NCH, 2, 512], BF16)
        rhsS = [sb('rhsS%d' % i, [128, 2, 8, 128], F32) for i in range(2)]
        E = [sb('E%d' % i, [128, 2, 8, 128], F32) for i in range(2)]
        SM = [sb('SM%d' % i, [128, 2, 2, 128], F32) for i in range(2)]
        G = [sb('G%d' % i, [128, 2, 8, 128], BF16) for i in range(2)]
        ta = [sb('ta%d' % i, [128, 512], F32) for i in range(2)]
        tb_ = [sb('tb%d' % i, [128, 512], F32) for i in range(2)]
        yt = [sb('yt%d' % i, [128, 512], F32) for i in range(2)]
        zt = [sb('zt%d' % i, [128, 512], F32) for i in range(2)]
        sq = [sb('sq%d' % i, [128, 512], F32) for i in range(2)]
        ss = [sb('ss%d' % i, [128, 8], F32) for i in range(2)]
        P = Prog(nc, C.sems)
        psr = RR(ps)
        P.op('pool', lambda e: e.memset(Srun[:, :, :], 0.0), [], [Srun.buf])
        for i in range(NCH):
            cc = (i, NCH - 1 - i)
            k2 = i % 2
            bank = psr()
            for d in range(2):
                P.mm(bank[:, d * 8:(d + 1) * 8], U if d == 0 else Lo, av[:, cc[d], d * 8:(d + 1) * 8], r=[cst, av], w=[bank])
                P.mm(bank[:, 16 + d * 8:16 + (d + 1) * 8], ones, av[:, cc[d], d * 8:(d + 1) * 8], r=[cst, av], w=[bank])
            P.cp('act', cs_sb[k2][:, :], bank[:, 0:32], r=[bank], w=[cs_sb[k2]])
            P.tt('dve', dd[k2][:, :], cs_sb[k2][:, 16:32], cs_sb[k2][:, 0:16], ALU.subtract, r=[cs_sb[k2]], w=[dd[k2]])
            P.act(dd[k2][:, :], dd[k2][:, :], AF.Exp, r=[dd[k2]], w=[dd[k2]])
            P.act(et[k2][:, :], cs_sb[k2][:, 16:32], AF.Exp, r=[cs_sb[k2]], w=[et[k2]])
            for d in range(2):
                sl = slice(d * 8, (d + 1) * 8)
                P.act(ysc[:, cc[d], sl], cs_sb[k2][:, sl], AF.Exp, r=[cs_sb[k2]], w=[ysc])
                P.tt('dve', wd[k2][:, sl], dd[k2][:, sl], dtv[:, cc[d], sl], ALU.mult, r=[dd[k2], dtv], w=[wd[k2]])
                P.tt('dve', xw[k2][:, d, :].rearrange("p (h q) -> p h q", q=64),
                     xbf[:, cc[d], :].rearrange("p (h q) -> p h q", q=64), bc3(wd[k2][:, sl], 64, 2), ALU.mult,
                     r=[xbf, wd[k2]], w=[xw[k2]])
            for d in range(2):
                bank = psr()
                for g in range(2):
                    P.mm(bank[:, g * 256:(g + 1) * 256], Btok[:, cc[d], g * 128:(g + 1) * 128], xw[k2][:, d, g * 256:(g + 1) * 256],
                         r=[Btok, xw[k2]], w=[bank])
                P.cp('pool', Sin[:, cc[d], d, :], Srun[:, d, :], r=[Srun], w=[Sin])
                P.tt('dve', Srun[:, d, :].rearrange("p (h q) -> p h q", q=64), Srun[:, d, :].rearrange("p (h q) -> p h q", q=64),
                     bc3(et[k2][:, d * 8:(d + 1) * 8], 64, 2), ALU.mult, r=[Srun, et[k2]], w=[Srun])
                P.tt('dve', Srun[:, d, :], Srun[:, d, :], bank[:, :], ALU.add, r=[Srun, bank], w=[Srun])
        def pass2(par):
            psr = RR(ps[4 * par:4 * par + 4])
            for c in range(par, NCH, 2):
                k2 = c % 2
                rows = slice(c * 128, (c + 1) * 128)
                csl = slice(c * 128, (c + 1) * 128)
                P.ld(zt[k2][:, :], C.zbuf[b, rows, :], w=[zt[k2]])
                for d in range(2):
                    P.tt('dve' if d == 0 else 'pool', rhsS[k2][:, d, :, :], bc3(U if d == 0 else Lo, 8, 1),
                         bc3(av[:, c, d * 8:(d + 1) * 8], 128, 2), ALU.mult, r=[cst, av], w=[rhsS[k2]])
                yield
                for d in range(2):
                    for hh in range(2):
                        bank = psr()
                        P.mm(bank[:, :], Ls if d == 0 else Us, rhsS[k2][:, d, hh * 4:(hh + 1) * 4, :].rearrange("p a b -> p (a b)"),
                             r=[cst, rhsS[k2]], w=[bank])
                        P.act(E[k2][:, d, hh * 4:(hh + 1) * 4, :].rearrange("p a b -> p (a b)"), bank[:, :], AF.Exp, r=[bank], w=[E[k2]])
                yield
                bank = psr()
                for g in range(2):
                    P.mm(bank[:, g * 128:(g + 1) * 128], BT[:, g, csl], CT[:, g, csl], r=[BT, CT], w=[bank])
                for d in range(2):
                    P.tt('dve', SM[k2][:, d, :, :], bank[:, 0:256].rearrange("p (g l) -> p g l", l=128), bc3(U if d == 0 else Lo, 2, 1),
                         ALU.mult, r=[bank, cst], w=[SM[k2]])
                yield
                for d in range(2):
                    for h in range(8):
                        P.stt(G[k2][:, d, h, :], E[k2][:, d, h, :], dtv[:, c, d * 8 + h:d * 8 + h + 1], SM[k2][:, d, h // 4, :],
                              ALU.mult, ALU.mult, r=[E[k2], dtv, SM[k2]], w=[G[k2]])
                yield
                bY1 = psr()
                for h in range(8):
                    for d in range(2):
                        P.mm(bY1[:, h * 64:(h + 1) * 64], G[k2][:, d, h, :], xbf[:, c, h * 64:(h + 1) * 64], start=(d == 0), stop=(d == 1),
                             r=[G[k2], xbf], w=[bY1])
                yield
                bY2 = [psr(), psr()]
                for d in range(2):
                    for g in range(2):
                        P.mm(bY2[d][:, g * 256:(g + 1) * 256], CT[:, g, csl], Sin[:, c, d, g * 256:(g + 1) * 256], r=[CT, Sin], w=[bY2[d]])
                yield
                v3 = lambda ap: ap.rearrange("p (h q) -> p h q", q=64)
                P.tt('dve', v3(ta[k2][:, :]), v3(bY2[0][:, :]), bc3(ysc[:, c, 0:8], 64, 2), ALU.mult, r=[bY2[0], ysc], w=[ta[k2]])
                P.tt('dve', v3(tb_[k2][:, :]), v3(bY2[1][:, :]), bc3(ysc[:, c, 8:16], 64, 2), ALU.mult, r=[bY2[1], ysc], w=[tb_[k2]])
                y = yt[k2]
                P.tt('pool', y[:, :], ta[k2][:, :], tb_[k2][:, :], ALU.add, r=[ta[k2], tb_[k2]], w=[y])
                P.tt('dve', y[:, :], y[:, :], bY1[:, :], ALU.add, r=[y, bY1], w=[y])
                P.tt('pool', sq[k2][:, :], xbf[:, c, :], dsk[:, :], ALU.mult, r=[xbf, dsk], w=[sq[k2]])
                P.tt('pool', y[:, :], y[:, :], sq[k2][:, :], ALU.add, r=[y, sq[k2]], w=[y])
                yield
                P.act(zt[k2][:, :], zt[k2][:, :], AF.Silu, r=[zt[k2]], w=[zt[k2]])
                P.tt('dve', y[:, :], y[:, :], zt[k2][:, :], ALU.mult, r=[y, zt[k2]], w=[y])
                P.act(sq[k2][:, :], y[:, :], AF.Square, r=[y], w=[sq[k2]])
                P.op('dve', lambda e, o=ss[k2][:, 0:2], i_=sq[k2][:, :].rearrange("p (g q) -> p g q", q=256): e.reduce_sum(o, i_, axis=AX.X),
                     [sq[k2].buf], [ss[k2].buf])
                yield
                P.ts('dve', ss[k2][:, 2:4], ss[k2][:, 0:2], 1.0 / 256.0, RMS_EPS, ALU.mult, ALU.add, r=[ss[k2]], w=[ss[k2]])
                P.act(ss[k2][:, 4:6], ss[k2][:, 2:4], AF.Sqrt, r=[ss[k2]], w=[ss[k2]])
                P.op('dve', lambda e, o=ss[k2][:, 6:8], i_=ss[k2][:, 4:6]: e.reciprocal(o, i_), [ss[k2].buf], [ss[k2].buf])
                for g in range(2):
                    P.ts('dve', y[:, g * 256:(g + 1) * 256], y[:, g * 256:(g + 1) * 256], ss[k2][:, 6 + g:7 + g], None, ALU.mult,
                         r=[y, ss[k2]], w=[y])
                P.tt('pool', y[:, :], y[:, :], ng[:, :], ALU.mult, r=[y, ng], w=[y])
                P.ld(C.ymix[b, rows, 0:512], y[:, :], r=[y])
                yield
        for par_ in range(2):
            P.set_lane(par_ + 1)
            for _ in pass2(par_):
                pass
        P.set_lane(0)
        P.emit()
        C.stats.append(('B', P.stats))


def stage_C(C, l, b):
    nc, T, NCH = C.nc, C.T, C.NCH
    cst = C.cst
    ident, U, Lo, Us, Ls, ones = (cst[:, i * 128:(i + 1) * 128] for i in range(6))
    v3 = lambda ap: ap.rearrange("p (h q) -> p h q", q=64)
    with contextlib.ExitStack() as st:
        sb = lambda n, sh, dt: _sb(st, nc, n, sh, dt)
        murow = sb('murow', [14, 128], F32)
        muT = sb('muT', [128, 3, 14], F32)
        Dm = sb('Dm', [128, 14, 128], BF16)
        Dh = sb('Dh', [128, 14, 128], BF16)
        kkb = sb('kkb', [128, 512], F32)
        kab = sb('kab', [128, 512], F32)
        rkb = sb('rkb', [128, 512], F32)
        w0b = sb('w0b', [128, 2, 512], F32)
        a0b = sb('a0b', [128, 2, 512], F32)
        LW = sb('LW', [128, 2, 512], BF16)
        GU = sb('GU', [128, 512], BF16)
        MK = sb('MK', [128, 2, 512], F32)
        MKa = sb('MKa', [128, 2, 128], F32)
        g32 = sb('g32', [128, 512], F32)
        def alloc_inst():
            RTc = [sb('RTc%d' % i, [128, 14, 130], BF16) for i in range(2)]
            bon = sb('bon', [128, NCH, 8], F32)
            r32 = sb('r32', [128, 512], F32)
            k32 = sb('k32', [128, 512], F32)
            v32 = sb('v32', [128, 512], F32)
            vbf = sb('vbf', [128, 512], BF16)
            LT = sb('LT', [128, 128], BF16)
            sg = sb('sg', [128, 128], BF16)
            lw32 = sb('lw32', [128, 512], F32)
            a32 = sb('a32', [128, 512], F32)
            kk = sb('kk', [128, 512], F32)
            tmp = sb('tmp', [128, 512], F32)
            tmp2 = sb('tmp2', [128, 512], F32)
            kd = sb('kd', [128, 512], F32)
            bb = sb('bb', [128, 512], F32)
            sm8 = sb('sm8', [128, 32], F32)
            gC = sb('gC', [128, 4], F32)
            Ep = sb('Ep', [128, 512], F32)
            En = sb('En', [128, 512], F32)
            bt_bf = sb('bt_bf', [128, 512], BF16)
            kdt_bf = sb('kdt_bf', [128, 512], BF16)
            kt_bf = sb('kt_bf', [128, 512], NEU_DT)
            KR = sb('KR', [128, 4, 2, 128], BF16)
            btT = sb('btT', [128, 4, 128], BF16)
            kdtT = sb('kdtT', [128, 4, 128], BF16)
            R3 = sb('R3', [128, 8, 3, 128], BF16)
            NM = [sb('NM%d' % i, [128, 8, 2, 128], NEU_DT) for i in range(2)]
            X = [sb('X%d' % i, [128, 8, 128], NEU_DT) for i in range(2)]
            Z1 = sb('Z1', [128, 512], NEU_DT)
            Ut = sb('Ut', [128, 512], F32)
            WT = sb('WT', [128, 4, 128], BF16)
            Mst = sb('Mst', [128, 4, 128], F32)
            Mbf = sb('Mbf', [128, 4, 128], BF16)
            Un = sb('Un', [128, 512], BF16)
            Yo = [sb('Yo%d' % i, [128, 512], F32) for i in range(2)]
            return dict(locals())
        IT = [alloc_inst(), alloc_inst()]
        ps = _psum(st, nc)
        P = Prog(nc, C.sems)
        psr = RR(ps)
        evr = RR(['act'])
        P.ld(murow[0:14, :], C.w['mu_rwkv'][l].rearrange("(g p) -> g p", p=128), w=[murow])
        bank = psr()
        P.tr(bank[:, 0:14], murow[0:14, :], ident[0:14, 0:14], r=[murow, cst], w=[bank])
        P.cp('dve', muT[:, 0, :], bank[:, 0:14], r=[bank], w=[muT])
        P.ts('dve', muT[:, 1, :], muT[:, 0, :], -1.0, 1.0, ALU.mult, ALU.add, r=[muT], w=[muT])
        P.ts('dve', muT[:, 2, :], muT[:, 0, :], 0.5, None, ALU.mult, r=[muT], w=[muT])
        er = RR(['dve', 'pool'])
        for g in range(14):
            P.ts(er(), Dm[:, g, :], ident, muT[:, 1, g:g + 1], None, ALU.mult, r=[muT, cst], w=[Dm])
            P.ts(er(), Dh[:, g, :], ident, muT[:, 2, g:g + 1], None, ALU.mult, r=[muT, cst], w=[Dh])
        load_bcast(P, kkb, C.w['k_k'][l], 512)
        load_bcast(P, kab, C.w['k_a'][l], 512)
        load_bcast(P, rkb, C.w['r_k'][l].rearrange("a b -> (a b)"), 512)
        for d in range(2):
            P.ld(w0b[:, d, :], C.w['w0'][l, d].partition_broadcast(128), w=[w0b])
            P.ld(a0b[:, d, :], C.w['a0'][l, d].partition_broadcast(128), w=[a0b])
            P.ld(LW[0:64, d, :], C.w['w_up'][l, d], w=[LW], eng='pool')
            P.ld(LW[64:128, d, :], C.w['a_up'][l, d], w=[LW], eng='pool')
        P.ld(GU[:, :], C.w['g_up'][l], w=[GU], eng='pool')
        for d in range(2):
            strict, incl, strict_ts = (Us, U, Ls) if d == 0 else (Ls, Lo, Us)
            P.ts('dve', MK[:, d, 0:128], strict, -1.0, None, ALU.mult, r=[cst], w=[MK])
            P.cp('dve', MK[:, d, 128:256], incl, r=[cst], w=[MK])
            P.cp('dve', MK[:, d, 256:384], strict, r=[cst], w=[MK])
            P.cp('dve', MK[:, d, 384:512], incl, r=[cst], w=[MK])
            P.ts('dve', MKa[:, d, :], strict_ts, -1.0, None, ALU.mult, r=[cst], w=[MKa])
        taps = (Dh, Dm, Dh)
        ydb = [[Buf('yd') for _ in range(NCH)] for _ in range(2)]
        vgb = [[Buf('vg') for _ in range(NCH)] for _ in range(2)]
        def inst(d):
            I_ = IT[d]
            psr = RR(ps[4 * d:4 * d + 4])
            (RTc, r32, k32, v32, vbf, LT, sg, lw32, a32, kk, tmp, tmp2, kd, bb, sm8, gC, Ep, En, bt_bf, kdt_bf, kt_bf, KR, btT, kdtT, R3, NM, X, Z1, Ut, WT, Mst, Mbf, Un, Yo, bon) = (I_[n] for n in ('RTc', 'r32', 'k32', 'v32', 'vbf', 'LT', 'sg', 'lw32', 'a32', 'kk', 'tmp', 'tmp2', 'kd', 'bb', 'sm8', 'gC', 'Ep', 'En', 'bt_bf', 'kdt_bf', 'kt_bf', 'KR', 'btT', 'kdtT', 'R3', 'NM', 'X', 'Z1', 'Ut', 'WT', 'Mst', 'Mbf', 'Un', 'Yo', 'bon'))
            P.op('pool', lambda e: e.memset(Mst[:, :, :], 0.0), [], [Mst.buf])
            P.op('pool', lambda e: e.memset(Mbf[:, :, :], 0.0), [], [Mbf.buf])
            for ci in range(NCH):
                c = ci if d == 0 else NCH - 1 - ci
                rows = slice(c * 128, (c + 1) * 128)
                RT = RTc[ci % 2]
                lo, hi = max(c * 128 - 1, 0), min(c * 128 + 129, T)
                if c == 0:
                    P.op('pool', lambda e, t=RT: e.memset(t[:, :, 0:1], 0.0), [], [RT.buf])
                if c == NCH - 1:
                    P.op('pool', lambda e, t=RT: e.memset(t[:, :, 129:130], 0.0), [], [RT.buf])
                P.ld(RT[:, :, lo - (c * 128 - 1):hi - (c * 128 - 1)],
                     C.rwT[b].rearrange("(g p) t -> p g t", p=128)[:, :, lo:hi], w=[RT])
                for (g0, dst) in ((0, r32), (4, k32), (8, v32)):
                    bank = psr()
                    for gi in range(4):
                        g = g0 + gi
                        for k in range(3):
                            P.mm(bank[:, gi * 128:(gi + 1) * 128], RT[:, g, k:k + 128], taps[k][:, g, :],
                                 start=(k == 0), stop=(k == 2), r=[RT, Dm, Dh], w=[bank])
                    if dst is v32 and d == 1:
                        P.cp(evr(), vbf[:, :], bank[:, :], r=[bank], w=[vbf])
                    else:
                        P.cp(evr(), dst[:, :], bank[:, :], r=[bank], w=[dst])
                    yield
                if d == 0:
                    P.cp('act', vbf[:, :], v32[:, :], r=[v32], w=[vbf])
                bank = psr()
                for gi in range(2):
                    g = 12 + gi
                    for k in range(3):
                        P.mm(bank[:, gi * 128:(gi + 1) * 128], taps[k][:, g, :], RT[:, g, k:k + 128],
                             start=(k == 0), stop=(k == 2), r=[RT, Dm, Dh], w=[bank])
                P.act(LT[0:64, :], bank[0:64, 0:128], AF.Tanh, r=[bank], w=[LT])
                P.cp('dve', LT[64:128, :], bank[64:128, 0:128], r=[bank], w=[LT])
                P.act(sg[:, :], bank[:, 128:256], AF.Sigmoid, r=[bank], w=[sg])
                yield
                bW, bA = psr(), psr()
                P.mm(bW[:, :], LT[0:64, :], LW[0:64, d, :], r=[LT, LW], w=[bW])
                yield
                P.mm(bA[:, :], LT[64:128, :], LW[64:128, d, :], r=[LT, LW], w=[bA])
                yield
                P.tt('dve', lw32[:, :], bW[:, :], w0b[:, d, :], ALU.add, r=[bW, w0b], w=[lw32])
                yield
                P.act(lw32[:, :], lw32[:, :], AF.Sigmoid, r=[lw32], w=[lw32])
                yield
                P.act(lw32[:, :], lw32[:, :], AF.Identity, scale=-DECAY_C, r=[lw32], w=[lw32])
                yield
                P.tt('dve', a32[:, :], bA[:, :], a0b[:, d, :], ALU.add, r=[bA, a0b], w=[a32])
                yield
                P.act(a32[:, :], a32[:, :], AF.Sigmoid, r=[a32], w=[a32])
                yield
                if d == 0:
                    bG = psr()
                    P.mm(bG[:, :], sg[:, :], GU[:, :], r=[sg, GU], w=[bG])
                    P.cp('act', g32[:, :], bG[:, :], r=[bG], w=[g32])
                    P.ld(C.vg[0, rows, :], v32[:, :], r=[v32], w=[vgb[0][c]])
                    P.ld(C.vg[1, rows, :], g32[:, :], r=[g32], w=[vgb[1][c]])
                yield
                P.tt('pool', kk[:, :], k32[:, :], kkb[:, :], ALU.mult, r=[k32, kkb], w=[kk])
                yield
                P.act(tmp[:, :], kk[:, :], AF.Square, r=[kk], w=[tmp])
                yield
                P.op('dve', lambda e, o=sm8[:, 0:8], i_=v3(tmp[:, :]): e.reduce_sum(o, i_, axis=AX.X), [tmp.buf], [sm8.buf])
                yield
                P.act(sm8[:, 8:16], sm8[:, 0:8], AF.Sqrt, r=[sm8], w=[sm8])
                yield
                P.ts('dve', sm8[:, 8:16], sm8[:, 8:16], 1e-12, None, ALU.max, r=[sm8], w=[sm8])
                yield
                P.op('dve', lambda e, o=sm8[:, 16:24], i_=sm8[:, 8:16]: e.reciprocal(o, i_), [sm8.buf], [sm8.buf])
                yield
                P.tt('dve', v3(kk[:, :]), v3(kk[:, :]), bc3(sm8[:, 16:24], 64, 2), ALU.mult, r=[kk, sm8], w=[kk])
                yield
                P.stt(tmp2[:, :], a32[:, :], -1.0, kab[:, :], ALU.add, ALU.mult, r=[a32, kab], w=[tmp2])
                yield
                P.stt(kd[:, :], tmp2[:, :], 1.0, k32[:, :], ALU.add, ALU.mult, r=[tmp2, k32], w=[kd])
                yield
                P.tt('pool', tmp[:, :], r32[:, :], kd[:, :], ALU.mult, r=[r32, kd], w=[tmp])
                yield
                P.tt('pool', tmp[:, :], tmp[:, :], rkb[:, :], ALU.mult, r=[tmp, rkb], w=[tmp])
                yield
                P.op('dve', lambda e, o=bon[:, c, :], i_=v3(tmp[:, :]): e.reduce_sum(o, i_, axis=AX.X), [tmp.buf], [bon.buf])
                yield
                P.tt('pool', bb[:, :], kk[:, :], a32[:, :], ALU.mult, r=[kk, a32], w=[bb])
                yield
                yield
                bC = psr()
                P.mm(bC[:, :], U if d == 0 else Lo, lw32[:, :], r=[cst, lw32], w=[bC])
                yield
                bT = psr()
                for jg in range(4):
                    P.mm(bT[:, 2 * jg:2 * jg + 2], lw32[:, jg * 128:(jg + 1) * 128], ones[:, 0:2], r=[lw32, cst], w=[bT])
                P.act(gC[:, :], bT[:, 0:8].rearrange("p (j two) -> p j two", two=2)[:, :, 0], AF.Exp, r=[bT], w=[gC])
                yield
                P.act(Ep[:, :], bC[:, :], AF.Exp, r=[bC], w=[Ep])
                yield
                P.act(En[:, :], bC[:, :], AF.Exp, scale=-1.0, r=[bC], w=[En])
                yield
                P.tt('dve', tmp2[:, :], bC[:, :], lw32[:, :], ALU.subtract, r=[bC, lw32], w=[tmp2])
                yield
                P.act(tmp2[:, :], tmp2[:, :], AF.Exp, r=[tmp2], w=[tmp2])
                yield
                P.tt('pool', r32[:, :], r32[:, :], Ep[:, :], ALU.mult, r=[r32, Ep], w=[r32])
                yield
                P.tt('dve', kk[:, :], kk[:, :], tmp2[:, :], ALU.mult, r=[kk, tmp2], w=[kk])
                yield
                P.tt('pool', bb[:, :], bb[:, :], En[:, :], ALU.mult, r=[bb, En], w=[bb])
                yield
                P.tt('dve', kd[:, :], kd[:, :], En[:, :], ALU.mult, r=[kd, En], w=[kd])
                yield
                P.cp('act', bt_bf[:, :], bb[:, :], r=[bb], w=[bt_bf])
                yield
                P.cp('act', kdt_bf[:, :], kd[:, :], r=[kd], w=[kdt_bf])
                yield
                P.cp('act', kt_bf[:, :], kk[:, :], r=[kk], w=[kt_bf])
                yield
                yield
                for (src, dstap, dbuf) in ((kk, KR[:, :, 0, :], KR), (r32, KR[:, :, 1, :], KR), (bb, btT[:, :, :], btT), (kd, kdtT[:, :, :], kdtT)):
                    bank = psr()
                    for jg in range(4):
                        P.tr(bank[:, jg * 128:(jg + 1) * 128], src[:, jg * 128:(jg + 1) * 128], ident, r=[src, cst], w=[bank])
                    P.cp(evr(), dstap, bank[:, :].rearrange("p (a b) -> p a b", b=128), r=[bank], w=[dbuf])
                    yield
                yield
                for h in range(8):
                    jg, rs = h // 2, slice((h % 2) * 64, (h % 2 + 1) * 64)
                    bM = psr()
                    krr = KR[rs, jg, :, :].rearrange("p a b -> p (a b)")
                    P.mm(bM[:, 0:256], btT[rs, jg, :], krr, r=[btT, KR], w=[bM])
                    P.mm(bM[:, 256:512], kdtT[rs, jg, :], krr, r=[kdtT, KR], w=[bM])
                    P.tt('dve', NM[0][:, h, 0, :], bM[:, 0:128], MK[:, d, 0:128], ALU.mult, r=[bM, MK], w=[NM[0]])
                    P.tt('dve', R3[:, h, :, :].rearrange("p a b -> p (a b)"), bM[:, 128:512], MK[:, d, 128:512], ALU.mult,
                         r=[bM, MK], w=[R3])
                    if h % 2 == 1:
                        yield
                yield
                for hh in range(2):
                    bank = psr()
                    rs = slice(hh * 64, (hh + 1) * 64)
                    for jg in range(4):
                        P.mm(bank[:, jg * 128:(jg + 1) * 128], KR[rs, jg, 0, :], btT[rs, jg, :], r=[KR, btT], w=[bank])
                    P.tt('dve', NM[0][:, hh:8:2, 1, :], bank[:, :].rearrange("p (a b) -> p a b", b=128),
                         bc3(MKa[:, d, :], 4, 1), ALU.mult, r=[bank, MKa], w=[NM[0]])
                P.tt('dve', X[0][:, :, :], NM[0][:, :, 0, :], bc3(ident, 8, 1), ALU.add, r=[NM[0], cst], w=[X[0]])
                yield
                for j in range(6):
                    yield
                    cur, nxt = NM[j % 2], NM[(j + 1) % 2]
                    Xc, Xn = X[j % 2], X[(j + 1) % 2]
                    for hp in range(4):
                        bank = psr()
                        for q in range(2):
                            h = hp * 2 + q
                            if j < 5:
                                P.mm(bank[:, q * 256:q * 256 + 128], cur[:, h, 1, :], cur[:, h, 0, :], r=[cur], w=[bank])
                            P.mm(bank[:, q * 256 + 128:q * 256 + 256], cur[:, h, 0, :], cur[:, h, 1, :], r=[cur], w=[bank])
                        if j < 5:
                            P.cp(evr(), nxt[:, hp * 2:hp * 2 + 2, :, :].rearrange("p a b c -> p (a b c)"), bank[:, :], r=[bank], w=[nxt])
                        else:
                            P.cp(evr(), nxt[:, hp * 2:hp * 2 + 2, 1, :],
                                 bank[:, :].rearrange("p (a b c) -> p a b c", b=2, c=128)[:, :, 1, :], r=[bank], w=[nxt])
                    yield
                    for hp in range(2):
                        bank = psr()
                        for q in range(4):
                            h = hp * 4 + q
                            P.mm(bank[:, q * 128:(q + 1) * 128], nxt[:, h, 1, :], Xc[:, h, :], r=[nxt, Xc], w=[bank])
                        P.tt('dve', Xn[:, hp * 4:(hp + 1) * 4, :].rearrange("p a b -> p (a b)"), bank[:, :],
                             Xc[:, hp * 4:(hp + 1) * 4, :].rearrange("p a b -> p (a b)"), ALU.add, r=[bank, Xc], w=[Xn])
                yield
                XF = X[0]
                bZ = psr()
                for h in range(8):
                    P.mm(bZ[:, h * 64:(h + 1) * 64], R3[:, h, 1, :], vbf[:, h * 64:(h + 1) * 64], r=[R3, vbf], w=[bZ])
                P.cp('act', Z1[:, :], bZ[:, :], r=[bZ], w=[Z1])
                bU = psr()
                for h in range(8):
                    P.mm(bU[:, h * 64:(h + 1) * 64], XF[:, h, :], Z1[:, h * 64:(h + 1) * 64], r=[XF, Z1], w=[bU])
                P.cp('act', Ut[:, :], bU[:, :], r=[bU], w=[Ut])
                for hp in range(2):
                    bank = psr()
                    for q in range(4):
                        h = hp * 4 + q
                        jg = h // 2
                        P.mm(bank[:, q * 128:(q + 1) * 128], kt_bf[:, jg * 128:(jg + 1) * 128], XF[:, h, :], r=[kt_bf, XF], w=[bank])
                    for q in range(4):
                        h = hp * 4 + q
                        jg, rs = h // 2, slice((h % 2) * 64, (h % 2 + 1) * 64)
                        P.cp(evr(), WT[rs, jg, :], bank[rs, q * 128:(q + 1) * 128], r=[bank], w=[WT])
                yield
                bP = psr()
                for jg in range(4):
                    P.mm(bP[:, jg * 128:(jg + 1) * 128], WT[:, jg, :], Mbf[:, jg, :], r=[WT, Mbf], w=[bP])
                P.stt(Un[:, :], bP[:, :], -1.0, Ut[:, :], ALU.mult, ALU.subtract, r=[bP, Ut], w=[Un])
                yield
                bY = psr()
                for jg in range(4):
                    P.mm(bY[:, jg * 128:(jg + 1) * 128], KR[:, jg, 1, :], Mbf[:, jg, :], start=True, stop=False, r=[KR, Mbf], w=[bY])
                    for h in (2 * jg, 2 * jg + 1):
                        hs = slice(h * 64, (h + 1) * 64)
                        P.mm(bY[:, hs], R3[:, h, 2, :], vbf[:, hs], start=False, stop=False, r=[R3, vbf], w=[bY])
                    for h in (2 * jg, 2 * jg + 1):
                        hs = slice(h * 64, (h + 1) * 64)
                        P.mm(bY[:, hs], R3[:, h, 0, :], Un[:, hs], start=False, stop=(h == 2 * jg + 1), r=[R3, Un], w=[bY])
                yo = Yo[ci % 2]
                P.cp('act', yo[:, :], bY[:, :], r=[bY], w=[yo])
                P.ld(C.ydir[d, rows, :], yo[:, :], r=[yo], w=[ydb[d][c]])
                yield
                bS = psr()
                for jg in range(4):
                    js = slice(jg * 128, (jg + 1) * 128)
                    P.mm(bS[:, js], bt_bf[:, js], Un[:, js], start=True, stop=False, r=[bt_bf, Un], w=[bS])
                    P.mm(bS[:, js], kdt_bf[:, js], vbf[:, js], start=False, stop=True, r=[kdt_bf, vbf], w=[bS])
                for hh in range(2):
                    rs = slice(hh * 64, (hh + 1) * 64)
                    src = bS[rs, :].rearrange("p (j q) -> p j q", q=128)[:, :, hh * 64:(hh + 1) * 64]
                    mv = Mst[rs, :, hh * 64:(hh + 1) * 64]
                    P.tt('dve', mv, mv, src, ALU.add, r=[Mst, bS], w=[Mst])
                    P.tt('dve', mv, mv, gC[rs, :].unsqueeze(2).broadcast_to([64, 4, 64]), ALU.mult, r=[Mst, gC], w=[Mst])
                P.cp('act', Mbf[:, :, :], Mst[:, :, :], r=[Mst], w=[Mbf])
                yield
        alive = [inst(0), inst(1)]
        while alive:
            for g_ in list(alive):
                try:
                    next(g_)
                except StopIteration:
                    alive.remove(g_)
        P.emit()
        C.stats.append(('C', P.stats))
        fy = [_Quad([IT[i][n] for n in ('r32', 'k32', 'v32', 'lw32')]) for i in range(2)]
        lgb, lbb = IT[1]['a32'], IT[1]['kk']
        sm8, tmp = IT[0]['sm8'], IT[0]['tmp']
        bon0, bon1 = IT[0]['bon'], IT[1]['bon']
        P = Prog(nc, C.sems)
        load_bcast(P, lgb, C.w['lnx_g'][l], 512)
        load_bcast(P, lbb, C.w['lnx_b'][l], 512)
        def fin(par):
          sm8, tmp = IT[par]['sm8'], IT[par]['tmp']
          for c in range(par, NCH, 2):
            rows = slice(c * 128, (c + 1) * 128)
            f = fy[par]
            P.ld(f[:, 0, :], C.ydir[0, rows, :], r=[ydb[0][c]], w=[f])
            P.ld(f[:, 1, :], C.ydir[1, rows, :], r=[ydb[1][c]], w=[f])
            P.ld(f[:, 2, :], C.vg[0, rows, :], r=[vgb[0][c]], w=[f])
            P.ld(f[:, 3, :], C.vg[1, rows, :], r=[vgb[1][c]], w=[f])
            y = f[:, 0, :]
            P.tt('dve', y, y, f[:, 1, :], ALU.add, r=[f], w=[f])
            P.op('dve', lambda e, o=sm8[:, 0:8], i_=v3(y): e.reduce_sum(o, i_, axis=AX.X), [f.buf], [sm8.buf])
            P.act(tmp[:, :], y, AF.Square, r=[f], w=[tmp])
            P.op('dve', lambda e, o=sm8[:, 8:16], i_=v3(tmp[:, :]): e.reduce_sum(o, i_, axis=AX.X), [tmp.buf], [sm8.buf])
            P.ts('dve', sm8[:, 0:16], sm8[:, 0:16], 1.0 / 64.0, None, ALU.mult, r=[sm8], w=[sm8])
            P.tt('dve', sm8[:, 16:24], sm8[:, 0:8], sm8[:, 0:8], ALU.mult, r=[sm8], w=[sm8])
            P.tt('dve', sm8[:, 16:24], sm8[:, 8:16], sm8[:, 16:24], ALU.subtract, r=[sm8], w=[sm8])
            P.ts('dve', sm8[:, 16:24], sm8[:, 16:24], GN_EPS, None, ALU.add, r=[sm8], w=[sm8])
            P.act(sm8[:, 16:24], sm8[:, 16:24], AF.Sqrt, r=[sm8], w=[sm8])
            P.op('dve', lambda e, o=sm8[:, 24:32], i_=sm8[:, 16:24]: e.reciprocal(o, i_), [sm8.buf], [sm8.buf])
            P.tt('dve', v3(y), v3(y), bc3(sm8[:, 0:8], 64, 2), ALU.subtract, r=[f, sm8], w=[f])
            P.tt('dve', v3(y), v3(y), bc3(sm8[:, 24:32], 64, 2), ALU.mult, r=[f, sm8], w=[f])
            P.tt('pool', y, y, lgb[:, :], ALU.mult, r=[f, lgb], w=[f])
            P.tt('pool', y, y, lbb[:, :], ALU.add, r=[f, lbb], w=[f])
            P.tt('dve', sm8[:, 0:8], bon0[:, c, :], bon1[:, c, :], ALU.add, r=[bon0, bon1, sm8], w=[sm8])
            P.tt('dve', v3(tmp[:, :]), v3(f[:, 2, :]), bc3(sm8[:, 0:8], 64, 2), ALU.mult, r=[f, sm8], w=[tmp])
            P.tt('pool', y, y, tmp[:, :], ALU.add, r=[f, tmp], w=[f])
            P.tt('pool', y, y, f[:, 3, :], ALU.mult, r=[f], w=[f])
            P.ld(C.ymix[b, rows, 512:1024], y, r=[f])
        for par_ in range(2):
            P.set_lane(par_ + 1)
            fin(par_)
        P.set_lane(0)
        P.emit()
        C.stats.append(('C', P.stats))


def build(T=2048, NS=2, depth=2, debug=False, stages='ABCD'):
    nc = bass.Bass("TRN2", target_bir_lowering=False)
    C = Ctx()
    C.nc, C.T, C.NS, C.NCH = nc, T, NS, T // 128
    C.stats = []
    C.x = nc.dram_tensor("x", [NS, T, D], F32, kind="ExternalInput").ap()
    C.w = {n: nc.dram_tensor(n, W_SHAPES[n], F32, kind="ExternalInput").ap() for n in W_NAMES}
    C.cst_d = nc.dram_tensor("cst", [128, 768], F32, kind="ExternalInput").ap()
    C.out = nc.dram_tensor("out", [NS, T, D], F32, kind="ExternalOutput").ap()
    kind = "ExternalOutput" if debug else "Internal"
    C.hbuf = nc.dram_tensor("hbuf", [NS, T, D], F32, kind=kind).ap()
    C.h1buf = nc.dram_tensor("h1buf", [NS, T, D], F32, kind=kind).ap()
    C.zbuf = nc.dram_tensor("zbuf", [NS, T, 512], F32, kind=kind).ap()
    C.dtbuf = nc.dram_tensor("dtbuf", [NS, T, 16], F32, kind=kind).ap()
    C.xbcT = nc.dram_tensor("xbcT", [NS, 1024, T], BF16, kind=kind).ap()
    C.rwT = nc.dram_tensor("rwT", [NS, 1792, T], BF16, kind=kind).ap()
    C.ymix = nc.dram_tensor("ymix", [NS, T, D], F32, kind=kind).ap()
    C.ydir = nc.dram_tensor("ydir", [2, T, 512], F32, kind=kind).ap()
    C.vg = nc.dram_tensor("vg", [3, T, 512], F32, kind=kind).ap()
    with nc.sbuf_tensor('cst_sb', [128, 768], F32) as cst_sb, contextlib.ExitStack() as semstack:
        C.cst = Tile(cst_sb, 'cst')
        C.sems = SemState(nc, 8, semstack)
        stage_consts(C)
        for l in range(depth):
            if 'A' in stages:
                stage_A(C, l)
            for b in range(NS):
                if 'B' in stages:
                    stage_B(C, l, b)
                if 'C' in stages:
                    stage_C(C, l, b)
            if 'D' in stages:
                with contextlib.ExitStack() as wst:
                    Wf = _sb(wst, nc, 'Wf', [128, 8, 4096], BF16)
                    Wp = _sb(wst, nc, 'Wp', [128, 32, D], BF16)
                    pre = (Wf, [Buf('Wf%d' % k) for k in range(8)], Wp, [Buf('Wp%d' % k) for k in range(32)])
                    stage_D1(C, l, pre)
                    stage_D2(C, l, (l == depth - 1), pre)
    return nc, C


_CACHE = {}


def kernel(**inputs):
    n_cores = 8
    x = np.ascontiguousarray(np.asarray(inputs["x"], dtype=np.float32))
    B, T, _ = x.shape
    NS = B // n_cores
    key = (T, NS)
    if key not in _CACHE:
        _CACHE[key] = build(T=T, NS=NS, depth=2, debug=False)[0]
    nc = _CACHE[key]
    cst = make_consts()
    ws = {n: np.ascontiguousarray(np.asarray(inputs[n], dtype=np.float32)) for n in W_NAMES}
    in_maps = []
    for i in range(n_cores):
        m = dict(ws)
        m["x"] = np.ascontiguousarray(x[i * NS:(i + 1) * NS])
        m["cst"] = cst
        in_maps.append(m)
    res = run_bass_kernel_spmd(nc, in_maps, core_ids=list(range(n_cores)))
    return np.concatenate([np.asarray(r["out"], dtype=np.float32) for r in res.results], axis=0)
```

```python
import contextlib
import numpy as np
import concourse.bass as bass
import concourse.mybir as mybir
from concourse.bass_utils import run_bass_kernel_spmd

F32 = mybir.dt.float32
BF16 = mybir.dt.bfloat16
AF = mybir.ActivationFunctionType
ALU = mybir.AluOpType
AX = mybir.AxisListType

ENGS = ('pe', 'act', 'dve', 'pool', 'sp')
LANE_GRAN = 8


class Buf:
    __slots__ = ('name', 'last_w', 'readers')

    def __init__(self, name=''):
        self.name = name
        self.last_w = None
        self.readers = []


class _Op:
    __slots__ = ('eng', 'idx', 'fn', 'deps', 'is_dma', 'dslot', 'dval', 'needs_inc', 'cnt', 'waits', 'prog', 'key')

    def __init__(self, eng, idx, fn, is_dma):
        self.eng = eng
        self.idx = idx
        self.fn = fn
        self.is_dma = is_dma
        self.deps = []
        self.dslot = 0
        self.dval = 0
        self.needs_inc = False
        self.cnt = 0
        self.waits = []
        self.prog = None
        self.key = (1 << 60, 0, 0)


class SemState:
    def __init__(self, nc, ring=8, stack=None):
        self.stack = stack if stack is not None else contextlib.ExitStack()
        st = self.stack
        self.csem = {e: st.enter_context(nc.semaphore('c_' + e)) for e in ENGS if e != 'sp'}
        self.dsem = {e: [st.enter_context(nc.semaphore('d_%s_%d' % (e, i))) for i in range(ring)] for e in ('sp', 'pool', 'act')}
        self.c_off = {e: 0 for e in ENGS}
        self.d_cnt = {e: 0 for e in ENGS}


class Prog:
    def __init__(self, nc, sems=None, ring=8):
        self.nc = nc
        self.q = {e: [] for e in ENGS}
        self.ring = ring
        self.dma_hist = {e: [] for e in ENGS}
        self.sems = sems if sems is not None else SemState(nc, ring)
        self.gseq = 0
        self.lane = 0
        self.lane_base = 0
        self.lane_cnt = {}

    def _add(self, eng, fn, reads, writes, is_dma):
        op = _Op(eng, len(self.q[eng]), fn, is_dma)
        op.prog = self
        deps = []
        for b in reads:
            if b.last_w is not None:
                deps.append(b.last_w)
        for b in writes:
            if b.last_w is not None:
                deps.append(b.last_w)
            deps.extend(b.readers)
        if self.lane == 0:
            op.key = (self.gseq, 0, 0)
            self.gseq += 1
        else:
            c = self.lane_cnt.get(self.lane, 0)
            op.key = (self.lane_base + c // LANE_GRAN, self.lane, c)
            self.lane_cnt[self.lane] = c + 1
        seen = set()
        for d in deps:
            if d is not op and id(d) not in seen and getattr(d, 'prog', self) is self:
                seen.add(id(d))
                op.deps.append(d)
        for b in writes:
            b.last_w = op
            b.readers = []
        for b in reads:
            if b.last_w is not op:
                b.readers.append(op)
        self.q[eng].append(op)
        return op

    def set_lane(self, k):
        if self.lane == 0 and k != 0:
            self.lane_base = self.gseq
            self.lane_cnt = {}
        if k == 0 and self.lane != 0:
            self.gseq = self.lane_base + max(self.lane_cnt.values(), default=0) // LANE_GRAN + 1
        self.lane = k

    def op(self, eng, fn, reads=(), writes=()):
        return self._add(eng, fn, reads, writes, False)

    def dma(self, eng, fn, reads=(), writes=()):
        return self._add(eng, fn, reads, writes, True)


    @staticmethod
    def _bufs(lst):
        return [b.buf if hasattr(b, 'buf') else b for b in lst]

    def mm(self, out, lhsT, rhs, start=True, stop=True, r=(), w=()):
        return self.op('pe', lambda e: e.matmul(out, lhsT, rhs, start=start, stop=stop), self._bufs(r), self._bufs(w))

    def tr(self, out, in_, ident, r=(), w=()):
        return self.op('pe', lambda e: e.transpose(out, in_, ident), self._bufs(r), self._bufs(w))

    def act(self, out, in_, func, bias=None, scale=None, accum=None, r=(), w=()):
        kw = {}
        if bias is not None:
            kw['bias'] = bias
        if scale is not None:
            kw['scale'] = scale
        if accum is not None:
            kw['accum_out'] = accum
        return self.op('act', lambda e: e.activation(out, in_, func, **kw), self._bufs(r), self._bufs(w))

    def tt(self, eng, out, in0, in1, op, r=(), w=()):
        return self.op(eng, lambda e: e.tensor_tensor(out, in0, in1, op), self._bufs(r), self._bufs(w))

    def ts(self, eng, out, in0, s1, s2=None, op0=None, op1=None, r=(), w=()):
        if op1 is None:
            return self.op(eng, lambda e: e.tensor_scalar(out, in0, s1, None, op0), self._bufs(r), self._bufs(w))
        return self.op(eng, lambda e: e.tensor_scalar(out, in0, s1, s2, op0, op1), self._bufs(r), self._bufs(w))

    def stt(self, out, in0, scalar, in1, op0, op1, r=(), w=()):
        return self.op('dve', lambda e: e.scalar_tensor_tensor(out, in0, scalar, in1, op0, op1), self._bufs(r), self._bufs(w))

    def cp(self, eng, out, in_, r=(), w=()):
        if eng == 'act':
            return self.op('act', lambda e: e.copy(out, in_), self._bufs(r), self._bufs(w))
        return self.op(eng, lambda e: e.tensor_copy(out, in_), self._bufs(r), self._bufs(w))

    def ld(self, out, in_, r=(), w=(), eng='sp', **kw):
        return self.dma(eng, lambda e: e.dma_start(out=out, in_=in_, **kw), self._bufs(r), self._bufs(w))

    def _last_ops(self, skip=None):
        out = []
        for e in ENGS:
            if e != skip and self.q[e]:
                for o in reversed(self.q[e]):
                    if not o.is_dma and o.fn is not None:
                        out.append(o)
                        break
            h = self.dma_hist[e]
            out.extend(h[-self.ring:])
        return out

    def barrier(self):
        deps_for = {e: self._last_ops(skip=e) for e in ENGS}
        for e in ENGS:
            op = _Op(e, len(self.q[e]), None, False)
            op.prog = self
            op.deps = deps_for[e]
            self.q[e].append(op)

    def finish(self):
        op = _Op('sp', len(self.q['sp']), None, False)
        op.prog = self
        for e in ENGS:
            op.deps.extend(self.dma_hist[e][-self.ring:])
        self.q['sp'].append(op)

    def emit(self):
        nc = self.nc
        self.set_lane(0)
        for e in ENGS:
            self.q[e].sort(key=lambda o: o.key)
            for i, o in enumerate(self.q[e]):
                o.idx = i
            h = [o for o in self.q[e] if o.is_dma]
            self.dma_hist[e] = h
            for k, o in enumerate(h):
                kg = k + self.sems.d_cnt[e]
                o.dslot = kg % self.ring
                o.dval = 16 * (kg // self.ring + 1)
                if k >= self.ring and h[k - self.ring] not in o.deps:
                    o.deps.append(h[k - self.ring])
        self.barrier()
        self.finish()
        for e in ENGS:
            seen = {p: -1 for p in ENGS}
            seen_dma = {}
            for o in self.q[e]:
                best = {}
                for d in o.deps:
                    if d.is_dma:
                        key = (d.eng, d.dslot)
                        if seen_dma.get(key, 0) >= d.dval:
                            continue
                        seen_dma[key] = d.dval
                        o.waits.append(d)
                    else:
                        if d.eng == 'pe' and e == 'pe':
                            continue
                        if d.fn is None:
                            continue
                        if d.idx <= seen[d.eng]:
                            continue
                        if d.eng not in best or best[d.eng].idx < d.idx:
                            best[d.eng] = d
                for p, d in best.items():
                    seen[p] = d.idx
                    d.needs_inc = True
                    o.waits.append(d)
        for e in ENGS:
            c = self.sems.c_off[e]
            for o in self.q[e]:
                if o.needs_inc:
                    c += 1
                o.cnt = c
            self.sems.c_off[e] = c
            self.sems.d_cnt[e] += len(self.dma_hist[e])
        n_wait = sum(len(o.waits) for e in ENGS for o in self.q[e])
        n_ops = sum(len(self.q[e]) for e in ENGS)
        self.stats = dict(n_ops=n_ops, n_wait=n_wait, per_eng={e: len(self.q[e]) for e in ENGS})
        with contextlib.ExitStack() as st:
            csem = self.sems.csem
            dsem = self.sems.dsem
            block = st.enter_context(nc.Block())

            def run(ename, eng):
                for o in self.q[ename]:
                    for d in o.waits:
                        if d.is_dma:
                            eng.wait_ge(dsem[d.eng][d.dslot], d.dval)
                        else:
                            eng.wait_ge(csem[d.eng], d.cnt)
                    if o.fn is None:
                        continue
                    ins = o.fn(eng)
                    if o.is_dma:
                        ins.then_inc(dsem[ename][o.dslot], 16)
                    elif o.needs_inc:
                        ins.then_inc(csem[ename], 1)

            @block.tensor
            def _(eng):
                run('pe', eng)

            @block.scalar
            def _(eng):
                run('act', eng)

            @block.vector
            def _(eng):
                run('dve', eng)

            @block.gpsimd
            def _(eng):
                run('pool', eng)

            @block.sync
            def _(eng):
                run('sp', eng)


D = 1024
IN_COLS = 3344
ALPHA = float(4 ** 0.25)
LN_EPS = 1e-5
RMS_EPS = 1e-5
GN_EPS = 64e-5
DECAY_C = float(np.exp(-0.5))
NEU_DT = BF16

W_NAMES = ["ln0_g", "ln0_b", "w_in", "conv_w", "conv_b", "dt_bias", "a_log", "d_skip", "ssd_norm_g",
           "mu_rwkv", "w0", "w_up", "a0", "a_up", "g_up", "k_k", "k_a", "r_k", "lnx_g", "lnx_b", "w_out",
           "ln1_g", "ln1_b", "w_fc", "w_proj", "ln2_g", "ln2_b"]
W_SHAPES = {
    "ln0_g": [1024], "ln0_b": [1024], "w_in": [2, 1024, 3344], "conv_w": [2, 5, 1024], "conv_b": [2, 1024],
    "dt_bias": [2, 2, 8], "a_log": [2, 2, 8], "d_skip": [2, 8], "ssd_norm_g": [2, 512], "mu_rwkv": [2, 1792],
    "w0": [2, 2, 512], "w_up": [2, 2, 64, 512], "a0": [2, 2, 512], "a_up": [2, 2, 64, 512], "g_up": [2, 128, 512],
    "k_k": [2, 512], "k_a": [2, 512], "r_k": [2, 8, 64], "lnx_g": [2, 512], "lnx_b": [2, 512],
    "w_out": [2, 1024, 1024], "ln1_g": [2, 1024], "ln1_b": [2, 1024], "w_fc": [2, 1024, 4096],
    "w_proj": [2, 4096, 1024], "ln2_g": [2, 1024], "ln2_b": [2, 1024],
}


def make_consts():
    i = np.arange(128)
    ident = np.eye(128, dtype=np.float32)
    U = (i[:, None] <= i[None, :]).astype(np.float32)
    Lo = (i[:, None] >= i[None, :]).astype(np.float32)
    Us = (i[:, None] < i[None, :]).astype(np.float32)
    Ls = (i[:, None] > i[None, :]).astype(np.float32)
    ones = np.ones((128, 128), np.float32)
    return np.ascontiguousarray(np.concatenate([ident, U, Lo, Us, Ls, ones], axis=1))


class Tile:
    def __init__(self, t, name=''):
        self.t = t
        self.buf = Buf(name)

    def __getitem__(self, k):
        return self.t[k]


class Ctx:
    pass


class _Quad:
    def __init__(self, tiles):
        self.tiles = tiles
        self.buf = Buf('quad')

    def __getitem__(self, k):
        p, i, c = k
        return self.tiles[i][p, c]


def layer_norm_tile(P, C, src, dst, g_t, b_t, stat, eps, r, w, geng='pool'):
    st = stat
    P.op('dve', lambda e: e.bn_stats(st[:, 0:6], src[:, 0:512]), P._bufs(r), [st.buf])
    P.op('dve', lambda e: e.bn_stats(st[:, 6:12], src[:, 512:1024]), P._bufs(r), [st.buf])
    P.op('dve', lambda e: e.bn_aggr(st[:, 12:14], st[:, 0:12].rearrange("p (a b) -> p a b", b=6)), [st.buf], [st.buf])
    P.ts('dve', st[:, 14:15], st[:, 13:14], eps, None, ALU.add, r=[st], w=[st])
    P.act(st[:, 15:16], st[:, 14:15], AF.Sqrt, r=[st], w=[st])
    P.op('dve', lambda e: e.reciprocal(st[:, 16:17], st[:, 15:16]), [st.buf], [st.buf])
    P.stt(st[:, 17:18], st[:, 12:13], -1.0, st[:, 16:17], ALU.mult, ALU.mult, r=[st], w=[st])
    P.act(dst, src, AF.Identity, bias=st[:, 17:18], scale=st[:, 16:17], r=list(r) + [st], w=w)
    P.tt(geng, dst, dst, g_t[:, :], ALU.mult, r=list(w) + [g_t], w=w)
    P.tt('dve', dst, dst, b_t[:, :], ALU.add, r=list(w) + [b_t], w=w)


_UID = [0]


_SBUSE = [0]
SB_BUDGET = 176 * 1024


def _sb_release(n):
    _SBUSE[0] -= n


def _sb(st, nc, name, shape, dt):
    _UID[0] += 1
    name = '%s_%d' % (name, _UID[0])
    n = int(np.prod(shape[1:])) * (2 if dt == BF16 else 4)
    n = (n + 31) // 32 * 32
    _SBUSE[0] += n
    assert _SBUSE[0] <= SB_BUDGET, ('SBUF budget exceeded', name, _SBUSE[0])
    t = Tile(st.enter_context(nc.sbuf_tensor(name, shape, dt)), name)
    st.callback(_sb_release, n)
    return t


def _psum(st, nc, n=8):
    _UID[0] += 1
    return [Tile(st.enter_context(nc.psum_tensor('ps%d_%d' % (i, _UID[0]), [128, 512], F32)), 'ps%d' % i) for i in range(n)]


class RR:
    def __init__(self, items):
        self.items = items
        self.i = 0

    def __call__(self):
        x = self.items[self.i % len(self.items)]
        self.i += 1
        return x


def load_bcast(P, tile, src_row, n, eng='sp'):
    P.ld(tile[:, 0:n], src_row.partition_broadcast(128), w=[tile], eng=eng)


def stage_consts(C):
    P = Prog(C.nc, C.sems)
    P.ld(C.cst[:, :], C.cst_d[:, :], w=[C.cst])
    P.emit()


def stage_A(C, l):
    nc, T, NS = C.nc, C.T, C.NS
    TB = min(512, T)
    NJ = TB // 128
    ident = C.cst[:, 0:128]
    with contextlib.ExitStack() as st:
        Win = _sb(st, nc, 'Win', [128, 8, IN_COLS], BF16)
        Wb = [Buf('Win%d' % k) for k in range(8)]
        hin = [_sb(st, nc, 'hin%d' % i, [128, D], F32) for i in range(2)]
        hT = [_sb(st, nc, 'hT%d' % i, [128, 8, TB], BF16) for i in range(2)]
        stat = [_sb(st, nc, 'stat%d' % i, [128, 32], F32) for i in range(2)]
        zo = [_sb(st, nc, 'zo%d' % i, [128, 512], F32) for i in range(2)]
        dto = [_sb(st, nc, 'dto%d' % i, [128, 16], F32) for i in range(2)]
        fo = [_sb(st, nc, 'fo%d' % i, [128, TB], BF16) for i in range(3)]
        if l == 0:
            g0 = _sb(st, nc, 'g0', [128, D], F32)
            b0 = _sb(st, nc, 'b0', [128, D], F32)
        ps = _psum(st, nc)
        P = Prog(nc, C.sems)
        for kc in range(8):
            P.ld(Win[:, kc, :], C.w['w_in'][l, kc * 128:(kc + 1) * 128, :], w=[Wb[kc]], eng='pool', max_dma_last_dim=4096)
        if l == 0:
            load_bcast(P, g0, C.w['ln0_g'], D)
            load_bcast(P, b0, C.w['ln0_b'], D)
        psr = RR(ps)
        evr = RR(['act', 'dve'])
        zor, dtor, forr = RR(zo), RR(dto), RR(fo)
        ti = 0
        for b in range(NS):
            src = C.x[b] if l == 0 else C.hbuf[b]
            for tb in range(T // TB):
                hTt = hT[(b * (T // TB) + tb) % 2]
                for j in range(NJ):
                    rows = slice(tb * TB + j * 128, tb * TB + (j + 1) * 128)
                    hi = hin[ti % 2]
                    P.ld(hi[:, :], src[rows, :], w=[hi])
                    if l == 0:
                        layer_norm_tile(P, C, hi[:, :], hi[:, :], g0, b0, stat[ti % 2], LN_EPS, r=[hi], w=[hi])
                        P.ld(C.hbuf[b, rows, :], hi[:, :], r=[hi])
                    for half in range(2):
                        bank = psr()
                        for q in range(4):
                            kc = half * 4 + q
                            P.tr(bank[:, q * 128:(q + 1) * 128], hi[:, kc * 128:(kc + 1) * 128], ident, r=[hi, C.cst], w=[bank])
                        P.cp(evr(), hTt[:, half * 4:half * 4 + 4, j * 128:(j + 1) * 128],
                             bank[:, :].rearrange("p (a b) -> p a b", b=128), r=[bank], w=[hTt])
                    ti += 1
                for j in range(NJ):
                    rows = slice(tb * TB + j * 128, tb * TB + (j + 1) * 128)
                    bank = psr()
                    for kc in range(8):
                        P.mm(bank[:, :], hTt[:, kc, j * 128:(j + 1) * 128], Win[:, kc, 0:512], start=(kc == 0), stop=(kc == 7),
                             r=[hTt, Wb[kc]], w=[bank])
                    z = zor()
                    P.cp(evr(), z[:, :], bank[:, :], r=[bank], w=[z])
                    P.ld(C.zbuf[b, rows, :], z[:, :], r=[z])
                    bank = psr()
                    for kc in range(8):
                        P.mm(bank[:, 0:16], hTt[:, kc, j * 128:(j + 1) * 128], Win[:, kc, 1536:1552], start=(kc == 0), stop=(kc == 7),
                             r=[hTt, Wb[kc]], w=[bank])
                    dt = dtor()
                    P.cp(evr(), dt[:, :], bank[:, 0:16], r=[bank], w=[dt])
                    P.ld(C.dtbuf[b, rows, :], dt[:, :], r=[dt])
                for cc in range(22):
                    col0 = 512 + cc * 128 if cc < 8 else 1552 + (cc - 8) * 128
                    bank = psr()
                    for kc in range(8):
                        P.mm(bank[:, 0:TB], Win[:, kc, col0:col0 + 128], hTt[:, kc, :], start=(kc == 0), stop=(kc == 7),
                             r=[hTt, Wb[kc]], w=[bank])
                    f = forr()
                    P.cp(evr(), f[:, :], bank[:, 0:TB], r=[bank], w=[f])
                    if cc < 8:
                        dst = C.xbcT[b, cc * 128:(cc + 1) * 128, tb * TB:(tb + 1) * TB]
                    else:
                        dst = C.rwT[b, (cc - 8) * 128:(cc - 7) * 128, tb * TB:(tb + 1) * TB]
                    P.ld(dst, f[:, :], r=[f])
        P.emit()
        C.stats.append(('A', P.stats))


def load_w_bf16(P, tile, bufs, src, nk, eng='pool'):
    for kc in range(nk):
        P.ld(tile[:, kc, :], src[kc * 128:(kc + 1) * 128, :], w=[bufs[kc]], eng=eng, max_dma_last_dim=4096)


def stage_D1(C, l, pre):
    nc, T, NS = C.nc, C.T, C.NS
    ident = C.cst[:, 0:128]
    with contextlib.ExitStack() as st:
        Wo = _sb(st, nc, 'Wo', [128, 8, D], BF16)
        Wob = [Buf('Wo%d' % k) for k in range(8)]
        g1 = _sb(st, nc, 'g1', [128, D], F32)
        b1 = _sb(st, nc, 'b1', [128, D], F32)
        ym = [_sb(st, nc, 'ym%d' % i, [128, D], F32) for i in range(2)]
        yT = [_sb(st, nc, 'yT%d' % i, [128, 8, 128], BF16) for i in range(2)]
        hr = [_sb(st, nc, 'hr%d' % i, [128, D], F32) for i in range(1)]
        t1 = [_sb(st, nc, 't1%d' % i, [128, D], F32) for i in range(1)]
        stat = [_sb(st, nc, 'stat%d' % i, [128, 32], F32) for i in range(2)]
        ps = _psum(st, nc)
        P = Prog(nc, C.sems)
        load_w_bf16(P, Wo, Wob, C.w['w_out'][l], 8)
        load_bcast(P, g1, C.w['ln1_g'][l], D)
        load_bcast(P, b1, C.w['ln1_b'][l], D)
        Wf, Wfb, Wp, Wpb = pre
        load_w_bf16(P, Wf, Wfb, C.w['w_fc'][l], 8)
        load_w_bf16(P, Wp, Wpb, C.w['w_proj'][l], 32)
        psr = RR(ps)
        evr = RR(['act', 'dve'])
        ti = 0
        for b in range(NS):
            for c in range(T // 128):
                rows = slice(c * 128, (c + 1) * 128)
                y, yt, h, t, sx = ym[ti % 2], yT[ti % 2], hr[0], t1[0], stat[ti % 2]
                P.ld(y[:, :], C.ymix[b, rows, :], w=[y])
                P.ld(h[:, :], C.hbuf[b, rows, :], w=[h])
                for half in range(2):
                    bank = psr()
                    for q in range(4):
                        kc = half * 4 + q
                        P.tr(bank[:, q * 128:(q + 1) * 128], y[:, kc * 128:(kc + 1) * 128], ident, r=[y, C.cst], w=[bank])
                    P.cp(evr(), yt[:, half * 4:half * 4 + 4, :], bank[:, :].rearrange("p (a b) -> p a b", b=128), r=[bank], w=[yt])
                for half in range(2):
                    bank = psr()
                    for kc in range(8):
                        P.mm(bank[:, :], yt[:, kc, :], Wo[:, kc, half * 512:(half + 1) * 512], start=(kc == 0), stop=(kc == 7),
                             r=[yt, Wob[kc]], w=[bank])
                    P.stt(t[:, half * 512:(half + 1) * 512], h[:, half * 512:(half + 1) * 512], ALPHA, bank[:, :], ALU.mult, ALU.add,
                          r=[h, bank], w=[t])
                layer_norm_tile(P, C, t[:, :], t[:, :], g1, b1, sx, LN_EPS, r=[t], w=[t], geng='dve')
                P.ld(C.h1buf[b, rows, :], t[:, :], r=[t])
                ti += 1
        P.emit()
        C.stats.append(('D1', P.stats))


def stage_D2(C, l, last, pre):
    nc, T, NS = C.nc, C.T, C.NS
    ident = C.cst[:, 0:128]
    TB = 256
    with contextlib.ExitStack() as st:
        Wf, Wfb, Wp, Wpb = pre
        g2 = _sb(st, nc, 'g2', [128, D], F32)
        b2 = _sb(st, nc, 'b2', [128, D], F32)
        h1 = [_sb(st, nc, 'h1%d' % i, [128, D], F32) for i in range(2)]
        h1T = _sb(st, nc, 'h1T', [128, 8, TB], BF16)
        aT = _sb(st, nc, 'aT', [128, 32, TB], BF16)
        tmp = [_sb(st, nc, 'tmp%d' % i, [128, TB], F32) for i in range(2)]
        t2 = [_sb(st, nc, 't2%d' % i, [128, D], F32) for i in range(1)]
        stat = [_sb(st, nc, 'stat%d' % i, [128, 32], F32) for i in range(2)]
        ps = _psum(st, nc)
        P = Prog(nc, C.sems)
        load_bcast(P, g2, C.w['ln2_g'][l], D)
        load_bcast(P, b2, C.w['ln2_b'][l], D)
        psr = RR(ps)
        evr = RR(['act', 'dve'])
        tmr = RR(tmp)
        ti = 0
        for b in range(NS):
            dstb = C.out[b] if last else C.hbuf[b]
            for tb in range(T // TB):
                for j in range(2):
                    rows = slice(tb * TB + j * 128, tb * TB + (j + 1) * 128)
                    h = h1[j]
                    P.ld(h[:, :], C.h1buf[b, rows, :], w=[h])
                    for half in range(2):
                        bank = psr()
                        for q in range(4):
                            kc = half * 4 + q
                            P.tr(bank[:, q * 128:(q + 1) * 128], h[:, kc * 128:(kc + 1) * 128], ident, r=[h, C.cst], w=[bank])
                        P.cp(evr(), h1T[:, half * 4:half * 4 + 4, j * 128:(j + 1) * 128],
                             bank[:, :].rearrange("p (a b) -> p a b", b=128), r=[bank], w=[h1T])
                for fc in range(32):
                    bank = psr()
                    for kc in range(8):
                        P.mm(bank[:, 0:TB], Wf[:, kc, fc * 128:(fc + 1) * 128], h1T[:, kc, :], start=(kc == 0), stop=(kc == 7),
                             r=[h1T, Wfb[kc]], w=[bank])
                    tm = tmr()
                    if fc % 2 == 0:
                        P.act(tm[:, :], bank[:, 0:TB], AF.Relu, r=[bank], w=[tm])
                    else:
                        P.ts('dve', tm[:, :], bank[:, 0:TB], 0.0, None, ALU.max, r=[bank], w=[tm])
                    P.tt('pool', aT[:, fc, :], tm[:, :], tm[:, :], ALU.mult, r=[tm], w=[aT])
                for j in range(2):
                    rows = slice(tb * TB + j * 128, tb * TB + (j + 1) * 128)
                    h = h1[j]
                    t = t2[0]
                    for half in range(2):
                        bank = psr()
                        for fc in range(32):
                            P.mm(bank[:, :], aT[:, fc, j * 128:(j + 1) * 128], Wp[:, fc, half * 512:(half + 1) * 512],
                                 start=(fc == 0), stop=(fc == 31), r=[aT, Wpb[fc]], w=[bank])
                        P.stt(t[:, half * 512:(half + 1) * 512], h[:, half * 512:(half + 1) * 512], ALPHA, bank[:, :], ALU.mult, ALU.add,
                              r=[h, bank], w=[t])
                    layer_norm_tile(P, C, t[:, :], t[:, :], g2, b2, stat[ti % 2], LN_EPS, r=[t], w=[t])
                    P.ld(dstb[rows, :], t[:, :], r=[t])
                    ti += 1
        P.emit()
        C.stats.append(('D2', P.stats))


def bc3(ap2, n, axis):
    k = ap2.shape[1]
    if axis == 1:
        return ap2.unsqueeze(1).broadcast_to([128, n, k])
    return ap2.unsqueeze(2).broadcast_to([128, k, n])


def stage_B(C, l, b):
    nc, T, NCH = C.nc, C.T, C.NCH
    TB = min(512, T)
    cst = C.cst
    ident, U, Lo, Us, Ls, ones = (cst[:, i * 128:(i + 1) * 128] for i in range(6))
    with contextlib.ExitStack() as st:
        sb = lambda n, sh, dt: _sb(st, nc, n, sh, dt)
        XTb = [Buf('XT%d' % g) for g in range(8)]
        BT = sb('BT', [128, 2, T], BF16)
        CT = sb('CT', [128, 2, T], BF16)
        xbf = sb('xbf', [128, NCH, 512], BF16)
        Btok = sb('Btok', [128, NCH, 256], BF16)
        dtraw = sb('dtraw', [128, NCH, 16], F32)
        dtv = sb('dtv', [128, NCH, 16], F32)
        av = sb('av', [128, NCH, 16], F32)
        dtb = sb('dtb', [128, 16], F32)
        negA = sb('negA', [128, 16], F32)
        dsk8 = sb('dsk8', [128, 8], F32)
        dsk = sb('dsk', [128, 512], F32)
        ng = sb('ng', [128, 512], F32)
        ps = _psum(st, nc)
        st0 = contextlib.ExitStack()
        sb0 = lambda n, sh, dt: _sb(st0, nc, n, sh, dt)
        XT = sb0('XT', [128, 8, T + 4], BF16)
        cw6 = sb0('cw6', [6, 1024], F32)
        cwb = sb0('cwb', [128, 8, 6], F32)
        Dg = sb0('Dg', [128, 5, 8, 128], BF16)
        cbrow = sb0('cbrow', [1, 768], F32)
        cbrow_bf = sb0('cbrow_bf', [1, 768], BF16)
        ones_bf = sb0('ones_bf', [1, 128], BF16)
        P = Prog(nc, C.sems)
        psr = RR(ps)
        P.op('pool', lambda e: e.memset(XT[:, :, 0:2], 0.0), [], XTb)
        P.op('pool', lambda e: e.memset(XT[:, :, T + 2:T + 4], 0.0), [], XTb)
        for g in range(8):
            P.ld(XT[:, g, 2:T + 2], C.xbcT[b, g * 128:(g + 1) * 128, :], w=[XTb[g]])
        P.ld(cw6[0:5, :], C.w['conv_w'][l], w=[cw6])
        P.ld(cw6[5:6, :], C.w['conv_b'][l:l + 1, :], w=[cw6])
        P.ld(cbrow[0:1, :], C.w['conv_b'][l:l + 1, 0:768], w=[cbrow])
        P.cp('dve', cbrow_bf[0:1, :], cbrow[0:1, :], r=[cbrow], w=[cbrow_bf])
        P.cp('dve', ones_bf[0:1, :], ones[0:1, :], r=[cst], w=[ones_bf])
        load_bcast(P, dtb, C.w['dt_bias'][l].rearrange("a b -> (a b)"), 16)
        load_bcast(P, negA, C.w['a_log'][l].rearrange("a b -> (a b)"), 16)
        load_bcast(P, dsk8, C.w['d_skip'][l], 8)
        load_bcast(P, ng, C.w['ssd_norm_g'][l], 512)
        P.act(negA[:, :], negA[:, :], AF.Exp, r=[negA], w=[negA])
        P.ts('dve', negA[:, :], negA[:, :], -1.0, None, ALU.mult, r=[negA], w=[negA])
        P.cp('dve', dsk[:, :].rearrange("p (h q) -> p h q", q=64), bc3(dsk8[:, :], 64, 2), r=[dsk8], w=[dsk])
        bank = psr()
        for g in range(8):
            P.tr(bank[:, g * 6:(g + 1) * 6], cw6[0:6, g * 128:(g + 1) * 128], ident[0:6, 0:6], r=[cw6, cst], w=[bank])
        P.cp('dve', cwb[:, :, :], bank[:, 0:48].rearrange("p (g k) -> p g k", k=6), r=[bank], w=[cwb])
        er = RR(['dve', 'pool'])
        for k in range(5):
            for g in range(8):
                P.ts(er(), Dg[:, k, g, :], ident, cwb[:, g, k:k + 1], None, ALU.mult, r=[cwb, cst], w=[Dg])
        P.ld(dtraw[:, :, :], C.dtbuf[b].rearrange("(c p) k -> p c k", p=128), w=[dtraw])
        P.tt('dve', dtv[:, :, :], dtraw[:, :, :], bc3(dtb[:, :], NCH, 1), ALU.add, r=[dtraw, dtb], w=[dtv])
        P.act(dtv[:, :, :], dtv[:, :, :], AF.Exp, r=[dtv], w=[dtv])
        P.ts('dve', dtv[:, :, :], dtv[:, :, :], 1.0, None, ALU.add, r=[dtv], w=[dtv])
        P.act(dtv[:, :, :], dtv[:, :, :], AF.Ln, r=[dtv], w=[dtv])
        P.tt('dve', av[:, :, :], dtv[:, :, :], bc3(negA[:, :], NCH, 1), ALU.mult, r=[dtv, negA], w=[av])
        for c in range(NCH):
            for (g0, ng_, dst, boff) in ((0, 4, xbf, 0), (4, 2, Btok, 512)):
                bank = psr()
                n = ng_ * 128
                P.mm(bank[:, 0:n], ones_bf[0:1, :], cbrow_bf[0:1, boff:boff + n], start=True, stop=False,
                     r=[ones_bf, cbrow_bf], w=[bank])
                for gi in range(ng_):
                    g = g0 + gi
                    for k in range(5):
                        P.mm(bank[:, gi * 128:(gi + 1) * 128], XT[:, g, c * 128 + k:c * 128 + k + 128], Dg[:, k, g, :],
                             start=False, stop=(gi == ng_ - 1 and k == 4), r=[XTb[g], Dg], w=[bank])
                P.act(dst[:, c, :], bank[:, 0:n], AF.Silu, r=[bank], w=[dst])
        for tb in range(T // TB):
            for gi in range(4):
                g = 4 + gi
                bank = psr()
                for k in range(5):
                    P.mm(bank[:, 0:TB], Dg[:, k, g, :], XT[:, g, tb * TB + k:tb * TB + k + TB], start=(k == 0), stop=(k == 4),
                         r=[XTb[g], Dg], w=[bank])
                dst = BT if gi < 2 else CT
                P.act(dst[:, gi % 2, tb * TB:(tb + 1) * TB], bank[:, 0:TB], AF.Silu, bias=cwb[:, g, 5:6], r=[bank, cwb], w=[dst])
        P.emit()
        C.stats.append(('B0', P.stats))
        st0.close()
        ysc = sb('ysc', [128, NCH, 16], F32)
        cs_sb = [sb('cs_sb%d' % i, [128, 32], F32) for i in range(2)]
        dd = [sb('dd%d' % i, [128, 16], F32) for i in range(2)]
        et = [sb('et%d' % i, [128, 16], F32) for i in range(2)]
        wd = [sb('wd%d' % i, [128, 16], F32) for i in range(2)]
        xw = [sb('xw%d' % i, [128, 2, 512], BF16) for i in range(2)]
        Srun = sb('Srun', [128, 2, 512], F32)
        Sin = sb('Sin', [128, NCH, 2, 512], BF16)
        rhsS = [sb('rhsS%d' % i, [128, 2, 8, 128], F32) for i in range(2)]
        E = [sb('E%d' % i, [128, 2, 8, 128], F32) for i in range(2)]
        SM = [sb('SM%d' % i, [128, 2, 2, 128], F32) for i in range(2)]
        G = [sb('G%d' % i, [128, 2, 8, 128], BF16) for i in range(2)]
        ta = [sb('ta%d' % i, [128, 512], F32) for i in range(2)]
        tb_ = [sb('tb%d' % i, [128, 512], F32) for i in range(2)]
        yt = [sb('yt%d' % i, [128, 512], F32) for i in range(2)]
        zt = [sb('zt%d' % i, [128, 512], F32) for i in range(2)]
        sq = [sb('sq%d' % i, [128, 512], F32) for i in range(2)]
        ss = [sb('ss%d' % i, [128, 8], F32) for i in range(2)]
        P = Prog(nc, C.sems)
        psr = RR(ps)
        P.op('pool', lambda e: e.memset(Srun[:, :, :], 0.0), [], [Srun.buf])
        for i in range(NCH):
            cc = (i, NCH - 1 - i)
            k2 = i % 2
            bank = psr()
            for d in range(2):
                P.mm(bank[:, d * 8:(d + 1) * 8], U if d == 0 else Lo, av[:, cc[d], d * 8:(d + 1) * 8], r=[cst, av], w=[bank])
                P.mm(bank[:, 16 + d * 8:16 + (d + 1) * 8], ones, av[:, cc[d], d * 8:(d + 1) * 8], r=[cst, av], w=[bank])
            P.cp('act', cs_sb[k2][:, :], bank[:, 0:32], r=[bank], w=[cs_sb[k2]])
            P.tt('dve', dd[k2][:, :], cs_sb[k2][:, 16:32], cs_sb[k2][:, 0:16], ALU.subtract, r=[cs_sb[k2]], w=[dd[k2]])
            P.act(dd[k2][:, :], dd[k2][:, :], AF.Exp, r=[dd[k2]], w=[dd[k2]])
            P.act(et[k2][:, :], cs_sb[k2][:, 16:32], AF.Exp, r=[cs_sb[k2]], w=[et[k2]])
            for d in range(2):
                sl = slice(d * 8, (d + 1) * 8)
                P.act(ysc[:, cc[d], sl], cs_sb[k2][:, sl], AF.Exp, r=[cs_sb[k2]], w=[ysc])
                P.tt('dve', wd[k2][:, sl], dd[k2][:, sl], dtv[:, cc[d], sl], ALU.mult, r=[dd[k2], dtv], w=[wd[k2]])
                P.tt('dve', xw[k2][:, d, :].rearrange("p (h q) -> p h q", q=64),
                     xbf[:, cc[d], :].rearrange("p (h q) -> p h q", q=64), bc3(wd[k2][:, sl], 64, 2), ALU.mult,
                     r=[xbf, wd[k2]], w=[xw[k2]])
            for d in range(2):
                bank = psr()
                for g in range(2):
                    P.mm(bank[:, g * 256:(g + 1) * 256], Btok[:, cc[d], g * 128:(g + 1) * 128], xw[k2][:, d, g * 256:(g + 1) * 256],
                         r=[Btok, xw[k2]], w=[bank])
                P.cp('pool', Sin[:, cc[d], d, :], Srun[:, d, :], r=[Srun], w=[Sin])
                P.tt('dve', Srun[:, d, :].rearrange("p (h q) -> p h q", q=64), Srun[:, d, :].rearrange("p (h q) -> p h q", q=64),
                     bc3(et[k2][:, d * 8:(d + 1) * 8], 64, 2), ALU.mult, r=[Srun, et[k2]], w=[Srun])
                P.tt('dve', Srun[:, d, :], Srun[:, d, :], bank[:, :], ALU.add, r=[Srun, bank], w=[Srun])
        def pass2(par):
            psr = RR(ps[4 * par:4 * par + 4])
            for c in range(par, NCH, 2):
                k2 = c % 2
                rows = slice(c * 128, (c + 1) * 128)
                csl = slice(c * 128, (c + 1) * 128)
                P.ld(zt[k2][:, :], C.zbuf[b, rows, :], w=[zt[k2]])
                for d in range(2):
                    P.tt('dve' if d == 0 else 'pool', rhsS[k2][:, d, :, :], bc3(U if d == 0 else Lo, 8, 1),
                         bc3(av[:, c, d * 8:(d + 1) * 8], 128, 2), ALU.mult, r=[cst, av], w=[rhsS[k2]])
                yield
                for d in range(2):
                    for hh in range(2):
                        bank = psr()
                        P.mm(bank[:, :], Ls if d == 0 else Us, rhsS[k2][:, d, hh * 4:(hh + 1) * 4, :].rearrange("p a b -> p (a b)"),
                             r=[cst, rhsS[k2]], w=[bank])
                        P.act(E[k2][:, d, hh * 4:(hh + 1) * 4, :].rearrange("p a b -> p (a b)"), bank[:, :], AF.Exp, r=[bank], w=[E[k2]])
                yield
                bank = psr()
                for g in range(2):
                    P.mm(bank[:, g * 128:(g + 1) * 128], BT[:, g, csl], CT[:, g, csl], r=[BT, CT], w=[bank])
                for d in range(2):
                    P.tt('dve', SM[k2][:, d, :, :], bank[:, 0:256].rearrange("p (g l) -> p g l", l=128), bc3(U if d == 0 else Lo, 2, 1),
                         ALU.mult, r=[bank, cst], w=[SM[k2]])
                yield
                for d in range(2):
                    for h in range(8):
                        P.stt(G[k2][:, d, h, :], E[k2][:, d, h, :], dtv[:, c, d * 8 + h:d * 8 + h + 1], SM[k2][:, d, h // 4, :],
                              ALU.mult, ALU.mult, r=[E[k2], dtv, SM[k2]], w=[G[k2]])
                yield
                bY1 = psr()
                for h in range(8):
                    for d in range(2):
                        P.mm(bY1[:, h * 64:(h + 1) * 64], G[k2][:, d, h, :], xbf[:, c, h * 64:(h + 1) * 64], start=(d == 0), stop=(d == 1),
                             r=[G[k2], xbf], w=[bY1])
                yield
                bY2 = [psr(), psr()]
                for d in range(2):
                    for g in range(2):
                        P.mm(bY2[d][:, g * 256:(g + 1) * 256], CT[:, g, csl], Sin[:, c, d, g * 256:(g + 1) * 256], r=[CT, Sin], w=[bY2[d]])
                yield
                v3 = lambda ap: ap.rearrange("p (h q) -> p h q", q=64)
                P.tt('dve', v3(ta[k2][:, :]), v3(bY2[0][:, :]), bc3(ysc[:, c, 0:8], 64, 2), ALU.mult, r=[bY2[0], ysc], w=[ta[k2]])
                P.tt('dve', v3(tb_[k2][:, :]), v3(bY2[1][:, :]), bc3(ysc[:, c, 8:16], 64, 2), ALU.mult, r=[bY2[1], ysc], w=[tb_[k2]])
                y = yt[k2]
                P.tt('pool', y[:, :], ta[k2][:, :], tb_[k2][:, :], ALU.add, r=[ta[k2], tb_[k2]], w=[y])
                P.tt('dve', y[:, :], y[:, :], bY1[:, :], ALU.add, r=[y, bY1], w=[y])
                P.tt('pool', sq[k2][:, :], xbf[:, c, :], dsk[:, :], ALU.mult, r=[xbf, dsk], w=[sq[k2]])
                P.tt('pool', y[:, :], y[:, :], sq[k2][:, :], ALU.add, r=[y, sq[k2]], w=[y])
                yield
                P.act(zt[k2][:, :], zt[k2][:, :], AF.Silu, r=[zt[k2]], w=[zt[k2]])
                P.tt('dve', y[:, :], y[:, :], zt[k2][:, :], ALU.mult, r=[y, zt[k2]], w=[y])
                P.act(sq[k2][:, :], y[:, :], AF.Square, r=[y], w=[sq[k2]])
                P.op('dve', lambda e, o=ss[k2][:, 0:2], i_=sq[k2][:, :].rearrange("p (g q) -> p g q", q=256): e.reduce_sum(o, i_, axis=AX.X),
                     [sq[k2].buf], [ss[k2].buf])
                yield
                P.ts('dve', ss[k2][:, 2:4], ss[k2][:, 0:2], 1.0 / 256.0, RMS_EPS, ALU.mult, ALU.add, r=[ss[k2]], w=[ss[k2]])
                P.act(ss[k2][:, 4:6], ss[k2][:, 2:4], AF.Sqrt, r=[ss[k2]], w=[ss[k2]])
                P.op('dve', lambda e, o=ss[k2][:, 6:8], i_=ss[k2][:, 4:6]: e.reciprocal(o, i_), [ss[k2].buf], [ss[k2].buf])
                for g in range(2):
                    P.ts('dve', y[:, g * 256:(g + 1) * 256], y[:, g * 256:(g + 1) * 256], ss[k2][:, 6 + g:7 + g], None, ALU.mult,
                         r=[y, ss[k2]], w=[y])
                P.tt('pool', y[:, :], y[:, :], ng[:, :], ALU.mult, r=[y, ng], w=[y])
                P.ld(C.ymix[b, rows, 0:512], y[:, :], r=[y])
                yield
        for par_ in range(2):
            P.set_lane(par_ + 1)
            for _ in pass2(par_):
                pass
        P.set_lane(0)
        P.emit()
        C.stats.append(('B', P.stats))


def stage_C(C, l, b):
    nc, T, NCH = C.nc, C.T, C.NCH
    cst = C.cst
    ident, U, Lo, Us, Ls, ones = (cst[:, i * 128:(i + 1) * 128] for i in range(6))
    v3 = lambda ap: ap.rearrange("p (h q) -> p h q", q=64)
    with contextlib.ExitStack() as st:
        sb = lambda n, sh, dt: _sb(st, nc, n, sh, dt)
        murow = sb('murow', [14, 128], F32)
        muT = sb('muT', [128, 3, 14], F32)
        Dm = sb('Dm', [128, 14, 128], BF16)
        Dh = sb('Dh', [128, 14, 128], BF16)
        kkb = sb('kkb', [128, 512], F32)
        kab = sb('kab', [128, 512], F32)
        rkb = sb('rkb', [128, 512], F32)
        w0b = sb('w0b', [128, 2, 512], F32)
        a0b = sb('a0b', [128, 2, 512], F32)
        LW = sb('LW', [128, 2, 512], BF16)
        GU = sb('GU', [128, 512], BF16)
        MK = sb('MK', [128, 2, 512], F32)
        MKa = sb('MKa', [128, 2, 128], F32)
        g32 = sb('g32', [128, 512], F32)
        def alloc_inst():
            RTc = [sb('RTc%d' % i, [128, 14, 130], BF16) for i in range(2)]
            bon = sb('bon', [128, NCH, 8], F32)
            r32 = sb('r32', [128, 512], F32)
            k32 = sb('k32', [128, 512], F32)
            v32 = sb('v32', [128, 512], F32)
            vbf = sb('vbf', [128, 512], BF16)
            LT = sb('LT', [128, 128], BF16)
            sg = sb('sg', [128, 128], BF16)
            lw32 = sb('lw32', [128, 512], F32)
            a32 = sb('a32', [128, 512], F32)
            kk = sb('kk', [128, 512], F32)
            tmp = sb('tmp', [128, 512], F32)
            tmp2 = sb('tmp2', [128, 512], F32)
            kd = sb('kd', [128, 512], F32)
            bb = sb('bb', [128, 512], F32)
            sm8 = sb('sm8', [128, 32], F32)
            gC = sb('gC', [128, 4], F32)
            Ep = sb('Ep', [128, 512], F32)
            En = sb('En', [128, 512], F32)
            bt_bf = sb('bt_bf', [128, 512], BF16)
            kdt_bf = sb('kdt_bf', [128, 512], BF16)
            kt_bf = sb('kt_bf', [128, 512], NEU_DT)
            KR = sb('KR', [128, 4, 2, 128], BF16)
            btT = sb('btT', [128, 4, 128], BF16)
            kdtT = sb('kdtT', [128, 4, 128], BF16)
            R3 = sb('R3', [128, 8, 3, 128], BF16)
            NM = [sb('NM%d' % i, [128, 8, 2, 128], NEU_DT) for i in range(2)]
            X = [sb('X%d' % i, [128, 8, 128], NEU_DT) for i in range(2)]
            Z1 = sb('Z1', [128, 512], NEU_DT)
            Ut = sb('Ut', [128, 512], F32)
            WT = sb('WT', [128, 4, 128], BF16)
            Mst = sb('Mst', [128, 4, 128], F32)
            Mbf = sb('Mbf', [128, 4, 128], BF16)
            Un = sb('Un', [128, 512], BF16)
            Yo = [sb('Yo%d' % i, [128, 512], F32) for i in range(2)]
            return dict(locals())
        IT = [alloc_inst(), alloc_inst()]
        ps = _psum(st, nc)
        P = Prog(nc, C.sems)
        psr = RR(ps)
        evr = RR(['act'])
        P.ld(murow[0:14, :], C.w['mu_rwkv'][l].rearrange("(g p) -> g p", p=128), w=[murow])
        bank = psr()
        P.tr(bank[:, 0:14], murow[0:14, :], ident[0:14, 0:14], r=[murow, cst], w=[bank])
        P.cp('dve', muT[:, 0, :], bank[:, 0:14], r=[bank], w=[muT])
        P.ts('dve', muT[:, 1, :], muT[:, 0, :], -1.0, 1.0, ALU.mult, ALU.add, r=[muT], w=[muT])
        P.ts('dve', muT[:, 2, :], muT[:, 0, :], 0.5, None, ALU.mult, r=[muT], w=[muT])
        er = RR(['dve', 'pool'])
        for g in range(14):
            P.ts(er(), Dm[:, g, :], ident, muT[:, 1, g:g + 1], None, ALU.mult, r=[muT, cst], w=[Dm])
            P.ts(er(), Dh[:, g, :], ident, muT[:, 2, g:g + 1], None, ALU.mult, r=[muT, cst], w=[Dh])
        load_bcast(P, kkb, C.w['k_k'][l], 512)
        load_bcast(P, kab, C.w['k_a'][l], 512)
        load_bcast(P, rkb, C.w['r_k'][l].rearrange("a b -> (a b)"), 512)
        for d in range(2):
            P.ld(w0b[:, d, :], C.w['w0'][l, d].partition_broadcast(128), w=[w0b])
            P.ld(a0b[:, d, :], C.w['a0'][l, d].partition_broadcast(128), w=[a0b])
            P.ld(LW[0:64, d, :], C.w['w_up'][l, d], w=[LW], eng='pool')
            P.ld(LW[64:128, d, :], C.w['a_up'][l, d], w=[LW], eng='pool')
        P.ld(GU[:, :], C.w['g_up'][l], w=[GU], eng='pool')
        for d in range(2):
            strict, incl, strict_ts = (Us, U, Ls) if d == 0 else (Ls, Lo, Us)
            P.ts('dve', MK[:, d, 0:128], strict, -1.0, None, ALU.mult, r=[cst], w=[MK])
            P.cp('dve', MK[:, d, 128:256], incl, r=[cst], w=[MK])
            P.cp('dve', MK[:, d, 256:384], strict, r=[cst], w=[MK])
            P.cp('dve', MK[:, d, 384:512], incl, r=[cst], w=[MK])
            P.ts('dve', MKa[:, d, :], strict_ts, -1.0, None, ALU.mult, r=[cst], w=[MKa])
        taps = (Dh, Dm, Dh)
        ydb = [[Buf('yd') for _ in range(NCH)] for _ in range(2)]
        vgb = [[Buf('vg') for _ in range(NCH)] for _ in range(2)]
        def inst(d):
            I_ = IT[d]
            psr = RR(ps[4 * d:4 * d + 4])
            (RTc, r32, k32, v32, vbf, LT, sg, lw32, a32, kk, tmp, tmp2, kd, bb, sm8, gC, Ep, En, bt_bf, kdt_bf, kt_bf, KR, btT, kdtT, R3, NM, X, Z1, Ut, WT, Mst, Mbf, Un, Yo, bon) = (I_[n] for n in ('RTc', 'r32', 'k32', 'v32', 'vbf', 'LT', 'sg', 'lw32', 'a32', 'kk', 'tmp', 'tmp2', 'kd', 'bb', 'sm8', 'gC', 'Ep', 'En', 'bt_bf', 'kdt_bf', 'kt_bf', 'KR', 'btT', 'kdtT', 'R3', 'NM', 'X', 'Z1', 'Ut', 'WT', 'Mst', 'Mbf', 'Un', 'Yo', 'bon'))
            P.op('pool', lambda e: e.memset(Mst[:, :, :], 0.0), [], [Mst.buf])
            P.op('pool', lambda e: e.memset(Mbf[:, :, :], 0.0), [], [Mbf.buf])
            for ci in range(NCH):
                c = ci if d == 0 else NCH - 1 - ci
                rows = slice(c * 128, (c + 1) * 128)
                RT = RTc[ci % 2]
                lo, hi = max(c * 128 - 1, 0), min(c * 128 + 129, T)
                if c == 0:
                    P.op('pool', lambda e, t=RT: e.memset(t[:, :, 0:1], 0.0), [], [RT.buf])
                if c == NCH - 1:
                    P.op('pool', lambda e, t=RT: e.memset(t[:, :, 129:130], 0.0), [], [RT.buf])
                P.ld(RT[:, :, lo - (c * 128 - 1):hi - (c * 128 - 1)],
                     C.rwT[b].rearrange("(g p) t -> p g t", p=128)[:, :, lo:hi], w=[RT])
                for (g0, dst) in ((0, r32), (4, k32), (8, v32)):
                    bank = psr()
                    for gi in range(4):
                        g = g0 + gi
                        for k in range(3):
                            P.mm(bank[:, gi * 128:(gi + 1) * 128], RT[:, g, k:k + 128], taps[k][:, g, :],
                                 start=(k == 0), stop=(k == 2), r=[RT, Dm, Dh], w=[bank])
                    if dst is v32 and d == 1:
                        P.cp(evr(), vbf[:, :], bank[:, :], r=[bank], w=[vbf])
                    else:
                        P.cp(evr(), dst[:, :], bank[:, :], r=[bank], w=[dst])
                    yield
                if d == 0:
                    P.cp('act', vbf[:, :], v32[:, :], r=[v32], w=[vbf])
                bank = psr()
                for gi in range(2):
                    g = 12 + gi
                    for k in range(3):
                        P.mm(bank[:, gi * 128:(gi + 1) * 128], taps[k][:, g, :], RT[:, g, k:k + 128],
                             start=(k == 0), stop=(k == 2), r=[RT, Dm, Dh], w=[bank])
                P.act(LT[0:64, :], bank[0:64, 0:128], AF.Tanh, r=[bank], w=[LT])
                P.cp('dve', LT[64:128, :], bank[64:128, 0:128], r=[bank], w=[LT])
                P.act(sg[:, :], bank[:, 128:256], AF.Sigmoid, r=[bank], w=[sg])
                yield
                bW, bA = psr(), psr()
                P.mm(bW[:, :], LT[0:64, :], LW[0:64, d, :], r=[LT, LW], w=[bW])
                yield
                P.mm(bA[:, :], LT[64:128, :], LW[64:128, d, :], r=[LT, LW], w=[bA])
                yield
                P.tt('dve', lw32[:, :], bW[:, :], w0b[:, d, :], ALU.add, r=[bW, w0b], w=[lw32])
                yield
                P.act(lw32[:, :], lw32[:, :], AF.Sigmoid, r=[lw32], w=[lw32])
                yield
                P.act(lw32[:, :], lw32[:, :], AF.Identity, scale=-DECAY_C, r=[lw32], w=[lw32])
                yield
                P.tt('dve', a32[:, :], bA[:, :], a0b[:, d, :], ALU.add, r=[bA, a0b], w=[a32])
                yield
                P.act(a32[:, :], a32[:, :], AF.Sigmoid, r=[a32], w=[a32])
                yield
                if d == 0:
                    bG = psr()
                    P.mm(bG[:, :], sg[:, :], GU[:, :], r=[sg, GU], w=[bG])
                    P.cp('act', g32[:, :], bG[:, :], r=[bG], w=[g32])
                    P.ld(C.vg[0, rows, :], v32[:, :], r=[v32], w=[vgb[0][c]])
                    P.ld(C.vg[1, rows, :], g32[:, :], r=[g32], w=[vgb[1][c]])
                yield
                P.tt('pool', kk[:, :], k32[:, :], kkb[:, :], ALU.mult, r=[k32, kkb], w=[kk])
                yield
                P.act(tmp[:, :], kk[:, :], AF.Square, r=[kk], w=[tmp])
                yield
                P.op('dve', lambda e, o=sm8[:, 0:8], i_=v3(tmp[:, :]): e.reduce_sum(o, i_, axis=AX.X), [tmp.buf], [sm8.buf])
                yield
                P.act(sm8[:, 8:16], sm8[:, 0:8], AF.Sqrt, r=[sm8], w=[sm8])
                yield
                P.ts('dve', sm8[:, 8:16], sm8[:, 8:16], 1e-12, None, ALU.max, r=[sm8], w=[sm8])
                yield
                P.op('dve', lambda e, o=sm8[:, 16:24], i_=sm8[:, 8:16]: e.reciprocal(o, i_), [sm8.buf], [sm8.buf])
                yield
                P.tt('dve', v3(kk[:, :]), v3(kk[:, :]), bc3(sm8[:, 16:24], 64, 2), ALU.mult, r=[kk, sm8], w=[kk])
                yield
                P.stt(tmp2[:, :], a32[:, :], -1.0, kab[:, :], ALU.add, ALU.mult, r=[a32, kab], w=[tmp2])
                yield
                P.stt(kd[:, :], tmp2[:, :], 1.0, k32[:, :], ALU.add, ALU.mult, r=[tmp2, k32], w=[kd])
                yield
                P.tt('pool', bb[:, :], kk[:, :], a32[:, :], ALU.mult, r=[kk, a32], w=[bb])
                yield
                P.tt('pool', tmp[:, :], r32[:, :], kd[:, :], ALU.mult, r=[r32, kd], w=[tmp])
                yield
                P.tt('pool', tmp[:, :], tmp[:, :], rkb[:, :], ALU.mult, r=[tmp, rkb], w=[tmp])
                yield
                P.op('dve', lambda e, o=bon[:, c, :], i_=v3(tmp[:, :]): e.reduce_sum(o, i_, axis=AX.X), [tmp.buf], [bon.buf])
                yield
                yield
                bC = psr()
                P.mm(bC[:, :], U if d == 0 else Lo, lw32[:, :], r=[cst, lw32], w=[bC])
                yield
                bT = psr()
                for jg in range(4):
                    P.mm(bT[:, 2 * jg:2 * jg + 2], lw32[:, jg * 128:(jg + 1) * 128], ones[:, 0:2], r=[lw32, cst], w=[bT])
                P.act(gC[:, :], bT[:, 0:8].rearrange("p (j two) -> p j two", two=2)[:, :, 0], AF.Exp, r=[bT], w=[gC])
                yield
                P.act(Ep[:, :], bC[:, :], AF.Exp, r=[bC], w=[Ep])
                yield
                P.act(En[:, :], bC[:, :], AF.Exp, scale=-1.0, r=[bC], w=[En])
                yield
                P.tt('dve', tmp2[:, :], bC[:, :], lw32[:, :], ALU.subtract, r=[bC, lw32], w=[tmp2])
                yield
                P.act(tmp2[:, :], tmp2[:, :], AF.Exp, r=[tmp2], w=[tmp2])
                yield
                P.tt('pool', r32[:, :], r32[:, :], Ep[:, :], ALU.mult, r=[r32, Ep], w=[r32])
                yield
                P.tt('dve', kk[:, :], kk[:, :], tmp2[:, :], ALU.mult, r=[kk, tmp2], w=[kk])
                yield
                P.tt('pool', bb[:, :], bb[:, :], En[:, :], ALU.mult, r=[bb, En], w=[bb])
                yield
                P.tt('dve', kd[:, :], kd[:, :], En[:, :], ALU.mult, r=[kd, En], w=[kd])
                yield
                P.cp('act', bt_bf[:, :], bb[:, :], r=[bb], w=[bt_bf])
                yield
                P.cp('act', kdt_bf[:, :], kd[:, :], r=[kd], w=[kdt_bf])
                yield
                P.cp('act', kt_bf[:, :], kk[:, :], r=[kk], w=[kt_bf])
                yield
                yield
                for (src, dstap, dbuf) in ((kk, KR[:, :, 0, :], KR), (r32, KR[:, :, 1, :], KR), (bb, btT[:, :, :], btT), (kd, kdtT[:, :, :], kdtT)):
                    bank = psr()
                    for jg in range(4):
                        P.tr(bank[:, jg * 128:(jg + 1) * 128], src[:, jg * 128:(jg + 1) * 128], ident, r=[src, cst], w=[bank])
                    P.cp(evr(), dstap, bank[:, :].rearrange("p (a b) -> p a b", b=128), r=[bank], w=[dbuf])
                    yield
                yield
                for h in range(8):
                    jg, rs = h // 2, slice((h % 2) * 64, (h % 2 + 1) * 64)
                    bM = psr()
                    krr = KR[rs, jg, :, :].rearrange("p a b -> p (a b)")
                    P.mm(bM[:, 0:256], btT[rs, jg, :], krr, r=[btT, KR], w=[bM])
                    P.mm(bM[:, 256:512], kdtT[rs, jg, :], krr, r=[kdtT, KR], w=[bM])
                    P.tt('dve', NM[0][:, h, 0, :], bM[:, 0:128], MK[:, d, 0:128], ALU.mult, r=[bM, MK], w=[NM[0]])
                    P.tt('dve', R3[:, h, :, :].rearrange("p a b -> p (a b)"), bM[:, 128:512], MK[:, d, 128:512], ALU.mult,
                         r=[bM, MK], w=[R3])
                    if h % 2 == 1:
                        yield
                yield
                for hh in range(2):
                    bank = psr()
                    rs = slice(hh * 64, (hh + 1) * 64)
                    for jg in range(4):
                        P.mm(bank[:, jg * 128:(jg + 1) * 128], KR[rs, jg, 0, :], btT[rs, jg, :], r=[KR, btT], w=[bank])
                    P.tt('dve', NM[0][:, hh:8:2, 1, :], bank[:, :].rearrange("p (a b) -> p a b", b=128),
                         bc3(MKa[:, d, :], 4, 1), ALU.mult, r=[bank, MKa], w=[NM[0]])
                P.tt('dve', X[0][:, :, :], NM[0][:, :, 0, :], bc3(ident, 8, 1), ALU.add, r=[NM[0], cst], w=[X[0]])
                yield
                for j in range(6):
                    yield
                    cur, nxt = NM[j % 2], NM[(j + 1) % 2]
                    Xc, Xn = X[j % 2], X[(j + 1) % 2]
                    for hp in range(4):
                        bank = psr()
                        for q in range(2):
                            h = hp * 2 + q
                            if j < 5:
                                P.mm(bank[:, q * 256:q * 256 + 128], cur[:, h, 1, :], cur[:, h, 0, :], r=[cur], w=[bank])
                            P.mm(bank[:, q * 256 + 128:q * 256 + 256], cur[:, h, 0, :], cur[:, h, 1, :], r=[cur], w=[bank])
                        if j < 5:
                            P.cp(evr(), nxt[:, hp * 2:hp * 2 + 2, :, :].rearrange("p a b c -> p (a b c)"), bank[:, :], r=[bank], w=[nxt])
                        else:
                            P.cp(evr(), nxt[:, hp * 2:hp * 2 + 2, 1, :],
                                 bank[:, :].rearrange("p (a b c) -> p a b c", b=2, c=128)[:, :, 1, :], r=[bank], w=[nxt])
                    yield
                    for hp in range(2):
                        bank = psr()
                        for q in range(4):
                            h = hp * 4 + q
                            P.mm(bank[:, q * 128:(q + 1) * 128], nxt[:, h, 1, :], Xc[:, h, :], r=[nxt, Xc], w=[bank])
                        P.tt('dve', Xn[:, hp * 4:(hp + 1) * 4, :].rearrange("p a b -> p (a b)"), bank[:, :],
                             Xc[:, hp * 4:(hp + 1) * 4, :].rearrange("p a b -> p (a b)"), ALU.add, r=[bank, Xc], w=[Xn])
                yield
                XF = X[0]
                bZ = psr()
                for h in range(8):
                    P.mm(bZ[:, h * 64:(h + 1) * 64], R3[:, h, 1, :], vbf[:, h * 64:(h + 1) * 64], r=[R3, vbf], w=[bZ])
                P.cp('act', Z1[:, :], bZ[:, :], r=[bZ], w=[Z1])
                bU = psr()
                for h in range(8):
                    P.mm(bU[:, h * 64:(h + 1) * 64], XF[:, h, :], Z1[:, h * 64:(h + 1) * 64], r=[XF, Z1], w=[bU])
                P.cp('act', Ut[:, :], bU[:, :], r=[bU], w=[Ut])
                for hp in range(2):
                    bank = psr()
                    for q in range(4):
                        h = hp * 4 + q
                        jg = h // 2
                        P.mm(bank[:, q * 128:(q + 1) * 128], kt_bf[:, jg * 128:(jg + 1) * 128], XF[:, h, :], r=[kt_bf, XF], w=[bank])
                    for q in range(4):
                        h = hp * 4 + q
                        jg, rs = h // 2, slice((h % 2) * 64, (h % 2 + 1) * 64)
                        P.cp(evr(), WT[rs, jg, :], bank[rs, q * 128:(q + 1) * 128], r=[bank], w=[WT])
                yield
                bP = psr()
                for jg in range(4):
                    P.mm(bP[:, jg * 128:(jg + 1) * 128], WT[:, jg, :], Mbf[:, jg, :], r=[WT, Mbf], w=[bP])
                P.stt(Un[:, :], bP[:, :], -1.0, Ut[:, :], ALU.mult, ALU.subtract, r=[bP, Ut], w=[Un])
                yield
                bY = psr()
                for jg in range(4):
                    P.mm(bY[:, jg * 128:(jg + 1) * 128], KR[:, jg, 1, :], Mbf[:, jg, :], start=True, stop=False, r=[KR, Mbf], w=[bY])
                    for h in (2 * jg, 2 * jg + 1):
                        hs = slice(h * 64, (h + 1) * 64)
                        P.mm(bY[:, hs], R3[:, h, 2, :], vbf[:, hs], start=False, stop=False, r=[R3, vbf], w=[bY])
                    for h in (2 * jg, 2 * jg + 1):
                        hs = slice(h * 64, (h + 1) * 64)
                        P.mm(bY[:, hs], R3[:, h, 0, :], Un[:, hs], start=False, stop=(h == 2 * jg + 1), r=[R3, Un], w=[bY])
                yo = Yo[ci % 2]
                P.cp('act', yo[:, :], bY[:, :], r=[bY], w=[yo])
                P.ld(C.ydir[d, rows, :], yo[:, :], r=[yo], w=[ydb[d][c]])
                yield
                bS = psr()
                for jg in range(4):
                    js = slice(jg * 128, (jg + 1) * 128)
                    P.mm(bS[:, js], bt_bf[:, js], Un[:, js], start=True, stop=False, r=[bt_bf, Un], w=[bS])
                    P.mm(bS[:, js], kdt_bf[:, js], vbf[:, js], start=False, stop=True, r=[kdt_bf, vbf], w=[bS])
                for hh in range(2):
                    rs = slice(hh * 64, (hh + 1) * 64)
                    src = bS[rs, :].rearrange("p (j q) -> p j q", q=128)[:, :, hh * 64:(hh + 1) * 64]
                    mv = Mst[rs, :, hh * 64:(hh + 1) * 64]
                    P.tt('dve', mv, mv, src, ALU.add, r=[Mst, bS], w=[Mst])
                    P.tt('dve', mv, mv, gC[rs, :].unsqueeze(2).broadcast_to([64, 4, 64]), ALU.mult, r=[Mst, gC], w=[Mst])
                P.cp('act', Mbf[:, :, :], Mst[:, :, :], r=[Mst], w=[Mbf])
                yield
        alive = [inst(0), inst(1)]
        while alive:
            for g_ in list(alive):
                try:
                    next(g_)
                except StopIteration:
                    alive.remove(g_)
        P.emit()
        C.stats.append(('C', P.stats))
        fy = [_Quad([IT[i][n] for n in ('r32', 'k32', 'v32', 'lw32')]) for i in range(2)]
        lgb, lbb = IT[1]['a32'], IT[1]['kk']
        sm8, tmp = IT[0]['sm8'], IT[0]['tmp']
        bon0, bon1 = IT[0]['bon'], IT[1]['bon']
        P = Prog(nc, C.sems)
        load_bcast(P, lgb, C.w['lnx_g'][l], 512)
        load_bcast(P, lbb, C.w['lnx_b'][l], 512)
        def fin(par):
          sm8, tmp = IT[par]['sm8'], IT[par]['tmp']
          for c in range(par, NCH, 2):
            rows = slice(c * 128, (c + 1) * 128)
            f = fy[par]
            P.ld(f[:, 0, :], C.ydir[0, rows, :], r=[ydb[0][c]], w=[f])
            P.ld(f[:, 1, :], C.ydir[1, rows, :], r=[ydb[1][c]], w=[f])
            P.ld(f[:, 2, :], C.vg[0, rows, :], r=[vgb[0][c]], w=[f])
            P.ld(f[:, 3, :], C.vg[1, rows, :], r=[vgb[1][c]], w=[f])
            y = f[:, 0, :]
            P.tt('dve', y, y, f[:, 1, :], ALU.add, r=[f], w=[f])
            P.op('dve', lambda e, o=sm8[:, 0:8], i_=v3(y): e.reduce_sum(o, i_, axis=AX.X), [f.buf], [sm8.buf])
            P.act(tmp[:, :], y, AF.Square, r=[f], w=[tmp])
            P.op('dve', lambda e, o=sm8[:, 8:16], i_=v3(tmp[:, :]): e.reduce_sum(o, i_, axis=AX.X), [tmp.buf], [sm8.buf])
            P.ts('dve', sm8[:, 0:16], sm8[:, 0:16], 1.0 / 64.0, None, ALU.mult, r=[sm8], w=[sm8])
            P.tt('dve', sm8[:, 16:24], sm8[:, 0:8], sm8[:, 0:8], ALU.mult, r=[sm8], w=[sm8])
            P.tt('dve', sm8[:, 16:24], sm8[:, 8:16], sm8[:, 16:24], ALU.subtract, r=[sm8], w=[sm8])
            P.ts('dve', sm8[:, 16:24], sm8[:, 16:24], GN_EPS, None, ALU.add, r=[sm8], w=[sm8])
            P.act(sm8[:, 16:24], sm8[:, 16:24], AF.Sqrt, r=[sm8], w=[sm8])
            P.op('dve', lambda e, o=sm8[:, 24:32], i_=sm8[:, 16:24]: e.reciprocal(o, i_), [sm8.buf], [sm8.buf])
            P.tt('dve', v3(y), v3(y), bc3(sm8[:, 0:8], 64, 2), ALU.subtract, r=[f, sm8], w=[f])
            P.tt('dve', v3(y), v3(y), bc3(sm8[:, 24:32], 64, 2), ALU.mult, r=[f, sm8], w=[f])
            P.tt('pool', y, y, lgb[:, :], ALU.mult, r=[f, lgb], w=[f])
            P.tt('pool', y, y, lbb[:, :], ALU.add, r=[f, lbb], w=[f])
            P.tt('dve', sm8[:, 0:8], bon0[:, c, :], bon1[:, c, :], ALU.add, r=[bon0, bon1, sm8], w=[sm8])
            P.tt('dve', v3(tmp[:, :]), v3(f[:, 2, :]), bc3(sm8[:, 0:8], 64, 2), ALU.mult, r=[f, sm8], w=[tmp])
            P.tt('pool', y, y, tmp[:, :], ALU.add, r=[f, tmp], w=[f])
            P.tt('pool', y, y, f[:, 3, :], ALU.mult, r=[f], w=[f])
            P.ld(C.ymix[b, rows, 512:1024], y, r=[f])
        for par_ in range(2):
            P.set_lane(par_ + 1)
            fin(par_)
        P.set_lane(0)
        P.emit()
        C.stats.append(('C', P.stats))


def build(T=2048, NS=2, depth=2, debug=False, stages='ABCD'):
    nc = bass.Bass("TRN2", target_bir_lowering=False)
    C = Ctx()
    C.nc, C.T, C.NS, C.NCH = nc, T, NS, T // 128
    C.stats = []
    C.x = nc.dram_tensor("x", [NS, T, D], F32, kind="ExternalInput").ap()
    C.w = {n: nc.dram_tensor(n, W_SHAPES[n], F32, kind="ExternalInput").ap() for n in W_NAMES}
    C.cst_d = nc.dram_tensor("cst", [128, 768], F32, kind="ExternalInput").ap()
    C.out = nc.dram_tensor("out", [NS, T, D], F32, kind="ExternalOutput").ap()
    kind = "ExternalOutput" if debug else "Internal"
    C.hbuf = nc.dram_tensor("hbuf", [NS, T, D], F32, kind=kind).ap()
    C.h1buf = nc.dram_tensor("h1buf", [NS, T, D], F32, kind=kind).ap()
    C.zbuf = nc.dram_tensor("zbuf", [NS, T, 512], F32, kind=kind).ap()
    C.dtbuf = nc.dram_tensor("dtbuf", [NS, T, 16], F32, kind=kind).ap()
    C.xbcT = nc.dram_tensor("xbcT", [NS, 1024, T], BF16, kind=kind).ap()
    C.rwT = nc.dram_tensor("rwT", [NS, 1792, T], BF16, kind=kind).ap()
    C.ymix = nc.dram_tensor("ymix", [NS, T, D], F32, kind=kind).ap()
    C.ydir = nc.dram_tensor("ydir", [2, T, 512], F32, kind=kind).ap()
    C.vg = nc.dram_tensor("vg", [3, T, 512], F32, kind=kind).ap()
    with nc.sbuf_tensor('cst_sb', [128, 768], F32) as cst_sb, contextlib.ExitStack() as semstack:
        C.cst = Tile(cst_sb, 'cst')
        C.sems = SemState(nc, 8, semstack)
        stage_consts(C)
        for l in range(depth):
            if 'A' in stages:
                stage_A(C, l)
            for b in range(NS):
                if 'B' in stages:
                    stage_B(C, l, b)
                if 'C' in stages:
                    stage_C(C, l, b)
            if 'D' in stages:
                with contextlib.ExitStack() as wst:
                    Wf = _sb(wst, nc, 'Wf', [128, 8, 4096], BF16)
                    Wp = _sb(wst, nc, 'Wp', [128, 32, D], BF16)
                    pre = (Wf, [Buf('Wf%d' % k) for k in range(8)], Wp, [Buf('Wp%d' % k) for k in range(32)])
                    stage_D1(C, l, pre)
                    stage_D2(C, l, (l == depth - 1), pre)
    return nc, C


_CACHE = {}


def kernel(**inputs):
    n_cores = 8
    x = np.ascontiguousarray(np.asarray(inputs["x"], dtype=np.float32))
    B, T, _ = x.shape
    NS = B // n_cores
    key = (T, NS)
    if key not in _CACHE:
        _CACHE[key] = build(T=T, NS=NS, depth=2, debug=False)[0]
    nc = _CACHE[key]
    cst = make_consts()
    ws = {n: np.ascontiguousarray(np.asarray(inputs[n], dtype=np.float32)) for n in W_NAMES}
    in_maps = []
    for i in range(n_cores):
        m = dict(ws)
        m["x"] = np.ascontiguousarray(x[i * NS:(i + 1) * NS])
        m["cst"] = cst
        in_maps.append(m)
    res = run_bass_kernel_spmd(nc, in_maps, core_ids=list(range(n_cores)))
    return np.concatenate([np.asarray(r["out"], dtype=np.float32) for r in res.results], axis=0)
```
